# Optimizing a Trainium2 kernel written in Bass

```python
import math
import jax
import jax.numpy as jnp
from jax import lax
import numpy as np

D_MODEL = 2048
BATCH = 8
SEQ = 2048
DEPTH = 2
DEC_BATCH = 32
DEC_SEQ = 4
PAST_LEN = 8192
PAGE_SIZE = 128

N_A = DEPTH // 2
N_B = DEPTH - N_A
HEAD_DIM = 128
MIX_WIDTH = D_MODEL
MEM_HEADS = 4
MEM_LEN = 256
MEM_WIDTH = MEM_HEADS * HEAD_DIM
TOK_WIDTH = MIX_WIDTH - MEM_WIDTH
H_A = TOK_WIDTH // HEAD_DIM
DK_A = HEAD_DIM
DV_A = HEAD_DIM
HGRN_CHUNK = 32
H_B = TOK_WIDTH // HEAD_DIM
G_B = 2
HPG = H_B // G_B
CMP_BLOCK = 32
CMP_STRIDE = 16
SEL_BLOCK = 64
N_SEL = 16
WINDOW = 512
SEL_QBLOCK = 16
WIN_QBLOCK = 128
FORCE_BONUS = 1e4
D_FF = 11 * D_MODEL // 4
CONV_W = 3
N_KV_ROWS = 4
EPS = 1e-6
NEG = -1e30
SCALE = HEAD_DIM ** -0.5

kernel_name = 'yoco_hgrn2_nsa_memory_step'


def rmsnorm(x, g):
    xf = x.astype(jnp.float32)
    y = xf * lax.rsqrt(jnp.mean(xf * xf, axis=-1, keepdims=True) + EPS)
    return (y * g.astype(jnp.float32)).astype(x.dtype)


def masked_softmax(s, mask):
    s = jnp.where(mask, s, NEG)
    m = jnp.max(s, axis=-1, keepdims=True)
    p = jnp.exp(s - m) * mask
    return p / jnp.maximum(jnp.sum(p, axis=-1, keepdims=True), 1e-30)


def alibi_slopes():
    h = np.arange(1, H_B + 1, dtype=np.float32)
    return jnp.asarray(np.exp2(-8.0 * h / H_B).astype(np.float32)).reshape(G_B, HPG)


def hgrn2_chunked(q, k, v, logf, s0):
    B, T, H, _ = q.shape
    C = math.gcd(T, HGRN_CHUNK)
    n = T // C

    def chunks(a):
        return a.reshape(B, n, C, *a.shape[2:]).swapaxes(0, 1)

    causal = jnp.tril(jnp.ones((C, C), dtype=bool))[None, :, :, None, None]

    def step(S, inp):
        qc, kc, vc, lc = inp
        b = jnp.cumsum(lc, axis=1)
        diff = b[:, :, None] - b[:, None, :]
        decay = jnp.where(causal, jnp.exp(jnp.where(causal, diff, 0.0)), 0.0)
        a = jnp.einsum('bthk,bshk,btshk->btsh', qc, kc, decay)
        o = jnp.einsum('btsh,bshv->bthv', a, vc) + jnp.einsum('bthk,bhkv->bthv', qc * jnp.exp(b), S)
        b_last = b[:, -1]
        S = jnp.exp(b_last)[..., None] * S + jnp.einsum(
            'bshk,bshv->bhkv', kc * jnp.exp(b_last[:, None] - b), vc)
        return S, o

    S, o = lax.scan(step, s0, (chunks(q), chunks(k), chunks(v), chunks(logf)))
    return o.swapaxes(0, 1).reshape(B, T, H, -1), S


def hgrn2_mixer(xn, w_in, lb, gnorm, s0):
    B, T, _ = xn.shape
    W = TOK_WIDTH
    u = xn @ w_in

    def hs(a):
        return a.reshape(B, T, H_A, -1)

    q = hs(jax.nn.silu(u[..., :W].astype(jnp.float32)))
    fg = lb + (1.0 - lb) * jax.nn.sigmoid(u[..., W:2 * W].astype(jnp.float32))
    k = hs(1.0 - fg)
    logf = hs(jnp.log(fg))
    v = hs(u[..., 2 * W:3 * W].astype(jnp.float32))
    og = hs(jax.nn.sigmoid(u[..., 3 * W:4 * W].astype(jnp.float32)))
    mem_q = u[..., 4 * W:].reshape(B, T, MEM_HEADS, HEAD_DIM)
    o, s_fin = hgrn2_chunked(q, k, v, logf, s0.astype(jnp.float32))
    o = rmsnorm(o, gnorm) * og
    return o.reshape(B, T, W).astype(xn.dtype), mem_q, s_fin


def compress(seq, pe, w1, w2):
    B, L = seq.shape[:2]
    r = CMP_BLOCK // CMP_STRIDE
    nch = L // CMP_STRIDE
    nc = nch - r + 1
    ch = seq[:, :nch * CMP_STRIDE].reshape(B, nch, CMP_STRIDE, G_B, HEAD_DIM)
    w1r = w1.reshape(r, CMP_STRIDE, HEAD_DIM, HEAD_DIM)
    part = jnp.einsum('bcsgd,jsdh->jbcgh', ch, w1r)
    pre = jnp.einsum('jsd,jsdh->h', pe.reshape(r, CMP_STRIDE, HEAD_DIM), w1r)
    for j in range(r):
        pre = pre + part[j][:, j:j + nc]
    return jax.nn.gelu(pre) @ w2


def shared_kv(h, kv_past, win_past, prm):
    B, T, _ = h.shape
    kv = (rmsnorm(h, prm['kv_norm']) @ prm['w_kv_b']).reshape(B, T, N_KV_ROWS + 2, G_B, HEAD_DIM)
    rows, win_rows = kv[:, :, :N_KV_ROWS], kv[:, :, N_KV_ROWS:]
    if kv_past is None:
        full = rows
        nb = T // WIN_QBLOCK
        kidx = jnp.arange(nb)[:, None] * WIN_QBLOCK + jnp.arange(WIN_QBLOCK + WINDOW)[None, :]
        wpad = jnp.pad(win_rows, ((0, 0), (WINDOW, 0), (0, 0), (0, 0), (0, 0)))
        wblk = wpad[:, kidx]
        wkpos = kidx - WINDOW
    else:
        past = kv_past.shape[1]
        full = jnp.concatenate([kv_past.astype(rows.dtype), rows], axis=1)
        wb = win_past.shape[1]
        wblk = jnp.concatenate([win_past.astype(win_rows.dtype), win_rows], axis=1)[:, None]
        wkpos = (past - wb + jnp.arange(wb + T))[None, :]
    L = full.shape[1]
    ns = -(-L // SEL_BLOCK)
    kc = compress(full[:, :, 0], prm['cmp_pos'][0], prm['w_cmp1'][0], prm['w_cmp2'][0])
    vc = compress(full[:, :, 1], prm['cmp_pos'][1], prm['w_cmp1'][1], prm['w_cmp2'][1])
    sel = jnp.pad(full[:, :, 2:4], ((0, 0), (0, ns * SEL_BLOCK - L), (0, 0), (0, 0), (0, 0)))
    sel = sel.reshape(B, ns, SEL_BLOCK, 2, G_B, HEAD_DIM)
    shared = dict(kc=kc, vc=vc, kb=sel[:, :, :, 0], vb=sel[:, :, :, 1],
                  wk=wblk[:, :, :, 0], wv=wblk[:, :, :, 1], wkpos=wkpos)
    return shared, rows, win_rows


def cmp_branch(q, qpos, kc, vc, slopes):
    nc = kc.shape[1]
    s = jnp.einsum('btgrd,bigd->btgri', q, kc).astype(jnp.float32) * SCALE
    end = jnp.arange(nc) * CMP_STRIDE + CMP_BLOCK - 1
    dist = qpos[:, None] - end[None, :]
    mask = (dist >= 0)[None, :, None, None, :]
    s = s - slopes[None, None, :, :, None] * dist.astype(jnp.float32)[None, :, None, None, :]
    p = masked_softmax(s, mask)
    o = jnp.einsum('btgri,bigd->btgrd', p.astype(vc.dtype), vc)
    return o, p.sum(axis=3)


def cmp_to_sel(p, ns):
    r = SEL_BLOCK // CMP_STRIDE
    back = CMP_BLOCK // CMP_STRIDE - 1
    nc = p.shape[-1]
    pp = jnp.pad(p, [(0, 0)] * (p.ndim - 1) + [(back, r * ns - nc)])
    out = pp[..., 0:r * (ns - 1) + 1:r]
    for o in range(1, back + r):
        out = out + pp[..., o:o + r * (ns - 1) + 1:r]
    return out


def select_blocks(imp, qpos, ns):
    p = cmp_to_sel(imp, ns)
    j = jnp.arange(ns)[None, :]
    cur = (qpos // SEL_BLOCK)[:, None]
    valid = (j <= cur)[None, :, None, :]
    forced = ((j == 0) | (j == cur) | (j == cur - 1))[None, :, None, :]
    score = jnp.where(valid, p + jnp.where(forced, FORCE_BONUS, 0.0), NEG)
    _, idx = lax.top_k(score, min(N_SEL, ns))
    ok = idx <= cur[None, :, :, None]
    return idx, ok


def sel_branch(q, qpos, idx, ok, kb, vb, slopes):
    B, T = q.shape[:2]
    qb = math.gcd(T, SEL_QBLOCK)
    nq = T // qb

    def split(a):
        return a.reshape(B, nq, qb, *a.shape[2:]).swapaxes(0, 1)

    bi = jnp.arange(B)[:, None, None, None]
    gi = jnp.arange(G_B)[None, None, :, None]

    def blk(args):
        qq, pp, ii, oo = args
        kk = kb[bi, ii, :, gi]
        vv = vb[bi, ii, :, gi]
        n = ii.shape[-1]
        s = jnp.einsum('bqgrd,bqgnkd->bqgrnk', qq, kk).astype(jnp.float32) * SCALE
        kpos = ii[..., None] * SEL_BLOCK + jnp.arange(SEL_BLOCK)
        dist = pp[None, :, None, None, None] - kpos
        mask = oo[..., None] & (dist >= 0)
        s = s - slopes[None, None, :, :, None, None] * dist.astype(jnp.float32)[:, :, :, None]
        s = s.reshape(B, qb, G_B, HPG, n * SEL_BLOCK)
        p = masked_softmax(s, mask.reshape(B, qb, G_B, 1, n * SEL_BLOCK))
        p = p.reshape(B, qb, G_B, HPG, n, SEL_BLOCK)
        return jnp.einsum('bqgrnk,bqgnkd->bqgrd', p.astype(vv.dtype), vv)

    o = lax.map(blk, (split(q), qpos.reshape(nq, qb), split(idx), split(ok)))
    return o.swapaxes(0, 1).reshape(B, T, G_B, HPG, HEAD_DIM)


def win_branch(q, qpos, wk, wv, wkpos, slopes):
    B, T = q.shape[:2]
    nb = wkpos.shape[0]
    qb = T // nb
    qs = q.reshape(B, nb, qb, G_B, HPG, HEAD_DIM).swapaxes(0, 1)

    def blk(args):
        qq, pp, kk, vv, kp = args
        s = jnp.einsum('bqgrd,bkgd->bqgrk', qq, kk).astype(jnp.float32) * SCALE
        dist = pp[:, None] - kp[None, :]
        mask = (dist >= 0) & (dist < WINDOW) & (kp[None, :] >= 0)
        s = s - slopes[None, None, :, :, None] * dist.astype(jnp.float32)[None, :, None, None, :]
        p = masked_softmax(s, mask[None, :, None, None, :])
        return jnp.einsum('bqgrk,bkgd->bqgrd', p.astype(vv.dtype), vv)

    o = lax.map(blk, (qs, qpos.reshape(nb, qb), wk.swapaxes(0, 1), wv.swapaxes(0, 1), wkpos))
    return o.swapaxes(0, 1).reshape(B, T, G_B, HPG, HEAD_DIM)


def nsa_mixer(xn, w_in, sh, qpos, slopes):
    B, T, _ = xn.shape
    u = xn @ w_in
    q = u[..., :TOK_WIDTH].reshape(B, T, G_B, HPG, HEAD_DIM)
    gates = jax.nn.sigmoid(u[..., TOK_WIDTH:TOK_WIDTH + 3 * H_B].astype(jnp.float32))
    gates = gates.reshape(B, T, G_B, HPG, 3)
    mem_q = u[..., TOK_WIDTH + 3 * H_B:].reshape(B, T, MEM_HEADS, HEAD_DIM)
    o_cmp, imp = cmp_branch(q, qpos, sh['kc'], sh['vc'], slopes)
    idx, ok = select_blocks(imp, qpos, sh['kb'].shape[1])
    o_sel = sel_branch(q, qpos, idx, ok, sh['kb'], sh['vb'], slopes)
    o_win = win_branch(q, qpos, sh['wk'], sh['wv'], sh['wkpos'], slopes)
    o = (gates[..., 0:1] * o_cmp.astype(jnp.float32) + gates[..., 1:2] * o_sel.astype(jnp.float32)
         + gates[..., 2:3] * o_win.astype(jnp.float32))
    return o.reshape(B, T, TOK_WIDTH).astype(xn.dtype), mem_q


def mem_attend(q, mkv):
    B, T = q.shape[:2]
    k, v = mkv[:, :, 0], mkv[:, :, 1]
    s = jnp.einsum('bthd,bmhd->bthm', q, k.astype(q.dtype)).astype(jnp.float32) * SCALE
    p = jax.nn.softmax(s, axis=-1)
    return jnp.einsum('bthm,bmhd->bthd', p.astype(q.dtype), v.astype(q.dtype)).reshape(B, T, MEM_WIDTH)


def conv_ffn(x, buf, w_in, w_conv, b_conv, w_out):
    T = x.shape[1]
    u = x @ w_in
    a, b = u[..., :D_FF], u[..., D_FF:]
    ext = jnp.concatenate([buf.astype(a.dtype), a], axis=1)
    c = b_conv
    for j in range(CONV_W):
        c = c + ext[:, j:j + T] * w_conv[j]
    y = (jax.nn.gelu(c) * b) @ w_out
    return y, ext[:, ext.shape[1] - (CONV_W - 1):]


def trunk(x, qpos, mem_kv, hgrn_s0, conv_buf0, kv_past, win_past, prm):
    B = x.shape[0]
    slopes = alibi_slopes()
    lbs = jnp.cumsum(jax.nn.softmax(prm['lb_logits'].astype(jnp.float32), axis=0), axis=0)
    h = x
    shared, kv_rows, win_rows = None, None, None
    new_s, new_buf = [], []
    for l in range(DEPTH):
        g = prm['norm_gains'][l]
        xn = rmsnorm(h, g[0])
        if l < N_A:
            s0 = (jnp.zeros((B, H_A, DK_A, DV_A), jnp.float32) if hgrn_s0 is None else hgrn_s0[l])
            mix, mem_q, s_fin = hgrn2_mixer(xn, prm['w_in_a'][l], lbs[l], prm['hgrn_norm'][l], s0)
            new_s.append(s_fin)
        else:
            mix, mem_q = nsa_mixer(xn, prm['w_in_b'][l - N_A], shared, qpos, slopes)
        mo = mem_attend(mem_q, mem_kv[l])
        o = jnp.concatenate([mix, mo.astype(mix.dtype)], axis=-1) @ prm['w_o'][l]
        h = h + rmsnorm(o, g[1])
        buf0 = (jnp.zeros((B, CONV_W - 1, D_FF), h.dtype) if conv_buf0 is None else conv_buf0[l])
        f, nb = conv_ffn(rmsnorm(h, g[2]), buf0, prm['w_ffn_in'][l], prm['w_ffn_conv'][l],
                         prm['b_ffn_conv'][l], prm['w_ffn_out'][l])
        new_buf.append(nb)
        h = h + rmsnorm(f, g[3])
        if l == N_A - 1:
            shared, kv_rows, win_rows = shared_kv(h, kv_past, win_past, prm)
    return h, jnp.stack(new_s), jnp.stack(new_buf), kv_rows, win_rows


def setup_inputs(seed: int = 0) -> dict:
    key = jax.random.key(seed)
    ks = jax.random.split(key, 26)
    n_pages = PAST_LEN // PAGE_SIZE
    n_phys = (DEC_BATCH * n_pages * 5) // 4
    wb = min(WINDOW, PAST_LEN)

    def nrm(k, shape, scale=1.0):
        return jax.random.normal(k, shape, jnp.float32) * scale

    page_table = jax.random.permutation(ks[8], n_phys)[:DEC_BATCH * n_pages]
    page_table = page_table.reshape(DEC_BATCH, n_pages).astype(jnp.int32)
    return {
        'x_prompt': nrm(ks[0], (BATCH, SEQ, D_MODEL)),
        'x_sample': nrm(ks[1], (DEC_BATCH, DEC_SEQ, D_MODEL)),
        'mem_prompt': nrm(ks[2], (BATCH, MEM_LEN, D_MODEL)),
        'state_hgrn': nrm(ks[3], (N_A, DEC_BATCH, H_A, DK_A, DV_A), 0.5),
        'cache_conv': nrm(ks[4], (DEPTH, DEC_BATCH, CONV_W - 1, D_FF)),
        'cache_mem': nrm(ks[5], (DEPTH, DEC_BATCH, MEM_LEN, 2, MEM_HEADS, HEAD_DIM)),
        'cache_kv': nrm(ks[6], (n_phys, PAGE_SIZE, N_KV_ROWS, G_B, HEAD_DIM)),
        'cache_win': nrm(ks[7], (DEC_BATCH, wb, 2, G_B, HEAD_DIM)),
        'page_table': page_table,
        'norm_gains': 1.0 + nrm(ks[9], (DEPTH, 4, D_MODEL), 0.05),
        'w_in_a': nrm(ks[10], (N_A, D_MODEL, 4 * TOK_WIDTH + MEM_WIDTH), D_MODEL ** -0.5),
        'lb_logits': nrm(ks[11], (N_A + 1, TOK_WIDTH), 0.5),
        'hgrn_norm': 1.0 + nrm(ks[12], (N_A, H_A, DV_A), 0.05),
        'w_in_b': nrm(ks[13], (N_B, D_MODEL, TOK_WIDTH + 3 * H_B + MEM_WIDTH), D_MODEL ** -0.5),
        'w_o': nrm(ks[14], (DEPTH, MIX_WIDTH, D_MODEL), MIX_WIDTH ** -0.5),
        'w_mem_kv': nrm(ks[15], (DEPTH, D_MODEL, 2 * MEM_WIDTH), D_MODEL ** -0.5),
        'kv_norm': 1.0 + nrm(ks[16], (D_MODEL,), 0.05),
        'w_kv_b': nrm(ks[17], (D_MODEL, (N_KV_ROWS + 2) * G_B * HEAD_DIM), D_MODEL ** -0.5),
        'cmp_pos': nrm(ks[18], (2, CMP_BLOCK, HEAD_DIM), 0.5),
        'w_cmp1': nrm(ks[19], (2, CMP_BLOCK * HEAD_DIM, HEAD_DIM), (CMP_BLOCK * HEAD_DIM) ** -0.5),
        'w_cmp2': nrm(ks[20], (2, HEAD_DIM, HEAD_DIM), HEAD_DIM ** -0.5),
        'w_ffn_in': nrm(ks[21], (DEPTH, D_MODEL, 2 * D_FF), D_MODEL ** -0.5),
        'w_ffn_conv': nrm(ks[22], (DEPTH, CONV_W, D_FF), CONV_W ** -0.5),
        'b_ffn_conv': nrm(ks[23], (DEPTH, D_FF), 0.01),
        'w_ffn_out': nrm(ks[24], (DEPTH, D_FF, D_MODEL), D_FF ** -0.5),
    }


def reference(x_prompt, x_sample, mem_prompt, state_hgrn, cache_conv, cache_mem, cache_kv, cache_win,
              page_table, norm_gains, w_in_a, lb_logits, hgrn_norm, w_in_b, w_o, w_mem_kv, kv_norm,
              w_kv_b, cmp_pos, w_cmp1, w_cmp2, w_ffn_in, w_ffn_conv, b_ffn_conv, w_ffn_out):
    prm = dict(norm_gains=norm_gains, w_in_a=w_in_a, lb_logits=lb_logits, hgrn_norm=hgrn_norm,
               w_in_b=w_in_b, w_o=w_o, kv_norm=kv_norm, w_kv_b=w_kv_b, cmp_pos=cmp_pos,
               w_cmp1=w_cmp1, w_cmp2=w_cmp2, w_ffn_in=w_ffn_in, w_ffn_conv=w_ffn_conv,
               b_ffn_conv=b_ffn_conv, w_ffn_out=w_ffn_out)
    bp, tp = x_prompt.shape[:2]
    bs, ts = x_sample.shape[:2]
    mem_kv_prompt = jnp.einsum('bmd,ldk->lbmk', mem_prompt, w_mem_kv)
    mem_kv_prompt = mem_kv_prompt.reshape(DEPTH, bp, MEM_LEN, 2, MEM_HEADS, HEAD_DIM)
    y_prompt, hgrn_p, conv_p, kv_p, win_p = trunk(
        x_prompt, jnp.arange(tp, dtype=jnp.int32), mem_kv_prompt, None, None, None, None, prm)
    past_len = page_table.shape[1] * PAGE_SIZE
    kv_past = cache_kv[page_table].reshape(bs, past_len, N_KV_ROWS, G_B, HEAD_DIM)
    y_sample, hgrn_s, conv_s, kv_s, win_s = trunk(
        x_sample, past_len + jnp.arange(ts, dtype=jnp.int32), cache_mem, state_hgrn, cache_conv,
        kv_past, cache_win, prm)
    kv_prompt_pages = kv_p.reshape(bp, tp // PAGE_SIZE, PAGE_SIZE, N_KV_ROWS, G_B, HEAD_DIM)
    win_prompt = win_p[:, tp - min(WINDOW, tp):]
    return (y_prompt, y_sample, hgrn_p, hgrn_s, conv_p, conv_s, mem_kv_prompt,
            kv_prompt_pages, kv_s, win_prompt, win_s)
```

```python
import numpy as np
import concourse.bass as bass
import concourse.mybir as mybir
from concourse.bass_utils import run_bass_kernel_spmd

F32 = mybir.dt.float32
BF16 = mybir.dt.bfloat16
I32 = mybir.dt.int32
AF = mybir.ActivationFunctionType
ALU = mybir.AluOpType
AX = mybir.AxisListType

NCORES = 8
D = 2048
KC = D // 128
T = 2048
NS = 4
TS = 4
NTOK = T + NS * TS
MEM = 256
W = 1536
H = 12
DFF = 5632
EPS = 1e-6
SCALE = 128 ** -0.5


class Tok:
    __slots__ = ("name", "w", "rs", "sem", "cnt")

    def __init__(self, name=""):
        self.name = name
        self.w = None
        self.rs = []
        self.sem = None
        self.cnt = 0


class Op:
    __slots__ = ("eng", "fn", "deps", "signal", "ndma", "sem", "val", "idx")


class Prog:
    ENGS = ("sp", "act", "dve", "pool", "pe")

    def __init__(self, nc):
        self.nc = nc
        self.ops = {e: [] for e in self.ENGS}
        self.allops = []
        self.dma_toks = []
        self.extra = {}
        self.slots = []
        self.rr = 0
        self.MAXSLOTS = 58
        self.dmas_since = []
        self.last = {}

    def tok(self, name=""):
        return Tok(name)

    def _add(self, eng, fn, reads, writes, ndma=0, dtok=None):
        o = Op()
        o.eng = eng
        o.fn = fn
        o.ndma = ndma
        o.signal = False
        o.sem = None
        o.val = 0
        deps = []
        seen = set()
        for t in reads:
            if t.w is not None and id(t.w) not in seen:
                seen.add(id(t.w))
                deps.append(t.w)
        for t in writes:
            if t.w is not None and id(t.w) not in seen:
                seen.add(id(t.w))
                deps.append(t.w)
            lastr = {}
            for r in t.rs:
                if r.ndma:
                    if id(r) not in seen:
                        seen.add(id(r))
                        deps.append(r)
                else:
                    lastr[r.eng] = r
            for r in lastr.values():
                if id(r) not in seen:
                    seen.add(id(r))
                    deps.append(r)
        if eng == "pe" and ndma == 0:
            deps = [d for d in deps if not (d.eng == "pe" and d.ndma == 0)]
        if self.extra.get(eng):
            for d in self.extra[eng]:
                if id(d) not in seen and not (d.eng == eng and d.ndma == 0):
                    seen.add(id(d))
                    deps.append(d)
            self.extra[eng] = None
        o.deps = deps
        for d in deps:
            d.signal = True
        for t in reads:
            t.rs.append(o)
        for t in writes:
            t.w = o
            t.rs = []
        if ndma:
            assert dtok is not None
            if dtok.sem is None:
                if len(self.slots) < self.MAXSLOTS:
                    sl = Tok("slot%d" % len(self.slots))
                    sl.rs = 1
                    self.slots.append(sl)
                else:
                    sl = self.slots[self.rr % self.MAXSLOTS]
                    self.rr += 1
                    sl.rs += 1
                dtok.sem = sl
            sl = dtok.sem
            if sl.rs > 1 and sl.w is not None and id(sl.w) not in seen:
                seen.add(id(sl.w))
                deps.append(sl.w)
                sl.w.signal = True
            sl.w = o
            sl.cnt += ndma
            o.sem = sl
            o.val = 16 * sl.cnt
        o.idx = len(self.ops[eng])
        if ndma:
            self.dmas_since.append(o)
        else:
            self.last[eng] = o
        self.ops[eng].append(o)
        self.allops.append(o)
        return o

    def barrier(self):
        deps = list(self.last.values()) + list(self.dmas_since)
        self.dmas_since = []
        for e in self.ENGS:
            self.extra[e] = list(deps)

    def op(self, eng, fn, reads=(), writes=()):
        return self._add(eng, fn, list(reads), list(writes))

    def dma(self, eng, fn, ndma, dtok, reads=(), writes=()):
        return self._add(eng, fn, list(reads), list(writes), ndma=ndma, dtok=dtok)

    def emit(self):
        nc = self.nc
        import contextlib
        with contextlib.ExitStack() as es:
            esem = {e: es.enter_context(nc.semaphore("sem_" + e)) for e in self.ENGS}
            for i, t in enumerate(self.slots):
                t.sem = es.enter_context(nc.semaphore("dsem%d" % i))
            for e in self.ENGS:
                c = 0
                for o in self.ops[e]:
                    if o.ndma == 0:
                        if o.signal:
                            c += 1
                            o.sem = esem[e]
                            o.val = c
                    else:
                        o.sem = o.sem.sem
            final_waits = [(t.sem, 16 * t.cnt) for t in self.slots]

            def run(ename, eng):
                waited = {}
                for o in self.ops[ename]:
                    for d in o.deps:
                        k = id(d.sem)
                        if waited.get(k, 0) >= d.val:
                            continue
                        waited[k] = d.val
                        eng.wait_ge(d.sem, d.val)
                    r = o.fn(eng)
                    if o.ndma:
                        assert len(r) == o.ndma, (len(r), o.ndma)
                        for ins in r:
                            ins.then_inc(o.sem, 16)
                    elif o.signal:
                        r.then_inc(o.sem, 1)
                if ename == "sp":
                    for s, v in final_waits:
                        if waited.get(id(s), 0) < v:
                            eng.wait_ge(s, v)

            with nc.Block() as block:
                @block.sync
                def _(e):
                    run("sp", e)

                @block.scalar
                def _(e):
                    run("act", e)

                @block.vector
                def _(e):
                    run("dve", e)

                @block.gpsimd
                def _(e):
                    run("pool", e)

                @block.tensor
                def _(e):
                    run("pe", e)


class Tile:
    def __init__(self, h, tok):
        self.h = h
        self.t = tok

    def __getitem__(self, k):
        return self.h[k]


class Alloc:
    LO = 17408
    HI = 228000

    def __init__(self, nc, P):
        self.nc = nc
        self.P = P
        self.top = self.LO
        self.n = 0
        self.peak = 0

    def mark(self):
        return self.top

    def release(self, m):
        self.P.barrier()
        self.top = m

    def tile(self, shape, dtype, name="t"):
        nbytes = int(np.prod(shape[1:])) * mybir.dt.size(dtype)
        off = (self.top + 63) // 64 * 64
        assert off + nbytes <= self.HI, ("SBUF overflow", name, off, nbytes)
        self.top = off + nbytes
        self.peak = max(self.peak, self.top)
        self.n += 1
        h = self.nc.alloc_sbuf_tensor_at("%s_%d" % (name, self.n), list(shape), dtype, offset=off)
        return Tile(h, self.P.tok(name))


def build():
    nc = bass.Bass("TRN2", target_bir_lowering=False)
    P = Prog(nc)
    A = Alloc(nc, P)

    def din(name, shape, dt=F32):
        return nc.dram_tensor(name, list(shape), dt, kind="ExternalInput").ap()

    def dout(name, shape, dt=F32):
        return nc.dram_tensor(name, list(shape), dt, kind="ExternalOutput").ap()

    xp = din("xp", [T, D])
    xs = din("xs", [NS * TS, D])
    memp = din("memp", [MEM, D])
    st_in = din("st_in", [NS, H, 128, 128])
    cmem = din("cmem", [2, NS, MEM, 2, 4, 128])
    norm_gains = din("norm_gains", [2, 4, D])
    w_in_a = din("w_in_a", [1, D, 4 * W + 512])
    lb_logits = din("lb_logits", [2, W])
    hgrn_norm = din("hgrn_norm", [1, H, 128])
    w_o = din("w_o", [2, D, D])
    w_mem_kv = din("w_mem_kv", [2, D, 1024])

    cconv = din("cconv", [2, NS, 2, DFF])
    kv_norm = din("kv_norm", [D])
    w_kv_b = din("w_kv_b", [D, 1536])
    w_ffn_in = din("w_ffn_in", [2, D, 2 * DFF])
    w_ffn_conv = din("w_ffn_conv", [2, 3, DFF])
    b_ffn_conv = din("b_ffn_conv", [2, DFF])
    w_ffn_out = din("w_ffn_out", [2, DFF, D])
    w_in_b = din("w_in_b", [1, D, W + 36 + 512])
    ckv = din("ckv", [2560, 128, 4, 2, 128])
    cwin = din("cwin", [NS, 512, 2, 2, 128])
    ptab = din("ptab", [NS, 64], I32)
    cmp_pos = din("cmp_pos", [2, 32, 128])
    w_cmp1 = din("w_cmp1", [2, 4096, 128])
    w_cmp2 = din("w_cmp2", [2, 128, 128])
    o_nm = dout("o_nm", [2, MEM, 1024])
    o_cvp = dout("o_cvp", [2, 2, DFF])
    o_cvs = dout("o_cvs", [2, NS, 2, DFF])
    o_kvp = dout("o_kvp", [T, 1024])
    o_kvs = dout("o_kvs", [NS * TS, 1024])
    o_winp = dout("o_winp", [512, 512])
    o_wins = dout("o_wins", [NS * TS, 512])
    o_yp = dout("o_yp", [T, D])
    o_ys = dout("o_ys", [NS * TS, D])
    o_hgp = dout("o_hgp", [H, 128, 128])
    o_hgs = dout("o_hgs", [NS, H, 128, 128])

    hT_d = nc.dram_tensor("hT_d", [KC, 128, NTOK], F32, kind="Internal").ap()
    mixT_d = nc.dram_tensor("mixT_d", [KC, 128, NTOK], BF16, kind="Internal").ap()
    hT_tok = [[P.tok("hT%d_%d" % (k, i)) for i in range(5)] for k in range(KC)]
    xn2T_d = nc.dram_tensor("xn2T_d", [KC, 128, NTOK], BF16, kind="Internal").ap()
    xn2_tok = [P.tok("xn2d%d" % i) for i in range(5)]
    xnA_d = nc.dram_tensor("xnA_d", [KC, 128, NTOK], BF16, kind="Internal").ap()
    xnA_tok = [P.tok("xnAd%d" % i) for i in range(4)]
    xkv_d = nc.dram_tensor("xkv_d", [KC, 128, NTOK], BF16, kind="Internal").ap()
    xkv_tok = [P.tok("xkvd%d" % i) for i in range(4)]
    fT_d = nc.dram_tensor("fT_d", [KC, 128, NTOK], F32, kind="Internal").ap()
    fT_tok = [P.tok("fTd%d" % i) for i in range(KC)]
    qT_d = nc.dram_tensor("qT_d", [H, 128, NTOK], BF16, kind="Internal").ap()
    qT_tok = [P.tok("qTd%d" % i) for i in range(H)]
    ocmp_d = nc.dram_tensor("ocmp_d", [H, 128, T], F32, kind="Internal").ap()
    ocmp_tok = [P.tok("ocmpd%d" % i) for i in range(H)]
    mix_tok = [P.tok("mixd%d" % k) for k in range(KC)]

    CT = [(i * 512, 512) for i in range(4)] + [(T, NS * TS)]

    def act(out, in_, func, reads, writes, **kw):
        P.op("act", lambda e: e.activation(out=out, in_=in_, func=func, **kw), reads, writes)

    def tt(out, in0, in1, op, reads, writes, eng="dve"):
        P.op(eng, lambda e: e.tensor_tensor(out=out, in0=in0, in1=in1, op=op), reads, writes)

    def ts(out, in0, s1, s2, op0, op1, reads, writes, eng="dve"):
        if op1 is None:
            P.op(eng, lambda e: e.tensor_scalar(out=out, in0=in0, scalar1=s1, scalar2=None, op0=op0), reads, writes)
        else:
            P.op(eng, lambda e: e.tensor_scalar(out=out, in0=in0, scalar1=s1, scalar2=s2, op0=op0, op1=op1),
                 reads, writes)

    def stt(out, in0, scalar, in1, op0, op1, reads, writes):
        P.op("dve", lambda e: e.scalar_tensor_tensor(out=out, in0=in0, scalar=scalar, in1=in1, op0=op0, op1=op1),
             reads, writes)

    def cp(eng, out, in_, reads, writes):
        if eng == "act":
            P.op("act", lambda e: e.copy(out, in_), reads, writes)
        else:
            P.op(eng, lambda e: e.tensor_copy(out, in_), reads, writes)

    def mm(out, lhsT, rhs, start, stop, reads, writes):
        P.op("pe", lambda e: e.matmul(out, lhsT, rhs, start=start, stop=stop), reads, writes)

    def tr(out, in_, ident, reads, writes):
        P.op("pe", lambda e: e.transpose(out, in_, ident), reads, writes)

    def rsqrt(out, in_, scale, reads, writes):
        n_p = out.shape[0]
        act(out, in_, AF.Sqrt, list(reads) + [eps_t.t], writes, scale=scale, bias=eps_t[0:n_p, :])
        P.op("dve", lambda e: e.reciprocal(out, out), writes, writes)

    def memset(eng, ap, v, writes):
        P.op(eng, lambda e: e.memset(ap, v), (), writes)

    def dma_in(tile_ap, tile_tok, src, reads=(), eng="sp", nonc=False):
        if nonc:
            P.dma(eng, lambda e: [e.dma_start(out=tile_ap, in_=src, allow_slow_non_contiguous=True)], 1, tile_tok,
                  reads=reads, writes=[tile_tok])
        else:
            P.dma(eng, lambda e: [e.dma_start(out=tile_ap, in_=src)], 1, tile_tok, reads=reads, writes=[tile_tok])

    def dma_out(dst, tile_ap, tile_tok, writes=(), eng="pool"):
        P.dma(eng, lambda e: [e.dma_start(out=dst, in_=tile_ap)], 1, tile_tok, reads=[tile_tok], writes=writes)

    eps_t = A.tile([128, 1], F32, "eps_t")
    one_t = A.tile([128, 1], F32, "one_t")
    P.op("pool", lambda e: e.memset(one_t[:], 1.0), (), [one_t.t])
    P.op("pool", lambda e: e.memset(eps_t[:], EPS), (), [eps_t.t])
    ident_f = A.tile([128, 128], F32, "ident_f")
    ident_b = A.tile([128, 128], BF16, "ident_b")
    ones_b = A.tile([128, 128], BF16, "ones_b")
    memset("pool", ident_b[:], 1.0, [ident_b.t])
    P.op("pool", lambda e: e.affine_select(out=ident_b[:], in_=ident_b[:], pattern=[[-1, 128]],
                                           compare_op=ALU.is_equal, fill=0.0, base=0, channel_multiplier=1),
         reads=[ident_b.t], writes=[ident_b.t])
    cp("pool", ident_f[:], ident_b[:], [ident_b.t], [ident_f.t])
    memset("pool", ones_b[:], 1.0, [ones_b.t])
    cmask = A.tile([32, 16, 32], F32, "cmask")
    memset("pool", cmask[:], 1.0, [cmask.t])
    P.op("pool", lambda e: e.affine_select(out=cmask[:], in_=cmask[:], pattern=[[0, 16], [1, 32]],
                                           compare_op=ALU.is_ge, fill=0.0, base=0, channel_multiplier=-1),
         reads=[cmask.t], writes=[cmask.t])
    rmask = A.tile([128, 512], F32, "rmask")
    memset("pool", rmask[:], 1.0, [rmask.t])
    memset("pool", rmask[:].rearrange("p (c j) -> p c j", j=32)[:, :, 0:1], 0.0, [rmask.t])
    rmask_s = A.tile([128, 16], F32, "rmask_s")
    memset("pool", rmask_s[:], 1.0, [rmask_s.t])
    memset("pool", rmask_s[:].rearrange("p (c j) -> p c j", j=4)[:, :, 0:1], 0.0, [rmask_s.t])
    gT = A.tile([128, 8, KC], F32, "gT")
    dma_in(gT[:], gT.t, norm_gains.rearrange("l j (k p) -> p (l j) k", p=128), nonc=True)
    gnT = A.tile([128, H], F32, "gnT")
    dma_in(gnT[:], gnT.t, hgrn_norm[0].rearrange("h d -> d h"), nonc=True)
    lbT = A.tile([128, 2, H], F32, "lbT")
    dma_in(lbT[:], lbT.t, lb_logits.rearrange("r (h d) -> d r h", d=128), nonc=True)
    kvgT = A.tile([128, KC], F32, "kvgT")
    dma_in(kvgT[:], kvgT.t, kv_norm.rearrange("(k p) -> p k", p=128), nonc=True)
    NJ = DFF // 128
    wcT = A.tile([128, 6, NJ], F32, "wcT")
    dma_in(wcT[:], wcT.t, w_ffn_conv.rearrange("l j (c p) -> p (l j) c", p=128), nonc=True)
    bcT = A.tile([128, 2, NJ], F32, "bcT")
    dma_in(bcT[:], bcT.t, b_ffn_conv.rearrange("l (c p) -> p l c", p=128), nonc=True)
    ccT = A.tile([128, 2, NJ, NS * 2], F32, "ccT")
    for l in range(2):
        for b_ in range(NS):
            for r_ in range(2):
                dma_in(ccT[:, l, :, b_ * 2 + r_], ccT.t, cconv[l, b_, r_].rearrange("(c p) -> p c", p=128), nonc=True)
    oml = A.tile([128, H], F32, "oml")
    tt(oml[:], lbT[:, 1, :], lbT[:, 0, :], ALU.subtract, [lbT.t], [oml.t])
    act(oml[:], oml[:], AF.Sigmoid, [oml.t], [oml.t])

    psum = []
    for i in range(8):
        h = nc.alloc_psum_tensor("ps%d" % i, [128, 512], F32)
        psum.append(Tile(h, P.tok("ps%d" % i)))

    def psb(i):
        return psum[i].h[:].bitcast(BF16)

    wst = [A.tile([128, 2048], F32, "wst%d" % i) for i in range(2)]
    wst_i = [0]

    def load_cast_multi(pairs, dst_tok):
        pairs = list(pairs)
        P.dma("pool", lambda e: [e.dma_start(out=d_, in_=s_) for (d_, s_) in pairs], len(pairs), dst_tok,
              writes=[dst_tok])

    def load_cast(dst_ap, dst_tok, src_ap, shape3, eng=None):
        load_cast_multi([(dst_ap, src_ap)], dst_tok)

    def load_cast_staged(dst_ap, dst_tok, src_ap, shape3, eng=None):
        st = wst[wst_i[0] % 2]
        if eng is None:
            eng = "dve" if wst_i[0] % 2 == 0 else "act"
        wst_i[0] += 1
        n = int(np.prod(shape3[1:]))
        sv = st[0:shape3[0], 0:n]
        if len(shape3) == 3:
            sv = sv.rearrange("p (a b) -> p a b", a=shape3[1])
        P.dma("sp", lambda e: [e.dma_start(out=sv, in_=src_ap)], 1, st.t, writes=[st.t])
        cp(eng, dst_ap, sv, [st.t], [dst_tok])

    import os
    PH = os.environ.get('PH', 'XMABCKNS')
    MS = os.environ.get('MS', 'VKS')
    memKT = A.tile([128, 5, 4, MEM], BF16, "memKT")
    memV = A.tile([128, 5, 2, 512], BF16, "memV")
    mXA = A.mark()
    xnT = A.tile([128, KC, NTOK], BF16, "xnT")
    xn_tok = [P.tok("xn%d" % i) for i in range(5)]
    mX = A.mark()
    xt2 = [A.tile([128, D], F32, "xt%d" % i) for i in range(2)]
    sq = A.tile([128, D], F32, "sq")
    xb = A.tile([128, D], BF16, "xb")
    ss = A.tile([128, 1], F32, "ss")
    hst = [A.tile([128, KC, 128], F32, "hst%d" % i) for i in range(2)]
    for i in (range(17) if 'X' in PH else []):
        rows = 128 if i < 16 else NS * TS
        src = xp[i * 128:(i + 1) * 128, :] if i < 16 else xs[:, :]
        c0 = i * 128
        cti = min(i // 4, 4)
        xt = xt2[i % 2]
        dma_in(xt[0:rows, :], xt.t, src)
        act(sq[0:rows, :], xt[0:rows, :], AF.Square, [xt.t], [sq.t])
        P.op("dve", lambda e, rows=rows: e.reduce_sum(out=ss[0:rows, :], in_=sq[0:rows, :], axis=AX.X), [sq.t], [ss.t])
        rsqrt(ss[0:rows, :], ss[0:rows, :], 1.0 / D, [ss.t], [ss.t])
        ts(xb[0:rows, :], xt[0:rows, :], ss[0:rows, 0:1], None, ALU.mult, None, [xt.t, ss.t], [xb.t])
        for half in range(2):
            pt = psum[half]
            pv = psb(half)
            for k in range(8):
                kc = half * 8 + k
                tr(pv[:, k * 128:k * 128 + rows], xb[0:rows, kc * 128:(kc + 1) * 128], ident_b[0:rows, 0:rows],
                   [xb.t, ident_b.t], [pt.t])
            tt(xnT[:, half * 8:(half + 1) * 8, c0:c0 + rows],
               pv.rearrange("p (k t) -> p k t", k=8)[:, :, 0:rows],
               gT[:, 0, half * 8:(half + 1) * 8].unsqueeze(2).to_broadcast([128, 8, rows]),
               ALU.mult, [pt.t, gT.t], [xn_tok[cti]])
        hs = hst[i % 2]
        for q4 in range(4):
            pt = psum[2 + q4 % 2]
            for k in range(4):
                kc = q4 * 4 + k
                tr(pt[:, k * 128:k * 128 + rows], xt[0:rows, kc * 128:(kc + 1) * 128], ident_f[0:rows, 0:rows],
                   [xt.t, ident_f.t], [pt.t])
            cp("act", hs[:, q4 * 4:(q4 + 1) * 4, 0:rows],
               pt[:].rearrange("p (k t) -> p k t", k=4)[:, :, 0:rows], [pt.t], [hs.t])
        P.dma("pool", lambda e, hs=hs, c0=c0, rows=rows: [
            e.dma_start(out=hT_d[:, :, c0:c0 + rows].rearrange("k p t -> p k t"), in_=hs[:, :, 0:rows])],
            1, hs.t, reads=[hs.t], writes=[hT_tok[k][cti] for k in range(KC)])
    A.release(mX)

    def phase_M(l):
        m0 = A.mark()
        memT = A.tile([128, KC, MEM], BF16, "memT")
        mtile = A.tile([128, D], F32, "mtile")
        mtile_b = A.tile([128, D], BF16, "mtile_b")
        for t2 in range(MEM // 128):
            dma_in(mtile[:], mtile.t, memp[t2 * 128:(t2 + 1) * 128, :])
            cp("dve", mtile_b[:], mtile[:], [mtile.t], [mtile_b.t])
            for half in range(2):
                pt = psum[half]
                pv = psb(half)
                for k in range(8):
                    kc = half * 8 + k
                    tr(pv[:, k * 128:(k + 1) * 128], mtile_b[:, kc * 128:(kc + 1) * 128], ident_b[:],
                       [mtile_b.t, ident_b.t], [pt.t])
                cp("act", memT[:, half * 8:(half + 1) * 8, t2 * 128:(t2 + 1) * 128],
                   pv.rearrange("p (k t) -> p k t", k=8), [pt.t], [memT.t])
        wmk = A.tile([128, KC, 512], BF16, "wmk")
        osb = [A.tile([128, 512], F32, "osb%d" % i) for i in range(2)]
        oi = 0
        for nb in range(2):
            load_cast_multi([(wmk[:, kc, :], w_mem_kv[l, kc * 128:(kc + 1) * 128, nb * 512:(nb + 1) * 512])
                             for kc in range(KC)], wmk.t)
            for t2 in range(2):
                pt = psum[2 + (oi % 2)]
                for kc in range(KC):
                    mm(pt[:], memT[:, kc, t2 * 128:(t2 + 1) * 128], wmk[:, kc, :], kc == 0, kc == KC - 1,
                       [memT.t, wmk.t], [pt.t])
                ob = osb[oi % 2]
                oi += 1
                cp("dve", ob[:], pt[:], [pt.t], [ob.t])
                dma_out(o_nm[l, t2 * 128:(t2 + 1) * 128, nb * 512:(nb + 1) * 512], ob[:], ob.t)
                if nb == 1:
                    cp("pool", memV[:, 0, t2, :], ob[:], [ob.t], [memV.t])
            if nb == 0:
                for hm in range(4):
                    pt = psum[4 + hm % 2]
                    for kc in range(KC):
                        mm(pt[:, 0:MEM], wmk[:, kc, hm * 128:(hm + 1) * 128], memT[:, kc, :], kc == 0, kc == KC - 1,
                           [memT.t, wmk.t], [pt.t])
                    cp("act", memKT[:, 0, hm, :], pt[:, 0:MEM], [pt.t], [memKT.t])
        cmt = A.tile([128, 1024], F32, "cmt")
        cmb = A.tile([128, 512], BF16, "cmb")
        for b in range(NS):
            for t2 in range(2):
                dma_in(cmt[:], cmt.t, cmem[l, b, t2 * 128:(t2 + 1) * 128].rearrange("m a h d -> m (a h d)"))
                cp("dve", memV[:, 1 + b, t2, :], cmt[:, 512:1024], [cmt.t], [memV.t])
                cp("dve", cmb[:], cmt[:, 0:512], [cmt.t], [cmb.t])
                pt = psum[6]
                pv = psb(6)
                for hm in range(4):
                    tr(pv[:, hm * 128:(hm + 1) * 128], cmb[:, hm * 128:(hm + 1) * 128], ident_b[:],
                       [cmb.t, ident_b.t], [pt.t])
                cp("act", memKT[:, 1 + b, :, t2 * 128:(t2 + 1) * 128],
                   pv[:, 0:512].rearrange("p (h m) -> p h m", h=4), [pt.t], [memKT.t])
        A.release(m0)

    phase_M(0)

    mA = A.mark()
    wb4 = [A.tile([128, KC, 128], BF16, "wb%d" % i) for i in range(4)]
    qT = A.tile([128, 512], BF16, "qT")
    kT = A.tile([128, 512], BF16, "kT")
    kpT = A.tile([128, 512], BF16, "kpT")
    vT = A.tile([128, 512], BF16, "vT")
    ogT = A.tile([128, 512], BF16, "ogT")
    mixo = A.tile([128, NTOK], BF16, "mixo")
    ebl = A.tile([128, 16], F32, "ebl")
    kp_tm = A.tile([32, 16, 128], BF16, "kp_tm")
    v_tm = A.tile([32, 16, 128], BF16, "v_tm")
    sig = A.tile([128, 512], F32, "sig")
    q32 = A.tile([128, 512], F32, "q32")
    k32 = A.tile([128, 512], F32, "k32")
    lf = A.tile([128, 512], F32, "lf")
    bb = A.tile([128, 512], F32, "bb")
    e1 = A.tile([128, 512], F32, "e1")
    e2 = A.tile([128, 512], F32, "e2")
    kf = A.tile([128, 512], F32, "kf")
    Sst = A.tile([128, 128], F32, "Sst")
    Sbf = [A.tile([128, 128], BF16, "Sbf%d" % i) for i in range(2)]
    AT = A.tile([32, 512], BF16, "AT")
    osq = A.tile([128, 512], BF16, "osq")
    rstd = A.tile([128, 512], F32, "rstd")
    tmpn = A.tile([128, 512], F32, "tmpn")
    o32 = A.tile([128, 512], F32, "o32")
    U_tok = [P.tok("U%d" % i) for i in range(4)]

    def proj(widx, ti, pbank):
        c0, n = CT[ti]
        pt = psum[pbank]
        for kc in range(KC):
            mm(pt[:, 0:n], wb4[widx][:, kc, :], xnT[:, kc, c0:c0 + n], kc == 0, kc == KC - 1,
               [wb4[widx].t, xn_tok[ti]], [pt.t])
        return pt

    for h in (range(int(os.environ.get('NH', H))) if 'A' in PH else []):
        for j in range(4):
            col = j * W + h * 128
            load_cast(wb4[j][:], wb4[j].t,
                      w_in_a[0, :, col:col + 128].rearrange("(k p) c -> p k c", p=128), [128, KC, 128])
        sbi = 0
        for ti in range(5):
            c0, n = CT[ti]
            L = 32 if ti < 4 else TS
            nch = n // L
            ch0 = c0 // 32 if ti < 4 else 64
            pq = proj(0, ti, 0)
            act(q32[:, 0:n], pq[:, 0:n], AF.Silu, [pq.t], [q32.t])
            pf = proj(1, ti, 1)
            act(sig[:, 0:n], pf[:, 0:n], AF.Sigmoid, [pf.t], [sig.t], scale=-1.0)
            ts(k32[:, 0:n], sig[:, 0:n], oml[:, h:h + 1], None, ALU.mult, None, [sig.t, oml.t], [k32.t])
            pvv = proj(2, ti, 0)
            cp("act", vT[:, 0:n], pvv[:, 0:n], [pvv.t], [vT.t])
            po = proj(3, ti, 1)
            act(ogT[:, 0:n], po[:, 0:n], AF.Sigmoid, [po.t], [ogT.t])
            act(lf[:, 0:n], k32[:, 0:n], AF.Ln, [k32.t], [lf.t], scale=-1.0, bias=1.0)
            rm = rmask if ti < 4 else rmask_s
            P.op("dve", lambda e, n=n, rm=rm: e.tensor_tensor_scan(out=bb[:, 0:n], data0=rm[:, 0:n], data1=lf[:, 0:n],
                                                                 initial=0.0, op0=ALU.mult, op1=ALU.add),
                 [rm.t, lf.t], [bb.t])
            act(e1[:, 0:n], bb[:, 0:n], AF.Exp, [bb.t], [e1.t])
            act(e2[:, 0:n], bb[:, 0:n], AF.Exp, [bb.t], [e2.t], scale=-1.0)
            blast = bb[:, 0:n].rearrange("p (c j) -> p c j", j=L)[:, :, L - 1]
            act(ebl[:, 0:nch], blast, AF.Exp, [bb.t], [ebl.t])
            tt(qT[:, 0:n], q32[:, 0:n], e1[:, 0:n], ALU.mult, [q32.t, e1.t], [qT.t])
            tt(kf[:, 0:n], k32[:, 0:n], e2[:, 0:n], ALU.mult, [k32.t, e2.t], [kf.t])
            cp("pool", kT[:, 0:n], kf[:, 0:n], [kf.t], [kT.t])
            tt(kpT[:, 0:n].rearrange("p (c j) -> p c j", j=L),
               kf[:, 0:n].rearrange("p (c j) -> p c j", j=L),
               ebl[:, 0:nch].unsqueeze(2).to_broadcast([128, nch, L]), ALU.mult, [kf.t, ebl.t], [kpT.t])
            p3 = psum[3]
            for ci in range(nch):
                cc = ci * L
                mm(p3[0:L, ci * 32:ci * 32 + L], kT[:, cc:cc + L], qT[:, cc:cc + L], True, True,
                   [kT.t, qT.t], [p3.t])
            tt(AT[0:L, 0:nch * 32].rearrange("p (c j) -> p c j", j=32)[:, :, 0:L],
               p3[0:L, 0:nch * 32].rearrange("p (c j) -> p c j", j=32)[:, :, 0:L],
               cmask[0:L, 0:nch, 0:L], ALU.mult, [p3.t, cmask.t], [AT.t])
            for g0 in range(0, nch, 8):
                g1 = min(nch, g0 + 8)
                for (srcT, dst, pbk) in ((kpT, kp_tm, 2), (vT, v_tm, 7)):
                    pt = psum[pbk]
                    pv2 = psb(pbk)
                    for ci in range(g0, g1):
                        cc = ci * L
                        tr(pv2[0:L, (ci - g0) * 128:(ci - g0 + 1) * 128], srcT[:, cc:cc + L], ident_b[:],
                           [srcT.t, ident_b.t], [pt.t])
                    cp("act", dst[0:L, g0:g1, :],
                       pv2[0:L, 0:(g1 - g0) * 128].rearrange("p (c d) -> p c d", d=128), [pt.t], [dst.t])
            p4 = psum[4]
            for ci in range(nch):
                ch = ch0 + ci
                cc = ci * L
                fresh = (ti < 4 and ch == 0)
                if ti == 4:
                    dma_in(Sst[:], Sst.t, st_in[ci, h])
                    sbi += 1
                    cp("act", Sbf[sbi % 2][:], Sst[:], [Sst.t], [Sbf[sbi % 2].t])
                mm(p4[:, ci * L:(ci + 1) * L], v_tm[0:L, ci, :], AT[0:L, ci * 32:ci * 32 + L], True, fresh,
                   [v_tm.t, AT.t], [p4.t])
                if not fresh:
                    sb = Sbf[sbi % 2]
                    mm(p4[:, ci * L:(ci + 1) * L], sb[:], qT[:, cc:cc + L], False, True, [sb.t, qT.t], [p4.t])
                ut = U_tok[ch % 4]
                pu = psum[5][:, (ch % 4) * 128:(ch % 4 + 1) * 128]
                mm(pu, kp_tm[0:L, ci, :], v_tm[0:L, ci, :], True, True, [kp_tm.t, v_tm.t], [ut])
                if fresh:
                    cp("dve", Sst[:], pu, [ut], [Sst.t])
                else:
                    stt(Sst[:], Sst[:], ebl[:, ci:ci + 1], pu, ALU.mult, ALU.add, [Sst.t, ebl.t, ut], [Sst.t])
                last = (ti < 4 and ch == 63) or ti == 4
                if last:
                    dst = o_hgp[h] if ti < 4 else o_hgs[ci, h]
                    dma_out(dst, Sst[:], Sst.t)
                else:
                    sbi += 1
                    cp("act", Sbf[sbi % 2][:], Sst[:], [Sst.t], [Sbf[sbi % 2].t])
            cp("dve", o32[:, 0:n], p4[:, 0:n], [p4.t], [o32.t])
            act(osq[:, 0:n], o32[:, 0:n], AF.Square, [o32.t], [osq.t])
            p6 = psum[6]
            mm(p6[:, 0:n], ones_b[:], osq[:, 0:n], True, True, [ones_b.t, osq.t], [p6.t])
            rsqrt(rstd[:, 0:n], p6[:, 0:n], 1.0 / 128, [p6.t], [rstd.t])
            tt(tmpn[:, 0:n], o32[:, 0:n], rstd[:, 0:n], ALU.mult, [o32.t, rstd.t], [tmpn.t])
            stt(mixo[:, c0:c0 + n], tmpn[:, 0:n], gnT[:, h:h + 1], ogT[:, 0:n], ALU.mult, ALU.mult,
                [tmpn.t, gnT.t, ogT.t], [mixo.t])
        dma_out(mixT_d[h], mixo[:], mixo.t, writes=[mix_tok[h]])

    pT2 = [A.tile([128, 512], BF16, "pT%d" % i) for i in range(2)]
    rz = A.tile([128, 512], F32, "rz")

    def mem_attn(l, wsrc):
        for hm in (range(4) if 'A' in PH else []):
            load_cast(wb4[0][:], wb4[0].t, wsrc(hm), [128, KC, 128])
            for ti in range(5):
                c0, n = CT[ti]
                pq = proj(0, ti, 0)
                act(qT[:, 0:n], pq[:, 0:n], AF.Copy, [pq.t], [qT.t], scale=SCALE)
                segs = [(0, 0, n)] if ti < 4 else [(1 + b_, b_ * TS, TS) for b_ in range(NS)]
                p4 = psum[4]
                p6 = psum[6]
                for (sq_, o0, nn) in segs:
                    for mt in range(2):
                        pscore = psum[1 if mt == 0 else 7]
                        mm(pscore[:, 0:nn], memKT[:, sq_, hm, mt * 128:(mt + 1) * 128], qT[:, o0:o0 + nn],
                           True, True, [memKT.t, qT.t], [pscore.t])
                        act(pT2[mt][:, 0:nn], pscore[:, 0:nn], AF.Exp, [pscore.t], [pT2[mt].t])
                    for mt in range(2):
                        mm(p4[:, o0:o0 + nn], memV[:, sq_, mt, hm * 128:(hm + 1) * 128], pT2[mt][:, 0:nn],
                           mt == 0, mt == 1, [memV.t, pT2[mt].t], [p4.t])
                    for mt in range(2):
                        mm(p6[:, o0:o0 + nn], ones_b[:], pT2[mt][:, 0:nn], mt == 0, mt == 1,
                           [ones_b.t, pT2[mt].t], [p6.t])
                P.op("dve", lambda e, n=n, p6=p6, rz=rz: e.reciprocal(rz[:, 0:n], p6[:, 0:n]), [p6.t], [rz.t])
                tt(mixo[:, c0:c0 + n], p4[:, 0:n], rz[:, 0:n], ALU.mult, [p4.t, rz.t], [mixo.t])
            dma_out(mixT_d[12 + hm], mixo[:], mixo.t, writes=[mix_tok[12 + hm]])

    mem_attn(0, lambda hm: w_in_a[0, :, 4 * W + hm * 128:4 * W + (hm + 1) * 128].rearrange("(k p) c -> p k c", p=128))
    A.release(mXA)

    def phase_B(l):
        mB = A.mark()
        wo_b = A.tile([128, KC, D], BF16, "wo_b")
        load_cast_multi([(wo_b[:, kc, :], w_o[l, kc * 128:(kc + 1) * 128, :]) for kc in range(KC)], wo_b.t)
        mt_t = A.tile([128, KC, 256], BF16, "mt_t")
        h_t = A.tile([128, KC, 256], F32, "h_t")
        o_t = A.tile([128, KC, 256], F32, "o_t")
        x2_t = A.tile([128, KC, 256], BF16, "x2_t")
        sqb = [A.tile([128, 512], BF16, "sqb%d" % i) for i in range(2)]
        r1 = A.tile([128, 512], F32, "r1")
        r2 = A.tile([128, 512], F32, "r2")
        tmpb = A.tile([128, 512], F32, "tmpb")
        for ti, c0, n in [(ti, CT[ti][0] + hf * 256, min(256, CT[ti][1] - hf * 256)) for ti in range(5)
                          for hf in range(2) if CT[ti][1] - hf * 256 > 0]:
            P.dma("sp", lambda e, c0=c0, n=n: [e.dma_start(out=mt_t[:, :, 0:n],
                                                           in_=mixT_d[:, :, c0:c0 + n].rearrange("k p t -> p k t"))],
                  1, mt_t.t, reads=mix_tok, writes=[mt_t.t])
            P.dma("sp", lambda e, c0=c0, n=n: [e.dma_start(out=h_t[:, :, 0:n],
                                                           in_=hT_d[:, :, c0:c0 + n].rearrange("k p t -> p k t"))],
                  1, h_t.t, reads=[hT_tok[k][ti] for k in range(KC)], writes=[h_t.t])
            for oc in range(KC):
                po = psum[oc % 2]
                for kc in range(KC):
                    mm(po[:, 0:n], wo_b[:, kc, oc * 128:(oc + 1) * 128], mt_t[:, kc, 0:n], kc == 0, kc == KC - 1,
                       [wo_b.t, mt_t.t], [po.t])
                cp("dve", o_t[:, oc, 0:n], po[:, 0:n], [po.t], [o_t.t])
                sb = sqb[oc % 2]
                act(sb[:, 0:n], o_t[:, oc, 0:n], AF.Square, [o_t.t], [sb.t])
                mm(psum[2][:, 0:n], ones_b[:], sb[:, 0:n], oc == 0, oc == KC - 1, [ones_b.t, sb.t], [psum[2].t])
            rsqrt(r1[:, 0:n], psum[2][:, 0:n], 1.0 / D, [psum[2].t], [r1.t])
            for oc in range(KC):
                tt(tmpb[:, 0:n], o_t[:, oc, 0:n], r1[:, 0:n], ALU.mult, [o_t.t, r1.t], [tmpb.t])
                stt(h_t[:, oc, 0:n], tmpb[:, 0:n], gT[:, l * 4 + 1, oc:oc + 1], h_t[:, oc, 0:n], ALU.mult, ALU.add,
                    [tmpb.t, gT.t, h_t.t], [h_t.t])
                sb = sqb[oc % 2]
                act(sb[:, 0:n], h_t[:, oc, 0:n], AF.Square, [h_t.t], [sb.t])
                mm(psum[3][:, 0:n], ones_b[:], sb[:, 0:n], oc == 0, oc == KC - 1, [ones_b.t, sb.t], [psum[3].t])
            rsqrt(r2[:, 0:n], psum[3][:, 0:n], 1.0 / D, [psum[3].t], [r2.t])
            for oc in range(KC):
                stt(x2_t[:, oc, 0:n], h_t[:, oc, 0:n], gT[:, l * 4 + 2, oc:oc + 1], r2[:, 0:n], ALU.mult, ALU.mult,
                    [h_t.t, gT.t, r2.t], [x2_t.t])
            P.dma("pool", lambda e, c0=c0, n=n: [e.dma_start(out=hT_d[:, :, c0:c0 + n].rearrange("k p t -> p k t"),
                                                             in_=h_t[:, :, 0:n])],
                  1, h_t.t, reads=[h_t.t], writes=[hT_tok[k][ti] for k in range(KC)])
            P.dma("pool", lambda e, c0=c0, n=n: [e.dma_start(out=xn2T_d[:, :, c0:c0 + n].rearrange("k p t -> p k t"),
                                                             in_=x2_t[:, :, 0:n])],
                  1, x2_t.t, reads=[x2_t.t], writes=[xn2_tok[ti]])
        A.release(mB)

    if 'B' in PH:
        phase_B(0)

    BLK = [(0, 512), (512, 512), (1024, 512), (1536, 528)]
    LASTB = len(BLK) - 1
    GC = 2.0 * (2.0 / np.pi) ** 0.5

    def phase_C(l):
        mC = A.mark()
        convo = A.tile([128, NJ, 2 + NS * 2], F32, "convo")
        aprev = A.tile([128, NJ, 2], F32, "aprev")
        rf = A.tile([128, 528], F32, "rf")
        r3 = A.tile([128, 528], F32, "r3")
        sqb = [A.tile([128, 512], BF16, "sqc%d" % i) for i in range(2)]
        for bi, (b0, NB) in enumerate(BLK):
            NT = [(o, min(512, NB - o)) for o in range(0, NB, 512)]
            mC1 = A.mark()
            x2b = A.tile([128, KC, 528], BF16, "x2b")
            yT = A.tile([128, NJ, 528], BF16, "yT")
            wab = [[A.tile([128, KC, 128], BF16, "wab%d%d" % (i, k)) for k in range(2)] for i in range(2)]
            a_ext = A.tile([128, 530], F32, "a_ext")
            t1 = A.tile([128, 528], F32, "t1")
            u1 = A.tile([128, 528], F32, "u1")
            exs = A.tile([128, NS, 6], F32, "exs")
            tis = [ti for ti in range(5) if CT[ti][0] >= b0 and CT[ti][0] < b0 + NB]
            P.dma("sp", lambda e, b0=b0, NB=NB: [e.dma_start(out=x2b[:, :, 0:NB],
                                                             in_=xn2T_d[:, :, b0:b0 + NB].rearrange("k p t -> p k t"))],
                  1, x2b.t, reads=[xn2_tok[ti] for ti in tis], writes=[x2b.t])
            for j in range(NJ):
                wa, wb = wab[j % 2]
                load_cast(wa[:], wa.t, w_ffn_in[l, :, j * 128:(j + 1) * 128].rearrange("(k p) c -> p k c", p=128),
                          [128, KC, 128])
                load_cast(wb[:], wb.t,
                          w_ffn_in[l, :, DFF + j * 128:DFF + (j + 1) * 128].rearrange("(k p) c -> p k c", p=128),
                          [128, KC, 128])
                if bi == 0:
                    memset("pool", a_ext[:, 0:2], 0.0, [a_ext.t])
                else:
                    cp("pool", a_ext[:, 0:2], aprev[:, j, :], [aprev.t], [a_ext.t])
                for ni, (o, n) in enumerate(NT):
                    pa = psum[(j % 2) * 2 + ni]
                    for kc in range(KC):
                        mm(pa[:, 0:n], wa[:, kc, :], x2b[:, kc, o:o + n], kc == 0, kc == KC - 1, [wa.t, x2b.t], [pa.t])
                    cp("act", a_ext[:, 2 + o:2 + o + n], pa[:, 0:n], [pa.t], [a_ext.t])
                for ni, (o, n) in enumerate(NT):
                    pb = psum[4 + (j % 2) * 2 + ni]
                    for kc in range(KC):
                        mm(pb[:, 0:n], wb[:, kc, :], x2b[:, kc, o:o + n], kc == 0, kc == KC - 1, [wb.t, x2b.t], [pb.t])
                w0 = wcT[:, l * 3 + 0, j:j + 1]
                w1 = wcT[:, l * 3 + 1, j:j + 1]
                w2 = wcT[:, l * 3 + 2, j:j + 1]
                bc = bcT[:, l, j:j + 1]
                ts(t1[:, 0:NB], a_ext[:, 2:2 + NB], w2, bc, ALU.mult, ALU.add, [a_ext.t, wcT.t, bcT.t], [t1.t])
                stt(t1[:, 0:NB], a_ext[:, 1:1 + NB], w1, t1[:, 0:NB], ALU.mult, ALU.add, [a_ext.t, wcT.t, t1.t], [t1.t])
                stt(t1[:, 0:NB], a_ext[:, 0:NB], w0, t1[:, 0:NB], ALU.mult, ALU.add, [a_ext.t, wcT.t, t1.t], [t1.t])
                if bi < LASTB:
                    cp("pool", aprev[:, j, :], a_ext[:, 2 + 510:2 + 512], [a_ext.t], [aprev.t])
                else:
                    so = 512
                    cp("pool", exs[:, :, 0:2], ccT[:, l, j, :].rearrange("p (b r) -> p b r", r=2), [ccT.t], [exs.t])
                    cp("pool", exs[:, :, 2:6], a_ext[:, 2 + so:2 + so + 16].rearrange("p (b t) -> p b t", t=TS),
                       [a_ext.t], [exs.t])
                    t1s = t1[:, so:so + 16].rearrange("p (b t) -> p b t", t=TS)
                    ts(t1s, exs[:, :, 2:6], w2, bc, ALU.mult, ALU.add, [exs.t, wcT.t, bcT.t, t1.t], [t1.t])
                    stt(t1s, exs[:, :, 1:5], w1, t1s, ALU.mult, ALU.add, [exs.t, wcT.t, t1.t], [t1.t])
                    stt(t1s, exs[:, :, 0:4], w0, t1s, ALU.mult, ALU.add, [exs.t, wcT.t, t1.t], [t1.t])
                    cp("pool", convo[:, j, 0:2], a_ext[:, 2 + 510:2 + 512], [a_ext.t], [convo.t])
                    cp("pool", convo[:, j, 2:2 + NS * 2].rearrange("p (b r) -> p b r", r=2), exs[:, :, 4:6],
                       [exs.t], [convo.t])
                act(u1[:, 0:NB], t1[:, 0:NB], AF.Square, [t1.t], [u1.t])
                act(u1[:, 0:NB], u1[:, 0:NB], AF.Identity, [u1.t, one_t.t], [u1.t], scale=0.044715, bias=one_t[:, 0:1])
                tt(u1[:, 0:NB], u1[:, 0:NB], t1[:, 0:NB], ALU.mult, [u1.t, t1.t], [u1.t])
                act(u1[:, 0:NB], u1[:, 0:NB], AF.Sigmoid, [u1.t], [u1.t], scale=GC)
                tt(u1[:, 0:NB], u1[:, 0:NB], t1[:, 0:NB], ALU.mult, [u1.t, t1.t], [u1.t], eng="pool")
                for ni, (o, n) in enumerate(NT):
                    pb = psum[4 + (j % 2) * 2 + ni]
                    tt(yT[:, j, o:o + n], pb[:, 0:n], u1[:, o:o + n], ALU.mult, [pb.t, u1.t], [yT.t])
            wo2 = [A.tile([128, NJ, 128], BF16, "wo2_%d" % i) for i in range(2)]
            fsb = [A.tile([128, 528], F32, "fsb%d" % i) for i in range(2)]
            for oc in range(KC):
                wt = wo2[oc % 2]
                load_cast_multi([(wt[:, j0:j1, :],
                                  w_ffn_out[l, j0 * 128:j1 * 128, oc * 128:(oc + 1) * 128].rearrange("(j p) c -> p j c", p=128))
                                 for (j0, j1) in ((0, 16), (16, 32), (32, NJ))], wt.t)
                fs = fsb[oc % 2]
                for ni, (o, n) in enumerate(NT):
                    pf_ = psum[ni]
                    for j in range(NJ):
                        mm(pf_[:, 0:n], wt[:, j, :], yT[:, j, o:o + n], j == 0, j == NJ - 1, [wt.t, yT.t], [pf_.t])
                    cp("dve", fs[:, o:o + n], pf_[:, 0:n], [pf_.t], [fs.t])
                    sb = sqb[ni % 2]
                    act(sb[:, 0:n], fs[:, o:o + n], AF.Square, [fs.t], [sb.t])
                    mm(psum[3 + ni][:, 0:n], ones_b[:], sb[:, 0:n], oc == 0, oc == KC - 1, [ones_b.t, sb.t],
                       [psum[3 + ni].t])
                P.dma("pool", lambda e, fs=fs, oc=oc, b0=b0, NB=NB: [e.dma_start(out=fT_d[oc, :, b0:b0 + NB],
                                                                               in_=fs[:, 0:NB])],
                      1, fs.t, reads=[fs.t], writes=[fT_tok[oc]])
            for ni, (o, n) in enumerate(NT):
                rsqrt(rf[:, o:o + n], psum[3 + ni][:, 0:n], 1.0 / D, [psum[3 + ni].t], [rf.t])
            A.release(mC1)
            mC2 = A.mark()
            h2 = A.tile([128, KC, 528], F32, "h2")
            fl = [A.tile([128, 528], F32, "fl%d" % i) for i in range(2)]
            xo = [A.tile([128, 528], BF16, "xo%d" % i) for i in range(2)]
            P.dma("sp", lambda e, b0=b0, NB=NB: [e.dma_start(out=h2[:, :, 0:NB],
                                                             in_=hT_d[:, :, b0:b0 + NB].rearrange("k p t -> p k t"))],
                  1, h2.t, reads=[hT_tok[k][ti] for k in range(KC) for ti in tis], writes=[h2.t])
            for oc in range(KC):
                f_ = fl[oc % 2]
                P.dma("sp", lambda e, f_=f_, oc=oc, b0=b0, NB=NB: [e.dma_start(out=f_[:, 0:NB],
                                                                              in_=fT_d[oc, :, b0:b0 + NB])],
                      1, f_.t, reads=[fT_tok[oc]], writes=[f_.t])
                tt(f_[:, 0:NB], f_[:, 0:NB], rf[:, 0:NB], ALU.mult, [f_.t, rf.t], [f_.t])
                stt(h2[:, oc, 0:NB], f_[:, 0:NB], gT[:, l * 4 + 3, oc:oc + 1], h2[:, oc, 0:NB], ALU.mult, ALU.add,
                    [f_.t, gT.t, h2.t], [h2.t])
                for ni, (o, n) in enumerate(NT):
                    sb = sqb[ni % 2]
                    act(sb[:, 0:n], h2[:, oc, o:o + n], AF.Square, [h2.t], [sb.t])
                    mm(psum[ni][:, 0:n], ones_b[:], sb[:, 0:n], oc == 0, oc == KC - 1, [ones_b.t, sb.t], [psum[ni].t])
            if l == 0:
                for ni, (o, n) in enumerate(NT):
                    rsqrt(r3[:, o:o + n], psum[ni][:, 0:n], 1.0 / D, [psum[ni].t], [r3.t])
                P.dma("pool", lambda e, b0=b0, NB=NB: [e.dma_start(out=hT_d[:, :, b0:b0 + NB].rearrange("k p t -> p k t"),
                                                                 in_=h2[:, :, 0:NB])],
                      1, h2.t, reads=[h2.t], writes=[hT_tok[k][ti] for k in range(KC) for ti in tis])
                for (gsel, dst, dtok) in ((gT[:, 4, :], xnA_d, xnA_tok[bi]), (kvgT[:, :], xkv_d, xkv_tok[bi])):
                    for oc in range(KC):
                        x_ = xo[oc % 2]
                        stt(x_[:, 0:NB], h2[:, oc, 0:NB], gsel[:, oc:oc + 1], r3[:, 0:NB], ALU.mult, ALU.mult,
                            [h2.t, gT.t, kvgT.t, r3.t], [x_.t])
                        P.dma("pool", lambda e, x_=x_, oc=oc, b0=b0, NB=NB, dst=dst: [
                            e.dma_start(out=dst[oc, :, b0:b0 + NB], in_=x_[:, 0:NB])],
                            1, x_.t, reads=[x_.t], writes=[dtok])
            else:
                yt = [A.tile([128, D], F32, "yt%d" % i) for i in range(2)]
                for si, s0 in enumerate(range(0, NB, 128)):
                    rows = min(128, NB - s0)
                    y_ = yt[si % 2]
                    for q4 in range(4):
                        pt = psum[4 + q4 % 2]
                        for k in range(4):
                            oc = q4 * 4 + k
                            tr(pt[0:rows, k * 128:(k + 1) * 128], h2[:, oc, s0:s0 + rows], ident_f[:],
                               [h2.t, ident_f.t], [pt.t])
                        cp("act", y_[0:rows, q4 * 512:(q4 + 1) * 512], pt[0:rows, :], [pt.t], [y_.t])
                    g0_ = b0 + s0
                    dst = o_yp[g0_:g0_ + rows, :] if g0_ < T else o_ys[:, :]
                    dma_out(dst, y_[0:rows, :], y_.t)
            A.release(mC2)
        cvt = A.tile([2 + NS * 2, DFF], F32, "cvt")
        nr = 2 + NS * 2
        for j0 in range(0, NJ, 4):
            pt = psum[6 + (j0 // 4) % 2]
            for k in range(4):
                tr(pt[0:nr, k * 128:(k + 1) * 128], convo[:, j0 + k, :], ident_f[:], [convo.t, ident_f.t], [pt.t])
            cp("act", cvt[:, j0 * 128:(j0 + 4) * 128], pt[0:nr, :], [pt.t], [cvt.t])
        dma_out(o_cvp[l], cvt[0:2, :], cvt.t)
        dma_out(o_cvs[l].rearrange("b r f -> (b r) f"), cvt[2:nr, :], cvt.t)
        A.release(mC)

    if 'C' in PH:
        phase_C(0)

    kvT_d = nc.dram_tensor("kvT_d", [8, 128, NTOK], BF16, kind="Internal").ap()
    kvT_tok = P.tok("kvT_d")
    vtm_d = nc.dram_tensor("vtm_d", [17, 128, 512], BF16, kind="Internal").ap()
    vtm_tok = P.tok("vtm_d")
    FB = [0, 1, 2, 3, 4, 5, 8, 9]

    def phase_K():
        mK = A.mark()
        wkv = A.tile([128, KC, 1536], BF16, "wkv")
        load_cast_multi([(wkv[:, kc, :], w_kv_b[kc * 128:(kc + 1) * 128, :]) for kc in range(KC)], wkv.t)
        xk2 = [A.tile([128, KC, 512], BF16, "xk%d" % i) for i in range(2)]
        kvo = [A.tile([128, 1536], F32, "kvo%d" % i) for i in range(2)]
        vst = [A.tile([128, 512], BF16, "vst%d" % i) for i in range(2)]
        kst = [A.tile([128, 8, 512], BF16, "kst%d" % i) for i in range(2)]
        i = 0
        for ti in range(5):
            c0t, nt = CT[ti]
            xk = xk2[ti % 2]
            P.dma("sp", lambda e, xk=xk, c0t=c0t, nt=nt: [e.dma_start(out=xk[:, :, 0:nt],
                                                                       in_=xkv_d[:, :, c0t:c0t + nt].rearrange("k p t -> p k t"))],
                  1, xk.t, reads=xkv_tok, writes=[xk.t])
            for sub in range((nt + 127) // 128):
                rows = min(128, nt - sub * 128)
                c0 = c0t + sub * 128
                lo = sub * 128
                ko = kvo[i % 2]
                vs = vst[i % 2]
                for nb in range(3):
                    pt = psum[nb + 3 * (i % 2)]
                    for kc in range(KC):
                        mm(pt[0:rows, :], xk[:, kc, lo:lo + rows], wkv[:, kc, nb * 512:(nb + 1) * 512], kc == 0,
                           kc == KC - 1, [xk.t, wkv.t], [pt.t])
                    cp("act" if nb % 2 else "dve", ko[0:rows, nb * 512:(nb + 1) * 512], pt[0:rows, :], [pt.t], [ko.t])
                cp("pool", vs[0:rows, 0:256], ko[0:rows, 768:1024], [ko.t], [vs.t])
                cp("pool", vs[0:rows, 256:512], ko[0:rows, 1280:1536], [ko.t], [vs.t])
                P.dma("pool", lambda e, vs=vs, i=i, rows=rows: [e.dma_start(out=vtm_d[i, 0:rows, :], in_=vs[0:rows, :])],
                      1, vs.t, reads=[vs.t], writes=[vtm_tok])
                if i < 16:
                    dma_out(o_kvp[c0:c0 + 128, :], ko[:, 0:1024], ko.t)
                    if i >= 12:
                        dma_out(o_winp[(i - 12) * 128:(i - 11) * 128, :], ko[:, 1024:1536], ko.t)
                else:
                    dma_out(o_kvs[:, :], ko[0:rows, 0:1024], ko.t)
                    dma_out(o_wins[:, :], ko[0:rows, 1024:1536], ko.t)
                i += 1
            ks = kst[ti % 2]
            for fi, cb in enumerate(FB):
                pt = psum[6 + fi % 2]
                for kc in range(KC):
                    mm(pt[:, 0:nt], wkv[:, kc, cb * 128:(cb + 1) * 128], xk[:, kc, 0:nt], kc == 0, kc == KC - 1,
                       [xk.t, wkv.t], [pt.t])
                cp("act", ks[:, fi, 0:nt], pt[:, 0:nt], [pt.t], [ks.t])
            P.dma("pool", lambda e, ks=ks, c0t=c0t, nt=nt: [e.dma_start(out=kvT_d[:, :, c0t:c0t + nt].rearrange("f p t -> p f t"),
                                                                        in_=ks[:, :, 0:nt])],
                  1, ks.t, reads=[ks.t], writes=[kvT_tok])
        A.release(mK)

    if 'K' in PH:
        phase_K()

    SLOPE = [2.0 ** (-8.0 * (i + 1) / H) for i in range(H)]
    NEGM = -30000.0

    def phase_N():
        dist0 = A.tile([128, 512], F32, "dist0")
        distc = A.tile([128, 512], F32, "distc")
        itmp = A.tile([128, 512], I32, "itmp")
        P.op("pool", lambda e: e.iota(itmp[:], pattern=[[1, 512]], base=0, channel_multiplier=-1), (), [itmp.t])
        cp("pool", dist0[:], itmp[:], [itmp.t], [dist0.t])
        P.op("pool", lambda e: e.iota(itmp[:], pattern=[[1, 512]], base=-31, channel_multiplier=-16), [dist0.t], [itmp.t])
        cp("pool", distc[:], itmp[:], [itmp.t], [distc.t])
        maug = A.tile([128, 33], BF16, "maug")
        memset("pool", maug[:], 1.0, [maug.t])
        P.op("pool", lambda e: e.affine_select(out=maug[:], in_=maug[:], pattern=[[-4, 33]], compare_op=ALU.is_ge,
                                               fill=0.0, base=1, channel_multiplier=1), [maug.t], [maug.t])
        P.op("pool", lambda e: e.affine_select(out=maug[:], in_=maug[:], pattern=[[4, 33]], compare_op=ALU.is_ge,
                                               fill=0.0, base=3, channel_multiplier=-1), [maug.t], [maug.t])
        memset("pool", maug[:, 32:33], 1.0, [maug.t])
        eall = A.tile([32, 16, 128], BF16, "eall")
        memset("pool", eall[:], 1.0, [eall.t])
        P.op("pool", lambda e: e.affine_select(out=eall[:], in_=eall[:], pattern=[[128, 16], [1, 128]],
                                               compare_op=ALU.is_ge, fill=0.0, base=0, channel_multiplier=-64),
             [eall.t], [eall.t])
        P.op("pool", lambda e: e.affine_select(out=eall[:], in_=eall[:], pattern=[[-128, 16], [-1, 128]],
                                               compare_op=ALU.is_ge, fill=0.0, base=63, channel_multiplier=64),
             [eall.t], [eall.t])
        memset("pool", sel36[:], 1.0, [sel36.t])
        P.op("pool", lambda e: e.affine_select(out=sel36[:], in_=sel36[:], pattern=[[-1, 36], [0, 128]],
                                               compare_op=ALU.is_equal, fill=0.0, base=0, channel_multiplier=1),
             [sel36.t], [sel36.t])
        validm = A.tile([128, 16, 32], F32, "validm")
        forced = A.tile([128, 16, 32], F32, "forced")
        VB = A.tile([128, 16, 32], F32, "VB")
        memset("pool", validm[:], 1.0, [validm.t])
        P.op("pool", lambda e: e.affine_select(out=validm[:], in_=validm[:], pattern=[[128, 16], [-64, 32]],
                                               compare_op=ALU.is_ge, fill=0.0, base=0, channel_multiplier=1),
             [validm.t], [validm.t])
        P.op("pool", lambda e: e.affine_select(out=forced[:], in_=validm[:], pattern=[[-128, 16], [64, 32]],
                                               compare_op=ALU.is_gt, fill=0.0, base=128, channel_multiplier=-1),
             [validm.t], [forced.t])
        memset("pool", forced[:, :, 0:1], 1.0, [forced.t])
        ts(VB[:], validm[:], -1.0, 1e30, ALU.add, ALU.mult, [validm.t], [VB.t], eng="pool")
        stt(VB[:], forced[:], 1e4, VB[:], ALU.mult, ALU.add, [forced.t, VB.t], [VB.t])
        tiny = 1e-30

        g36 = A.tile([36, NTOK], BF16, "g36")
        kcT = A.tile([128, 2, 128], BF16, "kcT")
        vc_tm = A.tile([128, 2, 128], BF16, "vc_tm")
        qh = A.tile([128, NTOK], BF16, "qh")
        impacc = A.tile([128, 2, 16, 32], F32, "impacc")
        bt = [None, None]
        mk = [A.tile([128, 512], F32, "mk0")] * 2
        BIG = 1.0e6
        dmc = [A.tile([128, 512], F32, "dmc%d" % i) for i in range(4)]
        for qt_ in range(4):
            ts(mk[0][:], distc[:], float(-512 * qt_), BIG, ALU.is_lt, ALU.mult, [distc.t], [mk[0].t], eng="pool")
            stt(dmc[qt_][:], distc[:], float(512 * qt_), mk[0][:], ALU.add, ALU.add, [distc.t, mk[0].t], [dmc[qt_].t])
        kk16 = A.tile([128, 16], F32, "kk16")
        P.op("pool", lambda e: e.iota(itmp[:, 0:16], pattern=[[128, 16]], base=0, channel_multiplier=0), [distc.t],
             [itmp.t])
        cp("pool", kk16[:], itmp[:, 0:16], [itmp.t], [kk16.t])
        hb = A.tile([128, 16], F32, "hb")
        sc = [A.tile([128, 512], F32, "sc%d" % i) for i in range(2)]
        pc = [A.tile([128, 512], BF16, "pc%d" % i) for i in range(2)]
        rzt = A.tile([128, 512], F32, "rzt")
        o32 = A.tile([128, 512], F32, "o32n")
        ps33 = A.tile([128, 4, 33], F32, "ps33")
        zr = A.tile([128, 4], F32, "zr")
        mN1 = A.mark()
        xnT = A.tile([128, KC, NTOK], BF16, "xnT1")
        P.dma("sp", lambda e: [e.dma_start(out=xnT[:], in_=xnA_d.rearrange("k p t -> p k t"))], 1, xnT.t,
              reads=xnA_tok, writes=[xnT.t])
        wq = A.tile([128, KC, 128], BF16, "wq")

        def projn(wt, ncols, ti, pbank):
            c0, n = CT[ti]
            pt = psum[pbank]
            for kc in range(KC):
                mm(pt[0:ncols, 0:n], wt[:, kc, 0:ncols], xnT[:, kc, c0:c0 + n], kc == 0, kc == KC - 1,
                   [wt.t, xnT.t], [pt.t])
            return pt

        load_cast(wq[:, :, 0:36], wq.t, w_in_b[0, :, W:W + 36].rearrange("(k p) c -> p k c", p=128), [128, KC, 36])
        for ti in range(5):
            c0, n = CT[ti]
            pt = projn(wq, 36, ti, ti % 2)
            act(g36[:, c0:c0 + n], pt[0:36, 0:n], AF.Sigmoid, [pt.t], [g36.t])
        cp("pool", g36s[:], g36[:, T:NTOK], [g36.t], [g36s.t])

        mcp = A.mark()
        kvc = A.tile([128, 2, T], BF16, "kvc")
        w1b = A.tile([128, 32, 128], BF16, "w1b")
        w2b = A.tile([128, 2, 128], BF16, "w2b")
        pe32 = A.tile([128, 2, 32], F32, "pe32")
        peb = A.tile([128, 2, 32], BF16, "peb")
        preb = A.tile([128, 2], F32, "preb")
        tg = A.tile([128, 128], F32, "tg")
        ug = A.tile([128, 128], F32, "ug")
        gl = A.tile([128, 128], BF16, "gl")
        dma_in(pe32[:], pe32.t, cmp_pos.rearrange("k j d -> d k j"), nonc=True)
        cp("pool", peb[:], pe32[:], [pe32.t], [peb.t])
        for kv in range(2):
            load_cast(w2b[:, kv, :], w2b.t, w_cmp2[kv], [128, 128])
        for kv in range(2):
            for hf in range(2):
                load_cast(w1b[:, hf * 16:(hf + 1) * 16, :], w1b.t,
                          w_cmp1[kv, hf * 2048:(hf + 1) * 2048, :].rearrange("(j d) h -> d j h", d=128), [128, 16, 128])
            pp = psum[2]
            for js in range(32):
                mm(pp[:, 0:1], w1b[:, js, :], peb[:, kv, js:js + 1], js == 0, js == 31, [w1b.t, peb.t], [pp.t])
            cp("dve", preb[:, kv:kv + 1], pp[:, 0:1], [pp.t], [preb.t])
            P.dma("sp", lambda e, kv=kv: [e.dma_start(out=kvc[:], in_=kvT_d[kv * 2:kv * 2 + 2, :, 0:T].rearrange("f p t -> p f t"))],
                  1, kvc.t, reads=[kvT_tok], writes=[kvc.t])
            for g in range(2):
                kview = kvc[:, g, :].rearrange("p (c s) -> p c s", s=16)
                pp = psum[3]
                for js in range(32):
                    j_, s_ = js // 16, js % 16
                    mm(pp[:, 0:127], w1b[:, js, :], kview[:, j_:j_ + 127, s_], js == 0, js == 31, [w1b.t, kvc.t], [pp.t])
                act(tg[:, 0:127], pp[:, 0:127], AF.Identity, [pp.t, preb.t], [tg.t], bias=preb[:, kv:kv + 1])
                act(ug[:, 0:127], tg[:, 0:127], AF.Square, [tg.t], [ug.t])
                ts(ug[:, 0:127], ug[:, 0:127], 0.044715, 1.0, ALU.mult, ALU.add, [ug.t], [ug.t])
                tt(ug[:, 0:127], ug[:, 0:127], tg[:, 0:127], ALU.mult, [ug.t, tg.t], [ug.t])
                act(ug[:, 0:127], ug[:, 0:127], AF.Sigmoid, [ug.t], [ug.t], scale=GC)
                tt(gl[:, 0:127], ug[:, 0:127], tg[:, 0:127], ALU.mult, [ug.t, tg.t], [gl.t])
                pq_ = psum[4]
                if kv == 0:
                    mm(pq_[:, 0:127], w2b[:, 0, :], gl[:, 0:127], True, True, [w2b.t, gl.t], [pq_.t])
                    cp("act", kcT[:, g, 0:127], pq_[:, 0:127], [pq_.t], [kcT.t])
                else:
                    mm(pq_[0:127, 0:128], gl[:, 0:127], w2b[:, 1, :], True, True, [w2b.t, gl.t], [pq_.t])
                    cp("act", vc_tm[0:127, g, :], pq_[0:127, 0:128], [pq_.t], [vc_tm.t])
        A.release(mcp)

        memset("pool", impacc[:], 0.0, [impacc.t])
        it = 0
        for hh in range(H):
            g, r = hh // 6, hh % 6
            sl = SLOPE[hh]
            load_cast(wq[:], wq.t, w_in_b[0, :, hh * 128:(hh + 1) * 128].rearrange("(k p) c -> p k c", p=128),
                      [128, KC, 128])
            for ti in range(5):
                c0, n = CT[ti]
                pt = projn(wq, 128, ti, ti % 2)
                act(qh[:, c0:c0 + n], pt[:, 0:n], AF.Copy, [pt.t], [qh.t], scale=SCALE)
            dma_out(qT_d[hh], qh[:], qh.t, writes=[qT_tok[hh]])
            for qt in range(4):
                t0 = qt * 512
                b_, m_, s_, p_ = bt[it % 2], mk[it % 2], sc[it % 2], pc[it % 2]
                it += 1
                psc = psum[2 + it % 2]
                mm(psc[0:127, :], kcT[:, g, 0:127], qh[:, t0:t0 + 512], True, True, [kcT.t, qh.t], [psc.t])
                stt(s_[0:127, :], dmc[qt][0:127, :], -sl, psc[0:127, :], ALU.mult, ALU.add, [dmc[qt].t, psc.t], [s_.t])
                act(p_[0:127, :], s_[0:127, :], AF.Exp, [s_.t], [p_.t])
                po, pz, pp = psum[4], psum[5], psum[6]
                mm(po[:, :], vc_tm[0:127, g, :], p_[0:127, :], True, True, [vc_tm.t, p_.t], [po.t])
                mm(pz[:, :], ones_b[0:127, :], p_[0:127, :], True, True, [ones_b.t, p_.t], [pz.t])
                for sub in range(4):
                    mm(pp[:, sub * 33:(sub + 1) * 33], p_[0:127, sub * 128:(sub + 1) * 128], maug[0:127, :], True, True,
                       [p_.t, maug.t], [pp.t])
                ts(rzt[:], pz[:], tiny, None, ALU.max, None, [pz.t], [rzt.t])
                P.op("dve", lambda e: e.reciprocal(rzt[:], rzt[:]), [rzt.t], [rzt.t])
                tt(o32[:], po[:], rzt[:], ALU.mult, [po.t, rzt.t], [o32.t])
                P.dma("pool", lambda e, hh=hh, t0=t0: [e.dma_start(out=ocmp_d[hh, :, t0:t0 + 512], in_=o32[:])], 1, o32.t,
                      reads=[o32.t], writes=[ocmp_tok[hh]])
                cp("dve", ps33[:], pp[:, 0:132].rearrange("p (a b) -> p a b", b=33), [pp.t], [ps33.t])
                ts(zr[:], ps33[:, :, 32], tiny, None, ALU.max, None, [ps33.t], [zr.t])
                P.op("dve", lambda e: e.reciprocal(zr[:], zr[:]), [zr.t], [zr.t])
                for sub in range(4):
                    stt(impacc[:, g, qt * 4 + sub, :], ps33[:, sub, 0:32], zr[:, sub:sub + 1],
                        impacc[:, g, qt * 4 + sub, :], ALU.mult, ALU.add, [ps33.t, zr.t, impacc.t], [impacc.t])

        nonlocal wb4, qT, mixo, pT2, rz, xn_tok
        wb4 = [A.tile([128, KC, 128], BF16, "wbn")]
        qT = A.tile([128, 512], BF16, "qTn")
        mixo = A.tile([128, NTOK], BF16, "mixo2")
        pT2 = [A.tile([128, 512], BF16, "pTn%d" % i) for i in range(2)]
        rz = A.tile([128, 512], F32, "rzn")
        xn_tok = [xnT.t] * 5
        mem_attn_l1(xnT)
        A.release(mN1)

        kvs = A.tile([128, 4, T], BF16, "kvs")
        P.dma("sp", lambda e: [e.dma_start(out=kvs[:], in_=kvT_d[4:8, :, 0:T].rearrange("f p t -> p f t"))], 1, kvs.t,
              reads=[kvT_tok], writes=[kvs.t])
        vtm = A.tile([128, 16, 512], BF16, "vtm")
        P.dma("sp", lambda e: [e.dma_start(out=vtm[:], in_=vtm_d[0:16].rearrange("i p c -> p i c"))], 1, vtm.t,
              reads=[vtm_tok], writes=[vtm.t])
        negselT = A.tile([32, 2, T], BF16, "negselT")
        dmb = {}
        for d_ in (-384, -256, -128, 0):
            tl_ = A.tile([128, 512], F32, "dmb%d" % (-d_))
            ts(mk[0][:], dist0[:], float(-d_), BIG, ALU.is_lt, ALU.mult, [dist0.t], [mk[0].t], eng="pool")
            stt(tl_[:], dist0[:], float(d_), mk[0][:], ALU.add, ALU.add, [dist0.t, mk[0].t], [tl_.t])
            dmb[d_] = tl_
        dmw = {}
        for d_ in (128, 256, 384, 512):
            tl_ = A.tile([128, 512], F32, "dmw%d" % d_)
            ts(mk[0][:], dist0[:], float(512 - d_), BIG, ALU.is_ge, ALU.mult, [dist0.t], [mk[0].t], eng="pool")
            stt(tl_[:], dist0[:], float(d_), mk[0][:], ALU.add, ALU.add, [dist0.t, mk[0].t], [tl_.t])
            dmw[d_] = tl_
        scr = A.tile([128, 32], F32, "scr")
        scr2 = A.tile([128, 32], F32, "scr2")
        m8a = A.tile([128, 8], F32, "m8a")
        m8b = A.tile([128, 8], F32, "m8b")
        selm = A.tile([128, 32], F32, "selm")
        for g in range(2):
            for t16 in range(16):
                tt(scr[:], impacc[:, g, t16, :], validm[:, t16, :], ALU.mult, [impacc.t, validm.t], [scr.t])
                tt(scr[:], scr[:], VB[:, t16, :], ALU.add, [scr.t, VB.t], [scr.t])
                P.op("dve", lambda e: e.max(out=m8a[:], in_=scr[:]), [scr.t], [m8a.t])
                P.op("dve", lambda e: e.match_replace(out=scr2[:], in_to_replace=m8a[:], in_values=scr[:],
                                                      imm_value=-1e30), [m8a.t, scr.t], [scr2.t])
                P.op("dve", lambda e: e.max(out=m8b[:], in_=scr2[:]), [scr2.t], [m8b.t])
                ts(selm[:], scr[:], m8b[:, 7:8], None, ALU.is_ge, None, [scr.t, m8b.t], [selm.t])
                tt(selm[:], selm[:], validm[:, t16, :], ALU.mult, [selm.t, validm.t], [selm.t])
                ts(selm[:], selm[:], -1.0, -NEGM, ALU.add, ALU.mult, [selm.t], [selm.t])
                ptn = psum[t16 % 2]
                tr(ptn[0:32, 0:128], selm[:], ident_f[:], [selm.t, ident_f.t], [ptn.t])
                cp("act", negselT[:, g, t16 * 128:(t16 + 1) * 128], ptn[0:32, 0:128], [ptn.t], [negselT.t])

        mix32 = A.tile([128, 512], F32, "mix32")
        oc32 = A.tile([128, 512], F32, "oc32")
        gsb = A.tile([128, 512], F32, "gsb")
        mixon = A.tile([128, NTOK], BF16, "mixon")
        for hh in range(H):
            g, r = hh // 6, hh % 6
            sl = SLOPE[hh]
            P.dma("sp", lambda e, hh=hh: [e.dma_start(out=qh[:], in_=qT_d[hh])], 1, qh.t, reads=[qT_tok[hh]],
                  writes=[qh.t])
            ts(hb[:], kk16[:], -sl, None, ALU.mult, None, [kk16.t], [hb.t], eng="pool")
            for qt in range(4):
                t0 = qt * 512
                P.dma("sp", lambda e, hh=hh, t0=t0: [e.dma_start(out=oc32[:], in_=ocmp_d[hh, :, t0:t0 + 512])], 1, oc32.t,
                      reads=[ocmp_tok[hh]], writes=[oc32.t])
                pg = psum[7]
                mm(pg[:, :], sel36[0:36, hh * 3 + 0, :], g36[0:36, t0:t0 + 512], True, True, [sel36.t, g36.t], [pg.t])
                tt(mix32[:], oc32[:], pg[:], ALU.mult, [oc32.t, pg.t], [mix32.t])
                for br in (1, 2):
                    kb_lo = 0 if br == 1 else max(0, 4 * qt - 4)
                    kb_hi = 4 * qt + 3
                    po, pz = (psum[4], psum[5]) if br == 1 else (psum[0], psum[6])
                    for kb in range(kb_lo, kb_hi + 1):
                        delta = t0 - kb * 128
                        b_, m_, s_, p_ = bt[it % 2], mk[it % 2], sc[it % 2], pc[it % 2]
                        it += 1
                        if delta <= 0:
                            dm_, eb_ = dmb[delta], None
                        elif br == 2:
                            dm_, eb_ = dmw[delta], None
                        else:
                            dm_, eb_ = dist0, hb[:, delta // 128:delta // 128 + 1]
                        psc = psum[2 + it % 2]
                        kblk = (0 if br == 1 else 2) + g
                        mm(psc[:, :], kvs[:, kblk, kb * 128:(kb + 1) * 128], qh[:, t0:t0 + 512], True, br == 2,
                           [kvs.t, qh.t], [psc.t])
                        if br == 1:
                            mm(psc[:, :], eall[:, kb, :], negselT[:, g, t0:t0 + 512], False, True,
                               [eall.t, negselT.t], [psc.t])
                        stt(s_[:], dm_[:], -sl, psc[:], ALU.mult, ALU.add, [dm_.t, psc.t], [s_.t])
                        if eb_ is None:
                            act(p_[:], s_[:], AF.Exp, [s_.t], [p_.t])
                        else:
                            act(p_[:], s_[:], AF.Exp, [s_.t, hb.t], [p_.t], bias=eb_)
                        vblk = (0 if br == 1 else 2) + g
                        mm(po[:, :], vtm[:, kb, vblk * 128:(vblk + 1) * 128], p_[:], kb == kb_lo, kb == kb_hi,
                           [vtm.t, p_.t], [po.t])
                        mm(pz[:, :], ones_b[:], p_[:], kb == kb_lo, kb == kb_hi, [ones_b.t, p_.t], [pz.t])
                    ts(rzt[:], pz[:], tiny, None, ALU.max, None, [pz.t], [rzt.t])
                    P.op("dve", lambda e: e.reciprocal(rzt[:], rzt[:]), [rzt.t], [rzt.t])
                    tt(o32[:], po[:], rzt[:], ALU.mult, [po.t, rzt.t], [o32.t])
                    pg = psum[7]
                    mm(pg[:, :], sel36[0:36, hh * 3 + br, :], g36[0:36, t0:t0 + 512], True, True, [sel36.t, g36.t], [pg.t])
                    tt(gsb[:], o32[:], pg[:], ALU.mult, [o32.t, pg.t], [gsb.t])
                    if br == 1:
                        tt(mix32[:], mix32[:], gsb[:], ALU.add, [mix32.t, gsb.t], [mix32.t])
                    else:
                        tt(mixon[:, t0:t0 + 512], mix32[:], gsb[:], ALU.add, [mix32.t, gsb.t], [mixon.t])
            P.dma("pool", lambda e, hh=hh: [e.dma_start(out=mixT_d[hh, :, 0:T], in_=mixon[:, 0:T])], 1, mixon.t,
                  reads=[mixon.t], writes=[mix_tok[hh]])

    def phase_S():
        tiny = 1e-30
        ckv_rows = ckv.rearrange("n p (h r) g d -> (n p h) (r g d)", h=2)
        w1b2 = A.tile([128, 2, 32, 128], BF16, "w1b2")
        w2b = A.tile([128, 2, 128], BF16, "w2bs")
        pe32 = A.tile([128, 2, 32], F32, "pe32s")
        peb = A.tile([128, 2, 32], BF16, "pebs")
        preb = A.tile([128, 2], F32, "prebs")
        dma_in(pe32[:], pe32.t, cmp_pos.rearrange("k j d -> d k j"), nonc=True)
        cp("pool", peb[:], pe32[:], [pe32.t], [peb.t])
        for kv in range(2):
            load_cast(w2b[:, kv, :], w2b.t, w_cmp2[kv], [128, 128])
            for hf in range(2):
                load_cast(w1b2[:, kv, hf * 16:(hf + 1) * 16, :], w1b2.t,
                          w_cmp1[kv, hf * 2048:(hf + 1) * 2048, :].rearrange("(j d) h -> d j h", d=128), [128, 16, 128])
            pp = psum[2]
            for js in range(32):
                mm(pp[:, 0:1], w1b2[:, kv, js, :], peb[:, kv, js:js + 1], js == 0, js == 31, [w1b2.t, peb.t], [pp.t])
            cp("dve", preb[:, kv:kv + 1], pp[:, 0:1], [pp.t], [preb.t])
        qs = A.tile([128, H, NS * TS], BF16, "qs")
        P.dma("sp", lambda e: [e.dma_start(out=qs[:], in_=qT_d[:, :, T:NTOK].rearrange("h p t -> p h t"))], 1, qs.t,
              reads=qT_tok, writes=[qs.t])
        gbs = A.tile([128, 3, H, NS * TS], F32, "gbs")
        for br in range(3):
            pg = psum[br]
            for hh in range(H):
                mm(pg[:, hh * 16:(hh + 1) * 16], sel36[0:36, hh * 3 + br, :], g36s[0:36, :], True, True,
                   [sel36.t, g36s.t], [pg.t])
            cp("act", gbs[:, br, :, :], pg[:, 0:H * 16].rearrange("p (h t) -> p h t", t=16), [pg.t], [gbs.t])

        def qsg(b_, g):
            return qs[:, g * 6:(g + 1) * 6, b_ * TS:(b_ + 1) * TS].rearrange("p r t -> p t r")

        ptb = A.tile([128, NS * 64], I32, "ptb")
        dma_in(ptb[:], ptb.t, ptab.rearrange("b n -> (b n)").partition_broadcast(128))
        piota = A.tile([128, NS * 64], I32, "piota")
        P.op("pool", lambda e: e.iota(piota[:], pattern=[[0, NS * 64]], base=0, channel_multiplier=1), (), [piota.t])
        idx_all = A.tile([128, NS * 64], I32, "idx_all")
        stt(idx_all[:], ptb[:], 128.0, piota[:], ALU.mult, ALU.add, [ptb.t, piota.t], [idx_all.t])
        idx_h = [A.tile([128, NS * 64], I32, "idx_h%d" % i) for i in range(2)]
        ts(idx_h[0][:], idx_all[:], 2.0, None, ALU.mult, None, [idx_all.t], [idx_h[0].t])
        ts(idx_h[1][:], idx_all[:], 2.0, 1.0, ALU.mult, ALU.add, [idx_all.t], [idx_h[1].t])
        it32 = A.tile([128, 16], I32, "it32")
        dcf = A.tile([128, 16], F32, "dcf")
        bcs = A.tile([128, 2, 4, 4, 6], F32, "bcs")
        P.op("pool", lambda e: e.iota(it32[:], pattern=[[2048, 4], [-1, 4]], base=31 - 8192, channel_multiplier=16),
             (), [it32.t])
        cp("pool", dcf[:], it32[:], [it32.t], [dcf.t])
        for hh in range(H):
            g, r = hh // 6, hh % 6
            ts(bcs[:, g, :, :, r], dcf[:].rearrange("p (i t) -> p i t", t=4), SLOPE[hh], None, ALU.mult, None,
               [dcf.t], [bcs.t], eng="pool")
        for g in range(2):
            P.op("pool", lambda e, g=g: e.affine_select(out=bcs[:, g, 3, :, :], in_=bcs[:, g, 3, :, :],
                                                        pattern=[[0, 4], [0, 6]], compare_op=ALU.is_ge, fill=NEGM,
                                                        base=126, channel_multiplier=-1), [bcs.t], [bcs.t])
        dsf = A.tile([128, 4], F32, "dsf")
        bS0 = A.tile([128, 2, 4, 6], F32, "bS0")
        slS = A.tile([128, 2, 4, 6], F32, "slS")
        P.op("pool", lambda e: e.iota(it32[:, 0:4], pattern=[[-1, 4]], base=-8192, channel_multiplier=1), [dcf.t], [it32.t])
        cp("pool", dsf[:], it32[:, 0:4], [it32.t], [dsf.t])
        for hh in range(H):
            g, r = hh // 6, hh % 6
            ts(bS0[:, g, :, r], dsf[:], SLOPE[hh], None, ALU.mult, None, [dsf.t], [bS0.t], eng="pool")
            memset("pool", slS[:, g, :, r], SLOPE[hh] * 128.0, [slS.t])
        bSall = A.tile([128, 64, 48], F32, "bSall")
        for pg_ in range(64):
            stt(bSall[:, pg_, :], slS[:].rearrange("p g t r -> p (g t r)"), float(pg_),
                bS0[:].rearrange("p g t r -> p (g t r)"), ALU.mult, ALU.add, [slS.t, bS0.t], [bSall.t])
        bN = A.tile([4, 2, 4, 6], F32, "bN")
        dnf = A.tile([4, 4], F32, "dnf")
        P.op("pool", lambda e: e.iota(it32[0:4, 0:4], pattern=[[-1, 4]], base=0, channel_multiplier=1), [dsf.t], [it32.t])
        cp("pool", dnf[:], it32[0:4, 0:4], [it32.t], [dnf.t])
        for hh in range(H):
            g, r = hh // 6, hh % 6
            ts(bN[:, g, :, r], dnf[:], SLOPE[hh], None, ALU.mult, None, [dnf.t], [bN.t], eng="pool")
        P.op("pool", lambda e: e.affine_select(out=bN[:], in_=bN[:], pattern=[[0, 2], [1, 4], [0, 6]],
                                               compare_op=ALU.is_ge, fill=NEGM, base=0, channel_multiplier=-1),
             [bN.t], [bN.t])
        bW = A.tile([128, 4, 2, 4, 6], F32, "bW")
        dwf = A.tile([128, 16], F32, "dwf")
        P.op("pool", lambda e: e.iota(it32[:], pattern=[[128, 4], [-1, 4]], base=-512, channel_multiplier=1), [dnf.t],
             [it32.t])
        cp("pool", dwf[:], it32[:], [it32.t], [dwf.t])
        for hh in range(H):
            g, r = hh // 6, hh % 6
            ts(bW[:, :, g, :, r], dwf[:].rearrange("p (w t) -> p w t", t=4), SLOPE[hh], None, ALU.mult, None,
               [dwf.t], [bW.t], eng="pool")
        for g in range(2):
            P.op("pool", lambda e, g=g: e.affine_select(out=bW[:, :, g, :, :], in_=bW[:, :, g, :, :],
                                                        pattern=[[128, 4], [-1, 4], [0, 6]], compare_op=ALU.is_ge,
                                                        fill=NEGM, base=-1, channel_multiplier=1), [bW.t], [bW.t])
        rselT = A.tile([4, 2, 4, 6], BF16, "rselT")
        memset("pool", rselT[:], 1.0, [rselT.t])
        P.op("pool", lambda e: e.affine_select(out=rselT[:], in_=rselT[:], pattern=[[0, 2], [1, 4], [0, 6]],
                                               compare_op=ALU.is_equal, fill=0.0, base=0, channel_multiplier=-1),
             [rselT.t], [rselT.t])
        rself = A.tile([24, 4], F32, "rself")
        memset("pool", rself[:], 1.0, [rself.t])
        P.op("pool", lambda e: e.affine_select(out=rself[:], in_=rself[:], pattern=[[-6, 4]], compare_op=ALU.is_ge,
                                               fill=0.0, base=0, channel_multiplier=1), [rself.t], [rself.t])
        P.op("pool", lambda e: e.affine_select(out=rself[:], in_=rself[:], pattern=[[6, 4]], compare_op=ALU.is_ge,
                                               fill=0.0, base=5, channel_multiplier=-1), [rself.t], [rself.t])
        maug_s = A.tile([128, 4, 130], BF16, "maug_s")
        memset("pool", maug_s[:], 1.0, [maug_s.t])
        P.op("pool", lambda e: e.affine_select(out=maug_s[:], in_=maug_s[:], pattern=[[128, 4], [-4, 130]],
                                               compare_op=ALU.is_ge, fill=0.0, base=1, channel_multiplier=1),
             [maug_s.t], [maug_s.t])
        P.op("pool", lambda e: e.affine_select(out=maug_s[:], in_=maug_s[:], pattern=[[-128, 4], [4, 130]],
                                               compare_op=ALU.is_ge, fill=0.0, base=3, channel_multiplier=-1),
             [maug_s.t], [maug_s.t])
        memset("pool", maug_s[:, :, 129:130], 1.0, [maug_s.t])
        VBs = A.tile([4, 129], F32, "VBs")
        memset("pool", VBs[:], 0.0, [VBs.t])
        memset("pool", VBs[:, 0:1], 1e4, [VBs.t])
        memset("pool", VBs[:, 127:129], 1e4, [VBs.t])

        kcmpT = A.tile([128, 4, 8192], BF16, "kcmpT")
        nse_h = nc.alloc_sbuf_tensor_at("nse_alias", [4, 2, 129 * 64], BF16, offset=int(kcmpT.h.manual_sbuf_range[0]))
        nse = Tile(nse_h, kcmpT.t)
        gt2 = [A.tile([128, 512], F32, "gt%d" % i) for i in range(2)]
        gb2 = [A.tile([128, 512], BF16, "gb%d" % i) for i in range(2)]
        ksT2 = [A.tile([128, 2, 128], BF16, "ksT%d" % i) for i in range(2)]
        kcT_s = A.tile([128, 2, 512], BF16, "kcT_s")
        vc_s = A.tile([128, 4, 2, 128], BF16, "vc_s")
        memset("pool", kcT_s[:], 0.0, [kcT_s.t])
        memset("pool", vc_s[:], 0.0, [vc_s.t])
        tg = A.tile([128, 512], F32, "tgs")
        ug = A.tile([128, 512], F32, "ugs")
        gl = A.tile([128, 512], BF16, "gls")
        s192 = A.tile([128, 192], F32, "s192")
        p192 = A.tile([128, 192], BF16, "p192")
        s48 = [A.tile([128, 48], F32, "s48_%d" % i) for i in range(2)]
        p48 = [A.tile([128, 48], BF16, "p48_%d" % i) for i in range(2)]
        rz48 = A.tile([128, 48], F32, "rz48")
        o48 = A.tile([128, 2, 4, 6], F32, "o48")
        acc48 = A.tile([128, 2, 4, 6], F32, "acc48")
        ps_s = A.tile([24, 2, 130], F32, "ps_s")
        zr_s = A.tile([24, 2], F32, "zr_s")
        pn_s = A.tile([24, 2, 129], F32, "pn_s")
        scs = A.tile([4, 2, 129], F32, "scs")
        scs2 = A.tile([4, 129], F32, "scs2")
        m8a = A.tile([4, 8], F32, "m8as")
        m8b = A.tile([4, 8], F32, "m8bs")
        sels = A.tile([4, 129], F32, "sels")
        knew = A.tile([128, 4, TS], BF16, "knew")
        vnew = A.tile([4, 512], BF16, "vnew")
        mixs = A.tile([128, H, NS * TS], BF16, "mixs")
        gi = 0

        def gather(b_, pg_, c0):
            nonlocal gi
            gt, gb = gt2[gi % 2], gb2[gi % 2]
            gi += 1
            col = b_ * 64 + pg_
            ih = idx_h[c0 // 512]
            P.dma("pool", lambda e: [e.indirect_dma_start(
                out=gt[:, :], out_offset=None, in_=ckv_rows[:, :],
                in_offset=bass.IndirectOffsetOnAxis(ap=ih[:, col:col + 1], axis=0))], 1, gt.t,
                reads=[ih.t], writes=[gt.t])
            cp("dve" if gi % 2 else "act", gb[:], gt[:], [gt.t], [gb.t])
            return gb

        def finish_branch(po, pz, br, b_, first):
            ts(rz48[:], pz[:, 0:48], tiny, None, ALU.max, None, [pz.t], [rz48.t])
            P.op("dve", lambda e: e.reciprocal(rz48[:], rz48[:]), [rz48.t], [rz48.t])
            o48f = o48[:].rearrange("p g t r -> p (g t r)")
            tt(o48f, po[:, 0:48], rz48[:], ALU.mult, [po.t, rz48.t], [o48.t])
            gview = gbs[:, br, :, b_ * TS:(b_ + 1) * TS].rearrange("p (g r) t -> p g t r", g=2)
            if first:
                tt(acc48[:], o48[:], gview, ALU.mult, [o48.t, gbs.t], [acc48.t])
            else:
                tt(o48[:], o48[:], gview, ALU.mult, [o48.t, gbs.t], [o48.t])
                tt(acc48[:], acc48[:], o48[:], ALU.add, [acc48.t, o48.t], [acc48.t])

        def new_tile(b_, kf0, vcol0, po, pz):
            P.dma("sp", lambda e: [e.dma_start(out=knew[:, 0:2, :],
                                               in_=kvT_d[kf0:kf0 + 2, :, T + b_ * TS:T + (b_ + 1) * TS].rearrange("f p t -> p f t"))],
                  1, knew.t, reads=[kvT_tok], writes=[knew.t])
            P.dma("sp", lambda e: [e.dma_start(out=vnew[:, :], in_=vtm_d[16, b_ * TS:(b_ + 1) * TS, :])], 1, vnew.t,
                  reads=[vtm_tok], writes=[vnew.t])
            psn = psum[3]
            for g in range(2):
                mm(psn[0:4, g * 24:(g + 1) * 24], knew[:, g, :], qsg(b_, g), True, True, [knew.t, qs.t], [psn.t])
            s_, p_ = s48[0], p48[0]
            tt(s_[0:4, :], psn[0:4, 0:48], bN[:].rearrange("p g t r -> p (g t r)"), ALU.add, [psn.t, bN.t], [s_.t])
            act(p_[0:4, :], s_[0:4, :], AF.Exp, [s_.t], [p_.t])
            for g in range(2):
                mm(po[:, g * 24:(g + 1) * 24], vnew[0:4, vcol0 + g * 128:vcol0 + (g + 1) * 128], p_[0:4, g * 24:(g + 1) * 24],
                   False, True, [vnew.t, p_.t], [po.t])
            mm(pz[:, 0:48], ones_b[0:4, :], p_[0:4, 0:48], False, True, [ones_b.t, p_.t], [pz.t])

        for b_ in range(NS):
            for pg_ in range(64):
                gb = gather(b_, pg_, 0)
                pt = psum[pg_ % 2]
                pv = psb(pg_ % 2)
                for k in range(4):
                    tr(pv[:, k * 128:(k + 1) * 128], gb[:, k * 128:(k + 1) * 128], ident_b[:], [gb.t, ident_b.t], [pt.t])
                cp("act" if pg_ % 2 else "dve", kcmpT[:, :, pg_ * 128:(pg_ + 1) * 128],
                   pv[:, 0:512].rearrange("p (k t) -> p k t", k=4), [pt.t], [kcmpT.t])
            for kv in range(2):
                for g in range(2):
                    kview = kcmpT[:, kv * 2 + g, :].rearrange("p (c s) -> p c s", s=16)
                    pp = psum[2]
                    for js in range(32):
                        j_, s_i = js // 16, js % 16
                        mm(pp[:, 0:511], w1b2[:, kv, js, :], kview[:, j_:j_ + 511, s_i], js == 0, js == 31,
                           [w1b2.t, kcmpT.t], [pp.t])
                    act(tg[:, 0:511], pp[:, 0:511], AF.Identity, [pp.t, preb.t], [tg.t], bias=preb[:, kv:kv + 1])
                    act(ug[:, 0:511], tg[:, 0:511], AF.Square, [tg.t], [ug.t])
                    ts(ug[:, 0:511], ug[:, 0:511], 0.044715, 1.0, ALU.mult, ALU.add, [ug.t], [ug.t])
                    tt(ug[:, 0:511], ug[:, 0:511], tg[:, 0:511], ALU.mult, [ug.t, tg.t], [ug.t])
                    act(ug[:, 0:511], ug[:, 0:511], AF.Sigmoid, [ug.t], [ug.t], scale=GC)
                    tt(gl[:, 0:511], ug[:, 0:511], tg[:, 0:511], ALU.mult, [ug.t, tg.t], [gl.t])
                    pq_ = psum[3]
                    if kv == 0:
                        mm(pq_[:, 0:511], w2b[:, 0, :], gl[:, 0:511], True, True, [w2b.t, gl.t], [pq_.t])
                        cp("act", kcT_s[:, g, 0:511], pq_[:, 0:511], [pq_.t], [kcT_s.t])
                    else:
                        for it_ in range(4):
                            n_i = 128 if it_ < 3 else 127
                            mm(pq_[0:n_i, it_ * 128:(it_ + 1) * 128], gl[:, it_ * 128:it_ * 128 + n_i], w2b[:, 1, :],
                               True, True, [w2b.t, gl.t], [pq_.t])
                        for it_ in range(4):
                            n_i = 128 if it_ < 3 else 127
                            cp("act", vc_s[0:n_i, it_, g, :], pq_[0:n_i, it_ * 128:(it_ + 1) * 128], [pq_.t], [vc_s.t])
            psc = psum[4]
            for g in range(2):
                for it_ in range(4):
                    c_ = (g * 4 + it_) * 24
                    mm(psc[:, c_:c_ + 24], kcT_s[:, g, it_ * 128:(it_ + 1) * 128], qsg(b_, g), True, True,
                       [kcT_s.t, qs.t], [psc.t])
            tt(s192[:], psc[:, 0:192], bcs[:].rearrange("p g i t r -> p (g i t r)"), ALU.add, [psc.t, bcs.t], [s192.t])
            act(p192[:], s192[:], AF.Exp, [s192.t], [p192.t])
            po, pz, pp = psum[5], psum[6], psum[7]
            for g in range(2):
                for it_ in range(4):
                    c_ = (g * 4 + it_) * 24
                    mm(po[:, g * 24:(g + 1) * 24], vc_s[:, it_, g, :], p192[:, c_:c_ + 24], it_ == 0, it_ == 3,
                       [vc_s.t, p192.t], [po.t])
                for it_ in range(4):
                    c_ = (g * 4 + it_) * 24
                    mm(pz[:, g * 24:(g + 1) * 24], ones_b[:], p192[:, c_:c_ + 24], it_ == 0, it_ == 3,
                       [ones_b.t, p192.t], [pz.t])
                for it_ in range(4):
                    c_ = (g * 4 + it_) * 24
                    mm(pp[0:24, g * 130:(g + 1) * 130], p192[:, c_:c_ + 24], maug_s[:, it_, :], it_ == 0, it_ == 3,
                       [p192.t, maug_s.t], [pp.t])
            finish_branch(po, pz, 0, b_, True)
            cp("dve", ps_s[:], pp[0:24, 0:260].rearrange("p (g j) -> p g j", g=2), [pp.t], [ps_s.t])
            ts(zr_s[:], ps_s[:, :, 129], tiny, None, ALU.max, None, [ps_s.t], [zr_s.t])
            P.op("dve", lambda e: e.reciprocal(zr_s[:], zr_s[:]), [zr_s.t], [zr_s.t])
            for g in range(2):
                ts(pn_s[:, g, :], ps_s[:, g, 0:129], zr_s[:, g:g + 1], None, ALU.mult, None, [ps_s.t, zr_s.t], [pn_s.t])
            pi_ = psum[3]
            for g in range(2):
                mm(pi_[0:4, g * 129:(g + 1) * 129], rself[:], pn_s[:, g, :], True, True, [rself.t, pn_s.t], [pi_.t])
            for g in range(2):
                tt(scs[:, g, :], pi_[0:4, g * 129:(g + 1) * 129], VBs[:], ALU.add, [pi_.t, VBs.t], [scs.t])
            for g in range(2):
                P.op("dve", lambda e, g=g: e.max(out=m8a[:], in_=scs[:, g, :]), [scs.t], [m8a.t])
                P.op("dve", lambda e, g=g: e.match_replace(out=scs2[:], in_to_replace=m8a[:], in_values=scs[:, g, :],
                                                           imm_value=-1e30), [m8a.t, scs.t], [scs2.t])
                P.op("dve", lambda e: e.max(out=m8b[:], in_=scs2[:]), [scs2.t], [m8b.t])
                ts(sels[:], scs[:, g, :], m8b[:, 7:8], None, ALU.is_ge, None, [scs.t, m8b.t], [sels.t])
                ts(sels[:], sels[:], -1.0, -NEGM, ALU.add, ALU.mult, [sels.t], [sels.t])
                cp("dve", nse[:, g, :].rearrange("p (j k) -> p j k", k=64),
                   sels[:].unsqueeze(2).to_broadcast([4, 129, 64]), [sels.t], [nse.t])
            po, pz = psum[5], psum[6]
            for pg_ in range(64):
                gb = gather(b_, pg_, 512)
                pt = psum[pg_ % 2]
                pv = psb(pg_ % 2)
                ks = ksT2[pg_ % 2]
                for k in range(2):
                    tr(pv[:, k * 128:(k + 1) * 128], gb[:, k * 128:(k + 1) * 128], ident_b[:], [gb.t, ident_b.t], [pt.t])
                cp("act", ks[:], pv[:, 0:256].rearrange("p (k t) -> p k t", k=2), [pt.t], [ks.t])
                psc = psum[2 + pg_ % 2]
                for g in range(2):
                    mm(psc[:, g * 24:(g + 1) * 24], ks[:, g, :], qsg(b_, g), True, False, [ks.t, qs.t], [psc.t])
                    mm(psc[:, g * 24:(g + 1) * 24], nse[0:4, g, pg_ * 128:(pg_ + 1) * 128],
                       rselT[0:4, g, :, :].rearrange("p t r -> p (t r)"), False, True, [nse.t, rselT.t], [psc.t])
                s_, p_ = s48[pg_ % 2], p48[pg_ % 2]
                tt(s_[:], psc[:, 0:48], bSall[:, pg_, :], ALU.add, [psc.t, bSall.t], [s_.t])
                act(p_[:], s_[:], AF.Exp, [s_.t], [p_.t])
                for g in range(2):
                    mm(po[:, g * 24:(g + 1) * 24], gb[:, 256 + g * 128:256 + (g + 1) * 128], p_[:, g * 24:(g + 1) * 24],
                       pg_ == 0, False, [gb.t, p_.t], [po.t])
                mm(pz[:, 0:48], ones_b[:], p_[:, 0:48], pg_ == 0, False, [ones_b.t, p_.t], [pz.t])
            new_tile(b_, 4, 0, po, pz)
            finish_branch(po, pz, 1, b_, False)
            for wt in range(4):
                gt, gb = gt2[gi % 2], gb2[gi % 2]
                gi += 1
                dma_in(gt[:], gt.t, cwin[b_, wt * 128:(wt + 1) * 128].rearrange("s a g d -> s (a g d)"))
                cp("dve", gb[:], gt[:], [gt.t], [gb.t])
                pt = psum[wt % 2]
                pv = psb(wt % 2)
                ks = ksT2[wt % 2]
                for k in range(2):
                    tr(pv[:, k * 128:(k + 1) * 128], gb[:, k * 128:(k + 1) * 128], ident_b[:], [gb.t, ident_b.t], [pt.t])
                cp("act", ks[:], pv[:, 0:256].rearrange("p (k t) -> p k t", k=2), [pt.t], [ks.t])
                psc = psum[2 + wt % 2]
                for g in range(2):
                    mm(psc[:, g * 24:(g + 1) * 24], ks[:, g, :], qsg(b_, g), True, True, [ks.t, qs.t], [psc.t])
                s_, p_ = s48[wt % 2], p48[wt % 2]
                tt(s_[:], psc[:, 0:48], bW[:, wt].rearrange("p g t r -> p (g t r)"), ALU.add, [psc.t, bW.t], [s_.t])
                act(p_[:], s_[:], AF.Exp, [s_.t], [p_.t])
                for g in range(2):
                    mm(po[:, g * 24:(g + 1) * 24], gb[:, 256 + g * 128:256 + (g + 1) * 128], p_[:, g * 24:(g + 1) * 24],
                       wt == 0, False, [gb.t, p_.t], [po.t])
                mm(pz[:, 0:48], ones_b[:], p_[:, 0:48], wt == 0, False, [ones_b.t, p_.t], [pz.t])
            new_tile(b_, 6, 256, po, pz)
            finish_branch(po, pz, 2, b_, False)
            cp("dve", mixs[:, :, b_ * TS:(b_ + 1) * TS].rearrange("p (g r) t -> p g t r", g=2), acc48[:],
               [acc48.t], [mixs.t])
        P.dma("pool", lambda e: [e.dma_start(out=mixT_d[0:H, :, T:NTOK].rearrange("h p t -> p h t"), in_=mixs[:])], 1,
              mixs.t, reads=[mixs.t], writes=mix_tok[0:H])

    def mem_attn_l1(xnT1):
        nonlocal xnT
        xnT = xnT1
        mem_attn(1, lambda hm: w_in_b[0, :, W + 36 + hm * 128:W + 36 + (hm + 1) * 128].rearrange("(k p) c -> p k c", p=128))

    if 'N' in PH:
        phase_M(1)
        sel36 = A.tile([36, 36, 128], BF16, "sel36")
        g36s = A.tile([36, NS * TS], BF16, "g36s")
        mN = A.mark()
        phase_N()
        A.release(mN)
        if 'S' in PH:
            phase_S()
            A.release(mN)
        phase_B(1)
        phase_C(1)

    P.emit()
    print("SBUF peak", A.peak, "ops", {e: len(P.ops[e]) for e in P.ENGS}, "signals",
          {e: sum(1 for o in P.ops[e] if o.signal and not o.ndma) for e in P.ENGS},
          "max dma val", max(16 * t.cnt for t in P.slots), "slots", len(P.slots))
    return nc


_NC_CACHE = {}


def kernel(x_prompt, x_sample, mem_prompt, state_hgrn, cache_conv, cache_mem, cache_kv, cache_win,
           page_table, norm_gains, w_in_a, lb_logits, hgrn_norm, w_in_b, w_o, w_mem_kv, kv_norm,
           w_kv_b, cmp_pos, w_cmp1, w_cmp2, w_ffn_in, w_ffn_conv, b_ffn_conv, w_ffn_out):
    f = lambda a: np.ascontiguousarray(np.asarray(a, dtype=np.float32))
    if "nc" not in _NC_CACHE:
        _NC_CACHE["nc"] = build()
    nc = _NC_CACHE["nc"]
    x_prompt = f(x_prompt); x_sample = f(x_sample); mem_prompt = f(mem_prompt)
    state_hgrn = f(state_hgrn); cache_mem = f(cache_mem)
    cache_conv = f(cache_conv)
    cache_kv_f = f(cache_kv)
    cache_win_f = f(cache_win)
    page_table_i = np.ascontiguousarray(np.asarray(page_table, dtype=np.int32))
    shared = {
        "norm_gains": f(norm_gains), "w_in_a": f(w_in_a), "lb_logits": f(lb_logits), "hgrn_norm": f(hgrn_norm),
        "w_o": f(w_o), "w_mem_kv": f(w_mem_kv), "kv_norm": f(kv_norm), "w_kv_b": f(w_kv_b),
        "w_in_b": f(w_in_b), "cmp_pos": f(cmp_pos), "w_cmp1": f(w_cmp1), "w_cmp2": f(w_cmp2),
        "w_ffn_in": f(w_ffn_in), "w_ffn_conv": f(w_ffn_conv), "b_ffn_conv": f(b_ffn_conv), "w_ffn_out": f(w_ffn_out),
    }
    in_maps = []
    for c in range(NCORES):
        s0, s1 = NS * c, NS * (c + 1)
        m = {
            "xp": x_prompt[c],
            "xs": x_sample[s0:s1].reshape(NS * TS, D),
            "memp": mem_prompt[c],
            "st_in": state_hgrn[0, s0:s1],
            "cmem": np.ascontiguousarray(cache_mem[:, s0:s1]),
            "cconv": np.ascontiguousarray(cache_conv[:, s0:s1]),
            "ckv": cache_kv_f,
            "cwin": np.ascontiguousarray(cache_win_f[s0:s1]),
            "ptab": np.ascontiguousarray(page_table_i[s0:s1]),
        }
        m.update(shared)
        in_maps.append(m)
    res = run_bass_kernel_spmd(nc, in_maps, core_ids=list(range(NCORES)))
    R = res.results
    B = 8
    SB = 32
    cat = lambda k, ax=0: np.concatenate([R[c][k] for c in range(NCORES)], axis=ax)
    stk = lambda k, ax=0: np.stack([R[c][k] for c in range(NCORES)], axis=ax)
    y_prompt = stk("o_yp")
    y_sample = cat("o_ys").reshape(SB, TS, D)
    hg_p = np.stack([R[c]["o_hgp"] for c in range(NCORES)], axis=0)[None]
    hg_s = np.concatenate([R[c]["o_hgs"] for c in range(NCORES)], axis=0)[None]
    cv_p = stk("o_cvp", 1)
    cv_s = cat("o_cvs", 1)
    nm = np.stack([R[c]["o_nm"] for c in range(NCORES)], axis=1).reshape(2, B, MEM, 2, 4, 128)
    kv_p = stk("o_kvp").reshape(B, T // 128, 128, 4, 2, 128)
    kv_s = cat("o_kvs").reshape(SB, TS, 4, 2, 128)
    win_p = stk("o_winp").reshape(B, 512, 2, 2, 128)
    win_s = cat("o_wins").reshape(SB, TS, 2, 2, 128)
    return (y_prompt, y_sample, hg_p, hg_s, cv_p, cv_s, nm, kv_p, kv_s, win_p, win_s)
```

```python
import numpy as np
import concourse.bass as bass
import concourse.mybir as mybir
from concourse.bass_utils import run_bass_kernel_spmd

F32 = mybir.dt.float32
BF16 = mybir.dt.bfloat16
I32 = mybir.dt.int32
AF = mybir.ActivationFunctionType
ALU = mybir.AluOpType
AX = mybir.AxisListType

NCORES = 8
D = 2048
KC = D // 128
T = 2048
NS = 4
TS = 4
NTOK = T + NS * TS
MEM = 256
W = 1536
H = 12
DFF = 5632
EPS = 1e-6
SCALE = 128 ** -0.5


class Tok:
    __slots__ = ("name", "w", "rs", "sem", "cnt")

    def __init__(self, name=""):
        self.name = name
        self.w = None
        self.rs = []
        self.sem = None
        self.cnt = 0


class Op:
    __slots__ = ("eng", "fn", "deps", "signal", "ndma", "sem", "val", "idx")


class Prog:
    ENGS = ("sp", "act", "dve", "pool", "pe")

    def __init__(self, nc):
        self.nc = nc
        self.ops = {e: [] for e in self.ENGS}
        self.allops = []
        self.dma_toks = []
        self.extra = {}
        self.slots = []
        self.rr = 0
        self.MAXSLOTS = 58
        self.dmas_since = []
        self.last = {}

    def tok(self, name=""):
        return Tok(name)

    def _add(self, eng, fn, reads, writes, ndma=0, dtok=None):
        o = Op()
        o.eng = eng
        o.fn = fn
        o.ndma = ndma
        o.signal = False
        o.sem = None
        o.val = 0
        deps = []
        seen = set()
        for t in reads:
            if t.w is not None and id(t.w) not in seen:
                seen.add(id(t.w))
                deps.append(t.w)
        for t in writes:
            if t.w is not None and id(t.w) not in seen:
                seen.add(id(t.w))
                deps.append(t.w)
            lastr = {}
            for r in t.rs:
                if r.ndma:
                    if id(r) not in seen:
                        seen.add(id(r))
                        deps.append(r)
                else:
                    lastr[r.eng] = r
            for r in lastr.values():
                if id(r) not in seen:
                    seen.add(id(r))
                    deps.append(r)
        if eng == "pe" and ndma == 0:
            deps = [d for d in deps if not (d.eng == "pe" and d.ndma == 0)]
        if self.extra.get(eng):
            for d in self.extra[eng]:
                if id(d) not in seen and not (d.eng == eng and d.ndma == 0):
                    seen.add(id(d))
                    deps.append(d)
            self.extra[eng] = None
        o.deps = deps
        for d in deps:
            d.signal = True
        for t in reads:
            t.rs.append(o)
        for t in writes:
            t.w = o
            t.rs = []
        if ndma:
            assert dtok is not None
            if dtok.sem is None:
                if len(self.slots) < self.MAXSLOTS:
                    sl = Tok("slot%d" % len(self.slots))
                    sl.rs = 1
                    self.slots.append(sl)
                else:
                    sl = self.slots[self.rr % self.MAXSLOTS]
                    self.rr += 1
                    sl.rs += 1
                dtok.sem = sl
            sl = dtok.sem
            if sl.rs > 1 and sl.w is not None and id(sl.w) not in seen:
                seen.add(id(sl.w))
                deps.append(sl.w)
                sl.w.signal = True
            sl.w = o
            sl.cnt += ndma
            o.sem = sl
            o.val = 16 * sl.cnt
        o.idx = len(self.ops[eng])
        if ndma:
            self.dmas_since.append(o)
        else:
            self.last[eng] = o
        self.ops[eng].append(o)
        self.allops.append(o)
        return o

    def barrier(self):
        deps = list(self.last.values()) + list(self.dmas_since)
        self.dmas_since = []
        for e in self.ENGS:
            self.extra[e] = list(deps)

    def op(self, eng, fn, reads=(), writes=()):
        return self._add(eng, fn, list(reads), list(writes))

    def dma(self, eng, fn, ndma, dtok, reads=(), writes=()):
        return self._add(eng, fn, list(reads), list(writes), ndma=ndma, dtok=dtok)

    def emit(self):
        nc = self.nc
        import contextlib
        with contextlib.ExitStack() as es:
            esem = {e: es.enter_context(nc.semaphore("sem_" + e)) for e in self.ENGS}
            for i, t in enumerate(self.slots):
                t.sem = es.enter_context(nc.semaphore("dsem%d" % i))
            for e in self.ENGS:
                c = 0
                for o in self.ops[e]:
                    if o.ndma == 0:
                        if o.signal:
                            c += 1
                            o.sem = esem[e]
                            o.val = c
                    else:
                        o.sem = o.sem.sem
            final_waits = [(t.sem, 16 * t.cnt) for t in self.slots]

            def run(ename, eng):
                waited = {}
                for o in self.ops[ename]:
                    for d in o.deps:
                        k = id(d.sem)
                        if waited.get(k, 0) >= d.val:
                            continue
                        waited[k] = d.val
                        eng.wait_ge(d.sem, d.val)
                    r = o.fn(eng)
                    if o.ndma:
                        assert len(r) == o.ndma, (len(r), o.ndma)
                        for ins in r:
                            ins.then_inc(o.sem, 16)
                    elif o.signal:
                        r.then_inc(o.sem, 1)
                if ename == "sp":
                    for s, v in final_waits:
                        if waited.get(id(s), 0) < v:
                            eng.wait_ge(s, v)

            with nc.Block() as block:
                @block.sync
                def _(e):
                    run("sp", e)

                @block.scalar
                def _(e):
                    run("act", e)

                @block.vector
                def _(e):
                    run("dve", e)

                @block.gpsimd
                def _(e):
                    run("pool", e)

                @block.tensor
                def _(e):
                    run("pe", e)


class Tile:
    def __init__(self, h, tok):
        self.h = h
        self.t = tok

    def __getitem__(self, k):
        return self.h[k]


class Alloc:
    LO = 17408
    HI = 228000

    def __init__(self, nc, P):
        self.nc = nc
        self.P = P
        self.top = self.LO
        self.n = 0
        self.peak = 0

    def mark(self):
        return self.top

    def release(self, m):
        self.P.barrier()
        self.top = m

    def tile(self, shape, dtype, name="t"):
        nbytes = int(np.prod(shape[1:])) * mybir.dt.size(dtype)
        off = (self.top + 63) // 64 * 64
        assert off + nbytes <= self.HI, ("SBUF overflow", name, off, nbytes)
        self.top = off + nbytes
        self.peak = max(self.peak, self.top)
        self.n += 1
        h = self.nc.alloc_sbuf_tensor_at("%s_%d" % (name, self.n), list(shape), dtype, offset=off)
        return Tile(h, self.P.tok(name))


def build():
    nc = bass.Bass("TRN2", target_bir_lowering=False)
    P = Prog(nc)
    A = Alloc(nc, P)

    def din(name, shape, dt=F32):
        return nc.dram_tensor(name, list(shape), dt, kind="ExternalInput").ap()

    def dout(name, shape, dt=F32):
        return nc.dram_tensor(name, list(shape), dt, kind="ExternalOutput").ap()

    xp = din("xp", [T, D])
    xs = din("xs", [NS * TS, D])
    memp = din("memp", [MEM, D])
    st_in = din("st_in", [NS, H, 128, 128])
    cmem = din("cmem", [2, NS, MEM, 2, 4, 128])
    norm_gains = din("norm_gains", [2, 4, D])
    w_in_a = din("w_in_a", [1, D, 4 * W + 512])
    lb_logits = din("lb_logits", [2, W])
    hgrn_norm = din("hgrn_norm", [1, H, 128])
    w_o = din("w_o", [2, D, D])
    w_mem_kv = din("w_mem_kv", [2, D, 1024])

    cconv = din("cconv", [2, NS, 2, DFF])
    kv_norm = din("kv_norm", [D])
    w_kv_b = din("w_kv_b", [D, 1536])
    w_ffn_in = din("w_ffn_in", [2, D, 2 * DFF])
    w_ffn_conv = din("w_ffn_conv", [2, 3, DFF])
    b_ffn_conv = din("b_ffn_conv", [2, DFF])
    w_ffn_out = din("w_ffn_out", [2, DFF, D])
    w_in_b = din("w_in_b", [1, D, W + 36 + 512])
    ckv = din("ckv", [2560, 128, 4, 2, 128])
    cwin = din("cwin", [NS, 512, 2, 2, 128])
    ptab = din("ptab", [NS, 64], I32)
    cmp_pos = din("cmp_pos", [2, 32, 128])
    w_cmp1 = din("w_cmp1", [2, 4096, 128])
    w_cmp2 = din("w_cmp2", [2, 128, 128])
    o_nm = dout("o_nm", [2, MEM, 1024])
    o_cvp = dout("o_cvp", [2, 2, DFF])
    o_cvs = dout("o_cvs", [2, NS, 2, DFF])
    o_kvp = dout("o_kvp", [T, 1024])
    o_kvs = dout("o_kvs", [NS * TS, 1024])
    o_winp = dout("o_winp", [512, 512])
    o_wins = dout("o_wins", [NS * TS, 512])
    o_yp = dout("o_yp", [T, D])
    o_ys = dout("o_ys", [NS * TS, D])
    o_hgp = dout("o_hgp", [H, 128, 128])
    o_hgs = dout("o_hgs", [NS, H, 128, 128])

    hT_d = nc.dram_tensor("hT_d", [KC, 128, NTOK], F32, kind="Internal").ap()
    mixT_d = nc.dram_tensor("mixT_d", [KC, 128, NTOK], BF16, kind="Internal").ap()
    hT_tok = [[P.tok("hT%d_%d" % (k, i)) for i in range(5)] for k in range(KC)]
    xn2T_d = nc.dram_tensor("xn2T_d", [KC, 128, NTOK], BF16, kind="Internal").ap()
    xn2_tok = [P.tok("xn2d%d" % i) for i in range(5)]
    xnA_d = nc.dram_tensor("xnA_d", [KC, 128, NTOK], BF16, kind="Internal").ap()
    xnA_tok = [P.tok("xnAd%d" % i) for i in range(4)]
    xkv_d = nc.dram_tensor("xkv_d", [KC, 128, NTOK], BF16, kind="Internal").ap()
    xkv_tok = [P.tok("xkvd%d" % i) for i in range(4)]
    fT_d = nc.dram_tensor("fT_d", [KC, 128, NTOK], F32, kind="Internal").ap()
    fT_tok = [P.tok("fTd%d" % i) for i in range(KC)]
    qT_d = nc.dram_tensor("qT_d", [H, 128, NTOK], BF16, kind="Internal").ap()
    qT_tok = [P.tok("qTd%d" % i) for i in range(H)]
    ocmp_d = nc.dram_tensor("ocmp_d", [H, 128, T], F32, kind="Internal").ap()
    ocmp_tok = [P.tok("ocmpd%d" % i) for i in range(H)]
    mix_tok = [P.tok("mixd%d" % k) for k in range(KC)]

    CT = [(i * 512, 512) for i in range(4)] + [(T, NS * TS)]

    def act(out, in_, func, reads, writes, **kw):
        P.op("act", lambda e: e.activation(out=out, in_=in_, func=func, **kw), reads, writes)

    def tt(out, in0, in1, op, reads, writes, eng="dve"):
        P.op(eng, lambda e: e.tensor_tensor(out=out, in0=in0, in1=in1, op=op), reads, writes)

    def ts(out, in0, s1, s2, op0, op1, reads, writes, eng="dve"):
        if op1 is None:
            P.op(eng, lambda e: e.tensor_scalar(out=out, in0=in0, scalar1=s1, scalar2=None, op0=op0), reads, writes)
        else:
            P.op(eng, lambda e: e.tensor_scalar(out=out, in0=in0, scalar1=s1, scalar2=s2, op0=op0, op1=op1),
                 reads, writes)

    def stt(out, in0, scalar, in1, op0, op1, reads, writes):
        P.op("dve", lambda e: e.scalar_tensor_tensor(out=out, in0=in0, scalar=scalar, in1=in1, op0=op0, op1=op1),
             reads, writes)

    def cp(eng, out, in_, reads, writes):
        if eng == "act":
            P.op("act", lambda e: e.copy(out, in_), reads, writes)
        else:
            P.op(eng, lambda e: e.tensor_copy(out, in_), reads, writes)

    def mm(out, lhsT, rhs, start, stop, reads, writes):
        P.op("pe", lambda e: e.matmul(out, lhsT, rhs, start=start, stop=stop), reads, writes)

    def tr(out, in_, ident, reads, writes):
        P.op("pe", lambda e: e.transpose(out, in_, ident), reads, writes)

    def rsqrt(out, in_, scale, reads, writes):
        n_p = out.shape[0]
        act(out, in_, AF.Sqrt, list(reads) + [eps_t.t], writes, scale=scale, bias=eps_t[0:n_p, :])
        P.op("dve", lambda e: e.reciprocal(out, out), writes, writes)

    def memset(eng, ap, v, writes):
        P.op(eng, lambda e: e.memset(ap, v), (), writes)

    def dma_in(tile_ap, tile_tok, src, reads=(), eng="sp", nonc=False):
        if nonc:
            P.dma(eng, lambda e: [e.dma_start(out=tile_ap, in_=src, allow_slow_non_contiguous=True)], 1, tile_tok,
                  reads=reads, writes=[tile_tok])
        else:
            P.dma(eng, lambda e: [e.dma_start(out=tile_ap, in_=src)], 1, tile_tok, reads=reads, writes=[tile_tok])

    def dma_out(dst, tile_ap, tile_tok, writes=(), eng="pool"):
        P.dma(eng, lambda e: [e.dma_start(out=dst, in_=tile_ap)], 1, tile_tok, reads=[tile_tok], writes=writes)

    eps_t = A.tile([128, 1], F32, "eps_t")
    one_t = A.tile([128, 1], F32, "one_t")
    P.op("pool", lambda e: e.memset(one_t[:], 1.0), (), [one_t.t])
    P.op("pool", lambda e: e.memset(eps_t[:], EPS), (), [eps_t.t])
    ident_f = A.tile([128, 128], F32, "ident_f")
    ident_b = A.tile([128, 128], BF16, "ident_b")
    ones_b = A.tile([128, 128], BF16, "ones_b")
    memset("pool", ident_b[:], 1.0, [ident_b.t])
    P.op("pool", lambda e: e.affine_select(out=ident_b[:], in_=ident_b[:], pattern=[[-1, 128]],
                                           compare_op=ALU.is_equal, fill=0.0, base=0, channel_multiplier=1),
         reads=[ident_b.t], writes=[ident_b.t])
    cp("pool", ident_f[:], ident_b[:], [ident_b.t], [ident_f.t])
    memset("pool", ones_b[:], 1.0, [ones_b.t])
    cmask = A.tile([32, 16, 32], F32, "cmask")
    memset("pool", cmask[:], 1.0, [cmask.t])
    P.op("pool", lambda e: e.affine_select(out=cmask[:], in_=cmask[:], pattern=[[0, 16], [1, 32]],
                                           compare_op=ALU.is_ge, fill=0.0, base=0, channel_multiplier=-1),
         reads=[cmask.t], writes=[cmask.t])
    rmask = A.tile([128, 512], F32, "rmask")
    memset("pool", rmask[:], 1.0, [rmask.t])
    memset("pool", rmask[:].rearrange("p (c j) -> p c j", j=32)[:, :, 0:1], 0.0, [rmask.t])
    rmask_s = A.tile([128, 16], F32, "rmask_s")
    memset("pool", rmask_s[:], 1.0, [rmask_s.t])
    memset("pool", rmask_s[:].rearrange("p (c j) -> p c j", j=4)[:, :, 0:1], 0.0, [rmask_s.t])
    gT = A.tile([128, 8, KC], F32, "gT")
    dma_in(gT[:], gT.t, norm_gains.rearrange("l j (k p) -> p (l j) k", p=128), nonc=True)
    gnT = A.tile([128, H], F32, "gnT")
    dma_in(gnT[:], gnT.t, hgrn_norm[0].rearrange("h d -> d h"), nonc=True)
    lbT = A.tile([128, 2, H], F32, "lbT")
    dma_in(lbT[:], lbT.t, lb_logits.rearrange("r (h d) -> d r h", d=128), nonc=True)
    kvgT = A.tile([128, KC], F32, "kvgT")
    dma_in(kvgT[:], kvgT.t, kv_norm.rearrange("(k p) -> p k", p=128), nonc=True)
    NJ = DFF // 128
    wcT = A.tile([128, 6, NJ], F32, "wcT")
    dma_in(wcT[:], wcT.t, w_ffn_conv.rearrange("l j (c p) -> p (l j) c", p=128), nonc=True)
    bcT = A.tile([128, 2, NJ], F32, "bcT")
    dma_in(bcT[:], bcT.t, b_ffn_conv.rearrange("l (c p) -> p l c", p=128), nonc=True)
    ccT = A.tile([128, 2, NJ, NS * 2], F32, "ccT")
    for l in range(2):
        for b_ in range(NS):
            for r_ in range(2):
                dma_in(ccT[:, l, :, b_ * 2 + r_], ccT.t, cconv[l, b_, r_].rearrange("(c p) -> p c", p=128), nonc=True)
    oml = A.tile([128, H], F32, "oml")
    tt(oml[:], lbT[:, 1, :], lbT[:, 0, :], ALU.subtract, [lbT.t], [oml.t])
    act(oml[:], oml[:], AF.Sigmoid, [oml.t], [oml.t])

    psum = []
    for i in range(8):
        h = nc.alloc_psum_tensor("ps%d" % i, [128, 512], F32)
        psum.append(Tile(h, P.tok("ps%d" % i)))

    def psb(i):
        return psum[i].h[:].bitcast(BF16)

    wst = [A.tile([128, 2048], F32, "wst%d" % i) for i in range(2)]
    wst_i = [0]

    def load_cast_multi(pairs, dst_tok):
        pairs = list(pairs)
        P.dma("pool", lambda e: [e.dma_start(out=d_, in_=s_) for (d_, s_) in pairs], len(pairs), dst_tok,
              writes=[dst_tok])

    def load_cast_swdge(dst_ap, dst_tok, src_ap, shape3, eng=None):
        load_cast_multi([(dst_ap, src_ap)], dst_tok)

    def load_cast(dst_ap, dst_tok, src_ap, shape3, eng=None):
        st = wst[wst_i[0] % 2]
        if eng is None:
            eng = "dve" if wst_i[0] % 2 == 0 else "act"
        wst_i[0] += 1
        n = int(np.prod(shape3[1:]))
        sv = st[0:shape3[0], 0:n]
        if len(shape3) == 3:
            sv = sv.rearrange("p (a b) -> p a b", a=shape3[1])
        P.dma("sp", lambda e: [e.dma_start(out=sv, in_=src_ap)], 1, st.t, writes=[st.t])
        cp(eng, dst_ap, sv, [st.t], [dst_tok])

    import os
    PH = os.environ.get('PH', 'XMABCKNS')
    MS = os.environ.get('MS', 'VKS')
    memKT = A.tile([128, 5, 4, MEM], BF16, "memKT")
    memV = A.tile([128, 5, 2, 512], BF16, "memV")
    mXA = A.mark()
    xnT = A.tile([128, KC, NTOK], BF16, "xnT")
    xn_tok = [P.tok("xn%d" % i) for i in range(5)]
    mX = A.mark()
    xt2 = [A.tile([128, D], F32, "xt%d" % i) for i in range(2)]
    sq = A.tile([128, D], F32, "sq")
    xb = A.tile([128, D], BF16, "xb")
    ss = A.tile([128, 1], F32, "ss")
    hst = [A.tile([128, KC, 128], F32, "hst%d" % i) for i in range(2)]
    for i in (range(17) if 'X' in PH else []):
        rows = 128 if i < 16 else NS * TS
        src = xp[i * 128:(i + 1) * 128, :] if i < 16 else xs[:, :]
        c0 = i * 128
        cti = min(i // 4, 4)
        xt = xt2[i % 2]
        dma_in(xt[0:rows, :], xt.t, src)
        act(sq[0:rows, :], xt[0:rows, :], AF.Square, [xt.t], [sq.t])
        P.op("dve", lambda e, rows=rows: e.reduce_sum(out=ss[0:rows, :], in_=sq[0:rows, :], axis=AX.X), [sq.t], [ss.t])
        rsqrt(ss[0:rows, :], ss[0:rows, :], 1.0 / D, [ss.t], [ss.t])
        ts(xb[0:rows, :], xt[0:rows, :], ss[0:rows, 0:1], None, ALU.mult, None, [xt.t, ss.t], [xb.t])
        for half in range(2):
            pt = psum[half]
            pv = psb(half)
            for k in range(8):
                kc = half * 8 + k
                tr(pv[:, k * 128:k * 128 + rows], xb[0:rows, kc * 128:(kc + 1) * 128], ident_b[0:rows, 0:rows],
                   [xb.t, ident_b.t], [pt.t])
            tt(xnT[:, half * 8:(half + 1) * 8, c0:c0 + rows],
               pv.rearrange("p (k t) -> p k t", k=8)[:, :, 0:rows],
               gT[:, 0, half * 8:(half + 1) * 8].unsqueeze(2).to_broadcast([128, 8, rows]),
               ALU.mult, [pt.t, gT.t], [xn_tok[cti]])
        hs = hst[i % 2]
        for q4 in range(4):
            pt = psum[2 + q4 % 2]
            for k in range(4):
                kc = q4 * 4 + k
                tr(pt[:, k * 128:k * 128 + rows], xt[0:rows, kc * 128:(kc + 1) * 128], ident_f[0:rows, 0:rows],
                   [xt.t, ident_f.t], [pt.t])
            cp("act", hs[:, q4 * 4:(q4 + 1) * 4, 0:rows],
               pt[:].rearrange("p (k t) -> p k t", k=4)[:, :, 0:rows], [pt.t], [hs.t])
        P.dma("pool", lambda e, hs=hs, c0=c0, rows=rows: [
            e.dma_start(out=hT_d[:, :, c0:c0 + rows].rearrange("k p t -> p k t"), in_=hs[:, :, 0:rows])],
            1, hs.t, reads=[hs.t], writes=[hT_tok[k][cti] for k in range(KC)])
    A.release(mX)

    def phase_M(l):
        m0 = A.mark()
        memT = A.tile([128, KC, MEM], BF16, "memT")
        mtile = A.tile([128, D], F32, "mtile")
        mtile_b = A.tile([128, D], BF16, "mtile_b")
        for t2 in range(MEM // 128):
            dma_in(mtile[:], mtile.t, memp[t2 * 128:(t2 + 1) * 128, :])
            cp("dve", mtile_b[:], mtile[:], [mtile.t], [mtile_b.t])
            for half in range(2):
                pt = psum[half]
                pv = psb(half)
                for k in range(8):
                    kc = half * 8 + k
                    tr(pv[:, k * 128:(k + 1) * 128], mtile_b[:, kc * 128:(kc + 1) * 128], ident_b[:],
                       [mtile_b.t, ident_b.t], [pt.t])
                cp("act", memT[:, half * 8:(half + 1) * 8, t2 * 128:(t2 + 1) * 128],
                   pv.rearrange("p (k t) -> p k t", k=8), [pt.t], [memT.t])
        wmk = A.tile([128, KC, 512], BF16, "wmk")
        osb = [A.tile([128, 512], F32, "osb%d" % i) for i in range(2)]
        oi = 0
        for nb in range(2):
            for kc in range(KC):
                load_cast(wmk[:, kc, :], wmk.t, w_mem_kv[l, kc * 128:(kc + 1) * 128, nb * 512:(nb + 1) * 512],
                          [128, 512])
            for t2 in range(2):
                pt = psum[2 + (oi % 2)]
                for kc in range(KC):
                    mm(pt[:], memT[:, kc, t2 * 128:(t2 + 1) * 128], wmk[:, kc, :], kc == 0, kc == KC - 1,
                       [memT.t, wmk.t], [pt.t])
                ob = osb[oi % 2]
                oi += 1
                cp("dve", ob[:], pt[:], [pt.t], [ob.t])
                dma_out(o_nm[l, t2 * 128:(t2 + 1) * 128, nb * 512:(nb + 1) * 512], ob[:], ob.t)
                if nb == 1:
                    cp("pool", memV[:, 0, t2, :], ob[:], [ob.t], [memV.t])
            if nb == 0:
                for hm in range(4):
                    pt = psum[4 + hm % 2]
                    for kc in range(KC):
                        mm(pt[:, 0:MEM], wmk[:, kc, hm * 128:(hm + 1) * 128], memT[:, kc, :], kc == 0, kc == KC - 1,
                           [memT.t, wmk.t], [pt.t])
                    cp("act", memKT[:, 0, hm, :], pt[:, 0:MEM], [pt.t], [memKT.t])
        cmt = A.tile([128, 1024], F32, "cmt")
        cmb = A.tile([128, 512], BF16, "cmb")
        for b in range(NS):
            for t2 in range(2):
                dma_in(cmt[:], cmt.t, cmem[l, b, t2 * 128:(t2 + 1) * 128].rearrange("m a h d -> m (a h d)"))
                cp("dve", memV[:, 1 + b, t2, :], cmt[:, 512:1024], [cmt.t], [memV.t])
                cp("dve", cmb[:], cmt[:, 0:512], [cmt.t], [cmb.t])
                pt = psum[6]
                pv = psb(6)
                for hm in range(4):
                    tr(pv[:, hm * 128:(hm + 1) * 128], cmb[:, hm * 128:(hm + 1) * 128], ident_b[:],
                       [cmb.t, ident_b.t], [pt.t])
                cp("act", memKT[:, 1 + b, :, t2 * 128:(t2 + 1) * 128],
                   pv[:, 0:512].rearrange("p (h m) -> p h m", h=4), [pt.t], [memKT.t])
        A.release(m0)

    phase_M(0)

    mA = A.mark()
    wb4 = [A.tile([128, KC, 128], BF16, "wb%d" % i) for i in range(4)]
    qT = A.tile([128, 512], BF16, "qT")
    kT = A.tile([128, 512], BF16, "kT")
    kpT = A.tile([128, 512], BF16, "kpT")
    vT = A.tile([128, 512], BF16, "vT")
    ogT = A.tile([128, 512], BF16, "ogT")
    mixo = A.tile([128, NTOK], BF16, "mixo")
    ebl = A.tile([128, 16], F32, "ebl")
    kp_tm = A.tile([32, 16, 128], BF16, "kp_tm")
    v_tm = A.tile([32, 16, 128], BF16, "v_tm")
    sig = A.tile([128, 512], F32, "sig")
    q32 = A.tile([128, 512], F32, "q32")
    k32 = A.tile([128, 512], F32, "k32")
    lf = A.tile([128, 512], F32, "lf")
    bb = A.tile([128, 512], F32, "bb")
    e1 = A.tile([128, 512], F32, "e1")
    e2 = A.tile([128, 512], F32, "e2")
    kf = A.tile([128, 512], F32, "kf")
    Sst = A.tile([128, 128], F32, "Sst")
    Sbf = [A.tile([128, 128], BF16, "Sbf%d" % i) for i in range(2)]
    AT = A.tile([32, 512], BF16, "AT")
    osq = A.tile([128, 512], BF16, "osq")
    rstd = A.tile([128, 512], F32, "rstd")
    tmpn = A.tile([128, 512], F32, "tmpn")
    o32 = A.tile([128, 512], F32, "o32")
    U_tok = [P.tok("U%d" % i) for i in range(4)]

    def proj(widx, ti, pbank):
        c0, n = CT[ti]
        pt = psum[pbank]
        for kc in range(KC):
            mm(pt[:, 0:n], wb4[widx][:, kc, :], xnT[:, kc, c0:c0 + n], kc == 0, kc == KC - 1,
               [wb4[widx].t, xn_tok[ti]], [pt.t])
        return pt

    for h in (range(int(os.environ.get('NH', H))) if 'A' in PH else []):
        for j in range(4):
            col = j * W + h * 128
            load_cast(wb4[j][:], wb4[j].t,
                      w_in_a[0, :, col:col + 128].rearrange("(k p) c -> p k c", p=128), [128, KC, 128])
        sbi = 0
        for ti in range(5):
            c0, n = CT[ti]
            L = 32 if ti < 4 else TS
            nch = n // L
            ch0 = c0 // 32 if ti < 4 else 64
            pq = proj(0, ti, 0)
            act(q32[:, 0:n], pq[:, 0:n], AF.Silu, [pq.t], [q32.t])
            pf = proj(1, ti, 1)
            act(sig[:, 0:n], pf[:, 0:n], AF.Sigmoid, [pf.t], [sig.t], scale=-1.0)
            ts(k32[:, 0:n], sig[:, 0:n], oml[:, h:h + 1], None, ALU.mult, None, [sig.t, oml.t], [k32.t])
            pvv = proj(2, ti, 0)
            cp("act", vT[:, 0:n], pvv[:, 0:n], [pvv.t], [vT.t])
            po = proj(3, ti, 1)
            act(ogT[:, 0:n], po[:, 0:n], AF.Sigmoid, [po.t], [ogT.t])
            act(lf[:, 0:n], k32[:, 0:n], AF.Ln, [k32.t], [lf.t], scale=-1.0, bias=1.0)
            rm = rmask if ti < 4 else rmask_s
            P.op("dve", lambda e, n=n, rm=rm: e.tensor_tensor_scan(out=bb[:, 0:n], data0=rm[:, 0:n], data1=lf[:, 0:n],
                                                                 initial=0.0, op0=ALU.mult, op1=ALU.add),
                 [rm.t, lf.t], [bb.t])
            act(e1[:, 0:n], bb[:, 0:n], AF.Exp, [bb.t], [e1.t])
            act(e2[:, 0:n], bb[:, 0:n], AF.Exp, [bb.t], [e2.t], scale=-1.0)
            blast = bb[:, 0:n].rearrange("p (c j) -> p c j", j=L)[:, :, L - 1]
            act(ebl[:, 0:nch], blast, AF.Exp, [bb.t], [ebl.t])
            tt(qT[:, 0:n], q32[:, 0:n], e1[:, 0:n], ALU.mult, [q32.t, e1.t], [qT.t])
            tt(kf[:, 0:n], k32[:, 0:n], e2[:, 0:n], ALU.mult, [k32.t, e2.t], [kf.t])
            cp("pool", kT[:, 0:n], kf[:, 0:n], [kf.t], [kT.t])
            tt(kpT[:, 0:n].rearrange("p (c j) -> p c j", j=L),
               kf[:, 0:n].rearrange("p (c j) -> p c j", j=L),
               ebl[:, 0:nch].unsqueeze(2).to_broadcast([128, nch, L]), ALU.mult, [kf.t, ebl.t], [kpT.t])
            p3 = psum[3]
            for ci in range(nch):
                cc = ci * L
                mm(p3[0:L, ci * 32:ci * 32 + L], kT[:, cc:cc + L], qT[:, cc:cc + L], True, True,
                   [kT.t, qT.t], [p3.t])
            tt(AT[0:L, 0:nch * 32].rearrange("p (c j) -> p c j", j=32)[:, :, 0:L],
               p3[0:L, 0:nch * 32].rearrange("p (c j) -> p c j", j=32)[:, :, 0:L],
               cmask[0:L, 0:nch, 0:L], ALU.mult, [p3.t, cmask.t], [AT.t])
            for g0 in range(0, nch, 8):
                g1 = min(nch, g0 + 8)
                for (srcT, dst, pbk) in ((kpT, kp_tm, 2), (vT, v_tm, 7)):
                    pt = psum[pbk]
                    pv2 = psb(pbk)
                    for ci in range(g0, g1):
                        cc = ci * L
                        tr(pv2[0:L, (ci - g0) * 128:(ci - g0 + 1) * 128], srcT[:, cc:cc + L], ident_b[:],
                           [srcT.t, ident_b.t], [pt.t])
                    cp("act", dst[0:L, g0:g1, :],
                       pv2[0:L, 0:(g1 - g0) * 128].rearrange("p (c d) -> p c d", d=128), [pt.t], [dst.t])
            p4 = psum[4]
            for ci in range(nch):
                ch = ch0 + ci
                cc = ci * L
                fresh = (ti < 4 and ch == 0)
                if ti == 4:
                    dma_in(Sst[:], Sst.t, st_in[ci, h])
                    sbi += 1
                    cp("act", Sbf[sbi % 2][:], Sst[:], [Sst.t], [Sbf[sbi % 2].t])
                mm(p4[:, ci * L:(ci + 1) * L], v_tm[0:L, ci, :], AT[0:L, ci * 32:ci * 32 + L], True, fresh,
                   [v_tm.t, AT.t], [p4.t])
                if not fresh:
                    sb = Sbf[sbi % 2]
                    mm(p4[:, ci * L:(ci + 1) * L], sb[:], qT[:, cc:cc + L], False, True, [sb.t, qT.t], [p4.t])
                ut = U_tok[ch % 4]
                pu = psum[5][:, (ch % 4) * 128:(ch % 4 + 1) * 128]
                mm(pu, kp_tm[0:L, ci, :], v_tm[0:L, ci, :], True, True, [kp_tm.t, v_tm.t], [ut])
                if fresh:
                    cp("dve", Sst[:], pu, [ut], [Sst.t])
                else:
                    stt(Sst[:], Sst[:], ebl[:, ci:ci + 1], pu, ALU.mult, ALU.add, [Sst.t, ebl.t, ut], [Sst.t])
                last = (ti < 4 and ch == 63) or ti == 4
                if last:
                    dst = o_hgp[h] if ti < 4 else o_hgs[ci, h]
                    dma_out(dst, Sst[:], Sst.t)
                else:
                    sbi += 1
                    cp("act", Sbf[sbi % 2][:], Sst[:], [Sst.t], [Sbf[sbi % 2].t])
            cp("dve", o32[:, 0:n], p4[:, 0:n], [p4.t], [o32.t])
            act(osq[:, 0:n], o32[:, 0:n], AF.Square, [o32.t], [osq.t])
            p6 = psum[6]
            mm(p6[:, 0:n], ones_b[:], osq[:, 0:n], True, True, [ones_b.t, osq.t], [p6.t])
            rsqrt(rstd[:, 0:n], p6[:, 0:n], 1.0 / 128, [p6.t], [rstd.t])
            tt(tmpn[:, 0:n], o32[:, 0:n], rstd[:, 0:n], ALU.mult, [o32.t, rstd.t], [tmpn.t])
            stt(mixo[:, c0:c0 + n], tmpn[:, 0:n], gnT[:, h:h + 1], ogT[:, 0:n], ALU.mult, ALU.mult,
                [tmpn.t, gnT.t, ogT.t], [mixo.t])
        dma_out(mixT_d[h], mixo[:], mixo.t, writes=[mix_tok[h]])

    pT2 = [A.tile([128, 512], BF16, "pT%d" % i) for i in range(2)]
    rz = A.tile([128, 512], F32, "rz")

    def mem_attn(l, wsrc):
        for hm in (range(4) if 'A' in PH else []):
            load_cast(wb4[0][:], wb4[0].t, wsrc(hm), [128, KC, 128])
            for ti in range(5):
                c0, n = CT[ti]
                pq = proj(0, ti, 0)
                act(qT[:, 0:n], pq[:, 0:n], AF.Copy, [pq.t], [qT.t], scale=SCALE)
                segs = [(0, 0, n)] if ti < 4 else [(1 + b_, b_ * TS, TS) for b_ in range(NS)]
                p4 = psum[4]
                p6 = psum[6]
                for (sq_, o0, nn) in segs:
                    for mt in range(2):
                        pscore = psum[1 if mt == 0 else 7]
                        mm(pscore[:, 0:nn], memKT[:, sq_, hm, mt * 128:(mt + 1) * 128], qT[:, o0:o0 + nn],
                           True, True, [memKT.t, qT.t], [pscore.t])
                        act(pT2[mt][:, 0:nn], pscore[:, 0:nn], AF.Exp, [pscore.t], [pT2[mt].t])
                    for mt in range(2):
                        mm(p4[:, o0:o0 + nn], memV[:, sq_, mt, hm * 128:(hm + 1) * 128], pT2[mt][:, 0:nn],
                           mt == 0, mt == 1, [memV.t, pT2[mt].t], [p4.t])
                    for mt in range(2):
                        mm(p6[:, o0:o0 + nn], ones_b[:], pT2[mt][:, 0:nn], mt == 0, mt == 1,
                           [ones_b.t, pT2[mt].t], [p6.t])
                P.op("dve", lambda e, n=n, p6=p6, rz=rz: e.reciprocal(rz[:, 0:n], p6[:, 0:n]), [p6.t], [rz.t])
                tt(mixo[:, c0:c0 + n], p4[:, 0:n], rz[:, 0:n], ALU.mult, [p4.t, rz.t], [mixo.t])
            dma_out(mixT_d[12 + hm], mixo[:], mixo.t, writes=[mix_tok[12 + hm]])

    mem_attn(0, lambda hm: w_in_a[0, :, 4 * W + hm * 128:4 * W + (hm + 1) * 128].rearrange("(k p) c -> p k c", p=128))
    A.release(mXA)

    def phase_B(l):
        mB = A.mark()
        wo_b = A.tile([128, KC, D], BF16, "wo_b")
        for kc in range(KC):
            if kc % 2 == 0:
                load_cast(wo_b[:, kc, :], wo_b.t, w_o[l, kc * 128:(kc + 1) * 128, :], [128, D])
        load_cast_multi([(wo_b[:, kc, :], w_o[l, kc * 128:(kc + 1) * 128, :]) for kc in range(1, KC, 2)], wo_b.t)
        mt_t = A.tile([128, KC, 256], BF16, "mt_t")
        h_t = A.tile([128, KC, 256], F32, "h_t")
        o_t = A.tile([128, KC, 256], F32, "o_t")
        x2_t = A.tile([128, KC, 256], BF16, "x2_t")
        sqb = [A.tile([128, 512], BF16, "sqb%d" % i) for i in range(2)]
        r1 = A.tile([128, 512], F32, "r1")
        r2 = A.tile([128, 512], F32, "r2")
        tmpb = A.tile([128, 512], F32, "tmpb")
        for ti, c0, n in [(ti, CT[ti][0] + hf * 256, min(256, CT[ti][1] - hf * 256)) for ti in range(5)
                          for hf in range(2) if CT[ti][1] - hf * 256 > 0]:
            P.dma("sp", lambda e, c0=c0, n=n: [e.dma_start(out=mt_t[:, :, 0:n],
                                                           in_=mixT_d[:, :, c0:c0 + n].rearrange("k p t -> p k t"))],
                  1, mt_t.t, reads=mix_tok, writes=[mt_t.t])
            P.dma("sp", lambda e, c0=c0, n=n: [e.dma_start(out=h_t[:, :, 0:n],
                                                           in_=hT_d[:, :, c0:c0 + n].rearrange("k p t -> p k t"))],
                  1, h_t.t, reads=[hT_tok[k][ti] for k in range(KC)], writes=[h_t.t])
            for oc in range(KC):
                po = psum[oc % 2]
                for kc in range(KC):
                    mm(po[:, 0:n], wo_b[:, kc, oc * 128:(oc + 1) * 128], mt_t[:, kc, 0:n], kc == 0, kc == KC - 1,
                       [wo_b.t, mt_t.t], [po.t])
                cp("dve", o_t[:, oc, 0:n], po[:, 0:n], [po.t], [o_t.t])
                sb = sqb[oc % 2]
                act(sb[:, 0:n], o_t[:, oc, 0:n], AF.Square, [o_t.t], [sb.t])
                mm(psum[2][:, 0:n], ones_b[:], sb[:, 0:n], oc == 0, oc == KC - 1, [ones_b.t, sb.t], [psum[2].t])
            rsqrt(r1[:, 0:n], psum[2][:, 0:n], 1.0 / D, [psum[2].t], [r1.t])
            for oc in range(KC):
                tt(tmpb[:, 0:n], o_t[:, oc, 0:n], r1[:, 0:n], ALU.mult, [o_t.t, r1.t], [tmpb.t])
                stt(h_t[:, oc, 0:n], tmpb[:, 0:n], gT[:, l * 4 + 1, oc:oc + 1], h_t[:, oc, 0:n], ALU.mult, ALU.add,
                    [tmpb.t, gT.t, h_t.t], [h_t.t])
                sb = sqb[oc % 2]
                act(sb[:, 0:n], h_t[:, oc, 0:n], AF.Square, [h_t.t], [sb.t])
                mm(psum[3][:, 0:n], ones_b[:], sb[:, 0:n], oc == 0, oc == KC - 1, [ones_b.t, sb.t], [psum[3].t])
            rsqrt(r2[:, 0:n], psum[3][:, 0:n], 1.0 / D, [psum[3].t], [r2.t])
            for oc in range(KC):
                stt(x2_t[:, oc, 0:n], h_t[:, oc, 0:n], gT[:, l * 4 + 2, oc:oc + 1], r2[:, 0:n], ALU.mult, ALU.mult,
                    [h_t.t, gT.t, r2.t], [x2_t.t])
            P.dma("pool", lambda e, c0=c0, n=n: [e.dma_start(out=hT_d[:, :, c0:c0 + n].rearrange("k p t -> p k t"),
                                                             in_=h_t[:, :, 0:n])],
                  1, h_t.t, reads=[h_t.t], writes=[hT_tok[k][ti] for k in range(KC)])
            P.dma("pool", lambda e, c0=c0, n=n: [e.dma_start(out=xn2T_d[:, :, c0:c0 + n].rearrange("k p t -> p k t"),
                                                             in_=x2_t[:, :, 0:n])],
                  1, x2_t.t, reads=[x2_t.t], writes=[xn2_tok[ti]])
        A.release(mB)

    if 'B' in PH:
        phase_B(0)

    BLK = [(0, 512), (512, 512), (1024, 512), (1536, 528)]
    LASTB = len(BLK) - 1
    GC = 2.0 * (2.0 / np.pi) ** 0.5

    def phase_C(l):
        mC = A.mark()
        convo = A.tile([128, NJ, 2 + NS * 2], F32, "convo")
        aprev = A.tile([128, NJ, 2], F32, "aprev")
        rf = A.tile([128, 528], F32, "rf")
        r3 = A.tile([128, 528], F32, "r3")
        sqb = [A.tile([128, 512], BF16, "sqc%d" % i) for i in range(2)]
        for bi, (b0, NB) in enumerate(BLK):
            NT = [(o, min(512, NB - o)) for o in range(0, NB, 512)]
            mC1 = A.mark()
            x2b = A.tile([128, KC, 528], BF16, "x2b")
            yT = A.tile([128, NJ, 528], BF16, "yT")
            wab = [[A.tile([128, KC, 128], BF16, "wab%d%d" % (i, k)) for k in range(2)] for i in range(2)]
            a_ext = A.tile([128, 530], F32, "a_ext")
            t1 = A.tile([128, 528], F32, "t1")
            u1 = A.tile([128, 528], F32, "u1")
            exs = A.tile([128, NS, 6], F32, "exs")
            tis = [ti for ti in range(5) if CT[ti][0] >= b0 and CT[ti][0] < b0 + NB]
            P.dma("sp", lambda e, b0=b0, NB=NB: [e.dma_start(out=x2b[:, :, 0:NB],
                                                             in_=xn2T_d[:, :, b0:b0 + NB].rearrange("k p t -> p k t"))],
                  1, x2b.t, reads=[xn2_tok[ti] for ti in tis], writes=[x2b.t])
            for j in range(NJ):
                wa, wb = wab[j % 2]
                load_cast(wa[:], wa.t, w_ffn_in[l, :, j * 128:(j + 1) * 128].rearrange("(k p) c -> p k c", p=128),
                          [128, KC, 128])
                load_cast_swdge(wb[:], wb.t,
                                w_ffn_in[l, :, DFF + j * 128:DFF + (j + 1) * 128].rearrange("(k p) c -> p k c", p=128),
                                [128, KC, 128])
                if bi == 0:
                    memset("pool", a_ext[:, 0:2], 0.0, [a_ext.t])
                else:
                    cp("pool", a_ext[:, 0:2], aprev[:, j, :], [aprev.t], [a_ext.t])
                for ni, (o, n) in enumerate(NT):
                    pa = psum[(j % 2) * 2 + ni]
                    for kc in range(KC):
                        mm(pa[:, 0:n], wa[:, kc, :], x2b[:, kc, o:o + n], kc == 0, kc == KC - 1, [wa.t, x2b.t], [pa.t])
                    cp("act", a_ext[:, 2 + o:2 + o + n], pa[:, 0:n], [pa.t], [a_ext.t])
                for ni, (o, n) in enumerate(NT):
                    pb = psum[4 + (j % 2) * 2 + ni]
                    for kc in range(KC):
                        mm(pb[:, 0:n], wb[:, kc, :], x2b[:, kc, o:o + n], kc == 0, kc == KC - 1, [wb.t, x2b.t], [pb.t])
                w0 = wcT[:, l * 3 + 0, j:j + 1]
                w1 = wcT[:, l * 3 + 1, j:j + 1]
                w2 = wcT[:, l * 3 + 2, j:j + 1]
                bc = bcT[:, l, j:j + 1]
                ts(t1[:, 0:NB], a_ext[:, 2:2 + NB], w2, bc, ALU.mult, ALU.add, [a_ext.t, wcT.t, bcT.t], [t1.t])
                stt(t1[:, 0:NB], a_ext[:, 1:1 + NB], w1, t1[:, 0:NB], ALU.mult, ALU.add, [a_ext.t, wcT.t, t1.t], [t1.t])
                stt(t1[:, 0:NB], a_ext[:, 0:NB], w0, t1[:, 0:NB], ALU.mult, ALU.add, [a_ext.t, wcT.t, t1.t], [t1.t])
                if bi < LASTB:
                    cp("pool", aprev[:, j, :], a_ext[:, 2 + 510:2 + 512], [a_ext.t], [aprev.t])
                else:
                    so = 512
                    cp("pool", exs[:, :, 0:2], ccT[:, l, j, :].rearrange("p (b r) -> p b r", r=2), [ccT.t], [exs.t])
                    cp("pool", exs[:, :, 2:6], a_ext[:, 2 + so:2 + so + 16].rearrange("p (b t) -> p b t", t=TS),
                       [a_ext.t], [exs.t])
                    t1s = t1[:, so:so + 16].rearrange("p (b t) -> p b t", t=TS)
                    ts(t1s, exs[:, :, 2:6], w2, bc, ALU.mult, ALU.add, [exs.t, wcT.t, bcT.t, t1.t], [t1.t])
                    stt(t1s, exs[:, :, 1:5], w1, t1s, ALU.mult, ALU.add, [exs.t, wcT.t, t1.t], [t1.t])
                    stt(t1s, exs[:, :, 0:4], w0, t1s, ALU.mult, ALU.add, [exs.t, wcT.t, t1.t], [t1.t])
                    cp("pool", convo[:, j, 0:2], a_ext[:, 2 + 510:2 + 512], [a_ext.t], [convo.t])
                    cp("pool", convo[:, j, 2:2 + NS * 2].rearrange("p (b r) -> p b r", r=2), exs[:, :, 4:6],
                       [exs.t], [convo.t])
                act(u1[:, 0:NB], t1[:, 0:NB], AF.Square, [t1.t], [u1.t])
                act(u1[:, 0:NB], u1[:, 0:NB], AF.Identity, [u1.t, one_t.t], [u1.t], scale=0.044715, bias=one_t[:, 0:1])
                tt(u1[:, 0:NB], u1[:, 0:NB], t1[:, 0:NB], ALU.mult, [u1.t, t1.t], [u1.t])
                act(u1[:, 0:NB], u1[:, 0:NB], AF.Sigmoid, [u1.t], [u1.t], scale=GC)
                tt(u1[:, 0:NB], u1[:, 0:NB], t1[:, 0:NB], ALU.mult, [u1.t, t1.t], [u1.t], eng="pool")
                for ni, (o, n) in enumerate(NT):
                    pb = psum[4 + (j % 2) * 2 + ni]
                    tt(yT[:, j, o:o + n], pb[:, 0:n], u1[:, o:o + n], ALU.mult, [pb.t, u1.t], [yT.t])
            wo2 = [A.tile([128, NJ, 128], BF16, "wo2_%d" % i) for i in range(2)]
            fsb = [A.tile([128, 528], F32, "fsb%d" % i) for i in range(2)]
            for oc in range(KC):
                wt = wo2[oc % 2]
                if oc % 2 == 1:
                    load_cast_multi([(wt[:, j0:j1, :],
                                      w_ffn_out[l, j0 * 128:j1 * 128, oc * 128:(oc + 1) * 128].rearrange("(j p) c -> p j c", p=128))
                                     for (j0, j1) in ((0, 16), (16, 32), (32, NJ))], wt.t)
                else:
                    for (j0, j1) in ((0, 16), (16, 32), (32, NJ)):
                        load_cast(wt[:, j0:j1, :], wt.t,
                                  w_ffn_out[l, j0 * 128:j1 * 128, oc * 128:(oc + 1) * 128].rearrange("(j p) c -> p j c", p=128),
                                  [128, j1 - j0, 128])
                fs = fsb[oc % 2]
                for ni, (o, n) in enumerate(NT):
                    pf_ = psum[ni]
                    for j in range(NJ):
                        mm(pf_[:, 0:n], wt[:, j, :], yT[:, j, o:o + n], j == 0, j == NJ - 1, [wt.t, yT.t], [pf_.t])
                    cp("dve", fs[:, o:o + n], pf_[:, 0:n], [pf_.t], [fs.t])
                    sb = sqb[ni % 2]
                    act(sb[:, 0:n], fs[:, o:o + n], AF.Square, [fs.t], [sb.t])
                    mm(psum[3 + ni][:, 0:n], ones_b[:], sb[:, 0:n], oc == 0, oc == KC - 1, [ones_b.t, sb.t],
                       [psum[3 + ni].t])
                P.dma("pool", lambda e, fs=fs, oc=oc, b0=b0, NB=NB: [e.dma_start(out=fT_d[oc, :, b0:b0 + NB],
                                                                               in_=fs[:, 0:NB])],
                      1, fs.t, reads=[fs.t], writes=[fT_tok[oc]])
            for ni, (o, n) in enumerate(NT):
                rsqrt(rf[:, o:o + n], psum[3 + ni][:, 0:n], 1.0 / D, [psum[3 + ni].t], [rf.t])
            A.release(mC1)
            mC2 = A.mark()
            h2 = A.tile([128, KC, 528], F32, "h2")
            fl = [A.tile([128, 528], F32, "fl%d" % i) for i in range(2)]
            xo = [A.tile([128, 528], BF16, "xo%d" % i) for i in range(2)]
            P.dma("sp", lambda e, b0=b0, NB=NB: [e.dma_start(out=h2[:, :, 0:NB],
                                                             in_=hT_d[:, :, b0:b0 + NB].rearrange("k p t -> p k t"))],
                  1, h2.t, reads=[hT_tok[k][ti] for k in range(KC) for ti in tis], writes=[h2.t])
            for oc in range(KC):
                f_ = fl[oc % 2]
                P.dma("sp", lambda e, f_=f_, oc=oc, b0=b0, NB=NB: [e.dma_start(out=f_[:, 0:NB],
                                                                              in_=fT_d[oc, :, b0:b0 + NB])],
                      1, f_.t, reads=[fT_tok[oc]], writes=[f_.t])
                tt(f_[:, 0:NB], f_[:, 0:NB], rf[:, 0:NB], ALU.mult, [f_.t, rf.t], [f_.t])
                stt(h2[:, oc, 0:NB], f_[:, 0:NB], gT[:, l * 4 + 3, oc:oc + 1], h2[:, oc, 0:NB], ALU.mult, ALU.add,
                    [f_.t, gT.t, h2.t], [h2.t])
                for ni, (o, n) in enumerate(NT):
                    sb = sqb[ni % 2]
                    act(sb[:, 0:n], h2[:, oc, o:o + n], AF.Square, [h2.t], [sb.t])
                    mm(psum[ni][:, 0:n], ones_b[:], sb[:, 0:n], oc == 0, oc == KC - 1, [ones_b.t, sb.t], [psum[ni].t])
            if l == 0:
                for ni, (o, n) in enumerate(NT):
                    rsqrt(r3[:, o:o + n], psum[ni][:, 0:n], 1.0 / D, [psum[ni].t], [r3.t])
                P.dma("pool", lambda e, b0=b0, NB=NB: [e.dma_start(out=hT_d[:, :, b0:b0 + NB].rearrange("k p t -> p k t"),
                                                                 in_=h2[:, :, 0:NB])],
                      1, h2.t, reads=[h2.t], writes=[hT_tok[k][ti] for k in range(KC) for ti in tis])
                for (gsel, dst, dtok) in ((gT[:, 4, :], xnA_d, xnA_tok[bi]), (kvgT[:, :], xkv_d, xkv_tok[bi])):
                    for oc in range(KC):
                        x_ = xo[oc % 2]
                        stt(x_[:, 0:NB], h2[:, oc, 0:NB], gsel[:, oc:oc + 1], r3[:, 0:NB], ALU.mult, ALU.mult,
                            [h2.t, gT.t, kvgT.t, r3.t], [x_.t])
                        P.dma("pool", lambda e, x_=x_, oc=oc, b0=b0, NB=NB, dst=dst: [
                            e.dma_start(out=dst[oc, :, b0:b0 + NB], in_=x_[:, 0:NB])],
                            1, x_.t, reads=[x_.t], writes=[dtok])
            else:
                yt = [A.tile([128, D], F32, "yt%d" % i) for i in range(2)]
                for si, s0 in enumerate(range(0, NB, 128)):
                    rows = min(128, NB - s0)
                    y_ = yt[si % 2]
                    for q4 in range(4):
                        pt = psum[4 + q4 % 2]
                        for k in range(4):
                            oc = q4 * 4 + k
                            tr(pt[0:rows, k * 128:(k + 1) * 128], h2[:, oc, s0:s0 + rows], ident_f[:],
                               [h2.t, ident_f.t], [pt.t])
                        cp("act", y_[0:rows, q4 * 512:(q4 + 1) * 512], pt[0:rows, :], [pt.t], [y_.t])
                    g0_ = b0 + s0
                    dst = o_yp[g0_:g0_ + rows, :] if g0_ < T else o_ys[:, :]
                    dma_out(dst, y_[0:rows, :], y_.t)
            A.release(mC2)
        cvt = A.tile([2 + NS * 2, DFF], F32, "cvt")
        nr = 2 + NS * 2
        for j0 in range(0, NJ, 4):
            pt = psum[6 + (j0 // 4) % 2]
            for k in range(4):
                tr(pt[0:nr, k * 128:(k + 1) * 128], convo[:, j0 + k, :], ident_f[:], [convo.t, ident_f.t], [pt.t])
            cp("act", cvt[:, j0 * 128:(j0 + 4) * 128], pt[0:nr, :], [pt.t], [cvt.t])
        dma_out(o_cvp[l], cvt[0:2, :], cvt.t)
        dma_out(o_cvs[l].rearrange("b r f -> (b r) f"), cvt[2:nr, :], cvt.t)
        A.release(mC)

    if 'C' in PH:
        phase_C(0)

    kvT_d = nc.dram_tensor("kvT_d", [8, 128, NTOK], BF16, kind="Internal").ap()
    kvT_tok = P.tok("kvT_d")
    vtm_d = nc.dram_tensor("vtm_d", [17, 128, 512], BF16, kind="Internal").ap()
    vtm_tok = P.tok("vtm_d")
    FB = [0, 1, 2, 3, 4, 5, 8, 9]

    def phase_K():
        mK = A.mark()
        wkv = A.tile([128, KC, 1536], BF16, "wkv")
        for kc in range(KC):
            load_cast(wkv[:, kc, :], wkv.t, w_kv_b[kc * 128:(kc + 1) * 128, :], [128, 1536])
        xk2 = [A.tile([128, KC, 512], BF16, "xk%d" % i) for i in range(2)]
        kvo = [A.tile([128, 1536], F32, "kvo%d" % i) for i in range(2)]
        vst = [A.tile([128, 512], BF16, "vst%d" % i) for i in range(2)]
        kst = [A.tile([128, 8, 512], BF16, "kst%d" % i) for i in range(2)]
        i = 0
        for ti in range(5):
            c0t, nt = CT[ti]
            xk = xk2[ti % 2]
            P.dma("sp", lambda e, xk=xk, c0t=c0t, nt=nt: [e.dma_start(out=xk[:, :, 0:nt],
                                                                       in_=xkv_d[:, :, c0t:c0t + nt].rearrange("k p t -> p k t"))],
                  1, xk.t, reads=xkv_tok, writes=[xk.t])
            for sub in range((nt + 127) // 128):
                rows = min(128, nt - sub * 128)
                c0 = c0t + sub * 128
                lo = sub * 128
                ko = kvo[i % 2]
                vs = vst[i % 2]
                for nb in range(3):
                    pt = psum[nb + 3 * (i % 2)]
                    for kc in range(KC):
                        mm(pt[0:rows, :], xk[:, kc, lo:lo + rows], wkv[:, kc, nb * 512:(nb + 1) * 512], kc == 0,
                           kc == KC - 1, [xk.t, wkv.t], [pt.t])
                    cp("act" if nb % 2 else "dve", ko[0:rows, nb * 512:(nb + 1) * 512], pt[0:rows, :], [pt.t], [ko.t])
                cp("pool", vs[0:rows, 0:256], ko[0:rows, 768:1024], [ko.t], [vs.t])
                cp("pool", vs[0:rows, 256:512], ko[0:rows, 1280:1536], [ko.t], [vs.t])
                P.dma("pool", lambda e, vs=vs, i=i, rows=rows: [e.dma_start(out=vtm_d[i, 0:rows, :], in_=vs[0:rows, :])],
                      1, vs.t, reads=[vs.t], writes=[vtm_tok])
                if i < 16:
                    dma_out(o_kvp[c0:c0 + 128, :], ko[:, 0:1024], ko.t)
                    if i >= 12:
                        dma_out(o_winp[(i - 12) * 128:(i - 11) * 128, :], ko[:, 1024:1536], ko.t)
                else:
                    dma_out(o_kvs[:, :], ko[0:rows, 0:1024], ko.t)
                    dma_out(o_wins[:, :], ko[0:rows, 1024:1536], ko.t)
                i += 1
            ks = kst[ti % 2]
            for fi, cb in enumerate(FB):
                pt = psum[6 + fi % 2]
                for kc in range(KC):
                    mm(pt[:, 0:nt], wkv[:, kc, cb * 128:(cb + 1) * 128], xk[:, kc, 0:nt], kc == 0, kc == KC - 1,
                       [xk.t, wkv.t], [pt.t])
                cp("act", ks[:, fi, 0:nt], pt[:, 0:nt], [pt.t], [ks.t])
            P.dma("pool", lambda e, ks=ks, c0t=c0t, nt=nt: [e.dma_start(out=kvT_d[:, :, c0t:c0t + nt].rearrange("f p t -> p f t"),
                                                                        in_=ks[:, :, 0:nt])],
                  1, ks.t, reads=[ks.t], writes=[kvT_tok])
        A.release(mK)

    if 'K' in PH:
        phase_K()

    SLOPE = [2.0 ** (-8.0 * (i + 1) / H) for i in range(H)]
    NEGM = -30000.0

    def phase_N():
        dist0 = A.tile([128, 512], F32, "dist0")
        distc = A.tile([128, 512], F32, "distc")
        itmp = A.tile([128, 512], I32, "itmp")
        P.op("pool", lambda e: e.iota(itmp[:], pattern=[[1, 512]], base=0, channel_multiplier=-1), (), [itmp.t])
        cp("pool", dist0[:], itmp[:], [itmp.t], [dist0.t])
        P.op("pool", lambda e: e.iota(itmp[:], pattern=[[1, 512]], base=-31, channel_multiplier=-16), [dist0.t], [itmp.t])
        cp("pool", distc[:], itmp[:], [itmp.t], [distc.t])
        maug = A.tile([128, 33], BF16, "maug")
        memset("pool", maug[:], 1.0, [maug.t])
        P.op("pool", lambda e: e.affine_select(out=maug[:], in_=maug[:], pattern=[[-4, 33]], compare_op=ALU.is_ge,
                                               fill=0.0, base=1, channel_multiplier=1), [maug.t], [maug.t])
        P.op("pool", lambda e: e.affine_select(out=maug[:], in_=maug[:], pattern=[[4, 33]], compare_op=ALU.is_ge,
                                               fill=0.0, base=3, channel_multiplier=-1), [maug.t], [maug.t])
        memset("pool", maug[:, 32:33], 1.0, [maug.t])
        eall = A.tile([32, 16, 128], BF16, "eall")
        memset("pool", eall[:], 1.0, [eall.t])
        P.op("pool", lambda e: e.affine_select(out=eall[:], in_=eall[:], pattern=[[128, 16], [1, 128]],
                                               compare_op=ALU.is_ge, fill=0.0, base=0, channel_multiplier=-64),
             [eall.t], [eall.t])
        P.op("pool", lambda e: e.affine_select(out=eall[:], in_=eall[:], pattern=[[-128, 16], [-1, 128]],
                                               compare_op=ALU.is_ge, fill=0.0, base=63, channel_multiplier=64),
             [eall.t], [eall.t])
        memset("pool", sel36[:], 1.0, [sel36.t])
        P.op("pool", lambda e: e.affine_select(out=sel36[:], in_=sel36[:], pattern=[[-1, 36], [0, 128]],
                                               compare_op=ALU.is_equal, fill=0.0, base=0, channel_multiplier=1),
             [sel36.t], [sel36.t])
        validm = A.tile([128, 16, 32], F32, "validm")
        forced = A.tile([128, 16, 32], F32, "forced")
        VB = A.tile([128, 16, 32], F32, "VB")
        memset("pool", validm[:], 1.0, [validm.t])
        P.op("pool", lambda e: e.affine_select(out=validm[:], in_=validm[:], pattern=[[128, 16], [-64, 32]],
                                               compare_op=ALU.is_ge, fill=0.0, base=0, channel_multiplier=1),
             [validm.t], [validm.t])
        P.op("pool", lambda e: e.affine_select(out=forced[:], in_=validm[:], pattern=[[-128, 16], [64, 32]],
                                               compare_op=ALU.is_gt, fill=0.0, base=128, channel_multiplier=-1),
             [validm.t], [forced.t])
        memset("pool", forced[:, :, 0:1], 1.0, [forced.t])
        ts(VB[:], validm[:], -1.0, 1e30, ALU.add, ALU.mult, [validm.t], [VB.t], eng="pool")
        stt(VB[:], forced[:], 1e4, VB[:], ALU.mult, ALU.add, [forced.t, VB.t], [VB.t])
        tiny = 1e-30

        g36 = A.tile([36, NTOK], BF16, "g36")
        kcT = A.tile([128, 2, 128], BF16, "kcT")
        vc_tm = A.tile([128, 2, 128], BF16, "vc_tm")
        qh = A.tile([128, NTOK], BF16, "qh")
        impacc = A.tile([128, 2, 16, 32], F32, "impacc")
        bt = [None, None]
        mk = [A.tile([128, 512], F32, "mk0")] * 2
        BIG = 1.0e6
        dmc = [A.tile([128, 512], F32, "dmc%d" % i) for i in range(4)]
        for qt_ in range(4):
            ts(mk[0][:], distc[:], float(-512 * qt_), BIG, ALU.is_lt, ALU.mult, [distc.t], [mk[0].t], eng="pool")
            stt(dmc[qt_][:], distc[:], float(512 * qt_), mk[0][:], ALU.add, ALU.add, [distc.t, mk[0].t], [dmc[qt_].t])
        kk16 = A.tile([128, 16], F32, "kk16")
        P.op("pool", lambda e: e.iota(itmp[:, 0:16], pattern=[[128, 16]], base=0, channel_multiplier=0), [distc.t],
             [itmp.t])
        cp("pool", kk16[:], itmp[:, 0:16], [itmp.t], [kk16.t])
        hb = A.tile([128, 16], F32, "hb")
        sc = [A.tile([128, 512], F32, "sc%d" % i) for i in range(2)]
        pc = [A.tile([128, 512], BF16, "pc%d" % i) for i in range(2)]
        rzt = A.tile([128, 512], F32, "rzt")
        o32 = A.tile([128, 512], F32, "o32n")
        ps33 = A.tile([128, 4, 33], F32, "ps33")
        zr = A.tile([128, 4], F32, "zr")
        mN1 = A.mark()
        xnT = A.tile([128, KC, NTOK], BF16, "xnT1")
        P.dma("sp", lambda e: [e.dma_start(out=xnT[:], in_=xnA_d.rearrange("k p t -> p k t"))], 1, xnT.t,
              reads=xnA_tok, writes=[xnT.t])
        wq = A.tile([128, KC, 128], BF16, "wq")

        def projn(wt, ncols, ti, pbank):
            c0, n = CT[ti]
            pt = psum[pbank]
            for kc in range(KC):
                mm(pt[0:ncols, 0:n], wt[:, kc, 0:ncols], xnT[:, kc, c0:c0 + n], kc == 0, kc == KC - 1,
                   [wt.t, xnT.t], [pt.t])
            return pt

        load_cast(wq[:, :, 0:36], wq.t, w_in_b[0, :, W:W + 36].rearrange("(k p) c -> p k c", p=128), [128, KC, 36])
        for ti in range(5):
            c0, n = CT[ti]
            pt = projn(wq, 36, ti, ti % 2)
            act(g36[:, c0:c0 + n], pt[0:36, 0:n], AF.Sigmoid, [pt.t], [g36.t])
        cp("pool", g36s[:], g36[:, T:NTOK], [g36.t], [g36s.t])

        mcp = A.mark()
        kvc = A.tile([128, 2, T], BF16, "kvc")
        w1b = A.tile([128, 32, 128], BF16, "w1b")
        w2b = A.tile([128, 2, 128], BF16, "w2b")
        pe32 = A.tile([128, 2, 32], F32, "pe32")
        peb = A.tile([128, 2, 32], BF16, "peb")
        preb = A.tile([128, 2], F32, "preb")
        tg = A.tile([128, 128], F32, "tg")
        ug = A.tile([128, 128], F32, "ug")
        gl = A.tile([128, 128], BF16, "gl")
        dma_in(pe32[:], pe32.t, cmp_pos.rearrange("k j d -> d k j"), nonc=True)
        cp("pool", peb[:], pe32[:], [pe32.t], [peb.t])
        for kv in range(2):
            load_cast(w2b[:, kv, :], w2b.t, w_cmp2[kv], [128, 128])
        for kv in range(2):
            for hf in range(2):
                load_cast(w1b[:, hf * 16:(hf + 1) * 16, :], w1b.t,
                          w_cmp1[kv, hf * 2048:(hf + 1) * 2048, :].rearrange("(j d) h -> d j h", d=128), [128, 16, 128])
            pp = psum[2]
            for js in range(32):
                mm(pp[:, 0:1], w1b[:, js, :], peb[:, kv, js:js + 1], js == 0, js == 31, [w1b.t, peb.t], [pp.t])
            cp("dve", preb[:, kv:kv + 1], pp[:, 0:1], [pp.t], [preb.t])
            P.dma("sp", lambda e, kv=kv: [e.dma_start(out=kvc[:], in_=kvT_d[kv * 2:kv * 2 + 2, :, 0:T].rearrange("f p t -> p f t"))],
                  1, kvc.t, reads=[kvT_tok], writes=[kvc.t])
            for g in range(2):
                kview = kvc[:, g, :].rearrange("p (c s) -> p c s", s=16)
                pp = psum[3]
                for js in range(32):
                    j_, s_ = js // 16, js % 16
                    mm(pp[:, 0:127], w1b[:, js, :], kview[:, j_:j_ + 127, s_], js == 0, js == 31, [w1b.t, kvc.t], [pp.t])
                act(tg[:, 0:127], pp[:, 0:127], AF.Identity, [pp.t, preb.t], [tg.t], bias=preb[:, kv:kv + 1])
                act(ug[:, 0:127], tg[:, 0:127], AF.Square, [tg.t], [ug.t])
                ts(ug[:, 0:127], ug[:, 0:127], 0.044715, 1.0, ALU.mult, ALU.add, [ug.t], [ug.t])
                tt(ug[:, 0:127], ug[:, 0:127], tg[:, 0:127], ALU.mult, [ug.t, tg.t], [ug.t])
                act(ug[:, 0:127], ug[:, 0:127], AF.Sigmoid, [ug.t], [ug.t], scale=GC)
                tt(gl[:, 0:127], ug[:, 0:127], tg[:, 0:127], ALU.mult, [ug.t, tg.t], [gl.t])
                pq_ = psum[4]
                if kv == 0:
                    mm(pq_[:, 0:127], w2b[:, 0, :], gl[:, 0:127], True, True, [w2b.t, gl.t], [pq_.t])
                    cp("act", kcT[:, g, 0:127], pq_[:, 0:127], [pq_.t], [kcT.t])
                else:
                    mm(pq_[0:127, 0:128], gl[:, 0:127], w2b[:, 1, :], True, True, [w2b.t, gl.t], [pq_.t])
                    cp("act", vc_tm[0:127, g, :], pq_[0:127, 0:128], [pq_.t], [vc_tm.t])
        A.release(mcp)

        memset("pool", impacc[:], 0.0, [impacc.t])
        it = 0
        for hh in range(H):
            g, r = hh // 6, hh % 6
            sl = SLOPE[hh]
            load_cast(wq[:], wq.t, w_in_b[0, :, hh * 128:(hh + 1) * 128].rearrange("(k p) c -> p k c", p=128),
                      [128, KC, 128])
            for ti in range(5):
                c0, n = CT[ti]
                pt = projn(wq, 128, ti, ti % 2)
                act(qh[:, c0:c0 + n], pt[:, 0:n], AF.Copy, [pt.t], [qh.t], scale=SCALE)
            dma_out(qT_d[hh], qh[:], qh.t, writes=[qT_tok[hh]])
            for qt in range(4):
                t0 = qt * 512
                b_, m_, s_, p_ = bt[it % 2], mk[it % 2], sc[it % 2], pc[it % 2]
                it += 1
                psc = psum[2 + it % 2]
                mm(psc[0:127, :], kcT[:, g, 0:127], qh[:, t0:t0 + 512], True, True, [kcT.t, qh.t], [psc.t])
                stt(s_[0:127, :], dmc[qt][0:127, :], -sl, psc[0:127, :], ALU.mult, ALU.add, [dmc[qt].t, psc.t], [s_.t])
                act(p_[0:127, :], s_[0:127, :], AF.Exp, [s_.t], [p_.t])
                po, pz, pp = psum[4], psum[5], psum[6]
                mm(po[:, :], vc_tm[0:127, g, :], p_[0:127, :], True, True, [vc_tm.t, p_.t], [po.t])
                mm(pz[:, :], ones_b[0:127, :], p_[0:127, :], True, True, [ones_b.t, p_.t], [pz.t])
                for sub in range(4):
                    mm(pp[:, sub * 33:(sub + 1) * 33], p_[0:127, sub * 128:(sub + 1) * 128], maug[0:127, :], True, True,
                       [p_.t, maug.t], [pp.t])
                ts(rzt[:], pz[:], tiny, None, ALU.max, None, [pz.t], [rzt.t])
                P.op("dve", lambda e: e.reciprocal(rzt[:], rzt[:]), [rzt.t], [rzt.t])
                tt(o32[:], po[:], rzt[:], ALU.mult, [po.t, rzt.t], [o32.t])
                P.dma("pool", lambda e, hh=hh, t0=t0: [e.dma_start(out=ocmp_d[hh, :, t0:t0 + 512], in_=o32[:])], 1, o32.t,
                      reads=[o32.t], writes=[ocmp_tok[hh]])
                cp("dve", ps33[:], pp[:, 0:132].rearrange("p (a b) -> p a b", b=33), [pp.t], [ps33.t])
                ts(zr[:], ps33[:, :, 32], tiny, None, ALU.max, None, [ps33.t], [zr.t])
                P.op("dve", lambda e: e.reciprocal(zr[:], zr[:]), [zr.t], [zr.t])
                for sub in range(4):
                    stt(impacc[:, g, qt * 4 + sub, :], ps33[:, sub, 0:32], zr[:, sub:sub + 1],
                        impacc[:, g, qt * 4 + sub, :], ALU.mult, ALU.add, [ps33.t, zr.t, impacc.t], [impacc.t])

        nonlocal wb4, qT, mixo, pT2, rz, xn_tok
        wb4 = [A.tile([128, KC, 128], BF16, "wbn")]
        qT = A.tile([128, 512], BF16, "qTn")
        mixo = A.tile([128, NTOK], BF16, "mixo2")
        pT2 = [A.tile([128, 512], BF16, "pTn%d" % i) for i in range(2)]
        rz = A.tile([128, 512], F32, "rzn")
        xn_tok = [xnT.t] * 5
        mem_attn_l1(xnT)
        A.release(mN1)

        kvs = A.tile([128, 4, T], BF16, "kvs")
        P.dma("sp", lambda e: [e.dma_start(out=kvs[:], in_=kvT_d[4:8, :, 0:T].rearrange("f p t -> p f t"))], 1, kvs.t,
              reads=[kvT_tok], writes=[kvs.t])
        vtm = A.tile([128, 16, 512], BF16, "vtm")
        P.dma("sp", lambda e: [e.dma_start(out=vtm[:], in_=vtm_d[0:16].rearrange("i p c -> p i c"))], 1, vtm.t,
              reads=[vtm_tok], writes=[vtm.t])
        negselT = A.tile([32, 2, T], BF16, "negselT")
        dmb = {}
        for d_ in (-384, -256, -128, 0):
            tl_ = A.tile([128, 512], F32, "dmb%d" % (-d_))
            ts(mk[0][:], dist0[:], float(-d_), BIG, ALU.is_lt, ALU.mult, [dist0.t], [mk[0].t], eng="pool")
            stt(tl_[:], dist0[:], float(d_), mk[0][:], ALU.add, ALU.add, [dist0.t, mk[0].t], [tl_.t])
            dmb[d_] = tl_
        dmw = {}
        for d_ in (128, 256, 384, 512):
            tl_ = A.tile([128, 512], F32, "dmw%d" % d_)
            ts(mk[0][:], dist0[:], float(512 - d_), BIG, ALU.is_ge, ALU.mult, [dist0.t], [mk[0].t], eng="pool")
            stt(tl_[:], dist0[:], float(d_), mk[0][:], ALU.add, ALU.add, [dist0.t, mk[0].t], [tl_.t])
            dmw[d_] = tl_
        scr = A.tile([128, 32], F32, "scr")
        scr2 = A.tile([128, 32], F32, "scr2")
        m8a = A.tile([128, 8], F32, "m8a")
        m8b = A.tile([128, 8], F32, "m8b")
        selm = A.tile([128, 32], F32, "selm")
        for g in range(2):
            for t16 in range(16):
                tt(scr[:], impacc[:, g, t16, :], validm[:, t16, :], ALU.mult, [impacc.t, validm.t], [scr.t])
                tt(scr[:], scr[:], VB[:, t16, :], ALU.add, [scr.t, VB.t], [scr.t])
                P.op("dve", lambda e: e.max(out=m8a[:], in_=scr[:]), [scr.t], [m8a.t])
                P.op("dve", lambda e: e.match_replace(out=scr2[:], in_to_replace=m8a[:], in_values=scr[:],
                                                      imm_value=-1e30), [m8a.t, scr.t], [scr2.t])
                P.op("dve", lambda e: e.max(out=m8b[:], in_=scr2[:]), [scr2.t], [m8b.t])
                ts(selm[:], scr[:], m8b[:, 7:8], None, ALU.is_ge, None, [scr.t, m8b.t], [selm.t])
                tt(selm[:], selm[:], validm[:, t16, :], ALU.mult, [selm.t, validm.t], [selm.t])
                ts(selm[:], selm[:], -1.0, -NEGM, ALU.add, ALU.mult, [selm.t], [selm.t])
                ptn = psum[t16 % 2]
                tr(ptn[0:32, 0:128], selm[:], ident_f[:], [selm.t, ident_f.t], [ptn.t])
                cp("act", negselT[:, g, t16 * 128:(t16 + 1) * 128], ptn[0:32, 0:128], [ptn.t], [negselT.t])

        mix32 = A.tile([128, 512], F32, "mix32")
        oc32 = A.tile([128, 512], F32, "oc32")
        gsb = A.tile([128, 512], F32, "gsb")
        mixon = A.tile([128, NTOK], BF16, "mixon")
        for hh in range(H):
            g, r = hh // 6, hh % 6
            sl = SLOPE[hh]
            P.dma("sp", lambda e, hh=hh: [e.dma_start(out=qh[:], in_=qT_d[hh])], 1, qh.t, reads=[qT_tok[hh]],
                  writes=[qh.t])
            ts(hb[:], kk16[:], -sl, None, ALU.mult, None, [kk16.t], [hb.t], eng="pool")
            for qt in range(4):
                t0 = qt * 512
                P.dma("sp", lambda e, hh=hh, t0=t0: [e.dma_start(out=oc32[:], in_=ocmp_d[hh, :, t0:t0 + 512])], 1, oc32.t,
                      reads=[ocmp_tok[hh]], writes=[oc32.t])
                pg = psum[7]
                mm(pg[:, :], sel36[0:36, hh * 3 + 0, :], g36[0:36, t0:t0 + 512], True, True, [sel36.t, g36.t], [pg.t])
                tt(mix32[:], oc32[:], pg[:], ALU.mult, [oc32.t, pg.t], [mix32.t])
                for br in (1, 2):
                    kb_lo = 0 if br == 1 else max(0, 4 * qt - 4)
                    kb_hi = 4 * qt + 3
                    po, pz = (psum[4], psum[5]) if br == 1 else (psum[0], psum[6])
                    for kb in range(kb_lo, kb_hi + 1):
                        delta = t0 - kb * 128
                        b_, m_, s_, p_ = bt[it % 2], mk[it % 2], sc[it % 2], pc[it % 2]
                        it += 1
                        if delta <= 0:
                            dm_, eb_ = dmb[delta], None
                        elif br == 2:
                            dm_, eb_ = dmw[delta], None
                        else:
                            dm_, eb_ = dist0, hb[:, delta // 128:delta // 128 + 1]
                        psc = psum[2 + it % 2]
                        kblk = (0 if br == 1 else 2) + g
                        mm(psc[:, :], kvs[:, kblk, kb * 128:(kb + 1) * 128], qh[:, t0:t0 + 512], True, br == 2,
                           [kvs.t, qh.t], [psc.t])
                        if br == 1:
                            mm(psc[:, :], eall[:, kb, :], negselT[:, g, t0:t0 + 512], False, True,
                               [eall.t, negselT.t], [psc.t])
                        stt(s_[:], dm_[:], -sl, psc[:], ALU.mult, ALU.add, [dm_.t, psc.t], [s_.t])
                        if eb_ is None:
                            act(p_[:], s_[:], AF.Exp, [s_.t], [p_.t])
                        else:
                            act(p_[:], s_[:], AF.Exp, [s_.t, hb.t], [p_.t], bias=eb_)
                        vblk = (0 if br == 1 else 2) + g
                        mm(po[:, :], vtm[:, kb, vblk * 128:(vblk + 1) * 128], p_[:], kb == kb_lo, kb == kb_hi,
                           [vtm.t, p_.t], [po.t])
                        mm(pz[:, :], ones_b[:], p_[:], kb == kb_lo, kb == kb_hi, [ones_b.t, p_.t], [pz.t])
                    ts(rzt[:], pz[:], tiny, None, ALU.max, None, [pz.t], [rzt.t])
                    P.op("dve", lambda e: e.reciprocal(rzt[:], rzt[:]), [rzt.t], [rzt.t])
                    tt(o32[:], po[:], rzt[:], ALU.mult, [po.t, rzt.t], [o32.t])
                    pg = psum[7]
                    mm(pg[:, :], sel36[0:36, hh * 3 + br, :], g36[0:36, t0:t0 + 512], True, True, [sel36.t, g36.t], [pg.t])
                    tt(gsb[:], o32[:], pg[:], ALU.mult, [o32.t, pg.t], [gsb.t])
                    if br == 1:
                        tt(mix32[:], mix32[:], gsb[:], ALU.add, [mix32.t, gsb.t], [mix32.t])
                    else:
                        tt(mixon[:, t0:t0 + 512], mix32[:], gsb[:], ALU.add, [mix32.t, gsb.t], [mixon.t])
            P.dma("pool", lambda e, hh=hh: [e.dma_start(out=mixT_d[hh, :, 0:T], in_=mixon[:, 0:T])], 1, mixon.t,
                  reads=[mixon.t], writes=[mix_tok[hh]])

    def phase_S():
        tiny = 1e-30
        ckv_rows = ckv.rearrange("n p (h r) g d -> (n p h) (r g d)", h=2)
        w1b2 = A.tile([128, 2, 32, 128], BF16, "w1b2")
        w2b = A.tile([128, 2, 128], BF16, "w2bs")
        pe32 = A.tile([128, 2, 32], F32, "pe32s")
        peb = A.tile([128, 2, 32], BF16, "pebs")
        preb = A.tile([128, 2], F32, "prebs")
        dma_in(pe32[:], pe32.t, cmp_pos.rearrange("k j d -> d k j"), nonc=True)
        cp("pool", peb[:], pe32[:], [pe32.t], [peb.t])
        for kv in range(2):
            load_cast(w2b[:, kv, :], w2b.t, w_cmp2[kv], [128, 128])
            for hf in range(2):
                load_cast(w1b2[:, kv, hf * 16:(hf + 1) * 16, :], w1b2.t,
                          w_cmp1[kv, hf * 2048:(hf + 1) * 2048, :].rearrange("(j d) h -> d j h", d=128), [128, 16, 128])
            pp = psum[2]
            for js in range(32):
                mm(pp[:, 0:1], w1b2[:, kv, js, :], peb[:, kv, js:js + 1], js == 0, js == 31, [w1b2.t, peb.t], [pp.t])
            cp("dve", preb[:, kv:kv + 1], pp[:, 0:1], [pp.t], [preb.t])
        qs = A.tile([128, H, NS * TS], BF16, "qs")
        P.dma("sp", lambda e: [e.dma_start(out=qs[:], in_=qT_d[:, :, T:NTOK].rearrange("h p t -> p h t"))], 1, qs.t,
              reads=qT_tok, writes=[qs.t])
        gbs = A.tile([128, 3, H, NS * TS], F32, "gbs")
        for br in range(3):
            pg = psum[br]
            for hh in range(H):
                mm(pg[:, hh * 16:(hh + 1) * 16], sel36[0:36, hh * 3 + br, :], g36s[0:36, :], True, True,
                   [sel36.t, g36s.t], [pg.t])
            cp("act", gbs[:, br, :, :], pg[:, 0:H * 16].rearrange("p (h t) -> p h t", t=16), [pg.t], [gbs.t])

        def qsg(b_, g):
            return qs[:, g * 6:(g + 1) * 6, b_ * TS:(b_ + 1) * TS].rearrange("p r t -> p t r")

        ptb = A.tile([128, NS * 64], I32, "ptb")
        dma_in(ptb[:], ptb.t, ptab.rearrange("b n -> (b n)").partition_broadcast(128))
        piota = A.tile([128, NS * 64], I32, "piota")
        P.op("pool", lambda e: e.iota(piota[:], pattern=[[0, NS * 64]], base=0, channel_multiplier=1), (), [piota.t])
        idx_all = A.tile([128, NS * 64], I32, "idx_all")
        stt(idx_all[:], ptb[:], 128.0, piota[:], ALU.mult, ALU.add, [ptb.t, piota.t], [idx_all.t])
        idx_h = [A.tile([128, NS * 64], I32, "idx_h%d" % i) for i in range(2)]
        ts(idx_h[0][:], idx_all[:], 2.0, None, ALU.mult, None, [idx_all.t], [idx_h[0].t])
        ts(idx_h[1][:], idx_all[:], 2.0, 1.0, ALU.mult, ALU.add, [idx_all.t], [idx_h[1].t])
        it32 = A.tile([128, 16], I32, "it32")
        dcf = A.tile([128, 16], F32, "dcf")
        bcs = A.tile([128, 2, 4, 4, 6], F32, "bcs")
        P.op("pool", lambda e: e.iota(it32[:], pattern=[[2048, 4], [-1, 4]], base=31 - 8192, channel_multiplier=16),
             (), [it32.t])
        cp("pool", dcf[:], it32[:], [it32.t], [dcf.t])
        for hh in range(H):
            g, r = hh // 6, hh % 6
            ts(bcs[:, g, :, :, r], dcf[:].rearrange("p (i t) -> p i t", t=4), SLOPE[hh], None, ALU.mult, None,
               [dcf.t], [bcs.t], eng="pool")
        for g in range(2):
            P.op("pool", lambda e, g=g: e.affine_select(out=bcs[:, g, 3, :, :], in_=bcs[:, g, 3, :, :],
                                                        pattern=[[0, 4], [0, 6]], compare_op=ALU.is_ge, fill=NEGM,
                                                        base=126, channel_multiplier=-1), [bcs.t], [bcs.t])
        dsf = A.tile([128, 4], F32, "dsf")
        bS0 = A.tile([128, 2, 4, 6], F32, "bS0")
        slS = A.tile([128, 2, 4, 6], F32, "slS")
        P.op("pool", lambda e: e.iota(it32[:, 0:4], pattern=[[-1, 4]], base=-8192, channel_multiplier=1), [dcf.t], [it32.t])
        cp("pool", dsf[:], it32[:, 0:4], [it32.t], [dsf.t])
        for hh in range(H):
            g, r = hh // 6, hh % 6
            ts(bS0[:, g, :, r], dsf[:], SLOPE[hh], None, ALU.mult, None, [dsf.t], [bS0.t], eng="pool")
            memset("pool", slS[:, g, :, r], SLOPE[hh] * 128.0, [slS.t])
        bSall = A.tile([128, 64, 48], F32, "bSall")
        for pg_ in range(64):
            stt(bSall[:, pg_, :], slS[:].rearrange("p g t r -> p (g t r)"), float(pg_),
                bS0[:].rearrange("p g t r -> p (g t r)"), ALU.mult, ALU.add, [slS.t, bS0.t], [bSall.t])
        bN = A.tile([4, 2, 4, 6], F32, "bN")
        dnf = A.tile([4, 4], F32, "dnf")
        P.op("pool", lambda e: e.iota(it32[0:4, 0:4], pattern=[[-1, 4]], base=0, channel_multiplier=1), [dsf.t], [it32.t])
        cp("pool", dnf[:], it32[0:4, 0:4], [it32.t], [dnf.t])
        for hh in range(H):
            g, r = hh // 6, hh % 6
            ts(bN[:, g, :, r], dnf[:], SLOPE[hh], None, ALU.mult, None, [dnf.t], [bN.t], eng="pool")
        P.op("pool", lambda e: e.affine_select(out=bN[:], in_=bN[:], pattern=[[0, 2], [1, 4], [0, 6]],
                                               compare_op=ALU.is_ge, fill=NEGM, base=0, channel_multiplier=-1),
             [bN.t], [bN.t])
        bW = A.tile([128, 4, 2, 4, 6], F32, "bW")
        dwf = A.tile([128, 16], F32, "dwf")
        P.op("pool", lambda e: e.iota(it32[:], pattern=[[128, 4], [-1, 4]], base=-512, channel_multiplier=1), [dnf.t],
             [it32.t])
        cp("pool", dwf[:], it32[:], [it32.t], [dwf.t])
        for hh in range(H):
            g, r = hh // 6, hh % 6
            ts(bW[:, :, g, :, r], dwf[:].rearrange("p (w t) -> p w t", t=4), SLOPE[hh], None, ALU.mult, None,
               [dwf.t], [bW.t], eng="pool")
        for g in range(2):
            P.op("pool", lambda e, g=g: e.affine_select(out=bW[:, :, g, :, :], in_=bW[:, :, g, :, :],
                                                        pattern=[[128, 4], [-1, 4], [0, 6]], compare_op=ALU.is_ge,
                                                        fill=NEGM, base=-1, channel_multiplier=1), [bW.t], [bW.t])
        rselT = A.tile([4, 2, 4, 6], BF16, "rselT")
        memset("pool", rselT[:], 1.0, [rselT.t])
        P.op("pool", lambda e: e.affine_select(out=rselT[:], in_=rselT[:], pattern=[[0, 2], [1, 4], [0, 6]],
                                               compare_op=ALU.is_equal, fill=0.0, base=0, channel_multiplier=-1),
             [rselT.t], [rselT.t])
        rself = A.tile([24, 4], F32, "rself")
        memset("pool", rself[:], 1.0, [rself.t])
        P.op("pool", lambda e: e.affine_select(out=rself[:], in_=rself[:], pattern=[[-6, 4]], compare_op=ALU.is_ge,
                                               fill=0.0, base=0, channel_multiplier=1), [rself.t], [rself.t])
        P.op("pool", lambda e: e.affine_select(out=rself[:], in_=rself[:], pattern=[[6, 4]], compare_op=ALU.is_ge,
                                               fill=0.0, base=5, channel_multiplier=-1), [rself.t], [rself.t])
        maug_s = A.tile([128, 4, 130], BF16, "maug_s")
        memset("pool", maug_s[:], 1.0, [maug_s.t])
        P.op("pool", lambda e: e.affine_select(out=maug_s[:], in_=maug_s[:], pattern=[[128, 4], [-4, 130]],
                                               compare_op=ALU.is_ge, fill=0.0, base=1, channel_multiplier=1),
             [maug_s.t], [maug_s.t])
        P.op("pool", lambda e: e.affine_select(out=maug_s[:], in_=maug_s[:], pattern=[[-128, 4], [4, 130]],
                                               compare_op=ALU.is_ge, fill=0.0, base=3, channel_multiplier=-1),
             [maug_s.t], [maug_s.t])
        memset("pool", maug_s[:, :, 129:130], 1.0, [maug_s.t])
        VBs = A.tile([4, 129], F32, "VBs")
        memset("pool", VBs[:], 0.0, [VBs.t])
        memset("pool", VBs[:, 0:1], 1e4, [VBs.t])
        memset("pool", VBs[:, 127:129], 1e4, [VBs.t])

        kcmpT = A.tile([128, 4, 8192], BF16, "kcmpT")
        nse_h = nc.alloc_sbuf_tensor_at("nse_alias", [4, 2, 129 * 64], BF16, offset=int(kcmpT.h.manual_sbuf_range[0]))
        nse = Tile(nse_h, kcmpT.t)
        gt2 = [A.tile([128, 512], F32, "gt%d" % i) for i in range(2)]
        gb2 = [A.tile([128, 512], BF16, "gb%d" % i) for i in range(2)]
        ksT2 = [A.tile([128, 2, 128], BF16, "ksT%d" % i) for i in range(2)]
        kcT_s = A.tile([128, 2, 512], BF16, "kcT_s")
        vc_s = A.tile([128, 4, 2, 128], BF16, "vc_s")
        memset("pool", kcT_s[:], 0.0, [kcT_s.t])
        memset("pool", vc_s[:], 0.0, [vc_s.t])
        tg = A.tile([128, 512], F32, "tgs")
        ug = A.tile([128, 512], F32, "ugs")
        gl = A.tile([128, 512], BF16, "gls")
        s192 = A.tile([128, 192], F32, "s192")
        p192 = A.tile([128, 192], BF16, "p192")
        s48 = [A.tile([128, 48], F32, "s48_%d" % i) for i in range(2)]
        p48 = [A.tile([128, 48], BF16, "p48_%d" % i) for i in range(2)]
        rz48 = A.tile([128, 48], F32, "rz48")
        o48 = A.tile([128, 2, 4, 6], F32, "o48")
        acc48 = A.tile([128, 2, 4, 6], F32, "acc48")
        ps_s = A.tile([24, 2, 130], F32, "ps_s")
        zr_s = A.tile([24, 2], F32, "zr_s")
        pn_s = A.tile([24, 2, 129], F32, "pn_s")
        scs = A.tile([4, 2, 129], F32, "scs")
        scs2 = A.tile([4, 129], F32, "scs2")
        m8a = A.tile([4, 8], F32, "m8as")
        m8b = A.tile([4, 8], F32, "m8bs")
        sels = A.tile([4, 129], F32, "sels")
        knew = A.tile([128, 4, TS], BF16, "knew")
        vnew = A.tile([4, 512], BF16, "vnew")
        mixs = A.tile([128, H, NS * TS], BF16, "mixs")
        gi = 0

        def gather(b_, pg_, c0):
            nonlocal gi
            gt, gb = gt2[gi % 2], gb2[gi % 2]
            gi += 1
            col = b_ * 64 + pg_
            ih = idx_h[c0 // 512]
            P.dma("pool", lambda e: [e.indirect_dma_start(
                out=gt[:, :], out_offset=None, in_=ckv_rows[:, :],
                in_offset=bass.IndirectOffsetOnAxis(ap=ih[:, col:col + 1], axis=0))], 1, gt.t,
                reads=[ih.t], writes=[gt.t])
            cp("dve" if gi % 2 else "act", gb[:], gt[:], [gt.t], [gb.t])
            return gb

        def finish_branch(po, pz, br, b_, first):
            ts(rz48[:], pz[:, 0:48], tiny, None, ALU.max, None, [pz.t], [rz48.t])
            P.op("dve", lambda e: e.reciprocal(rz48[:], rz48[:]), [rz48.t], [rz48.t])
            o48f = o48[:].rearrange("p g t r -> p (g t r)")
            tt(o48f, po[:, 0:48], rz48[:], ALU.mult, [po.t, rz48.t], [o48.t])
            gview = gbs[:, br, :, b_ * TS:(b_ + 1) * TS].rearrange("p (g r) t -> p g t r", g=2)
            if first:
                tt(acc48[:], o48[:], gview, ALU.mult, [o48.t, gbs.t], [acc48.t])
            else:
                tt(o48[:], o48[:], gview, ALU.mult, [o48.t, gbs.t], [o48.t])
                tt(acc48[:], acc48[:], o48[:], ALU.add, [acc48.t, o48.t], [acc48.t])

        def new_tile(b_, kf0, vcol0, po, pz):
            P.dma("sp", lambda e: [e.dma_start(out=knew[:, 0:2, :],
                                               in_=kvT_d[kf0:kf0 + 2, :, T + b_ * TS:T + (b_ + 1) * TS].rearrange("f p t -> p f t"))],
                  1, knew.t, reads=[kvT_tok], writes=[knew.t])
            P.dma("sp", lambda e: [e.dma_start(out=vnew[:, :], in_=vtm_d[16, b_ * TS:(b_ + 1) * TS, :])], 1, vnew.t,
                  reads=[vtm_tok], writes=[vnew.t])
            psn = psum[3]
            for g in range(2):
                mm(psn[0:4, g * 24:(g + 1) * 24], knew[:, g, :], qsg(b_, g), True, True, [knew.t, qs.t], [psn.t])
            s_, p_ = s48[0], p48[0]
            tt(s_[0:4, :], psn[0:4, 0:48], bN[:].rearrange("p g t r -> p (g t r)"), ALU.add, [psn.t, bN.t], [s_.t])
            act(p_[0:4, :], s_[0:4, :], AF.Exp, [s_.t], [p_.t])
            for g in range(2):
                mm(po[:, g * 24:(g + 1) * 24], vnew[0:4, vcol0 + g * 128:vcol0 + (g + 1) * 128], p_[0:4, g * 24:(g + 1) * 24],
                   False, True, [vnew.t, p_.t], [po.t])
            mm(pz[:, 0:48], ones_b[0:4, :], p_[0:4, 0:48], False, True, [ones_b.t, p_.t], [pz.t])

        for b_ in range(NS):
            for pg_ in range(64):
                gb = gather(b_, pg_, 0)
                pt = psum[pg_ % 2]
                pv = psb(pg_ % 2)
                for k in range(4):
                    tr(pv[:, k * 128:(k + 1) * 128], gb[:, k * 128:(k + 1) * 128], ident_b[:], [gb.t, ident_b.t], [pt.t])
                cp("act" if pg_ % 2 else "dve", kcmpT[:, :, pg_ * 128:(pg_ + 1) * 128],
                   pv[:, 0:512].rearrange("p (k t) -> p k t", k=4), [pt.t], [kcmpT.t])
            for kv in range(2):
                for g in range(2):
                    kview = kcmpT[:, kv * 2 + g, :].rearrange("p (c s) -> p c s", s=16)
                    pp = psum[2]
                    for js in range(32):
                        j_, s_i = js // 16, js % 16
                        mm(pp[:, 0:511], w1b2[:, kv, js, :], kview[:, j_:j_ + 511, s_i], js == 0, js == 31,
                           [w1b2.t, kcmpT.t], [pp.t])
                    act(tg[:, 0:511], pp[:, 0:511], AF.Identity, [pp.t, preb.t], [tg.t], bias=preb[:, kv:kv + 1])
                    act(ug[:, 0:511], tg[:, 0:511], AF.Square, [tg.t], [ug.t])
                    ts(ug[:, 0:511], ug[:, 0:511], 0.044715, 1.0, ALU.mult, ALU.add, [ug.t], [ug.t])
                    tt(ug[:, 0:511], ug[:, 0:511], tg[:, 0:511], ALU.mult, [ug.t, tg.t], [ug.t])
                    act(ug[:, 0:511], ug[:, 0:511], AF.Sigmoid, [ug.t], [ug.t], scale=GC)
                    tt(gl[:, 0:511], ug[:, 0:511], tg[:, 0:511], ALU.mult, [ug.t, tg.t], [gl.t])
                    pq_ = psum[3]
                    if kv == 0:
                        mm(pq_[:, 0:511], w2b[:, 0, :], gl[:, 0:511], True, True, [w2b.t, gl.t], [pq_.t])
                        cp("act", kcT_s[:, g, 0:511], pq_[:, 0:511], [pq_.t], [kcT_s.t])
                    else:
                        for it_ in range(4):
                            n_i = 128 if it_ < 3 else 127
                            mm(pq_[0:n_i, it_ * 128:(it_ + 1) * 128], gl[:, it_ * 128:it_ * 128 + n_i], w2b[:, 1, :],
                               True, True, [w2b.t, gl.t], [pq_.t])
                        for it_ in range(4):
                            n_i = 128 if it_ < 3 else 127
                            cp("act", vc_s[0:n_i, it_, g, :], pq_[0:n_i, it_ * 128:(it_ + 1) * 128], [pq_.t], [vc_s.t])
            psc = psum[4]
            for g in range(2):
                for it_ in range(4):
                    c_ = (g * 4 + it_) * 24
                    mm(psc[:, c_:c_ + 24], kcT_s[:, g, it_ * 128:(it_ + 1) * 128], qsg(b_, g), True, True,
                       [kcT_s.t, qs.t], [psc.t])
            tt(s192[:], psc[:, 0:192], bcs[:].rearrange("p g i t r -> p (g i t r)"), ALU.add, [psc.t, bcs.t], [s192.t])
            act(p192[:], s192[:], AF.Exp, [s192.t], [p192.t])
            po, pz, pp = psum[5], psum[6], psum[7]
            for g in range(2):
                for it_ in range(4):
                    c_ = (g * 4 + it_) * 24
                    mm(po[:, g * 24:(g + 1) * 24], vc_s[:, it_, g, :], p192[:, c_:c_ + 24], it_ == 0, it_ == 3,
                       [vc_s.t, p192.t], [po.t])
                for it_ in range(4):
                    c_ = (g * 4 + it_) * 24
                    mm(pz[:, g * 24:(g + 1) * 24], ones_b[:], p192[:, c_:c_ + 24], it_ == 0, it_ == 3,
                       [ones_b.t, p192.t], [pz.t])
                for it_ in range(4):
                    c_ = (g * 4 + it_) * 24
                    mm(pp[0:24, g * 130:(g + 1) * 130], p192[:, c_:c_ + 24], maug_s[:, it_, :], it_ == 0, it_ == 3,
                       [p192.t, maug_s.t], [pp.t])
            finish_branch(po, pz, 0, b_, True)
            cp("dve", ps_s[:], pp[0:24, 0:260].rearrange("p (g j) -> p g j", g=2), [pp.t], [ps_s.t])
            ts(zr_s[:], ps_s[:, :, 129], tiny, None, ALU.max, None, [ps_s.t], [zr_s.t])
            P.op("dve", lambda e: e.reciprocal(zr_s[:], zr_s[:]), [zr_s.t], [zr_s.t])
            for g in range(2):
                ts(pn_s[:, g, :], ps_s[:, g, 0:129], zr_s[:, g:g + 1], None, ALU.mult, None, [ps_s.t, zr_s.t], [pn_s.t])
            pi_ = psum[3]
            for g in range(2):
                mm(pi_[0:4, g * 129:(g + 1) * 129], rself[:], pn_s[:, g, :], True, True, [rself.t, pn_s.t], [pi_.t])
            for g in range(2):
                tt(scs[:, g, :], pi_[0:4, g * 129:(g + 1) * 129], VBs[:], ALU.add, [pi_.t, VBs.t], [scs.t])
            for g in range(2):
                P.op("dve", lambda e, g=g: e.max(out=m8a[:], in_=scs[:, g, :]), [scs.t], [m8a.t])
                P.op("dve", lambda e, g=g: e.match_replace(out=scs2[:], in_to_replace=m8a[:], in_values=scs[:, g, :],
                                                           imm_value=-1e30), [m8a.t, scs.t], [scs2.t])
                P.op("dve", lambda e: e.max(out=m8b[:], in_=scs2[:]), [scs2.t], [m8b.t])
                ts(sels[:], scs[:, g, :], m8b[:, 7:8], None, ALU.is_ge, None, [scs.t, m8b.t], [sels.t])
                ts(sels[:], sels[:], -1.0, -NEGM, ALU.add, ALU.mult, [sels.t], [sels.t])
                cp("dve", nse[:, g, :].rearrange("p (j k) -> p j k", k=64),
                   sels[:].unsqueeze(2).to_broadcast([4, 129, 64]), [sels.t], [nse.t])
            po, pz = psum[5], psum[6]
            for pg_ in range(64):
                gb = gather(b_, pg_, 512)
                pt = psum[pg_ % 2]
                pv = psb(pg_ % 2)
                ks = ksT2[pg_ % 2]
                for k in range(2):
                    tr(pv[:, k * 128:(k + 1) * 128], gb[:, k * 128:(k + 1) * 128], ident_b[:], [gb.t, ident_b.t], [pt.t])
                cp("act", ks[:], pv[:, 0:256].rearrange("p (k t) -> p k t", k=2), [pt.t], [ks.t])
                psc = psum[2 + pg_ % 2]
                for g in range(2):
                    mm(psc[:, g * 24:(g + 1) * 24], ks[:, g, :], qsg(b_, g), True, False, [ks.t, qs.t], [psc.t])
                    mm(psc[:, g * 24:(g + 1) * 24], nse[0:4, g, pg_ * 128:(pg_ + 1) * 128],
                       rselT[0:4, g, :, :].rearrange("p t r -> p (t r)"), False, True, [nse.t, rselT.t], [psc.t])
                s_, p_ = s48[pg_ % 2], p48[pg_ % 2]
                tt(s_[:], psc[:, 0:48], bSall[:, pg_, :], ALU.add, [psc.t, bSall.t], [s_.t])
                act(p_[:], s_[:], AF.Exp, [s_.t], [p_.t])
                for g in range(2):
                    mm(po[:, g * 24:(g + 1) * 24], gb[:, 256 + g * 128:256 + (g + 1) * 128], p_[:, g * 24:(g + 1) * 24],
                       pg_ == 0, False, [gb.t, p_.t], [po.t])
                mm(pz[:, 0:48], ones_b[:], p_[:, 0:48], pg_ == 0, False, [ones_b.t, p_.t], [pz.t])
            new_tile(b_, 4, 0, po, pz)
            finish_branch(po, pz, 1, b_, False)
            for wt in range(4):
                gt, gb = gt2[gi % 2], gb2[gi % 2]
                gi += 1
                dma_in(gt[:], gt.t, cwin[b_, wt * 128:(wt + 1) * 128].rearrange("s a g d -> s (a g d)"))
                cp("dve", gb[:], gt[:], [gt.t], [gb.t])
                pt = psum[wt % 2]
                pv = psb(wt % 2)
                ks = ksT2[wt % 2]
                for k in range(2):
                    tr(pv[:, k * 128:(k + 1) * 128], gb[:, k * 128:(k + 1) * 128], ident_b[:], [gb.t, ident_b.t], [pt.t])
                cp("act", ks[:], pv[:, 0:256].rearrange("p (k t) -> p k t", k=2), [pt.t], [ks.t])
                psc = psum[2 + wt % 2]
                for g in range(2):
                    mm(psc[:, g * 24:(g + 1) * 24], ks[:, g, :], qsg(b_, g), True, True, [ks.t, qs.t], [psc.t])
                s_, p_ = s48[wt % 2], p48[wt % 2]
                tt(s_[:], psc[:, 0:48], bW[:, wt].rearrange("p g t r -> p (g t r)"), ALU.add, [psc.t, bW.t], [s_.t])
                act(p_[:], s_[:], AF.Exp, [s_.t], [p_.t])
                for g in range(2):
                    mm(po[:, g * 24:(g + 1) * 24], gb[:, 256 + g * 128:256 + (g + 1) * 128], p_[:, g * 24:(g + 1) * 24],
                       wt == 0, False, [gb.t, p_.t], [po.t])
                mm(pz[:, 0:48], ones_b[:], p_[:, 0:48], wt == 0, False, [ones_b.t, p_.t], [pz.t])
            new_tile(b_, 6, 256, po, pz)
            finish_branch(po, pz, 2, b_, False)
            cp("dve", mixs[:, :, b_ * TS:(b_ + 1) * TS].rearrange("p (g r) t -> p g t r", g=2), acc48[:],
               [acc48.t], [mixs.t])
        P.dma("pool", lambda e: [e.dma_start(out=mixT_d[0:H, :, T:NTOK].rearrange("h p t -> p h t"), in_=mixs[:])], 1,
              mixs.t, reads=[mixs.t], writes=mix_tok[0:H])

    def mem_attn_l1(xnT1):
        nonlocal xnT
        xnT = xnT1
        mem_attn(1, lambda hm: w_in_b[0, :, W + 36 + hm * 128:W + 36 + (hm + 1) * 128].rearrange("(k p) c -> p k c", p=128))

    if 'N' in PH:
        phase_M(1)
        sel36 = A.tile([36, 36, 128], BF16, "sel36")
        g36s = A.tile([36, NS * TS], BF16, "g36s")
        mN = A.mark()
        phase_N()
        A.release(mN)
        if 'S' in PH:
            phase_S()
            A.release(mN)
        phase_B(1)
        phase_C(1)

    P.emit()
    print("SBUF peak", A.peak, "ops", {e: len(P.ops[e]) for e in P.ENGS}, "signals",
          {e: sum(1 for o in P.ops[e] if o.signal and not o.ndma) for e in P.ENGS},
          "max dma val", max(16 * t.cnt for t in P.slots), "slots", len(P.slots))
    return nc


_NC_CACHE = {}


def kernel(x_prompt, x_sample, mem_prompt, state_hgrn, cache_conv, cache_mem, cache_kv, cache_win,
           page_table, norm_gains, w_in_a, lb_logits, hgrn_norm, w_in_b, w_o, w_mem_kv, kv_norm,
           w_kv_b, cmp_pos, w_cmp1, w_cmp2, w_ffn_in, w_ffn_conv, b_ffn_conv, w_ffn_out):
    f = lambda a: np.ascontiguousarray(np.asarray(a, dtype=np.float32))
    if "nc" not in _NC_CACHE:
        _NC_CACHE["nc"] = build()
    nc = _NC_CACHE["nc"]
    x_prompt = f(x_prompt); x_sample = f(x_sample); mem_prompt = f(mem_prompt)
    state_hgrn = f(state_hgrn); cache_mem = f(cache_mem)
    cache_conv = f(cache_conv)
    cache_kv_f = f(cache_kv)
    cache_win_f = f(cache_win)
    page_table_i = np.ascontiguousarray(np.asarray(page_table, dtype=np.int32))
    shared = {
        "norm_gains": f(norm_gains), "w_in_a": f(w_in_a), "lb_logits": f(lb_logits), "hgrn_norm": f(hgrn_norm),
        "w_o": f(w_o), "w_mem_kv": f(w_mem_kv), "kv_norm": f(kv_norm), "w_kv_b": f(w_kv_b),
        "w_in_b": f(w_in_b), "cmp_pos": f(cmp_pos), "w_cmp1": f(w_cmp1), "w_cmp2": f(w_cmp2),
        "w_ffn_in": f(w_ffn_in), "w_ffn_conv": f(w_ffn_conv), "b_ffn_conv": f(b_ffn_conv), "w_ffn_out": f(w_ffn_out),
    }
    in_maps = []
    for c in range(NCORES):
        s0, s1 = NS * c, NS * (c + 1)
        m = {
            "xp": x_prompt[c],
            "xs": x_sample[s0:s1].reshape(NS * TS, D),
            "memp": mem_prompt[c],
            "st_in": state_hgrn[0, s0:s1],
            "cmem": np.ascontiguousarray(cache_mem[:, s0:s1]),
            "cconv": np.ascontiguousarray(cache_conv[:, s0:s1]),
            "ckv": cache_kv_f,
            "cwin": np.ascontiguousarray(cache_win_f[s0:s1]),
            "ptab": np.ascontiguousarray(page_table_i[s0:s1]),
        }
        m.update(shared)
        in_maps.append(m)
    res = run_bass_kernel_spmd(nc, in_maps, core_ids=list(range(NCORES)))
    R = res.results
    B = 8
    SB = 32
    cat = lambda k, ax=0: np.concatenate([R[c][k] for c in range(NCORES)], axis=ax)
    stk = lambda k, ax=0: np.stack([R[c][k] for c in range(NCORES)], axis=ax)
    y_prompt = stk("o_yp")
    y_sample = cat("o_ys").reshape(SB, TS, D)
    hg_p = np.stack([R[c]["o_hgp"] for c in range(NCORES)], axis=0)[None]
    hg_s = np.concatenate([R[c]["o_hgs"] for c in range(NCORES)], axis=0)[None]
    cv_p = stk("o_cvp", 1)
    cv_s = cat("o_cvs", 1)
    nm = np.stack([R[c]["o_nm"] for c in range(NCORES)], axis=1).reshape(2, B, MEM, 2, 4, 128)
    kv_p = stk("o_kvp").reshape(B, T // 128, 128, 4, 2, 128)
    kv_s = cat("o_kvs").reshape(SB, TS, 4, 2, 128)
    win_p = stk("o_winp").reshape(B, 512, 2, 2, 128)
    win_s = cat("o_wins").reshape(SB, TS, 2, 2, 128)
    return (y_prompt, y_sample, hg_p, hg_s, cv_p, cv_s, nm, kv_p, kv_s, win_p, win_s)
```

```python
import numpy as np
import concourse.bass as bass
import concourse.mybir as mybir
from concourse.bass_utils import run_bass_kernel_spmd

F32 = mybir.dt.float32
BF16 = mybir.dt.bfloat16
I32 = mybir.dt.int32
AF = mybir.ActivationFunctionType
ALU = mybir.AluOpType
AX = mybir.AxisListType

NCORES = 8
D = 2048
KC = D // 128
T = 2048
NS = 4
TS = 4
NTOK = T + NS * TS
MEM = 256
W = 1536
H = 12
DFF = 5632
EPS = 1e-6
SCALE = 128 ** -0.5


class Tok:
    __slots__ = ("name", "w", "rs", "sem", "cnt")

    def __init__(self, name=""):
        self.name = name
        self.w = None
        self.rs = []
        self.sem = None
        self.cnt = 0


class Op:
    __slots__ = ("eng", "fn", "deps", "signal", "ndma", "sem", "val", "idx")


class Prog:
    ENGS = ("sp", "act", "dve", "pool", "pe")

    def __init__(self, nc):
        self.nc = nc
        self.ops = {e: [] for e in self.ENGS}
        self.allops = []
        self.dma_toks = []
        self.extra = {}
        self.slots = []
        self.rr = 0
        self.MAXSLOTS = 58
        self.dmas_since = []
        self.last = {}

    def tok(self, name=""):
        return Tok(name)

    def _add(self, eng, fn, reads, writes, ndma=0, dtok=None):
        o = Op()
        o.eng = eng
        o.fn = fn
        o.ndma = ndma
        o.signal = False
        o.sem = None
        o.val = 0
        deps = []
        seen = set()
        for t in reads:
            if t.w is not None and id(t.w) not in seen:
                seen.add(id(t.w))
                deps.append(t.w)
        for t in writes:
            if t.w is not None and id(t.w) not in seen:
                seen.add(id(t.w))
                deps.append(t.w)
            lastr = {}
            for r in t.rs:
                if r.ndma:
                    if id(r) not in seen:
                        seen.add(id(r))
                        deps.append(r)
                else:
                    lastr[r.eng] = r
            for r in lastr.values():
                if id(r) not in seen:
                    seen.add(id(r))
                    deps.append(r)
        if eng == "pe" and ndma == 0:
            deps = [d for d in deps if not (d.eng == "pe" and d.ndma == 0)]
        if self.extra.get(eng):
            for d in self.extra[eng]:
                if id(d) not in seen and not (d.eng == eng and d.ndma == 0):
                    seen.add(id(d))
                    deps.append(d)
            self.extra[eng] = None
        o.deps = deps
        for d in deps:
            d.signal = True
        for t in reads:
            t.rs.append(o)
        for t in writes:
            t.w = o
            t.rs = []
        if ndma:
            assert dtok is not None
            if dtok.sem is None:
                if len(self.slots) < self.MAXSLOTS:
                    sl = Tok("slot%d" % len(self.slots))
                    sl.rs = 1
                    self.slots.append(sl)
                else:
                    sl = self.slots[self.rr % self.MAXSLOTS]
                    self.rr += 1
                    sl.rs += 1
                dtok.sem = sl
            sl = dtok.sem
            if sl.rs > 1 and sl.w is not None and id(sl.w) not in seen:
                seen.add(id(sl.w))
                deps.append(sl.w)
                sl.w.signal = True
            sl.w = o
            sl.cnt += ndma
            o.sem = sl
            o.val = 16 * sl.cnt
        o.idx = len(self.ops[eng])
        if ndma:
            self.dmas_since.append(o)
        else:
            self.last[eng] = o
        self.ops[eng].append(o)
        self.allops.append(o)
        return o

    def barrier(self):
        deps = list(self.last.values()) + list(self.dmas_since)
        self.dmas_since = []
        for e in self.ENGS:
            self.extra[e] = list(deps)

    def op(self, eng, fn, reads=(), writes=()):
        return self._add(eng, fn, list(reads), list(writes))

    def dma(self, eng, fn, ndma, dtok, reads=(), writes=()):
        return self._add(eng, fn, list(reads), list(writes), ndma=ndma, dtok=dtok)

    def emit(self):
        nc = self.nc
        import contextlib
        with contextlib.ExitStack() as es:
            esem = {e: es.enter_context(nc.semaphore("sem_" + e)) for e in self.ENGS}
            for i, t in enumerate(self.slots):
                t.sem = es.enter_context(nc.semaphore("dsem%d" % i))
            for e in self.ENGS:
                c = 0
                for o in self.ops[e]:
                    if o.ndma == 0:
                        if o.signal:
                            c += 1
                            o.sem = esem[e]
                            o.val = c
                    else:
                        o.sem = o.sem.sem
            final_waits = [(t.sem, 16 * t.cnt) for t in self.slots]

            def run(ename, eng):
                waited = {}
                for o in self.ops[ename]:
                    for d in o.deps:
                        k = id(d.sem)
                        if waited.get(k, 0) >= d.val:
                            continue
                        waited[k] = d.val
                        eng.wait_ge(d.sem, d.val)
                    r = o.fn(eng)
                    if o.ndma:
                        assert len(r) == o.ndma, (len(r), o.ndma)
                        for ins in r:
                            ins.then_inc(o.sem, 16)
                    elif o.signal:
                        r.then_inc(o.sem, 1)
                if ename == "sp":
                    for s, v in final_waits:
                        if waited.get(id(s), 0) < v:
                            eng.wait_ge(s, v)

            with nc.Block() as block:
                @block.sync
                def _(e):
                    run("sp", e)

                @block.scalar
                def _(e):
                    run("act", e)

                @block.vector
                def _(e):
                    run("dve", e)

                @block.gpsimd
                def _(e):
                    run("pool", e)

                @block.tensor
                def _(e):
                    run("pe", e)


class Tile:
    def __init__(self, h, tok):
        self.h = h
        self.t = tok

    def __getitem__(self, k):
        return self.h[k]


class Alloc:
    LO = 17408
    HI = 228000

    def __init__(self, nc, P):
        self.nc = nc
        self.P = P
        self.top = self.LO
        self.n = 0
        self.peak = 0

    def mark(self):
        return self.top

    def release(self, m):
        self.P.barrier()
        self.top = m

    def tile(self, shape, dtype, name="t"):
        nbytes = int(np.prod(shape[1:])) * mybir.dt.size(dtype)
        off = (self.top + 63) // 64 * 64
        assert off + nbytes <= self.HI, ("SBUF overflow", name, off, nbytes)
        self.top = off + nbytes
        self.peak = max(self.peak, self.top)
        self.n += 1
        h = self.nc.alloc_sbuf_tensor_at("%s_%d" % (name, self.n), list(shape), dtype, offset=off)
        return Tile(h, self.P.tok(name))


def build():
    nc = bass.Bass("TRN2", target_bir_lowering=False)
    P = Prog(nc)
    A = Alloc(nc, P)

    def din(name, shape, dt=F32):
        return nc.dram_tensor(name, list(shape), dt, kind="ExternalInput").ap()

    def dout(name, shape, dt=F32):
        return nc.dram_tensor(name, list(shape), dt, kind="ExternalOutput").ap()

    xp = din("xp", [T, D])
    xs = din("xs", [NS * TS, D])
    memp = din("memp", [MEM, D])
    st_in = din("st_in", [NS, H, 128, 128])
    cmem = din("cmem", [2, NS, MEM, 2, 4, 128])
    norm_gains = din("norm_gains", [2, 4, D])
    w_in_a = din("w_in_a", [1, D, 4 * W + 512])
    lb_logits = din("lb_logits", [2, W])
    hgrn_norm = din("hgrn_norm", [1, H, 128])
    w_o = din("w_o", [2, D, D])
    w_mem_kv = din("w_mem_kv", [2, D, 1024])

    cconv = din("cconv", [2, NS, 2, DFF])
    kv_norm = din("kv_norm", [D])
    w_kv_b = din("w_kv_b", [D, 1536])
    w_ffn_in = din("w_ffn_in", [2, D, 2 * DFF])
    w_ffn_conv = din("w_ffn_conv", [2, 3, DFF])
    b_ffn_conv = din("b_ffn_conv", [2, DFF])
    w_ffn_out = din("w_ffn_out", [2, DFF, D])
    w_in_b = din("w_in_b", [1, D, W + 36 + 512])
    ckv = din("ckv", [2560, 128, 4, 2, 128])
    cwin = din("cwin", [NS, 512, 2, 2, 128])
    ptab = din("ptab", [NS, 64], I32)
    cmp_pos = din("cmp_pos", [2, 32, 128])
    w_cmp1 = din("w_cmp1", [2, 4096, 128])
    w_cmp2 = din("w_cmp2", [2, 128, 128])
    o_nm = dout("o_nm", [2, MEM, 1024])
    o_cvp = dout("o_cvp", [2, 2, DFF])
    o_cvs = dout("o_cvs", [2, NS, 2, DFF])
    o_kvp = dout("o_kvp", [T, 1024])
    o_kvs = dout("o_kvs", [NS * TS, 1024])
    o_winp = dout("o_winp", [512, 512])
    o_wins = dout("o_wins", [NS * TS, 512])
    o_yp = dout("o_yp", [T, D])
    o_ys = dout("o_ys", [NS * TS, D])
    o_hgp = dout("o_hgp", [H, 128, 128])
    o_hgs = dout("o_hgs", [NS, H, 128, 128])

    hT_d = nc.dram_tensor("hT_d", [KC, 128, NTOK], F32, kind="Internal").ap()
    mixT_d = nc.dram_tensor("mixT_d", [KC, 128, NTOK], BF16, kind="Internal").ap()
    hT_tok = [[P.tok("hT%d_%d" % (k, i)) for i in range(5)] for k in range(KC)]
    xn2T_d = nc.dram_tensor("xn2T_d", [KC, 128, NTOK], BF16, kind="Internal").ap()
    xn2_tok = [P.tok("xn2d%d" % i) for i in range(5)]
    xnA_d = nc.dram_tensor("xnA_d", [KC, 128, NTOK], BF16, kind="Internal").ap()
    xnA_tok = [P.tok("xnAd%d" % i) for i in range(4)]
    xkv_d = nc.dram_tensor("xkv_d", [KC, 128, NTOK], BF16, kind="Internal").ap()
    xkv_tok = [P.tok("xkvd%d" % i) for i in range(4)]
    fT_d = nc.dram_tensor("fT_d", [KC, 128, NTOK], F32, kind="Internal").ap()
    fT_tok = [P.tok("fTd%d" % i) for i in range(KC)]
    qT_d = nc.dram_tensor("qT_d", [H, 128, NTOK], BF16, kind="Internal").ap()
    qT_tok = [P.tok("qTd%d" % i) for i in range(H)]
    ocmp_d = nc.dram_tensor("ocmp_d", [H, 128, T], F32, kind="Internal").ap()
    ocmp_tok = [P.tok("ocmpd%d" % i) for i in range(H)]
    mix_tok = [P.tok("mixd%d" % k) for k in range(KC)]

    CT = [(i * 512, 512) for i in range(4)] + [(T, NS * TS)]

    def act(out, in_, func, reads, writes, **kw):
        P.op("act", lambda e: e.activation(out=out, in_=in_, func=func, **kw), reads, writes)

    def tt(out, in0, in1, op, reads, writes, eng="dve"):
        P.op(eng, lambda e: e.tensor_tensor(out=out, in0=in0, in1=in1, op=op), reads, writes)

    def ts(out, in0, s1, s2, op0, op1, reads, writes, eng="dve"):
        if op1 is None:
            P.op(eng, lambda e: e.tensor_scalar(out=out, in0=in0, scalar1=s1, scalar2=None, op0=op0), reads, writes)
        else:
            P.op(eng, lambda e: e.tensor_scalar(out=out, in0=in0, scalar1=s1, scalar2=s2, op0=op0, op1=op1),
                 reads, writes)

    def stt(out, in0, scalar, in1, op0, op1, reads, writes):
        P.op("dve", lambda e: e.scalar_tensor_tensor(out=out, in0=in0, scalar=scalar, in1=in1, op0=op0, op1=op1),
             reads, writes)

    def cp(eng, out, in_, reads, writes):
        if eng == "act":
            P.op("act", lambda e: e.copy(out, in_), reads, writes)
        else:
            P.op(eng, lambda e: e.tensor_copy(out, in_), reads, writes)

    def mm(out, lhsT, rhs, start, stop, reads, writes):
        P.op("pe", lambda e: e.matmul(out, lhsT, rhs, start=start, stop=stop), reads, writes)

    def tr(out, in_, ident, reads, writes):
        P.op("pe", lambda e: e.transpose(out, in_, ident), reads, writes)

    def rsqrt(out, in_, scale, reads, writes):
        n_p = out.shape[0]
        act(out, in_, AF.Sqrt, list(reads) + [eps_t.t], writes, scale=scale, bias=eps_t[0:n_p, :])
        P.op("dve", lambda e: e.reciprocal(out, out), writes, writes)

    def memset(eng, ap, v, writes):
        P.op(eng, lambda e: e.memset(ap, v), (), writes)

    def dma_in(tile_ap, tile_tok, src, reads=(), eng="sp", nonc=False):
        if nonc:
            P.dma(eng, lambda e: [e.dma_start(out=tile_ap, in_=src, allow_slow_non_contiguous=True)], 1, tile_tok,
                  reads=reads, writes=[tile_tok])
        else:
            P.dma(eng, lambda e: [e.dma_start(out=tile_ap, in_=src)], 1, tile_tok, reads=reads, writes=[tile_tok])

    def dma_out(dst, tile_ap, tile_tok, writes=(), eng="pool"):
        P.dma(eng, lambda e: [e.dma_start(out=dst, in_=tile_ap)], 1, tile_tok, reads=[tile_tok], writes=writes)

    eps_t = A.tile([128, 1], F32, "eps_t")
    one_t = A.tile([128, 1], F32, "one_t")
    P.op("pool", lambda e: e.memset(one_t[:], 1.0), (), [one_t.t])
    P.op("pool", lambda e: e.memset(eps_t[:], EPS), (), [eps_t.t])
    ident_f = A.tile([128, 128], F32, "ident_f")
    ident_b = A.tile([128, 128], BF16, "ident_b")
    ones_b = A.tile([128, 128], BF16, "ones_b")
    memset("pool", ident_b[:], 1.0, [ident_b.t])
    P.op("pool", lambda e: e.affine_select(out=ident_b[:], in_=ident_b[:], pattern=[[-1, 128]],
                                           compare_op=ALU.is_equal, fill=0.0, base=0, channel_multiplier=1),
         reads=[ident_b.t], writes=[ident_b.t])
    cp("pool", ident_f[:], ident_b[:], [ident_b.t], [ident_f.t])
    memset("pool", ones_b[:], 1.0, [ones_b.t])
    cmask = A.tile([32, 16, 32], F32, "cmask")
    memset("pool", cmask[:], 1.0, [cmask.t])
    P.op("pool", lambda e: e.affine_select(out=cmask[:], in_=cmask[:], pattern=[[0, 16], [1, 32]],
                                           compare_op=ALU.is_ge, fill=0.0, base=0, channel_multiplier=-1),
         reads=[cmask.t], writes=[cmask.t])
    rmask = A.tile([128, 512], F32, "rmask")
    memset("pool", rmask[:], 1.0, [rmask.t])
    memset("pool", rmask[:].rearrange("p (c j) -> p c j", j=32)[:, :, 0:1], 0.0, [rmask.t])
    rmask_s = A.tile([128, 16], F32, "rmask_s")
    memset("pool", rmask_s[:], 1.0, [rmask_s.t])
    memset("pool", rmask_s[:].rearrange("p (c j) -> p c j", j=4)[:, :, 0:1], 0.0, [rmask_s.t])
    gT = A.tile([128, 8, KC], F32, "gT")
    dma_in(gT[:], gT.t, norm_gains.rearrange("l j (k p) -> p (l j) k", p=128), nonc=True)
    gnT = A.tile([128, H], F32, "gnT")
    dma_in(gnT[:], gnT.t, hgrn_norm[0].rearrange("h d -> d h"), nonc=True)
    lbT = A.tile([128, 2, H], F32, "lbT")
    dma_in(lbT[:], lbT.t, lb_logits.rearrange("r (h d) -> d r h", d=128), nonc=True)
    kvgT = A.tile([128, KC], F32, "kvgT")
    dma_in(kvgT[:], kvgT.t, kv_norm.rearrange("(k p) -> p k", p=128), nonc=True)
    NJ = DFF // 128
    wcT = A.tile([128, 6, NJ], F32, "wcT")
    dma_in(wcT[:], wcT.t, w_ffn_conv.rearrange("l j (c p) -> p (l j) c", p=128), nonc=True)
    bcT = A.tile([128, 2, NJ], F32, "bcT")
    dma_in(bcT[:], bcT.t, b_ffn_conv.rearrange("l (c p) -> p l c", p=128), nonc=True)
    ccT = A.tile([128, 2, NJ, NS * 2], F32, "ccT")
    for l in range(2):
        for b_ in range(NS):
            for r_ in range(2):
                dma_in(ccT[:, l, :, b_ * 2 + r_], ccT.t, cconv[l, b_, r_].rearrange("(c p) -> p c", p=128), nonc=True)
    oml = A.tile([128, H], F32, "oml")
    tt(oml[:], lbT[:, 1, :], lbT[:, 0, :], ALU.subtract, [lbT.t], [oml.t])
    act(oml[:], oml[:], AF.Sigmoid, [oml.t], [oml.t])

    psum = []
    for i in range(8):
        h = nc.alloc_psum_tensor("ps%d" % i, [128, 512], F32)
        psum.append(Tile(h, P.tok("ps%d" % i)))

    def psb(i):
        return psum[i].h[:].bitcast(BF16)

    wst = [A.tile([128, 2048], F32, "wst%d" % i) for i in range(2)]
    wst_i = [0]

    def load_cast(dst_ap, dst_tok, src_ap, shape3, eng=None):
        st = wst[wst_i[0] % 2]
        if eng is None:
            eng = "dve" if wst_i[0] % 2 == 0 else "act"
        wst_i[0] += 1
        n = int(np.prod(shape3[1:]))
        sv = st[0:shape3[0], 0:n]
        if len(shape3) == 3:
            sv = sv.rearrange("p (a b) -> p a b", a=shape3[1])
        P.dma("sp", lambda e: [e.dma_start(out=sv, in_=src_ap)], 1, st.t, writes=[st.t])
        cp(eng, dst_ap, sv, [st.t], [dst_tok])

    import os
    PH = os.environ.get('PH', 'XMABCKNS')
    MS = os.environ.get('MS', 'VKS')
    memKT = A.tile([128, 5, 4, MEM], BF16, "memKT")
    memV = A.tile([128, 5, 2, 512], BF16, "memV")
    mXA = A.mark()
    xnT = A.tile([128, KC, NTOK], BF16, "xnT")
    xn_tok = [P.tok("xn%d" % i) for i in range(5)]
    mX = A.mark()
    xt2 = [A.tile([128, D], F32, "xt%d" % i) for i in range(2)]
    sq = A.tile([128, D], F32, "sq")
    xb = A.tile([128, D], BF16, "xb")
    ss = A.tile([128, 1], F32, "ss")
    hst = [A.tile([128, KC, 128], F32, "hst%d" % i) for i in range(2)]
    for i in (range(17) if 'X' in PH else []):
        rows = 128 if i < 16 else NS * TS
        src = xp[i * 128:(i + 1) * 128, :] if i < 16 else xs[:, :]
        c0 = i * 128
        cti = min(i // 4, 4)
        xt = xt2[i % 2]
        dma_in(xt[0:rows, :], xt.t, src)
        act(sq[0:rows, :], xt[0:rows, :], AF.Square, [xt.t], [sq.t])
        P.op("dve", lambda e, rows=rows: e.reduce_sum(out=ss[0:rows, :], in_=sq[0:rows, :], axis=AX.X), [sq.t], [ss.t])
        rsqrt(ss[0:rows, :], ss[0:rows, :], 1.0 / D, [ss.t], [ss.t])
        ts(xb[0:rows, :], xt[0:rows, :], ss[0:rows, 0:1], None, ALU.mult, None, [xt.t, ss.t], [xb.t])
        for half in range(2):
            pt = psum[half]
            pv = psb(half)
            for k in range(8):
                kc = half * 8 + k
                tr(pv[:, k * 128:k * 128 + rows], xb[0:rows, kc * 128:(kc + 1) * 128], ident_b[0:rows, 0:rows],
                   [xb.t, ident_b.t], [pt.t])
            tt(xnT[:, half * 8:(half + 1) * 8, c0:c0 + rows],
               pv.rearrange("p (k t) -> p k t", k=8)[:, :, 0:rows],
               gT[:, 0, half * 8:(half + 1) * 8].unsqueeze(2).to_broadcast([128, 8, rows]),
               ALU.mult, [pt.t, gT.t], [xn_tok[cti]])
        hs = hst[i % 2]
        for q4 in range(4):
            pt = psum[2 + q4 % 2]
            for k in range(4):
                kc = q4 * 4 + k
                tr(pt[:, k * 128:k * 128 + rows], xt[0:rows, kc * 128:(kc + 1) * 128], ident_f[0:rows, 0:rows],
                   [xt.t, ident_f.t], [pt.t])
            cp("act", hs[:, q4 * 4:(q4 + 1) * 4, 0:rows],
               pt[:].rearrange("p (k t) -> p k t", k=4)[:, :, 0:rows], [pt.t], [hs.t])
        P.dma("pool", lambda e, hs=hs, c0=c0, rows=rows: [
            e.dma_start(out=hT_d[:, :, c0:c0 + rows].rearrange("k p t -> p k t"), in_=hs[:, :, 0:rows])],
            1, hs.t, reads=[hs.t], writes=[hT_tok[k][cti] for k in range(KC)])
    A.release(mX)

    def phase_M(l):
        m0 = A.mark()
        memT = A.tile([128, KC, MEM], BF16, "memT")
        mtile = A.tile([128, D], F32, "mtile")
        mtile_b = A.tile([128, D], BF16, "mtile_b")
        for t2 in range(MEM // 128):
            dma_in(mtile[:], mtile.t, memp[t2 * 128:(t2 + 1) * 128, :])
            cp("dve", mtile_b[:], mtile[:], [mtile.t], [mtile_b.t])
            for half in range(2):
                pt = psum[half]
                pv = psb(half)
                for k in range(8):
                    kc = half * 8 + k
                    tr(pv[:, k * 128:(k + 1) * 128], mtile_b[:, kc * 128:(kc + 1) * 128], ident_b[:],
                       [mtile_b.t, ident_b.t], [pt.t])
                cp("act", memT[:, half * 8:(half + 1) * 8, t2 * 128:(t2 + 1) * 128],
                   pv.rearrange("p (k t) -> p k t", k=8), [pt.t], [memT.t])
        wmk = A.tile([128, KC, 512], BF16, "wmk")
        osb = [A.tile([128, 512], F32, "osb%d" % i) for i in range(2)]
        oi = 0
        for nb in range(2):
            for kc in range(KC):
                load_cast(wmk[:, kc, :], wmk.t, w_mem_kv[l, kc * 128:(kc + 1) * 128, nb * 512:(nb + 1) * 512],
                          [128, 512])
            for t2 in range(2):
                pt = psum[2 + (oi % 2)]
                for kc in range(KC):
                    mm(pt[:], memT[:, kc, t2 * 128:(t2 + 1) * 128], wmk[:, kc, :], kc == 0, kc == KC - 1,
                       [memT.t, wmk.t], [pt.t])
                ob = osb[oi % 2]
                oi += 1
                cp("dve", ob[:], pt[:], [pt.t], [ob.t])
                dma_out(o_nm[l, t2 * 128:(t2 + 1) * 128, nb * 512:(nb + 1) * 512], ob[:], ob.t)
                if nb == 1:
                    cp("pool", memV[:, 0, t2, :], ob[:], [ob.t], [memV.t])
            if nb == 0:
                for hm in range(4):
                    pt = psum[4 + hm % 2]
                    for kc in range(KC):
                        mm(pt[:, 0:MEM], wmk[:, kc, hm * 128:(hm + 1) * 128], memT[:, kc, :], kc == 0, kc == KC - 1,
                           [memT.t, wmk.t], [pt.t])
                    cp("act", memKT[:, 0, hm, :], pt[:, 0:MEM], [pt.t], [memKT.t])
        cmt = A.tile([128, 1024], F32, "cmt")
        cmb = A.tile([128, 512], BF16, "cmb")
        for b in range(NS):
            for t2 in range(2):
                dma_in(cmt[:], cmt.t, cmem[l, b, t2 * 128:(t2 + 1) * 128].rearrange("m a h d -> m (a h d)"))
                cp("dve", memV[:, 1 + b, t2, :], cmt[:, 512:1024], [cmt.t], [memV.t])
                cp("dve", cmb[:], cmt[:, 0:512], [cmt.t], [cmb.t])
                pt = psum[6]
                pv = psb(6)
                for hm in range(4):
                    tr(pv[:, hm * 128:(hm + 1) * 128], cmb[:, hm * 128:(hm + 1) * 128], ident_b[:],
                       [cmb.t, ident_b.t], [pt.t])
                cp("act", memKT[:, 1 + b, :, t2 * 128:(t2 + 1) * 128],
                   pv[:, 0:512].rearrange("p (h m) -> p h m", h=4), [pt.t], [memKT.t])
        A.release(m0)

    phase_M(0)

    mA = A.mark()
    wb4 = [A.tile([128, KC, 128], BF16, "wb%d" % i) for i in range(4)]
    qT = A.tile([128, 512], BF16, "qT")
    kT = A.tile([128, 512], BF16, "kT")
    kpT = A.tile([128, 512], BF16, "kpT")
    vT = A.tile([128, 512], BF16, "vT")
    ogT = A.tile([128, 512], BF16, "ogT")
    mixo = A.tile([128, NTOK], BF16, "mixo")
    ebl = A.tile([128, 16], F32, "ebl")
    kp_tm = A.tile([32, 16, 128], BF16, "kp_tm")
    v_tm = A.tile([32, 16, 128], BF16, "v_tm")
    sig = A.tile([128, 512], F32, "sig")
    q32 = A.tile([128, 512], F32, "q32")
    k32 = A.tile([128, 512], F32, "k32")
    lf = A.tile([128, 512], F32, "lf")
    bb = A.tile([128, 512], F32, "bb")
    e1 = A.tile([128, 512], F32, "e1")
    e2 = A.tile([128, 512], F32, "e2")
    kf = A.tile([128, 512], F32, "kf")
    Sst = A.tile([128, 128], F32, "Sst")
    Sbf = [A.tile([128, 128], BF16, "Sbf%d" % i) for i in range(2)]
    AT = A.tile([32, 512], BF16, "AT")
    osq = A.tile([128, 512], BF16, "osq")
    rstd = A.tile([128, 512], F32, "rstd")
    tmpn = A.tile([128, 512], F32, "tmpn")
    o32 = A.tile([128, 512], F32, "o32")
    U_tok = [P.tok("U%d" % i) for i in range(4)]

    def proj(widx, ti, pbank):
        c0, n = CT[ti]
        pt = psum[pbank]
        for kc in range(KC):
            mm(pt[:, 0:n], wb4[widx][:, kc, :], xnT[:, kc, c0:c0 + n], kc == 0, kc == KC - 1,
               [wb4[widx].t, xn_tok[ti]], [pt.t])
        return pt

    for h in (range(int(os.environ.get('NH', H))) if 'A' in PH else []):
        for j in range(4):
            col = j * W + h * 128
            load_cast(wb4[j][:], wb4[j].t,
                      w_in_a[0, :, col:col + 128].rearrange("(k p) c -> p k c", p=128), [128, KC, 128])
        sbi = 0
        for ti in range(5):
            c0, n = CT[ti]
            L = 32 if ti < 4 else TS
            nch = n // L
            ch0 = c0 // 32 if ti < 4 else 64
            pq = proj(0, ti, 0)
            act(q32[:, 0:n], pq[:, 0:n], AF.Silu, [pq.t], [q32.t])
            pf = proj(1, ti, 1)
            act(sig[:, 0:n], pf[:, 0:n], AF.Sigmoid, [pf.t], [sig.t], scale=-1.0)
            ts(k32[:, 0:n], sig[:, 0:n], oml[:, h:h + 1], None, ALU.mult, None, [sig.t, oml.t], [k32.t])
            pvv = proj(2, ti, 0)
            cp("act", vT[:, 0:n], pvv[:, 0:n], [pvv.t], [vT.t])
            po = proj(3, ti, 1)
            act(ogT[:, 0:n], po[:, 0:n], AF.Sigmoid, [po.t], [ogT.t])
            act(lf[:, 0:n], k32[:, 0:n], AF.Ln, [k32.t], [lf.t], scale=-1.0, bias=1.0)
            rm = rmask if ti < 4 else rmask_s
            P.op("dve", lambda e, n=n, rm=rm: e.tensor_tensor_scan(out=bb[:, 0:n], data0=rm[:, 0:n], data1=lf[:, 0:n],
                                                                 initial=0.0, op0=ALU.mult, op1=ALU.add),
                 [rm.t, lf.t], [bb.t])
            act(e1[:, 0:n], bb[:, 0:n], AF.Exp, [bb.t], [e1.t])
            act(e2[:, 0:n], bb[:, 0:n], AF.Exp, [bb.t], [e2.t], scale=-1.0)
            blast = bb[:, 0:n].rearrange("p (c j) -> p c j", j=L)[:, :, L - 1]
            act(ebl[:, 0:nch], blast, AF.Exp, [bb.t], [ebl.t])
            tt(qT[:, 0:n], q32[:, 0:n], e1[:, 0:n], ALU.mult, [q32.t, e1.t], [qT.t])
            tt(kf[:, 0:n], k32[:, 0:n], e2[:, 0:n], ALU.mult, [k32.t, e2.t], [kf.t])
            cp("pool", kT[:, 0:n], kf[:, 0:n], [kf.t], [kT.t])
            tt(kpT[:, 0:n].rearrange("p (c j) -> p c j", j=L),
               kf[:, 0:n].rearrange("p (c j) -> p c j", j=L),
               ebl[:, 0:nch].unsqueeze(2).to_broadcast([128, nch, L]), ALU.mult, [kf.t, ebl.t], [kpT.t])
            p3 = psum[3]
            for ci in range(nch):
                cc = ci * L
                mm(p3[0:L, ci * 32:ci * 32 + L], kT[:, cc:cc + L], qT[:, cc:cc + L], True, True,
                   [kT.t, qT.t], [p3.t])
            tt(AT[0:L, 0:nch * 32].rearrange("p (c j) -> p c j", j=32)[:, :, 0:L],
               p3[0:L, 0:nch * 32].rearrange("p (c j) -> p c j", j=32)[:, :, 0:L],
               cmask[0:L, 0:nch, 0:L], ALU.mult, [p3.t, cmask.t], [AT.t])
            for g0 in range(0, nch, 8):
                g1 = min(nch, g0 + 8)
                for (srcT, dst, pbk) in ((kpT, kp_tm, 2), (vT, v_tm, 7)):
                    pt = psum[pbk]
                    pv2 = psb(pbk)
                    for ci in range(g0, g1):
                        cc = ci * L
                        tr(pv2[0:L, (ci - g0) * 128:(ci - g0 + 1) * 128], srcT[:, cc:cc + L], ident_b[:],
                           [srcT.t, ident_b.t], [pt.t])
                    cp("act", dst[0:L, g0:g1, :],
                       pv2[0:L, 0:(g1 - g0) * 128].rearrange("p (c d) -> p c d", d=128), [pt.t], [dst.t])
            p4 = psum[4]
            for ci in range(nch):
                ch = ch0 + ci
                cc = ci * L
                fresh = (ti < 4 and ch == 0)
                if ti == 4:
                    dma_in(Sst[:], Sst.t, st_in[ci, h])
                    sbi += 1
                    cp("act", Sbf[sbi % 2][:], Sst[:], [Sst.t], [Sbf[sbi % 2].t])
                mm(p4[:, ci * L:(ci + 1) * L], v_tm[0:L, ci, :], AT[0:L, ci * 32:ci * 32 + L], True, fresh,
                   [v_tm.t, AT.t], [p4.t])
                if not fresh:
                    sb = Sbf[sbi % 2]
                    mm(p4[:, ci * L:(ci + 1) * L], sb[:], qT[:, cc:cc + L], False, True, [sb.t, qT.t], [p4.t])
                ut = U_tok[ch % 4]
                pu = psum[5][:, (ch % 4) * 128:(ch % 4 + 1) * 128]
                mm(pu, kp_tm[0:L, ci, :], v_tm[0:L, ci, :], True, True, [kp_tm.t, v_tm.t], [ut])
                if fresh:
                    cp("dve", Sst[:], pu, [ut], [Sst.t])
                else:
                    stt(Sst[:], Sst[:], ebl[:, ci:ci + 1], pu, ALU.mult, ALU.add, [Sst.t, ebl.t, ut], [Sst.t])
                last = (ti < 4 and ch == 63) or ti == 4
                if last:
                    dst = o_hgp[h] if ti < 4 else o_hgs[ci, h]
                    dma_out(dst, Sst[:], Sst.t)
                else:
                    sbi += 1
                    cp("act", Sbf[sbi % 2][:], Sst[:], [Sst.t], [Sbf[sbi % 2].t])
            cp("dve", o32[:, 0:n], p4[:, 0:n], [p4.t], [o32.t])
            act(osq[:, 0:n], o32[:, 0:n], AF.Square, [o32.t], [osq.t])
            p6 = psum[6]
            mm(p6[:, 0:n], ones_b[:], osq[:, 0:n], True, True, [ones_b.t, osq.t], [p6.t])
            rsqrt(rstd[:, 0:n], p6[:, 0:n], 1.0 / 128, [p6.t], [rstd.t])
            tt(tmpn[:, 0:n], o32[:, 0:n], rstd[:, 0:n], ALU.mult, [o32.t, rstd.t], [tmpn.t])
            stt(mixo[:, c0:c0 + n], tmpn[:, 0:n], gnT[:, h:h + 1], ogT[:, 0:n], ALU.mult, ALU.mult,
                [tmpn.t, gnT.t, ogT.t], [mixo.t])
        dma_out(mixT_d[h], mixo[:], mixo.t, writes=[mix_tok[h]])

    pT2 = [A.tile([128, 512], BF16, "pT%d" % i) for i in range(2)]
    rz = A.tile([128, 512], F32, "rz")

    def mem_attn(l, wsrc):
        for hm in (range(4) if 'A' in PH else []):
            load_cast(wb4[0][:], wb4[0].t, wsrc(hm), [128, KC, 128])
            for ti in range(5):
                c0, n = CT[ti]
                pq = proj(0, ti, 0)
                act(qT[:, 0:n], pq[:, 0:n], AF.Copy, [pq.t], [qT.t], scale=SCALE)
                segs = [(0, 0, n)] if ti < 4 else [(1 + b_, b_ * TS, TS) for b_ in range(NS)]
                p4 = psum[4]
                p6 = psum[6]
                for (sq_, o0, nn) in segs:
                    for mt in range(2):
                        pscore = psum[1 if mt == 0 else 7]
                        mm(pscore[:, 0:nn], memKT[:, sq_, hm, mt * 128:(mt + 1) * 128], qT[:, o0:o0 + nn],
                           True, True, [memKT.t, qT.t], [pscore.t])
                        act(pT2[mt][:, 0:nn], pscore[:, 0:nn], AF.Exp, [pscore.t], [pT2[mt].t])
                    for mt in range(2):
                        mm(p4[:, o0:o0 + nn], memV[:, sq_, mt, hm * 128:(hm + 1) * 128], pT2[mt][:, 0:nn],
                           mt == 0, mt == 1, [memV.t, pT2[mt].t], [p4.t])
                    for mt in range(2):
                        mm(p6[:, o0:o0 + nn], ones_b[:], pT2[mt][:, 0:nn], mt == 0, mt == 1,
                           [ones_b.t, pT2[mt].t], [p6.t])
                P.op("dve", lambda e, n=n, p6=p6, rz=rz: e.reciprocal(rz[:, 0:n], p6[:, 0:n]), [p6.t], [rz.t])
                tt(mixo[:, c0:c0 + n], p4[:, 0:n], rz[:, 0:n], ALU.mult, [p4.t, rz.t], [mixo.t])
            dma_out(mixT_d[12 + hm], mixo[:], mixo.t, writes=[mix_tok[12 + hm]])

    mem_attn(0, lambda hm: w_in_a[0, :, 4 * W + hm * 128:4 * W + (hm + 1) * 128].rearrange("(k p) c -> p k c", p=128))
    A.release(mXA)

    def phase_B(l):
        mB = A.mark()
        wo_b = A.tile([128, KC, D], BF16, "wo_b")
        for kc in range(KC):
            load_cast(wo_b[:, kc, :], wo_b.t, w_o[l, kc * 128:(kc + 1) * 128, :], [128, D])
        mt_t = A.tile([128, KC, 256], BF16, "mt_t")
        h_t = A.tile([128, KC, 256], F32, "h_t")
        o_t = A.tile([128, KC, 256], F32, "o_t")
        x2_t = A.tile([128, KC, 256], BF16, "x2_t")
        sqb = [A.tile([128, 512], BF16, "sqb%d" % i) for i in range(2)]
        r1 = A.tile([128, 512], F32, "r1")
        r2 = A.tile([128, 512], F32, "r2")
        tmpb = A.tile([128, 512], F32, "tmpb")
        for ti, c0, n in [(ti, CT[ti][0] + hf * 256, min(256, CT[ti][1] - hf * 256)) for ti in range(5)
                          for hf in range(2) if CT[ti][1] - hf * 256 > 0]:
            P.dma("sp", lambda e, c0=c0, n=n: [e.dma_start(out=mt_t[:, :, 0:n],
                                                           in_=mixT_d[:, :, c0:c0 + n].rearrange("k p t -> p k t"))],
                  1, mt_t.t, reads=mix_tok, writes=[mt_t.t])
            P.dma("sp", lambda e, c0=c0, n=n: [e.dma_start(out=h_t[:, :, 0:n],
                                                           in_=hT_d[:, :, c0:c0 + n].rearrange("k p t -> p k t"))],
                  1, h_t.t, reads=[hT_tok[k][ti] for k in range(KC)], writes=[h_t.t])
            for oc in range(KC):
                po = psum[oc % 2]
                for kc in range(KC):
                    mm(po[:, 0:n], wo_b[:, kc, oc * 128:(oc + 1) * 128], mt_t[:, kc, 0:n], kc == 0, kc == KC - 1,
                       [wo_b.t, mt_t.t], [po.t])
                cp("dve", o_t[:, oc, 0:n], po[:, 0:n], [po.t], [o_t.t])
                sb = sqb[oc % 2]
                act(sb[:, 0:n], o_t[:, oc, 0:n], AF.Square, [o_t.t], [sb.t])
                mm(psum[2][:, 0:n], ones_b[:], sb[:, 0:n], oc == 0, oc == KC - 1, [ones_b.t, sb.t], [psum[2].t])
            rsqrt(r1[:, 0:n], psum[2][:, 0:n], 1.0 / D, [psum[2].t], [r1.t])
            for oc in range(KC):
                tt(tmpb[:, 0:n], o_t[:, oc, 0:n], r1[:, 0:n], ALU.mult, [o_t.t, r1.t], [tmpb.t])
                stt(h_t[:, oc, 0:n], tmpb[:, 0:n], gT[:, l * 4 + 1, oc:oc + 1], h_t[:, oc, 0:n], ALU.mult, ALU.add,
                    [tmpb.t, gT.t, h_t.t], [h_t.t])
                sb = sqb[oc % 2]
                act(sb[:, 0:n], h_t[:, oc, 0:n], AF.Square, [h_t.t], [sb.t])
                mm(psum[3][:, 0:n], ones_b[:], sb[:, 0:n], oc == 0, oc == KC - 1, [ones_b.t, sb.t], [psum[3].t])
            rsqrt(r2[:, 0:n], psum[3][:, 0:n], 1.0 / D, [psum[3].t], [r2.t])
            for oc in range(KC):
                stt(x2_t[:, oc, 0:n], h_t[:, oc, 0:n], gT[:, l * 4 + 2, oc:oc + 1], r2[:, 0:n], ALU.mult, ALU.mult,
                    [h_t.t, gT.t, r2.t], [x2_t.t])
            P.dma("pool", lambda e, c0=c0, n=n: [e.dma_start(out=hT_d[:, :, c0:c0 + n].rearrange("k p t -> p k t"),
                                                             in_=h_t[:, :, 0:n])],
                  1, h_t.t, reads=[h_t.t], writes=[hT_tok[k][ti] for k in range(KC)])
            P.dma("pool", lambda e, c0=c0, n=n: [e.dma_start(out=xn2T_d[:, :, c0:c0 + n].rearrange("k p t -> p k t"),
                                                             in_=x2_t[:, :, 0:n])],
                  1, x2_t.t, reads=[x2_t.t], writes=[xn2_tok[ti]])
        A.release(mB)

    if 'B' in PH:
        phase_B(0)

    BLK = [(0, 688), (688, 688), (1376, 688)]
    SO = T - 1376
    LASTB = len(BLK) - 1
    GC = 2.0 * (2.0 / np.pi) ** 0.5

    def phase_C(l):
        mC = A.mark()
        convo = A.tile([128, NJ, 2 + NS * 2], F32, "convo")
        aprev = A.tile([128, NJ, 2], F32, "aprev")
        rf = A.tile([128, 688], F32, "rf")
        r3 = A.tile([128, 688], F32, "r3")
        sqb = [A.tile([128, 512], BF16, "sqc%d" % i) for i in range(2)]
        for bi, (b0, NB) in enumerate(BLK):
            NT = [(o, min(512, NB - o)) for o in range(0, NB, 512)]
            mC1 = A.mark()
            x2b = A.tile([128, KC, 688], BF16, "x2b")
            yT = A.tile([128, NJ, 688], BF16, "yT")
            wab = [[A.tile([128, KC, 128], BF16, "wab%d%d" % (i, k)) for k in range(2)] for i in range(2)]
            a_ext = A.tile([128, 690], F32, "a_ext")
            t1 = A.tile([128, 688], F32, "t1")
            u1 = A.tile([128, 688], F32, "u1")
            exs = A.tile([128, NS, 6], F32, "exs")
            tis = [ti for ti in range(5) if CT[ti][0] < b0 + NB and CT[ti][0] + CT[ti][1] > b0]
            P.dma("sp", lambda e, b0=b0, NB=NB: [e.dma_start(out=x2b[:, :, 0:NB],
                                                             in_=xn2T_d[:, :, b0:b0 + NB].rearrange("k p t -> p k t"))],
                  1, x2b.t, reads=[xn2_tok[ti] for ti in tis], writes=[x2b.t])
            for j in range(NJ):
                wa, wb = wab[j % 2]
                load_cast(wa[:], wa.t, w_ffn_in[l, :, j * 128:(j + 1) * 128].rearrange("(k p) c -> p k c", p=128),
                          [128, KC, 128])
                load_cast(wb[:], wb.t,
                          w_ffn_in[l, :, DFF + j * 128:DFF + (j + 1) * 128].rearrange("(k p) c -> p k c", p=128),
                          [128, KC, 128])
                if bi == 0:
                    memset("pool", a_ext[:, 0:2], 0.0, [a_ext.t])
                else:
                    cp("pool", a_ext[:, 0:2], aprev[:, j, :], [aprev.t], [a_ext.t])
                for ni, (o, n) in enumerate(NT):
                    pa = psum[(j % 2) * 2 + ni]
                    for kc in range(KC):
                        mm(pa[:, 0:n], wa[:, kc, :], x2b[:, kc, o:o + n], kc == 0, kc == KC - 1, [wa.t, x2b.t], [pa.t])
                    cp("act", a_ext[:, 2 + o:2 + o + n], pa[:, 0:n], [pa.t], [a_ext.t])
                for ni, (o, n) in enumerate(NT):
                    pb = psum[4 + (j % 2) * 2 + ni]
                    for kc in range(KC):
                        mm(pb[:, 0:n], wb[:, kc, :], x2b[:, kc, o:o + n], kc == 0, kc == KC - 1, [wb.t, x2b.t], [pb.t])
                w0 = wcT[:, l * 3 + 0, j:j + 1]
                w1 = wcT[:, l * 3 + 1, j:j + 1]
                w2 = wcT[:, l * 3 + 2, j:j + 1]
                bc = bcT[:, l, j:j + 1]
                ts(t1[:, 0:NB], a_ext[:, 2:2 + NB], w2, bc, ALU.mult, ALU.add, [a_ext.t, wcT.t, bcT.t], [t1.t])
                stt(t1[:, 0:NB], a_ext[:, 1:1 + NB], w1, t1[:, 0:NB], ALU.mult, ALU.add, [a_ext.t, wcT.t, t1.t], [t1.t])
                stt(t1[:, 0:NB], a_ext[:, 0:NB], w0, t1[:, 0:NB], ALU.mult, ALU.add, [a_ext.t, wcT.t, t1.t], [t1.t])
                if bi < LASTB:
                    cp("pool", aprev[:, j, :], a_ext[:, NB:NB + 2], [a_ext.t], [aprev.t])
                else:
                    so = SO
                    cp("pool", exs[:, :, 0:2], ccT[:, l, j, :].rearrange("p (b r) -> p b r", r=2), [ccT.t], [exs.t])
                    cp("pool", exs[:, :, 2:6], a_ext[:, 2 + so:2 + so + 16].rearrange("p (b t) -> p b t", t=TS),
                       [a_ext.t], [exs.t])
                    t1s = t1[:, so:so + 16].rearrange("p (b t) -> p b t", t=TS)
                    ts(t1s, exs[:, :, 2:6], w2, bc, ALU.mult, ALU.add, [exs.t, wcT.t, bcT.t, t1.t], [t1.t])
                    stt(t1s, exs[:, :, 1:5], w1, t1s, ALU.mult, ALU.add, [exs.t, wcT.t, t1.t], [t1.t])
                    stt(t1s, exs[:, :, 0:4], w0, t1s, ALU.mult, ALU.add, [exs.t, wcT.t, t1.t], [t1.t])
                    cp("pool", convo[:, j, 0:2], a_ext[:, SO:SO + 2], [a_ext.t], [convo.t])
                    cp("pool", convo[:, j, 2:2 + NS * 2].rearrange("p (b r) -> p b r", r=2), exs[:, :, 4:6],
                       [exs.t], [convo.t])
                act(u1[:, 0:NB], t1[:, 0:NB], AF.Square, [t1.t], [u1.t])
                act(u1[:, 0:NB], u1[:, 0:NB], AF.Identity, [u1.t, one_t.t], [u1.t], scale=0.044715, bias=one_t[:, 0:1])
                tt(u1[:, 0:NB], u1[:, 0:NB], t1[:, 0:NB], ALU.mult, [u1.t, t1.t], [u1.t])
                act(u1[:, 0:NB], u1[:, 0:NB], AF.Sigmoid, [u1.t], [u1.t], scale=GC)
                tt(u1[:, 0:NB], u1[:, 0:NB], t1[:, 0:NB], ALU.mult, [u1.t, t1.t], [u1.t], eng="pool")
                for ni, (o, n) in enumerate(NT):
                    pb = psum[4 + (j % 2) * 2 + ni]
                    tt(yT[:, j, o:o + n], pb[:, 0:n], u1[:, o:o + n], ALU.mult, [pb.t, u1.t], [yT.t])
            wo2 = [A.tile([128, NJ, 128], BF16, "wo2_%d" % i) for i in range(2)]
            fsb = [A.tile([128, 688], F32, "fsb%d" % i) for i in range(2)]
            for oc in range(KC):
                wt = wo2[oc % 2]
                for (j0, j1) in ((0, 16), (16, 32), (32, NJ)):
                    load_cast(wt[:, j0:j1, :], wt.t,
                              w_ffn_out[l, j0 * 128:j1 * 128, oc * 128:(oc + 1) * 128].rearrange("(j p) c -> p j c", p=128),
                              [128, j1 - j0, 128])
                fs = fsb[oc % 2]
                for ni, (o, n) in enumerate(NT):
                    pf_ = psum[ni]
                    for j in range(NJ):
                        mm(pf_[:, 0:n], wt[:, j, :], yT[:, j, o:o + n], j == 0, j == NJ - 1, [wt.t, yT.t], [pf_.t])
                    cp("dve", fs[:, o:o + n], pf_[:, 0:n], [pf_.t], [fs.t])
                    sb = sqb[ni % 2]
                    act(sb[:, 0:n], fs[:, o:o + n], AF.Square, [fs.t], [sb.t])
                    mm(psum[3 + ni][:, 0:n], ones_b[:], sb[:, 0:n], oc == 0, oc == KC - 1, [ones_b.t, sb.t],
                       [psum[3 + ni].t])
                P.dma("pool", lambda e, fs=fs, oc=oc, b0=b0, NB=NB: [e.dma_start(out=fT_d[oc, :, b0:b0 + NB],
                                                                               in_=fs[:, 0:NB])],
                      1, fs.t, reads=[fs.t], writes=[fT_tok[oc]])
            for ni, (o, n) in enumerate(NT):
                rsqrt(rf[:, o:o + n], psum[3 + ni][:, 0:n], 1.0 / D, [psum[3 + ni].t], [rf.t])
            A.release(mC1)
            mC2 = A.mark()
            h2 = A.tile([128, KC, 688], F32, "h2")
            fl = [A.tile([128, 688], F32, "fl%d" % i) for i in range(2)]
            xo = [A.tile([128, 688], BF16, "xo%d" % i) for i in range(2)]
            P.dma("sp", lambda e, b0=b0, NB=NB: [e.dma_start(out=h2[:, :, 0:NB],
                                                             in_=hT_d[:, :, b0:b0 + NB].rearrange("k p t -> p k t"))],
                  1, h2.t, reads=[hT_tok[k][ti] for k in range(KC) for ti in tis], writes=[h2.t])
            for oc in range(KC):
                f_ = fl[oc % 2]
                P.dma("sp", lambda e, f_=f_, oc=oc, b0=b0, NB=NB: [e.dma_start(out=f_[:, 0:NB],
                                                                              in_=fT_d[oc, :, b0:b0 + NB])],
                      1, f_.t, reads=[fT_tok[oc]], writes=[f_.t])
                tt(f_[:, 0:NB], f_[:, 0:NB], rf[:, 0:NB], ALU.mult, [f_.t, rf.t], [f_.t])
                stt(h2[:, oc, 0:NB], f_[:, 0:NB], gT[:, l * 4 + 3, oc:oc + 1], h2[:, oc, 0:NB], ALU.mult, ALU.add,
                    [f_.t, gT.t, h2.t], [h2.t])
                for ni, (o, n) in enumerate(NT):
                    sb = sqb[ni % 2]
                    act(sb[:, 0:n], h2[:, oc, o:o + n], AF.Square, [h2.t], [sb.t])
                    mm(psum[ni][:, 0:n], ones_b[:], sb[:, 0:n], oc == 0, oc == KC - 1, [ones_b.t, sb.t], [psum[ni].t])
            if l == 0:
                for ni, (o, n) in enumerate(NT):
                    rsqrt(r3[:, o:o + n], psum[ni][:, 0:n], 1.0 / D, [psum[ni].t], [r3.t])
                P.dma("pool", lambda e, b0=b0, NB=NB: [e.dma_start(out=hT_d[:, :, b0:b0 + NB].rearrange("k p t -> p k t"),
                                                                 in_=h2[:, :, 0:NB])],
                      1, h2.t, reads=[h2.t], writes=[hT_tok[k][ti] for k in range(KC) for ti in tis])
                for (gsel, dst, dtok) in ((gT[:, 4, :], xnA_d, xnA_tok[bi]), (kvgT[:, :], xkv_d, xkv_tok[bi])):
                    for oc in range(KC):
                        x_ = xo[oc % 2]
                        stt(x_[:, 0:NB], h2[:, oc, 0:NB], gsel[:, oc:oc + 1], r3[:, 0:NB], ALU.mult, ALU.mult,
                            [h2.t, gT.t, kvgT.t, r3.t], [x_.t])
                        P.dma("pool", lambda e, x_=x_, oc=oc, b0=b0, NB=NB, dst=dst: [
                            e.dma_start(out=dst[oc, :, b0:b0 + NB], in_=x_[:, 0:NB])],
                            1, x_.t, reads=[x_.t], writes=[dtok])
            else:
                yt = [A.tile([128, D], F32, "yt%d" % i) for i in range(2)]
                segs_ = []
                pend = NB if bi < LASTB else SO
                for s0 in range(0, pend, 128):
                    segs_.append((s0, min(128, pend - s0)))
                if bi == LASTB:
                    segs_.append((SO, NS * TS))
                for si, (s0, rows) in enumerate(segs_):
                    y_ = yt[si % 2]
                    for q4 in range(4):
                        pt = psum[4 + q4 % 2]
                        for k in range(4):
                            oc = q4 * 4 + k
                            tr(pt[0:rows, k * 128:(k + 1) * 128], h2[:, oc, s0:s0 + rows], ident_f[:],
                               [h2.t, ident_f.t], [pt.t])
                        cp("act", y_[0:rows, q4 * 512:(q4 + 1) * 512], pt[0:rows, :], [pt.t], [y_.t])
                    g0_ = b0 + s0
                    dst = o_yp[g0_:g0_ + rows, :] if g0_ < T else o_ys[:, :]
                    dma_out(dst, y_[0:rows, :], y_.t)
            A.release(mC2)
        cvt = A.tile([2 + NS * 2, DFF], F32, "cvt")
        nr = 2 + NS * 2
        for j0 in range(0, NJ, 4):
            pt = psum[6 + (j0 // 4) % 2]
            for k in range(4):
                tr(pt[0:nr, k * 128:(k + 1) * 128], convo[:, j0 + k, :], ident_f[:], [convo.t, ident_f.t], [pt.t])
            cp("act", cvt[:, j0 * 128:(j0 + 4) * 128], pt[0:nr, :], [pt.t], [cvt.t])
        dma_out(o_cvp[l], cvt[0:2, :], cvt.t)
        dma_out(o_cvs[l].rearrange("b r f -> (b r) f"), cvt[2:nr, :], cvt.t)
        A.release(mC)

    if 'C' in PH:
        phase_C(0)

    kvT_d = nc.dram_tensor("kvT_d", [8, 128, NTOK], BF16, kind="Internal").ap()
    kvT_tok = P.tok("kvT_d")
    vtm_d = nc.dram_tensor("vtm_d", [17, 128, 512], BF16, kind="Internal").ap()
    vtm_tok = P.tok("vtm_d")
    FB = [0, 1, 2, 3, 4, 5, 8, 9]

    def phase_K():
        mK = A.mark()
        wkv = A.tile([128, KC, 1536], BF16, "wkv")
        for kc in range(KC):
            load_cast(wkv[:, kc, :], wkv.t, w_kv_b[kc * 128:(kc + 1) * 128, :], [128, 1536])
        xk2 = [A.tile([128, KC, 512], BF16, "xk%d" % i) for i in range(2)]
        kvo = [A.tile([128, 1536], F32, "kvo%d" % i) for i in range(2)]
        vst = [A.tile([128, 512], BF16, "vst%d" % i) for i in range(2)]
        kst = [A.tile([128, 8, 512], BF16, "kst%d" % i) for i in range(2)]
        i = 0
        for ti in range(5):
            c0t, nt = CT[ti]
            xk = xk2[ti % 2]
            P.dma("sp", lambda e, xk=xk, c0t=c0t, nt=nt: [e.dma_start(out=xk[:, :, 0:nt],
                                                                       in_=xkv_d[:, :, c0t:c0t + nt].rearrange("k p t -> p k t"))],
                  1, xk.t, reads=xkv_tok, writes=[xk.t])
            for sub in range((nt + 127) // 128):
                rows = min(128, nt - sub * 128)
                c0 = c0t + sub * 128
                lo = sub * 128
                ko = kvo[i % 2]
                vs = vst[i % 2]
                for nb in range(3):
                    pt = psum[nb + 3 * (i % 2)]
                    for kc in range(KC):
                        mm(pt[0:rows, :], xk[:, kc, lo:lo + rows], wkv[:, kc, nb * 512:(nb + 1) * 512], kc == 0,
                           kc == KC - 1, [xk.t, wkv.t], [pt.t])
                    cp("act" if nb % 2 else "dve", ko[0:rows, nb * 512:(nb + 1) * 512], pt[0:rows, :], [pt.t], [ko.t])
                cp("pool", vs[0:rows, 0:256], ko[0:rows, 768:1024], [ko.t], [vs.t])
                cp("pool", vs[0:rows, 256:512], ko[0:rows, 1280:1536], [ko.t], [vs.t])
                P.dma("pool", lambda e, vs=vs, i=i, rows=rows: [e.dma_start(out=vtm_d[i, 0:rows, :], in_=vs[0:rows, :])],
                      1, vs.t, reads=[vs.t], writes=[vtm_tok])
                if i < 16:
                    dma_out(o_kvp[c0:c0 + 128, :], ko[:, 0:1024], ko.t)
                    if i >= 12:
                        dma_out(o_winp[(i - 12) * 128:(i - 11) * 128, :], ko[:, 1024:1536], ko.t)
                else:
                    dma_out(o_kvs[:, :], ko[0:rows, 0:1024], ko.t)
                    dma_out(o_wins[:, :], ko[0:rows, 1024:1536], ko.t)
                i += 1
            ks = kst[ti % 2]
            for fi, cb in enumerate(FB):
                pt = psum[6 + fi % 2]
                for kc in range(KC):
                    mm(pt[:, 0:nt], wkv[:, kc, cb * 128:(cb + 1) * 128], xk[:, kc, 0:nt], kc == 0, kc == KC - 1,
                       [xk.t, wkv.t], [pt.t])
                cp("act", ks[:, fi, 0:nt], pt[:, 0:nt], [pt.t], [ks.t])
            P.dma("pool", lambda e, ks=ks, c0t=c0t, nt=nt: [e.dma_start(out=kvT_d[:, :, c0t:c0t + nt].rearrange("f p t -> p f t"),
                                                                        in_=ks[:, :, 0:nt])],
                  1, ks.t, reads=[ks.t], writes=[kvT_tok])
        A.release(mK)

    if 'K' in PH:
        phase_K()

    SLOPE = [2.0 ** (-8.0 * (i + 1) / H) for i in range(H)]
    NEGM = -30000.0

    def phase_N():
        dist0 = A.tile([128, 512], F32, "dist0")
        distc = A.tile([128, 512], F32, "distc")
        itmp = A.tile([128, 512], I32, "itmp")
        P.op("pool", lambda e: e.iota(itmp[:], pattern=[[1, 512]], base=0, channel_multiplier=-1), (), [itmp.t])
        cp("pool", dist0[:], itmp[:], [itmp.t], [dist0.t])
        P.op("pool", lambda e: e.iota(itmp[:], pattern=[[1, 512]], base=-31, channel_multiplier=-16), [dist0.t], [itmp.t])
        cp("pool", distc[:], itmp[:], [itmp.t], [distc.t])
        maug = A.tile([128, 33], BF16, "maug")
        memset("pool", maug[:], 1.0, [maug.t])
        P.op("pool", lambda e: e.affine_select(out=maug[:], in_=maug[:], pattern=[[-4, 33]], compare_op=ALU.is_ge,
                                               fill=0.0, base=1, channel_multiplier=1), [maug.t], [maug.t])
        P.op("pool", lambda e: e.affine_select(out=maug[:], in_=maug[:], pattern=[[4, 33]], compare_op=ALU.is_ge,
                                               fill=0.0, base=3, channel_multiplier=-1), [maug.t], [maug.t])
        memset("pool", maug[:, 32:33], 1.0, [maug.t])
        eall = A.tile([32, 16, 128], BF16, "eall")
        memset("pool", eall[:], 1.0, [eall.t])
        P.op("pool", lambda e: e.affine_select(out=eall[:], in_=eall[:], pattern=[[128, 16], [1, 128]],
                                               compare_op=ALU.is_ge, fill=0.0, base=0, channel_multiplier=-64),
             [eall.t], [eall.t])
        P.op("pool", lambda e: e.affine_select(out=eall[:], in_=eall[:], pattern=[[-128, 16], [-1, 128]],
                                               compare_op=ALU.is_ge, fill=0.0, base=63, channel_multiplier=64),
             [eall.t], [eall.t])
        memset("pool", sel36[:], 1.0, [sel36.t])
        P.op("pool", lambda e: e.affine_select(out=sel36[:], in_=sel36[:], pattern=[[-1, 36], [0, 128]],
                                               compare_op=ALU.is_equal, fill=0.0, base=0, channel_multiplier=1),
             [sel36.t], [sel36.t])
        validm = A.tile([128, 16, 32], F32, "validm")
        forced = A.tile([128, 16, 32], F32, "forced")
        VB = A.tile([128, 16, 32], F32, "VB")
        memset("pool", validm[:], 1.0, [validm.t])
        P.op("pool", lambda e: e.affine_select(out=validm[:], in_=validm[:], pattern=[[128, 16], [-64, 32]],
                                               compare_op=ALU.is_ge, fill=0.0, base=0, channel_multiplier=1),
             [validm.t], [validm.t])
        P.op("pool", lambda e: e.affine_select(out=forced[:], in_=validm[:], pattern=[[-128, 16], [64, 32]],
                                               compare_op=ALU.is_gt, fill=0.0, base=128, channel_multiplier=-1),
             [validm.t], [forced.t])
        memset("pool", forced[:, :, 0:1], 1.0, [forced.t])
        ts(VB[:], validm[:], -1.0, 1e30, ALU.add, ALU.mult, [validm.t], [VB.t], eng="pool")
        stt(VB[:], forced[:], 1e4, VB[:], ALU.mult, ALU.add, [forced.t, VB.t], [VB.t])
        tiny = 1e-30

        g36 = A.tile([36, NTOK], BF16, "g36")
        kcT = A.tile([128, 2, 128], BF16, "kcT")
        vc_tm = A.tile([128, 2, 128], BF16, "vc_tm")
        qh = A.tile([128, NTOK], BF16, "qh")
        impacc = A.tile([128, 2, 16, 32], F32, "impacc")
        bt = [None, None]
        mk = [A.tile([128, 512], F32, "mk0")] * 2
        BIG = 1.0e6
        dmc = [A.tile([128, 512], F32, "dmc%d" % i) for i in range(4)]
        for qt_ in range(4):
            ts(mk[0][:], distc[:], float(-512 * qt_), BIG, ALU.is_lt, ALU.mult, [distc.t], [mk[0].t], eng="pool")
            stt(dmc[qt_][:], distc[:], float(512 * qt_), mk[0][:], ALU.add, ALU.add, [distc.t, mk[0].t], [dmc[qt_].t])
        kk16 = A.tile([128, 16], F32, "kk16")
        P.op("pool", lambda e: e.iota(itmp[:, 0:16], pattern=[[128, 16]], base=0, channel_multiplier=0), [distc.t],
             [itmp.t])
        cp("pool", kk16[:], itmp[:, 0:16], [itmp.t], [kk16.t])
        hb = A.tile([128, 16], F32, "hb")
        sc = [A.tile([128, 512], F32, "sc%d" % i) for i in range(2)]
        pc = [A.tile([128, 512], BF16, "pc%d" % i) for i in range(2)]
        rzt = A.tile([128, 512], F32, "rzt")
        o32 = A.tile([128, 512], F32, "o32n")
        ps33 = A.tile([128, 4, 33], F32, "ps33")
        zr = A.tile([128, 4], F32, "zr")
        mN1 = A.mark()
        xnT = A.tile([128, KC, NTOK], BF16, "xnT1")
        P.dma("sp", lambda e: [e.dma_start(out=xnT[:], in_=xnA_d.rearrange("k p t -> p k t"))], 1, xnT.t,
              reads=xnA_tok, writes=[xnT.t])
        wq = A.tile([128, KC, 128], BF16, "wq")

        def projn(wt, ncols, ti, pbank):
            c0, n = CT[ti]
            pt = psum[pbank]
            for kc in range(KC):
                mm(pt[0:ncols, 0:n], wt[:, kc, 0:ncols], xnT[:, kc, c0:c0 + n], kc == 0, kc == KC - 1,
                   [wt.t, xnT.t], [pt.t])
            return pt

        load_cast(wq[:, :, 0:36], wq.t, w_in_b[0, :, W:W + 36].rearrange("(k p) c -> p k c", p=128), [128, KC, 36])
        for ti in range(5):
            c0, n = CT[ti]
            pt = projn(wq, 36, ti, ti % 2)
            act(g36[:, c0:c0 + n], pt[0:36, 0:n], AF.Sigmoid, [pt.t], [g36.t])
        cp("pool", g36s[:], g36[:, T:NTOK], [g36.t], [g36s.t])

        mcp = A.mark()
        kvc = A.tile([128, 2, T], BF16, "kvc")
        w1b = A.tile([128, 32, 128], BF16, "w1b")
        w2b = A.tile([128, 2, 128], BF16, "w2b")
        pe32 = A.tile([128, 2, 32], F32, "pe32")
        peb = A.tile([128, 2, 32], BF16, "peb")
        preb = A.tile([128, 2], F32, "preb")
        tg = A.tile([128, 128], F32, "tg")
        ug = A.tile([128, 128], F32, "ug")
        gl = A.tile([128, 128], BF16, "gl")
        dma_in(pe32[:], pe32.t, cmp_pos.rearrange("k j d -> d k j"), nonc=True)
        cp("pool", peb[:], pe32[:], [pe32.t], [peb.t])
        for kv in range(2):
            load_cast(w2b[:, kv, :], w2b.t, w_cmp2[kv], [128, 128])
        for kv in range(2):
            for hf in range(2):
                load_cast(w1b[:, hf * 16:(hf + 1) * 16, :], w1b.t,
                          w_cmp1[kv, hf * 2048:(hf + 1) * 2048, :].rearrange("(j d) h -> d j h", d=128), [128, 16, 128])
            pp = psum[2]
            for js in range(32):
                mm(pp[:, 0:1], w1b[:, js, :], peb[:, kv, js:js + 1], js == 0, js == 31, [w1b.t, peb.t], [pp.t])
            cp("dve", preb[:, kv:kv + 1], pp[:, 0:1], [pp.t], [preb.t])
            P.dma("sp", lambda e, kv=kv: [e.dma_start(out=kvc[:], in_=kvT_d[kv * 2:kv * 2 + 2, :, 0:T].rearrange("f p t -> p f t"))],
                  1, kvc.t, reads=[kvT_tok], writes=[kvc.t])
            for g in range(2):
                kview = kvc[:, g, :].rearrange("p (c s) -> p c s", s=16)
                pp = psum[3]
                for js in range(32):
                    j_, s_ = js // 16, js % 16
                    mm(pp[:, 0:127], w1b[:, js, :], kview[:, j_:j_ + 127, s_], js == 0, js == 31, [w1b.t, kvc.t], [pp.t])
                act(tg[:, 0:127], pp[:, 0:127], AF.Identity, [pp.t, preb.t], [tg.t], bias=preb[:, kv:kv + 1])
                act(ug[:, 0:127], tg[:, 0:127], AF.Square, [tg.t], [ug.t])
                ts(ug[:, 0:127], ug[:, 0:127], 0.044715, 1.0, ALU.mult, ALU.add, [ug.t], [ug.t])
                tt(ug[:, 0:127], ug[:, 0:127], tg[:, 0:127], ALU.mult, [ug.t, tg.t], [ug.t])
                act(ug[:, 0:127], ug[:, 0:127], AF.Sigmoid, [ug.t], [ug.t], scale=GC)
                tt(gl[:, 0:127], ug[:, 0:127], tg[:, 0:127], ALU.mult, [ug.t, tg.t], [gl.t])
                pq_ = psum[4]
                if kv == 0:
                    mm(pq_[:, 0:127], w2b[:, 0, :], gl[:, 0:127], True, True, [w2b.t, gl.t], [pq_.t])
                    cp("act", kcT[:, g, 0:127], pq_[:, 0:127], [pq_.t], [kcT.t])
                else:
                    mm(pq_[0:127, 0:128], gl[:, 0:127], w2b[:, 1, :], True, True, [w2b.t, gl.t], [pq_.t])
                    cp("act", vc_tm[0:127, g, :], pq_[0:127, 0:128], [pq_.t], [vc_tm.t])
        A.release(mcp)

        memset("pool", impacc[:], 0.0, [impacc.t])
        it = 0
        for hh in range(H):
            g, r = hh // 6, hh % 6
            sl = SLOPE[hh]
            load_cast(wq[:], wq.t, w_in_b[0, :, hh * 128:(hh + 1) * 128].rearrange("(k p) c -> p k c", p=128),
                      [128, KC, 128])
            for ti in range(5):
                c0, n = CT[ti]
                pt = projn(wq, 128, ti, ti % 2)
                act(qh[:, c0:c0 + n], pt[:, 0:n], AF.Copy, [pt.t], [qh.t], scale=SCALE)
            dma_out(qT_d[hh], qh[:], qh.t, writes=[qT_tok[hh]])
            for qt in range(4):
                t0 = qt * 512
                b_, m_, s_, p_ = bt[it % 2], mk[it % 2], sc[it % 2], pc[it % 2]
                it += 1
                psc = psum[2 + it % 2]
                mm(psc[0:127, :], kcT[:, g, 0:127], qh[:, t0:t0 + 512], True, True, [kcT.t, qh.t], [psc.t])
                stt(s_[0:127, :], dmc[qt][0:127, :], -sl, psc[0:127, :], ALU.mult, ALU.add, [dmc[qt].t, psc.t], [s_.t])
                act(p_[0:127, :], s_[0:127, :], AF.Exp, [s_.t], [p_.t])
                po, pz, pp = psum[4], psum[5], psum[6]
                mm(po[:, :], vc_tm[0:127, g, :], p_[0:127, :], True, True, [vc_tm.t, p_.t], [po.t])
                mm(pz[:, :], ones_b[0:127, :], p_[0:127, :], True, True, [ones_b.t, p_.t], [pz.t])
                for sub in range(4):
                    mm(pp[:, sub * 33:(sub + 1) * 33], p_[0:127, sub * 128:(sub + 1) * 128], maug[0:127, :], True, True,
                       [p_.t, maug.t], [pp.t])
                ts(rzt[:], pz[:], tiny, None, ALU.max, None, [pz.t], [rzt.t])
                P.op("dve", lambda e: e.reciprocal(rzt[:], rzt[:]), [rzt.t], [rzt.t])
                tt(o32[:], po[:], rzt[:], ALU.mult, [po.t, rzt.t], [o32.t])
                P.dma("pool", lambda e, hh=hh, t0=t0: [e.dma_start(out=ocmp_d[hh, :, t0:t0 + 512], in_=o32[:])], 1, o32.t,
                      reads=[o32.t], writes=[ocmp_tok[hh]])
                cp("dve", ps33[:], pp[:, 0:132].rearrange("p (a b) -> p a b", b=33), [pp.t], [ps33.t])
                ts(zr[:], ps33[:, :, 32], tiny, None, ALU.max, None, [ps33.t], [zr.t])
                P.op("dve", lambda e: e.reciprocal(zr[:], zr[:]), [zr.t], [zr.t])
                for sub in range(4):
                    stt(impacc[:, g, qt * 4 + sub, :], ps33[:, sub, 0:32], zr[:, sub:sub + 1],
                        impacc[:, g, qt * 4 + sub, :], ALU.mult, ALU.add, [ps33.t, zr.t, impacc.t], [impacc.t])

        nonlocal wb4, qT, mixo, pT2, rz, xn_tok
        wb4 = [A.tile([128, KC, 128], BF16, "wbn")]
        qT = A.tile([128, 512], BF16, "qTn")
        mixo = A.tile([128, NTOK], BF16, "mixo2")
        pT2 = [A.tile([128, 512], BF16, "pTn%d" % i) for i in range(2)]
        rz = A.tile([128, 512], F32, "rzn")
        xn_tok = [xnT.t] * 5
        mem_attn_l1(xnT)
        A.release(mN1)

        kvs = A.tile([128, 4, T], BF16, "kvs")
        P.dma("sp", lambda e: [e.dma_start(out=kvs[:], in_=kvT_d[4:8, :, 0:T].rearrange("f p t -> p f t"))], 1, kvs.t,
              reads=[kvT_tok], writes=[kvs.t])
        vtm = A.tile([128, 16, 512], BF16, "vtm")
        P.dma("sp", lambda e: [e.dma_start(out=vtm[:], in_=vtm_d[0:16].rearrange("i p c -> p i c"))], 1, vtm.t,
              reads=[vtm_tok], writes=[vtm.t])
        negselT = A.tile([32, 2, T], BF16, "negselT")
        dmb = {}
        for d_ in (-384, -256, -128, 0):
            tl_ = A.tile([128, 512], F32, "dmb%d" % (-d_))
            ts(mk[0][:], dist0[:], float(-d_), BIG, ALU.is_lt, ALU.mult, [dist0.t], [mk[0].t], eng="pool")
            stt(tl_[:], dist0[:], float(d_), mk[0][:], ALU.add, ALU.add, [dist0.t, mk[0].t], [tl_.t])
            dmb[d_] = tl_
        dmw = {}
        for d_ in (128, 256, 384, 512):
            tl_ = A.tile([128, 512], F32, "dmw%d" % d_)
            ts(mk[0][:], dist0[:], float(512 - d_), BIG, ALU.is_ge, ALU.mult, [dist0.t], [mk[0].t], eng="pool")
            stt(tl_[:], dist0[:], float(d_), mk[0][:], ALU.add, ALU.add, [dist0.t, mk[0].t], [tl_.t])
            dmw[d_] = tl_
        scr = A.tile([128, 32], F32, "scr")
        scr2 = A.tile([128, 32], F32, "scr2")
        m8a = A.tile([128, 8], F32, "m8a")
        m8b = A.tile([128, 8], F32, "m8b")
        selm = A.tile([128, 32], F32, "selm")
        for g in range(2):
            for t16 in range(16):
                tt(scr[:], impacc[:, g, t16, :], validm[:, t16, :], ALU.mult, [impacc.t, validm.t], [scr.t])
                tt(scr[:], scr[:], VB[:, t16, :], ALU.add, [scr.t, VB.t], [scr.t])
                P.op("dve", lambda e: e.max(out=m8a[:], in_=scr[:]), [scr.t], [m8a.t])
                P.op("dve", lambda e: e.match_replace(out=scr2[:], in_to_replace=m8a[:], in_values=scr[:],
                                                      imm_value=-1e30), [m8a.t, scr.t], [scr2.t])
                P.op("dve", lambda e: e.max(out=m8b[:], in_=scr2[:]), [scr2.t], [m8b.t])
                ts(selm[:], scr[:], m8b[:, 7:8], None, ALU.is_ge, None, [scr.t, m8b.t], [selm.t])
                tt(selm[:], selm[:], validm[:, t16, :], ALU.mult, [selm.t, validm.t], [selm.t])
                ts(selm[:], selm[:], -1.0, -NEGM, ALU.add, ALU.mult, [selm.t], [selm.t])
                ptn = psum[t16 % 2]
                tr(ptn[0:32, 0:128], selm[:], ident_f[:], [selm.t, ident_f.t], [ptn.t])
                cp("act", negselT[:, g, t16 * 128:(t16 + 1) * 128], ptn[0:32, 0:128], [ptn.t], [negselT.t])

        mix32 = A.tile([128, 512], F32, "mix32")
        oc32 = A.tile([128, 512], F32, "oc32")
        gsb = A.tile([128, 512], F32, "gsb")
        mixon = A.tile([128, NTOK], BF16, "mixon")
        for hh in range(H):
            g, r = hh // 6, hh % 6
            sl = SLOPE[hh]
            P.dma("sp", lambda e, hh=hh: [e.dma_start(out=qh[:], in_=qT_d[hh])], 1, qh.t, reads=[qT_tok[hh]],
                  writes=[qh.t])
            ts(hb[:], kk16[:], -sl, None, ALU.mult, None, [kk16.t], [hb.t], eng="pool")
            for qt in range(4):
                t0 = qt * 512
                P.dma("sp", lambda e, hh=hh, t0=t0: [e.dma_start(out=oc32[:], in_=ocmp_d[hh, :, t0:t0 + 512])], 1, oc32.t,
                      reads=[ocmp_tok[hh]], writes=[oc32.t])
                pg = psum[7]
                mm(pg[:, :], sel36[0:36, hh * 3 + 0, :], g36[0:36, t0:t0 + 512], True, True, [sel36.t, g36.t], [pg.t])
                tt(mix32[:], oc32[:], pg[:], ALU.mult, [oc32.t, pg.t], [mix32.t])
                for br in (1, 2):
                    kb_lo = 0 if br == 1 else max(0, 4 * qt - 4)
                    kb_hi = 4 * qt + 3
                    po, pz = (psum[4], psum[5]) if br == 1 else (psum[0], psum[6])
                    for kb in range(kb_lo, kb_hi + 1):
                        delta = t0 - kb * 128
                        b_, m_, s_, p_ = bt[it % 2], mk[it % 2], sc[it % 2], pc[it % 2]
                        it += 1
                        if delta <= 0:
                            dm_, eb_ = dmb[delta], None
                        elif br == 2:
                            dm_, eb_ = dmw[delta], None
                        else:
                            dm_, eb_ = dist0, hb[:, delta // 128:delta // 128 + 1]
                        psc = psum[2 + it % 2]
                        kblk = (0 if br == 1 else 2) + g
                        mm(psc[:, :], kvs[:, kblk, kb * 128:(kb + 1) * 128], qh[:, t0:t0 + 512], True, br == 2,
                           [kvs.t, qh.t], [psc.t])
                        if br == 1:
                            mm(psc[:, :], eall[:, kb, :], negselT[:, g, t0:t0 + 512], False, True,
                               [eall.t, negselT.t], [psc.t])
                        stt(s_[:], dm_[:], -sl, psc[:], ALU.mult, ALU.add, [dm_.t, psc.t], [s_.t])
                        if eb_ is None:
                            act(p_[:], s_[:], AF.Exp, [s_.t], [p_.t])
                        else:
                            act(p_[:], s_[:], AF.Exp, [s_.t, hb.t], [p_.t], bias=eb_)
                        vblk = (0 if br == 1 else 2) + g
                        mm(po[:, :], vtm[:, kb, vblk * 128:(vblk + 1) * 128], p_[:], kb == kb_lo, kb == kb_hi,
                           [vtm.t, p_.t], [po.t])
                        mm(pz[:, :], ones_b[:], p_[:], kb == kb_lo, kb == kb_hi, [ones_b.t, p_.t], [pz.t])
                    ts(rzt[:], pz[:], tiny, None, ALU.max, None, [pz.t], [rzt.t])
                    P.op("dve", lambda e: e.reciprocal(rzt[:], rzt[:]), [rzt.t], [rzt.t])
                    tt(o32[:], po[:], rzt[:], ALU.mult, [po.t, rzt.t], [o32.t])
                    pg = psum[7]
                    mm(pg[:, :], sel36[0:36, hh * 3 + br, :], g36[0:36, t0:t0 + 512], True, True, [sel36.t, g36.t], [pg.t])
                    tt(gsb[:], o32[:], pg[:], ALU.mult, [o32.t, pg.t], [gsb.t])
                    if br == 1:
                        tt(mix32[:], mix32[:], gsb[:], ALU.add, [mix32.t, gsb.t], [mix32.t])
                    else:
                        tt(mixon[:, t0:t0 + 512], mix32[:], gsb[:], ALU.add, [mix32.t, gsb.t], [mixon.t])
            P.dma("pool", lambda e, hh=hh: [e.dma_start(out=mixT_d[hh, :, 0:T], in_=mixon[:, 0:T])], 1, mixon.t,
                  reads=[mixon.t], writes=[mix_tok[hh]])

    def phase_S():
        tiny = 1e-30
        ckv_rows = ckv.rearrange("n p (h r) g d -> (n p h) (r g d)", h=2)
        w1b2 = A.tile([128, 2, 32, 128], BF16, "w1b2")
        w2b = A.tile([128, 2, 128], BF16, "w2bs")
        pe32 = A.tile([128, 2, 32], F32, "pe32s")
        peb = A.tile([128, 2, 32], BF16, "pebs")
        preb = A.tile([128, 2], F32, "prebs")
        dma_in(pe32[:], pe32.t, cmp_pos.rearrange("k j d -> d k j"), nonc=True)
        cp("pool", peb[:], pe32[:], [pe32.t], [peb.t])
        for kv in range(2):
            load_cast(w2b[:, kv, :], w2b.t, w_cmp2[kv], [128, 128])
            for hf in range(2):
                load_cast(w1b2[:, kv, hf * 16:(hf + 1) * 16, :], w1b2.t,
                          w_cmp1[kv, hf * 2048:(hf + 1) * 2048, :].rearrange("(j d) h -> d j h", d=128), [128, 16, 128])
            pp = psum[2]
            for js in range(32):
                mm(pp[:, 0:1], w1b2[:, kv, js, :], peb[:, kv, js:js + 1], js == 0, js == 31, [w1b2.t, peb.t], [pp.t])
            cp("dve", preb[:, kv:kv + 1], pp[:, 0:1], [pp.t], [preb.t])
        qs = A.tile([128, H, NS * TS], BF16, "qs")
        P.dma("sp", lambda e: [e.dma_start(out=qs[:], in_=qT_d[:, :, T:NTOK].rearrange("h p t -> p h t"))], 1, qs.t,
              reads=qT_tok, writes=[qs.t])
        gbs = A.tile([128, 3, H, NS * TS], F32, "gbs")
        for br in range(3):
            pg = psum[br]
            for hh in range(H):
                mm(pg[:, hh * 16:(hh + 1) * 16], sel36[0:36, hh * 3 + br, :], g36s[0:36, :], True, True,
                   [sel36.t, g36s.t], [pg.t])
            cp("act", gbs[:, br, :, :], pg[:, 0:H * 16].rearrange("p (h t) -> p h t", t=16), [pg.t], [gbs.t])

        def qsg(b_, g):
            return qs[:, g * 6:(g + 1) * 6, b_ * TS:(b_ + 1) * TS].rearrange("p r t -> p t r")

        ptb = A.tile([128, NS * 64], I32, "ptb")
        dma_in(ptb[:], ptb.t, ptab.rearrange("b n -> (b n)").partition_broadcast(128))
        piota = A.tile([128, NS * 64], I32, "piota")
        P.op("pool", lambda e: e.iota(piota[:], pattern=[[0, NS * 64]], base=0, channel_multiplier=1), (), [piota.t])
        idx_all = A.tile([128, NS * 64], I32, "idx_all")
        stt(idx_all[:], ptb[:], 128.0, piota[:], ALU.mult, ALU.add, [ptb.t, piota.t], [idx_all.t])
        idx_h = [A.tile([128, NS * 64], I32, "idx_h%d" % i) for i in range(2)]
        ts(idx_h[0][:], idx_all[:], 2.0, None, ALU.mult, None, [idx_all.t], [idx_h[0].t])
        ts(idx_h[1][:], idx_all[:], 2.0, 1.0, ALU.mult, ALU.add, [idx_all.t], [idx_h[1].t])
        it32 = A.tile([128, 16], I32, "it32")
        dcf = A.tile([128, 16], F32, "dcf")
        bcs = A.tile([128, 2, 4, 4, 6], F32, "bcs")
        P.op("pool", lambda e: e.iota(it32[:], pattern=[[2048, 4], [-1, 4]], base=31 - 8192, channel_multiplier=16),
             (), [it32.t])
        cp("pool", dcf[:], it32[:], [it32.t], [dcf.t])
        for hh in range(H):
            g, r = hh // 6, hh % 6
            ts(bcs[:, g, :, :, r], dcf[:].rearrange("p (i t) -> p i t", t=4), SLOPE[hh], None, ALU.mult, None,
               [dcf.t], [bcs.t], eng="pool")
        for g in range(2):
            P.op("pool", lambda e, g=g: e.affine_select(out=bcs[:, g, 3, :, :], in_=bcs[:, g, 3, :, :],
                                                        pattern=[[0, 4], [0, 6]], compare_op=ALU.is_ge, fill=NEGM,
                                                        base=126, channel_multiplier=-1), [bcs.t], [bcs.t])
        dsf = A.tile([128, 4], F32, "dsf")
        bS0 = A.tile([128, 2, 4, 6], F32, "bS0")
        slS = A.tile([128, 2, 4, 6], F32, "slS")
        P.op("pool", lambda e: e.iota(it32[:, 0:4], pattern=[[-1, 4]], base=-8192, channel_multiplier=1), [dcf.t], [it32.t])
        cp("pool", dsf[:], it32[:, 0:4], [it32.t], [dsf.t])
        for hh in range(H):
            g, r = hh // 6, hh % 6
            ts(bS0[:, g, :, r], dsf[:], SLOPE[hh], None, ALU.mult, None, [dsf.t], [bS0.t], eng="pool")
            memset("pool", slS[:, g, :, r], SLOPE[hh] * 128.0, [slS.t])
        bSall = A.tile([128, 64, 48], F32, "bSall")
        for pg_ in range(64):
            stt(bSall[:, pg_, :], slS[:].rearrange("p g t r -> p (g t r)"), float(pg_),
                bS0[:].rearrange("p g t r -> p (g t r)"), ALU.mult, ALU.add, [slS.t, bS0.t], [bSall.t])
        bN = A.tile([4, 2, 4, 6], F32, "bN")
        dnf = A.tile([4, 4], F32, "dnf")
        P.op("pool", lambda e: e.iota(it32[0:4, 0:4], pattern=[[-1, 4]], base=0, channel_multiplier=1), [dsf.t], [it32.t])
        cp("pool", dnf[:], it32[0:4, 0:4], [it32.t], [dnf.t])
        for hh in range(H):
            g, r = hh // 6, hh % 6
            ts(bN[:, g, :, r], dnf[:], SLOPE[hh], None, ALU.mult, None, [dnf.t], [bN.t], eng="pool")
        P.op("pool", lambda e: e.affine_select(out=bN[:], in_=bN[:], pattern=[[0, 2], [1, 4], [0, 6]],
                                               compare_op=ALU.is_ge, fill=NEGM, base=0, channel_multiplier=-1),
             [bN.t], [bN.t])
        bW = A.tile([128, 4, 2, 4, 6], F32, "bW")
        dwf = A.tile([128, 16], F32, "dwf")
        P.op("pool", lambda e: e.iota(it32[:], pattern=[[128, 4], [-1, 4]], base=-512, channel_multiplier=1), [dnf.t],
             [it32.t])
        cp("pool", dwf[:], it32[:], [it32.t], [dwf.t])
        for hh in range(H):
            g, r = hh // 6, hh % 6
            ts(bW[:, :, g, :, r], dwf[:].rearrange("p (w t) -> p w t", t=4), SLOPE[hh], None, ALU.mult, None,
               [dwf.t], [bW.t], eng="pool")
        for g in range(2):
            P.op("pool", lambda e, g=g: e.affine_select(out=bW[:, :, g, :, :], in_=bW[:, :, g, :, :],
                                                        pattern=[[128, 4], [-1, 4], [0, 6]], compare_op=ALU.is_ge,
                                                        fill=NEGM, base=-1, channel_multiplier=1), [bW.t], [bW.t])
        rselT = A.tile([4, 2, 4, 6], BF16, "rselT")
        memset("pool", rselT[:], 1.0, [rselT.t])
        P.op("pool", lambda e: e.affine_select(out=rselT[:], in_=rselT[:], pattern=[[0, 2], [1, 4], [0, 6]],
                                               compare_op=ALU.is_equal, fill=0.0, base=0, channel_multiplier=-1),
             [rselT.t], [rselT.t])
        rself = A.tile([24, 4], F32, "rself")
        memset("pool", rself[:], 1.0, [rself.t])
        P.op("pool", lambda e: e.affine_select(out=rself[:], in_=rself[:], pattern=[[-6, 4]], compare_op=ALU.is_ge,
                                               fill=0.0, base=0, channel_multiplier=1), [rself.t], [rself.t])
        P.op("pool", lambda e: e.affine_select(out=rself[:], in_=rself[:], pattern=[[6, 4]], compare_op=ALU.is_ge,
                                               fill=0.0, base=5, channel_multiplier=-1), [rself.t], [rself.t])
        maug_s = A.tile([128, 4, 130], BF16, "maug_s")
        memset("pool", maug_s[:], 1.0, [maug_s.t])
        P.op("pool", lambda e: e.affine_select(out=maug_s[:], in_=maug_s[:], pattern=[[128, 4], [-4, 130]],
                                               compare_op=ALU.is_ge, fill=0.0, base=1, channel_multiplier=1),
             [maug_s.t], [maug_s.t])
        P.op("pool", lambda e: e.affine_select(out=maug_s[:], in_=maug_s[:], pattern=[[-128, 4], [4, 130]],
                                               compare_op=ALU.is_ge, fill=0.0, base=3, channel_multiplier=-1),
             [maug_s.t], [maug_s.t])
        memset("pool", maug_s[:, :, 129:130], 1.0, [maug_s.t])
        VBs = A.tile([4, 129], F32, "VBs")
        memset("pool", VBs[:], 0.0, [VBs.t])
        memset("pool", VBs[:, 0:1], 1e4, [VBs.t])
        memset("pool", VBs[:, 127:129], 1e4, [VBs.t])

        kcmpT = A.tile([128, 4, 8192], BF16, "kcmpT")
        nse_h = nc.alloc_sbuf_tensor_at("nse_alias", [4, 2, 129 * 64], BF16, offset=int(kcmpT.h.manual_sbuf_range[0]))
        nse = Tile(nse_h, kcmpT.t)
        gt2 = [A.tile([128, 512], F32, "gt%d" % i) for i in range(2)]
        gb2 = [A.tile([128, 512], BF16, "gb%d" % i) for i in range(2)]
        ksT2 = [A.tile([128, 2, 128], BF16, "ksT%d" % i) for i in range(2)]
        kcT_s = A.tile([128, 2, 512], BF16, "kcT_s")
        vc_s = A.tile([128, 4, 2, 128], BF16, "vc_s")
        memset("pool", kcT_s[:], 0.0, [kcT_s.t])
        memset("pool", vc_s[:], 0.0, [vc_s.t])
        tg = A.tile([128, 512], F32, "tgs")
        ug = A.tile([128, 512], F32, "ugs")
        gl = A.tile([128, 512], BF16, "gls")
        s192 = A.tile([128, 192], F32, "s192")
        p192 = A.tile([128, 192], BF16, "p192")
        s48 = [A.tile([128, 48], F32, "s48_%d" % i) for i in range(2)]
        p48 = [A.tile([128, 48], BF16, "p48_%d" % i) for i in range(2)]
        rz48 = A.tile([128, 48], F32, "rz48")
        o48 = A.tile([128, 2, 4, 6], F32, "o48")
        acc48 = A.tile([128, 2, 4, 6], F32, "acc48")
        ps_s = A.tile([24, 2, 130], F32, "ps_s")
        zr_s = A.tile([24, 2], F32, "zr_s")
        pn_s = A.tile([24, 2, 129], F32, "pn_s")
        scs = A.tile([4, 2, 129], F32, "scs")
        scs2 = A.tile([4, 129], F32, "scs2")
        m8a = A.tile([4, 8], F32, "m8as")
        m8b = A.tile([4, 8], F32, "m8bs")
        sels = A.tile([4, 129], F32, "sels")
        knew = A.tile([128, 4, TS], BF16, "knew")
        vnew = A.tile([4, 512], BF16, "vnew")
        mixs = A.tile([128, H, NS * TS], BF16, "mixs")
        gi = 0

        def gather(b_, pg_, c0):
            nonlocal gi
            gt, gb = gt2[gi % 2], gb2[gi % 2]
            gi += 1
            col = b_ * 64 + pg_
            ih = idx_h[c0 // 512]
            P.dma("pool", lambda e: [e.indirect_dma_start(
                out=gt[:, :], out_offset=None, in_=ckv_rows[:, :],
                in_offset=bass.IndirectOffsetOnAxis(ap=ih[:, col:col + 1], axis=0))], 1, gt.t,
                reads=[ih.t], writes=[gt.t])
            cp("dve" if gi % 2 else "act", gb[:], gt[:], [gt.t], [gb.t])
            return gb

        def finish_branch(po, pz, br, b_, first):
            ts(rz48[:], pz[:, 0:48], tiny, None, ALU.max, None, [pz.t], [rz48.t])
            P.op("dve", lambda e: e.reciprocal(rz48[:], rz48[:]), [rz48.t], [rz48.t])
            o48f = o48[:].rearrange("p g t r -> p (g t r)")
            tt(o48f, po[:, 0:48], rz48[:], ALU.mult, [po.t, rz48.t], [o48.t])
            gview = gbs[:, br, :, b_ * TS:(b_ + 1) * TS].rearrange("p (g r) t -> p g t r", g=2)
            if first:
                tt(acc48[:], o48[:], gview, ALU.mult, [o48.t, gbs.t], [acc48.t])
            else:
                tt(o48[:], o48[:], gview, ALU.mult, [o48.t, gbs.t], [o48.t])
                tt(acc48[:], acc48[:], o48[:], ALU.add, [acc48.t, o48.t], [acc48.t])

        def new_tile(b_, kf0, vcol0, po, pz):
            P.dma("sp", lambda e: [e.dma_start(out=knew[:, 0:2, :],
                                               in_=kvT_d[kf0:kf0 + 2, :, T + b_ * TS:T + (b_ + 1) * TS].rearrange("f p t -> p f t"))],
                  1, knew.t, reads=[kvT_tok], writes=[knew.t])
            P.dma("sp", lambda e: [e.dma_start(out=vnew[:, :], in_=vtm_d[16, b_ * TS:(b_ + 1) * TS, :])], 1, vnew.t,
                  reads=[vtm_tok], writes=[vnew.t])
            psn = psum[3]
            for g in range(2):
                mm(psn[0:4, g * 24:(g + 1) * 24], knew[:, g, :], qsg(b_, g), True, True, [knew.t, qs.t], [psn.t])
            s_, p_ = s48[0], p48[0]
            tt(s_[0:4, :], psn[0:4, 0:48], bN[:].rearrange("p g t r -> p (g t r)"), ALU.add, [psn.t, bN.t], [s_.t])
            act(p_[0:4, :], s_[0:4, :], AF.Exp, [s_.t], [p_.t])
            for g in range(2):
                mm(po[:, g * 24:(g + 1) * 24], vnew[0:4, vcol0 + g * 128:vcol0 + (g + 1) * 128], p_[0:4, g * 24:(g + 1) * 24],
                   False, True, [vnew.t, p_.t], [po.t])
            mm(pz[:, 0:48], ones_b[0:4, :], p_[0:4, 0:48], False, True, [ones_b.t, p_.t], [pz.t])

        for b_ in range(NS):
            for pg_ in range(64):
                gb = gather(b_, pg_, 0)
                pt = psum[pg_ % 2]
                pv = psb(pg_ % 2)
                for k in range(4):
                    tr(pv[:, k * 128:(k + 1) * 128], gb[:, k * 128:(k + 1) * 128], ident_b[:], [gb.t, ident_b.t], [pt.t])
                cp("act" if pg_ % 2 else "dve", kcmpT[:, :, pg_ * 128:(pg_ + 1) * 128],
                   pv[:, 0:512].rearrange("p (k t) -> p k t", k=4), [pt.t], [kcmpT.t])
            for kv in range(2):
                for g in range(2):
                    kview = kcmpT[:, kv * 2 + g, :].rearrange("p (c s) -> p c s", s=16)
                    pp = psum[2]
                    for js in range(32):
                        j_, s_i = js // 16, js % 16
                        mm(pp[:, 0:511], w1b2[:, kv, js, :], kview[:, j_:j_ + 511, s_i], js == 0, js == 31,
                           [w1b2.t, kcmpT.t], [pp.t])
                    act(tg[:, 0:511], pp[:, 0:511], AF.Identity, [pp.t, preb.t], [tg.t], bias=preb[:, kv:kv + 1])
                    act(ug[:, 0:511], tg[:, 0:511], AF.Square, [tg.t], [ug.t])
                    ts(ug[:, 0:511], ug[:, 0:511], 0.044715, 1.0, ALU.mult, ALU.add, [ug.t], [ug.t])
                    tt(ug[:, 0:511], ug[:, 0:511], tg[:, 0:511], ALU.mult, [ug.t, tg.t], [ug.t])
                    act(ug[:, 0:511], ug[:, 0:511], AF.Sigmoid, [ug.t], [ug.t], scale=GC)
                    tt(gl[:, 0:511], ug[:, 0:511], tg[:, 0:511], ALU.mult, [ug.t, tg.t], [gl.t])
                    pq_ = psum[3]
                    if kv == 0:
                        mm(pq_[:, 0:511], w2b[:, 0, :], gl[:, 0:511], True, True, [w2b.t, gl.t], [pq_.t])
                        cp("act", kcT_s[:, g, 0:511], pq_[:, 0:511], [pq_.t], [kcT_s.t])
                    else:
                        for it_ in range(4):
                            n_i = 128 if it_ < 3 else 127
                            mm(pq_[0:n_i, it_ * 128:(it_ + 1) * 128], gl[:, it_ * 128:it_ * 128 + n_i], w2b[:, 1, :],
                               True, True, [w2b.t, gl.t], [pq_.t])
                        for it_ in range(4):
                            n_i = 128 if it_ < 3 else 127
                            cp("act", vc_s[0:n_i, it_, g, :], pq_[0:n_i, it_ * 128:(it_ + 1) * 128], [pq_.t], [vc_s.t])
            psc = psum[4]
            for g in range(2):
                for it_ in range(4):
                    c_ = (g * 4 + it_) * 24
                    mm(psc[:, c_:c_ + 24], kcT_s[:, g, it_ * 128:(it_ + 1) * 128], qsg(b_, g), True, True,
                       [kcT_s.t, qs.t], [psc.t])
            tt(s192[:], psc[:, 0:192], bcs[:].rearrange("p g i t r -> p (g i t r)"), ALU.add, [psc.t, bcs.t], [s192.t])
            act(p192[:], s192[:], AF.Exp, [s192.t], [p192.t])
            po, pz, pp = psum[5], psum[6], psum[7]
            for g in range(2):
                for it_ in range(4):
                    c_ = (g * 4 + it_) * 24
                    mm(po[:, g * 24:(g + 1) * 24], vc_s[:, it_, g, :], p192[:, c_:c_ + 24], it_ == 0, it_ == 3,
                       [vc_s.t, p192.t], [po.t])
                for it_ in range(4):
                    c_ = (g * 4 + it_) * 24
                    mm(pz[:, g * 24:(g + 1) * 24], ones_b[:], p192[:, c_:c_ + 24], it_ == 0, it_ == 3,
                       [ones_b.t, p192.t], [pz.t])
                for it_ in range(4):
                    c_ = (g * 4 + it_) * 24
                    mm(pp[0:24, g * 130:(g + 1) * 130], p192[:, c_:c_ + 24], maug_s[:, it_, :], it_ == 0, it_ == 3,
                       [p192.t, maug_s.t], [pp.t])
            finish_branch(po, pz, 0, b_, True)
            cp("dve", ps_s[:], pp[0:24, 0:260].rearrange("p (g j) -> p g j", g=2), [pp.t], [ps_s.t])
            ts(zr_s[:], ps_s[:, :, 129], tiny, None, ALU.max, None, [ps_s.t], [zr_s.t])
            P.op("dve", lambda e: e.reciprocal(zr_s[:], zr_s[:]), [zr_s.t], [zr_s.t])
            for g in range(2):
                ts(pn_s[:, g, :], ps_s[:, g, 0:129], zr_s[:, g:g + 1], None, ALU.mult, None, [ps_s.t, zr_s.t], [pn_s.t])
            pi_ = psum[3]
            for g in range(2):
                mm(pi_[0:4, g * 129:(g + 1) * 129], rself[:], pn_s[:, g, :], True, True, [rself.t, pn_s.t], [pi_.t])
            for g in range(2):
                tt(scs[:, g, :], pi_[0:4, g * 129:(g + 1) * 129], VBs[:], ALU.add, [pi_.t, VBs.t], [scs.t])
            for g in range(2):
                P.op("dve", lambda e, g=g: e.max(out=m8a[:], in_=scs[:, g, :]), [scs.t], [m8a.t])
                P.op("dve", lambda e, g=g: e.match_replace(out=scs2[:], in_to_replace=m8a[:], in_values=scs[:, g, :],
                                                           imm_value=-1e30), [m8a.t, scs.t], [scs2.t])
                P.op("dve", lambda e: e.max(out=m8b[:], in_=scs2[:]), [scs2.t], [m8b.t])
                ts(sels[:], scs[:, g, :], m8b[:, 7:8], None, ALU.is_ge, None, [scs.t, m8b.t], [sels.t])
                ts(sels[:], sels[:], -1.0, -NEGM, ALU.add, ALU.mult, [sels.t], [sels.t])
                cp("dve", nse[:, g, :].rearrange("p (j k) -> p j k", k=64),
                   sels[:].unsqueeze(2).to_broadcast([4, 129, 64]), [sels.t], [nse.t])
            po, pz = psum[5], psum[6]
            for pg_ in range(64):
                gb = gather(b_, pg_, 512)
                pt = psum[pg_ % 2]
                pv = psb(pg_ % 2)
                ks = ksT2[pg_ % 2]
                for k in range(2):
                    tr(pv[:, k * 128:(k + 1) * 128], gb[:, k * 128:(k + 1) * 128], ident_b[:], [gb.t, ident_b.t], [pt.t])
                cp("act", ks[:], pv[:, 0:256].rearrange("p (k t) -> p k t", k=2), [pt.t], [ks.t])
                psc = psum[2 + pg_ % 2]
                for g in range(2):
                    mm(psc[:, g * 24:(g + 1) * 24], ks[:, g, :], qsg(b_, g), True, False, [ks.t, qs.t], [psc.t])
                    mm(psc[:, g * 24:(g + 1) * 24], nse[0:4, g, pg_ * 128:(pg_ + 1) * 128],
                       rselT[0:4, g, :, :].rearrange("p t r -> p (t r)"), False, True, [nse.t, rselT.t], [psc.t])
                s_, p_ = s48[pg_ % 2], p48[pg_ % 2]
                tt(s_[:], psc[:, 0:48], bSall[:, pg_, :], ALU.add, [psc.t, bSall.t], [s_.t])
                act(p_[:], s_[:], AF.Exp, [s_.t], [p_.t])
                for g in range(2):
                    mm(po[:, g * 24:(g + 1) * 24], gb[:, 256 + g * 128:256 + (g + 1) * 128], p_[:, g * 24:(g + 1) * 24],
                       pg_ == 0, False, [gb.t, p_.t], [po.t])
                mm(pz[:, 0:48], ones_b[:], p_[:, 0:48], pg_ == 0, False, [ones_b.t, p_.t], [pz.t])
            new_tile(b_, 4, 0, po, pz)
            finish_branch(po, pz, 1, b_, False)
            for wt in range(4):
                gt, gb = gt2[gi % 2], gb2[gi % 2]
                gi += 1
                dma_in(gt[:], gt.t, cwin[b_, wt * 128:(wt + 1) * 128].rearrange("s a g d -> s (a g d)"))
                cp("dve", gb[:], gt[:], [gt.t], [gb.t])
                pt = psum[wt % 2]
                pv = psb(wt % 2)
                ks = ksT2[wt % 2]
                for k in range(2):
                    tr(pv[:, k * 128:(k + 1) * 128], gb[:, k * 128:(k + 1) * 128], ident_b[:], [gb.t, ident_b.t], [pt.t])
                cp("act", ks[:], pv[:, 0:256].rearrange("p (k t) -> p k t", k=2), [pt.t], [ks.t])
                psc = psum[2 + wt % 2]
                for g in range(2):
                    mm(psc[:, g * 24:(g + 1) * 24], ks[:, g, :], qsg(b_, g), True, True, [ks.t, qs.t], [psc.t])
                s_, p_ = s48[wt % 2], p48[wt % 2]
                tt(s_[:], psc[:, 0:48], bW[:, wt].rearrange("p g t r -> p (g t r)"), ALU.add, [psc.t, bW.t], [s_.t])
                act(p_[:], s_[:], AF.Exp, [s_.t], [p_.t])
                for g in range(2):
                    mm(po[:, g * 24:(g + 1) * 24], gb[:, 256 + g * 128:256 + (g + 1) * 128], p_[:, g * 24:(g + 1) * 24],
                       wt == 0, False, [gb.t, p_.t], [po.t])
                mm(pz[:, 0:48], ones_b[:], p_[:, 0:48], wt == 0, False, [ones_b.t, p_.t], [pz.t])
            new_tile(b_, 6, 256, po, pz)
            finish_branch(po, pz, 2, b_, False)
            cp("dve", mixs[:, :, b_ * TS:(b_ + 1) * TS].rearrange("p (g r) t -> p g t r", g=2), acc48[:],
               [acc48.t], [mixs.t])
        P.dma("pool", lambda e: [e.dma_start(out=mixT_d[0:H, :, T:NTOK].rearrange("h p t -> p h t"), in_=mixs[:])], 1,
              mixs.t, reads=[mixs.t], writes=mix_tok[0:H])

    def mem_attn_l1(xnT1):
        nonlocal xnT
        xnT = xnT1
        mem_attn(1, lambda hm: w_in_b[0, :, W + 36 + hm * 128:W + 36 + (hm + 1) * 128].rearrange("(k p) c -> p k c", p=128))

    if 'N' in PH:
        phase_M(1)
        sel36 = A.tile([36, 36, 128], BF16, "sel36")
        g36s = A.tile([36, NS * TS], BF16, "g36s")
        mN = A.mark()
        phase_N()
        A.release(mN)
        if 'S' in PH:
            phase_S()
            A.release(mN)
        phase_B(1)
        phase_C(1)

    P.emit()
    print("SBUF peak", A.peak, "ops", {e: len(P.ops[e]) for e in P.ENGS}, "signals",
          {e: sum(1 for o in P.ops[e] if o.signal and not o.ndma) for e in P.ENGS},
          "max dma val", max(16 * t.cnt for t in P.slots), "slots", len(P.slots))
    return nc


_NC_CACHE = {}


def kernel(x_prompt, x_sample, mem_prompt, state_hgrn, cache_conv, cache_mem, cache_kv, cache_win,
           page_table, norm_gains, w_in_a, lb_logits, hgrn_norm, w_in_b, w_o, w_mem_kv, kv_norm,
           w_kv_b, cmp_pos, w_cmp1, w_cmp2, w_ffn_in, w_ffn_conv, b_ffn_conv, w_ffn_out):
    f = lambda a: np.ascontiguousarray(np.asarray(a, dtype=np.float32))
    if "nc" not in _NC_CACHE:
        _NC_CACHE["nc"] = build()
    nc = _NC_CACHE["nc"]
    x_prompt = f(x_prompt); x_sample = f(x_sample); mem_prompt = f(mem_prompt)
    state_hgrn = f(state_hgrn); cache_mem = f(cache_mem)
    cache_conv = f(cache_conv)
    cache_kv_f = f(cache_kv)
    cache_win_f = f(cache_win)
    page_table_i = np.ascontiguousarray(np.asarray(page_table, dtype=np.int32))
    shared = {
        "norm_gains": f(norm_gains), "w_in_a": f(w_in_a), "lb_logits": f(lb_logits), "hgrn_norm": f(hgrn_norm),
        "w_o": f(w_o), "w_mem_kv": f(w_mem_kv), "kv_norm": f(kv_norm), "w_kv_b": f(w_kv_b),
        "w_in_b": f(w_in_b), "cmp_pos": f(cmp_pos), "w_cmp1": f(w_cmp1), "w_cmp2": f(w_cmp2),
        "w_ffn_in": f(w_ffn_in), "w_ffn_conv": f(w_ffn_conv), "b_ffn_conv": f(b_ffn_conv), "w_ffn_out": f(w_ffn_out),
    }
    in_maps = []
    for c in range(NCORES):
        s0, s1 = NS * c, NS * (c + 1)
        m = {
            "xp": x_prompt[c],
            "xs": x_sample[s0:s1].reshape(NS * TS, D),
            "memp": mem_prompt[c],
            "st_in": state_hgrn[0, s0:s1],
            "cmem": np.ascontiguousarray(cache_mem[:, s0:s1]),
            "cconv": np.ascontiguousarray(cache_conv[:, s0:s1]),
            "ckv": cache_kv_f,
            "cwin": np.ascontiguousarray(cache_win_f[s0:s1]),
            "ptab": np.ascontiguousarray(page_table_i[s0:s1]),
        }
        m.update(shared)
        in_maps.append(m)
    res = run_bass_kernel_spmd(nc, in_maps, core_ids=list(range(NCORES)))
    R = res.results
    B = 8
    SB = 32
    cat = lambda k, ax=0: np.concatenate([R[c][k] for c in range(NCORES)], axis=ax)
    stk = lambda k, ax=0: np.stack([R[c][k] for c in range(NCORES)], axis=ax)
    y_prompt = stk("o_yp")
    y_sample = cat("o_ys").reshape(SB, TS, D)
    hg_p = np.stack([R[c]["o_hgp"] for c in range(NCORES)], axis=0)[None]
    hg_s = np.concatenate([R[c]["o_hgs"] for c in range(NCORES)], axis=0)[None]
    cv_p = stk("o_cvp", 1)
    cv_s = cat("o_cvs", 1)
    nm = np.stack([R[c]["o_nm"] for c in range(NCORES)], axis=1).reshape(2, B, MEM, 2, 4, 128)
    kv_p = stk("o_kvp").reshape(B, T // 128, 128, 4, 2, 128)
    kv_s = cat("o_kvs").reshape(SB, TS, 4, 2, 128)
    win_p = stk("o_winp").reshape(B, 512, 2, 2, 128)
    win_s = cat("o_wins").reshape(SB, TS, 2, 2, 128)
    return (y_prompt, y_sample, hg_p, hg_s, cv_p, cv_s, nm, kv_p, kv_s, win_p, win_s)
```

```python
import numpy as np
import concourse.bass as bass
import concourse.mybir as mybir
from concourse.bass_utils import run_bass_kernel_spmd

F32 = mybir.dt.float32
BF16 = mybir.dt.bfloat16
I32 = mybir.dt.int32
AF = mybir.ActivationFunctionType
ALU = mybir.AluOpType
AX = mybir.AxisListType

NCORES = 8
D = 2048
KC = D // 128
T = 2048
NS = 4
TS = 4
NTOK = T + NS * TS
MEM = 256
W = 1536
H = 12
DFF = 5632
EPS = 1e-6
SCALE = 128 ** -0.5


class Tok:
    __slots__ = ("name", "w", "rs", "sem", "cnt")

    def __init__(self, name=""):
        self.name = name
        self.w = None
        self.rs = []
        self.sem = None
        self.cnt = 0


class Op:
    __slots__ = ("eng", "fn", "deps", "signal", "ndma", "sem", "val", "idx")


class Prog:
    ENGS = ("sp", "act", "dve", "pool", "pe")

    def __init__(self, nc):
        self.nc = nc
        self.ops = {e: [] for e in self.ENGS}
        self.allops = []
        self.dma_toks = []
        self.extra = {}
        self.slots = []
        self.rr = 0
        self.MAXSLOTS = 58
        self.dmas_since = []
        self.last = {}

    def tok(self, name=""):
        return Tok(name)

    def _add(self, eng, fn, reads, writes, ndma=0, dtok=None):
        o = Op()
        o.eng = eng
        o.fn = fn
        o.ndma = ndma
        o.signal = False
        o.sem = None
        o.val = 0
        deps = []
        seen = set()
        for t in reads:
            if t.w is not None and id(t.w) not in seen:
                seen.add(id(t.w))
                deps.append(t.w)
        for t in writes:
            if t.w is not None and id(t.w) not in seen:
                seen.add(id(t.w))
                deps.append(t.w)
            lastr = {}
            for r in t.rs:
                if r.ndma:
                    if id(r) not in seen:
                        seen.add(id(r))
                        deps.append(r)
                else:
                    lastr[r.eng] = r
            for r in lastr.values():
                if id(r) not in seen:
                    seen.add(id(r))
                    deps.append(r)
        if eng == "pe" and ndma == 0:
            deps = [d for d in deps if not (d.eng == "pe" and d.ndma == 0)]
        if self.extra.get(eng):
            for d in self.extra[eng]:
                if id(d) not in seen and not (d.eng == eng and d.ndma == 0):
                    seen.add(id(d))
                    deps.append(d)
            self.extra[eng] = None
        o.deps = deps
        for d in deps:
            d.signal = True
        for t in reads:
            t.rs.append(o)
        for t in writes:
            t.w = o
            t.rs = []
        if ndma:
            assert dtok is not None
            if dtok.sem is None:
                if len(self.slots) < self.MAXSLOTS:
                    sl = Tok("slot%d" % len(self.slots))
                    sl.rs = 1
                    self.slots.append(sl)
                else:
                    sl = self.slots[self.rr % self.MAXSLOTS]
                    self.rr += 1
                    sl.rs += 1
                dtok.sem = sl
            sl = dtok.sem
            if sl.rs > 1 and sl.w is not None and id(sl.w) not in seen:
                seen.add(id(sl.w))
                deps.append(sl.w)
                sl.w.signal = True
            sl.w = o
            sl.cnt += ndma
            o.sem = sl
            o.val = 16 * sl.cnt
        o.idx = len(self.ops[eng])
        if ndma:
            self.dmas_since.append(o)
        else:
            self.last[eng] = o
        self.ops[eng].append(o)
        self.allops.append(o)
        return o

    def barrier(self):
        deps = list(self.last.values()) + list(self.dmas_since)
        self.dmas_since = []
        for e in self.ENGS:
            self.extra[e] = list(deps)

    def op(self, eng, fn, reads=(), writes=()):
        return self._add(eng, fn, list(reads), list(writes))

    def dma(self, eng, fn, ndma, dtok, reads=(), writes=()):
        return self._add(eng, fn, list(reads), list(writes), ndma=ndma, dtok=dtok)

    def emit(self):
        nc = self.nc
        import contextlib
        with contextlib.ExitStack() as es:
            esem = {e: es.enter_context(nc.semaphore("sem_" + e)) for e in self.ENGS}
            for i, t in enumerate(self.slots):
                t.sem = es.enter_context(nc.semaphore("dsem%d" % i))
            for e in self.ENGS:
                c = 0
                for o in self.ops[e]:
                    if o.ndma == 0:
                        if o.signal:
                            c += 1
                            o.sem = esem[e]
                            o.val = c
                    else:
                        o.sem = o.sem.sem
            final_waits = [(t.sem, 16 * t.cnt) for t in self.slots]

            def run(ename, eng):
                waited = {}
                for o in self.ops[ename]:
                    for d in o.deps:
                        k = id(d.sem)
                        if waited.get(k, 0) >= d.val:
                            continue
                        waited[k] = d.val
                        eng.wait_ge(d.sem, d.val)
                    r = o.fn(eng)
                    if o.ndma:
                        assert len(r) == o.ndma, (len(r), o.ndma)
                        for ins in r:
                            ins.then_inc(o.sem, 16)
                    elif o.signal:
                        r.then_inc(o.sem, 1)
                if ename == "sp":
                    for s, v in final_waits:
                        if waited.get(id(s), 0) < v:
                            eng.wait_ge(s, v)

            with nc.Block() as block:
                @block.sync
                def _(e):
                    run("sp", e)

                @block.scalar
                def _(e):
                    run("act", e)

                @block.vector
                def _(e):
                    run("dve", e)

                @block.gpsimd
                def _(e):
                    run("pool", e)

                @block.tensor
                def _(e):
                    run("pe", e)


class Tile:
    def __init__(self, h, tok):
        self.h = h
        self.t = tok

    def __getitem__(self, k):
        return self.h[k]


class Alloc:
    LO = 17408
    HI = 228000

    def __init__(self, nc, P):
        self.nc = nc
        self.P = P
        self.top = self.LO
        self.n = 0
        self.peak = 0

    def mark(self):
        return self.top

    def release(self, m):
        self.P.barrier()
        self.top = m

    def tile(self, shape, dtype, name="t"):
        nbytes = int(np.prod(shape[1:])) * mybir.dt.size(dtype)
        off = (self.top + 63) // 64 * 64
        assert off + nbytes <= self.HI, ("SBUF overflow", name, off, nbytes)
        self.top = off + nbytes
        self.peak = max(self.peak, self.top)
        self.n += 1
        h = self.nc.alloc_sbuf_tensor_at("%s_%d" % (name, self.n), list(shape), dtype, offset=off)
        return Tile(h, self.P.tok(name))


def build():
    nc = bass.Bass("TRN2", target_bir_lowering=False)
    P = Prog(nc)
    A = Alloc(nc, P)

    def din(name, shape, dt=F32):
        return nc.dram_tensor(name, list(shape), dt, kind="ExternalInput").ap()

    def dout(name, shape, dt=F32):
        return nc.dram_tensor(name, list(shape), dt, kind="ExternalOutput").ap()

    xp = din("xp", [T, D])
    xs = din("xs", [NS * TS, D])
    memp = din("memp", [MEM, D])
    st_in = din("st_in", [NS, H, 128, 128])
    cmem = din("cmem", [2, NS, MEM, 2, 4, 128])
    norm_gains = din("norm_gains", [2, 4, D])
    w_in_a = din("w_in_a", [1, D, 4 * W + 512])
    lb_logits = din("lb_logits", [2, W])
    hgrn_norm = din("hgrn_norm", [1, H, 128])
    w_o = din("w_o", [2, D, D])
    w_mem_kv = din("w_mem_kv", [2, D, 1024])

    cconv = din("cconv", [2, NS, 2, DFF])
    kv_norm = din("kv_norm", [D])
    w_kv_b = din("w_kv_b", [D, 1536])
    w_ffn_in = din("w_ffn_in", [2, D, 2 * DFF])
    w_ffn_conv = din("w_ffn_conv", [2, 3, DFF])
    b_ffn_conv = din("b_ffn_conv", [2, DFF])
    w_ffn_out = din("w_ffn_out", [2, DFF, D])
    w_in_b = din("w_in_b", [1, D, W + 36 + 512])
    ckv = din("ckv", [2560, 128, 4, 2, 128])
    cwin = din("cwin", [NS, 512, 2, 2, 128])
    ptab = din("ptab", [NS, 64], I32)
    cmp_pos = din("cmp_pos", [2, 32, 128])
    w_cmp1 = din("w_cmp1", [2, 4096, 128])
    w_cmp2 = din("w_cmp2", [2, 128, 128])
    o_nm = dout("o_nm", [2, MEM, 1024])
    o_cvp = dout("o_cvp", [2, 2, DFF])
    o_cvs = dout("o_cvs", [2, NS, 2, DFF])
    o_kvp = dout("o_kvp", [T, 1024])
    o_kvs = dout("o_kvs", [NS * TS, 1024])
    o_winp = dout("o_winp", [512, 512])
    o_wins = dout("o_wins", [NS * TS, 512])
    o_yp = dout("o_yp", [T, D])
    o_ys = dout("o_ys", [NS * TS, D])
    o_hgp = dout("o_hgp", [H, 128, 128])
    o_hgs = dout("o_hgs", [NS, H, 128, 128])

    hT_d = nc.dram_tensor("hT_d", [KC, 128, NTOK], F32, kind="Internal").ap()
    mixT_d = nc.dram_tensor("mixT_d", [KC, 128, NTOK], BF16, kind="Internal").ap()
    hT_tok = [[P.tok("hT%d_%d" % (k, i)) for i in range(5)] for k in range(KC)]
    xn2T_d = nc.dram_tensor("xn2T_d", [KC, 128, NTOK], BF16, kind="Internal").ap()
    xn2_tok = [P.tok("xn2d%d" % i) for i in range(5)]
    xnA_d = nc.dram_tensor("xnA_d", [KC, 128, NTOK], BF16, kind="Internal").ap()
    xnA_tok = [P.tok("xnAd%d" % i) for i in range(4)]
    xkv_d = nc.dram_tensor("xkv_d", [KC, 128, NTOK], BF16, kind="Internal").ap()
    xkv_tok = [P.tok("xkvd%d" % i) for i in range(4)]
    fT_d = nc.dram_tensor("fT_d", [KC, 128, NTOK], F32, kind="Internal").ap()
    fT_tok = [P.tok("fTd%d" % i) for i in range(KC)]
    qT_d = nc.dram_tensor("qT_d", [H, 128, NTOK], BF16, kind="Internal").ap()
    qT_tok = [P.tok("qTd%d" % i) for i in range(H)]
    ocmp_d = nc.dram_tensor("ocmp_d", [H, 128, T], F32, kind="Internal").ap()
    ocmp_tok = [P.tok("ocmpd%d" % i) for i in range(H)]
    mix_tok = [P.tok("mixd%d" % k) for k in range(KC)]

    CT = [(i * 512, 512) for i in range(4)] + [(T, NS * TS)]

    def act(out, in_, func, reads, writes, **kw):
        P.op("act", lambda e: e.activation(out=out, in_=in_, func=func, **kw), reads, writes)

    def tt(out, in0, in1, op, reads, writes, eng="dve"):
        P.op(eng, lambda e: e.tensor_tensor(out=out, in0=in0, in1=in1, op=op), reads, writes)

    def ts(out, in0, s1, s2, op0, op1, reads, writes, eng="dve"):
        if op1 is None:
            P.op(eng, lambda e: e.tensor_scalar(out=out, in0=in0, scalar1=s1, scalar2=None, op0=op0), reads, writes)
        else:
            P.op(eng, lambda e: e.tensor_scalar(out=out, in0=in0, scalar1=s1, scalar2=s2, op0=op0, op1=op1),
                 reads, writes)

    def stt(out, in0, scalar, in1, op0, op1, reads, writes):
        P.op("dve", lambda e: e.scalar_tensor_tensor(out=out, in0=in0, scalar=scalar, in1=in1, op0=op0, op1=op1),
             reads, writes)

    def cp(eng, out, in_, reads, writes):
        if eng == "act":
            P.op("act", lambda e: e.copy(out, in_), reads, writes)
        else:
            P.op(eng, lambda e: e.tensor_copy(out, in_), reads, writes)

    def mm(out, lhsT, rhs, start, stop, reads, writes):
        P.op("pe", lambda e: e.matmul(out, lhsT, rhs, start=start, stop=stop), reads, writes)

    def tr(out, in_, ident, reads, writes):
        P.op("pe", lambda e: e.transpose(out, in_, ident), reads, writes)

    def rsqrt(out, in_, scale, reads, writes):
        n_p = out.shape[0]
        act(out, in_, AF.Sqrt, list(reads) + [eps_t.t], writes, scale=scale, bias=eps_t[0:n_p, :])
        P.op("dve", lambda e: e.reciprocal(out, out), writes, writes)

    def memset(eng, ap, v, writes):
        P.op(eng, lambda e: e.memset(ap, v), (), writes)

    def dma_in(tile_ap, tile_tok, src, reads=(), eng="sp", nonc=False):
        if nonc:
            P.dma(eng, lambda e: [e.dma_start(out=tile_ap, in_=src, allow_slow_non_contiguous=True)], 1, tile_tok,
                  reads=reads, writes=[tile_tok])
        else:
            P.dma(eng, lambda e: [e.dma_start(out=tile_ap, in_=src)], 1, tile_tok, reads=reads, writes=[tile_tok])

    def dma_out(dst, tile_ap, tile_tok, writes=(), eng="pool"):
        P.dma(eng, lambda e: [e.dma_start(out=dst, in_=tile_ap)], 1, tile_tok, reads=[tile_tok], writes=writes)

    eps_t = A.tile([128, 1], F32, "eps_t")
    one_t = A.tile([128, 1], F32, "one_t")
    P.op("pool", lambda e: e.memset(one_t[:], 1.0), (), [one_t.t])
    P.op("pool", lambda e: e.memset(eps_t[:], EPS), (), [eps_t.t])
    ident_f = A.tile([128, 128], F32, "ident_f")
    ident_b = A.tile([128, 128], BF16, "ident_b")
    ones_b = A.tile([128, 128], BF16, "ones_b")
    memset("pool", ident_b[:], 1.0, [ident_b.t])
    P.op("pool", lambda e: e.affine_select(out=ident_b[:], in_=ident_b[:], pattern=[[-1, 128]],
                                           compare_op=ALU.is_equal, fill=0.0, base=0, channel_multiplier=1),
         reads=[ident_b.t], writes=[ident_b.t])
    cp("pool", ident_f[:], ident_b[:], [ident_b.t], [ident_f.t])
    memset("pool", ones_b[:], 1.0, [ones_b.t])
    cmask = A.tile([32, 16, 32], F32, "cmask")
    memset("pool", cmask[:], 1.0, [cmask.t])
    P.op("pool", lambda e: e.affine_select(out=cmask[:], in_=cmask[:], pattern=[[0, 16], [1, 32]],
                                           compare_op=ALU.is_ge, fill=0.0, base=0, channel_multiplier=-1),
         reads=[cmask.t], writes=[cmask.t])
    rmask = A.tile([128, 512], F32, "rmask")
    memset("pool", rmask[:], 1.0, [rmask.t])
    memset("pool", rmask[:].rearrange("p (c j) -> p c j", j=32)[:, :, 0:1], 0.0, [rmask.t])
    rmask_s = A.tile([128, 16], F32, "rmask_s")
    memset("pool", rmask_s[:], 1.0, [rmask_s.t])
    memset("pool", rmask_s[:].rearrange("p (c j) -> p c j", j=4)[:, :, 0:1], 0.0, [rmask_s.t])
    gT = A.tile([128, 8, KC], F32, "gT")
    dma_in(gT[:], gT.t, norm_gains.rearrange("l j (k p) -> p (l j) k", p=128), nonc=True)
    gnT = A.tile([128, H], F32, "gnT")
    dma_in(gnT[:], gnT.t, hgrn_norm[0].rearrange("h d -> d h"), nonc=True)
    lbT = A.tile([128, 2, H], F32, "lbT")
    dma_in(lbT[:], lbT.t, lb_logits.rearrange("r (h d) -> d r h", d=128), nonc=True)
    kvgT = A.tile([128, KC], F32, "kvgT")
    dma_in(kvgT[:], kvgT.t, kv_norm.rearrange("(k p) -> p k", p=128), nonc=True)
    NJ = DFF // 128
    wcT = A.tile([128, 6, NJ], F32, "wcT")
    dma_in(wcT[:], wcT.t, w_ffn_conv.rearrange("l j (c p) -> p (l j) c", p=128), nonc=True)
    bcT = A.tile([128, 2, NJ], F32, "bcT")
    dma_in(bcT[:], bcT.t, b_ffn_conv.rearrange("l (c p) -> p l c", p=128), nonc=True)
    ccT = A.tile([128, 2, NJ, NS * 2], F32, "ccT")
    for l in range(2):
        for b_ in range(NS):
            for r_ in range(2):
                dma_in(ccT[:, l, :, b_ * 2 + r_], ccT.t, cconv[l, b_, r_].rearrange("(c p) -> p c", p=128), nonc=True)
    oml = A.tile([128, H], F32, "oml")
    tt(oml[:], lbT[:, 1, :], lbT[:, 0, :], ALU.subtract, [lbT.t], [oml.t])
    act(oml[:], oml[:], AF.Sigmoid, [oml.t], [oml.t])

    psum = []
    for i in range(8):
        h = nc.alloc_psum_tensor("ps%d" % i, [128, 512], F32)
        psum.append(Tile(h, P.tok("ps%d" % i)))

    def psb(i):
        return psum[i].h[:].bitcast(BF16)

    wst = [A.tile([128, 2048], F32, "wst%d" % i) for i in range(3)]
    wst_i = [0]

    def load_cast(dst_ap, dst_tok, src_ap, shape3, eng=None):
        st = wst[wst_i[0] % 3]
        if eng is None:
            eng = "dve" if wst_i[0] % 2 == 0 else "act"
        wst_i[0] += 1
        n = int(np.prod(shape3[1:]))
        sv = st[0:shape3[0], 0:n]
        if len(shape3) == 3:
            sv = sv.rearrange("p (a b) -> p a b", a=shape3[1])
        P.dma("sp", lambda e: [e.dma_start(out=sv, in_=src_ap)], 1, st.t, writes=[st.t])
        cp(eng, dst_ap, sv, [st.t], [dst_tok])

    import os
    PH = os.environ.get('PH', 'XMABCKNS')
    MS = os.environ.get('MS', 'VKS')
    memKT = A.tile([128, 5, 4, MEM], BF16, "memKT")
    memV = A.tile([128, 5, 2, 512], BF16, "memV")
    mXA = A.mark()
    xnT = A.tile([128, KC, NTOK], BF16, "xnT")
    xn_tok = [P.tok("xn%d" % i) for i in range(5)]
    mX = A.mark()
    xt2 = [A.tile([128, D], F32, "xt%d" % i) for i in range(2)]
    sq = A.tile([128, D], F32, "sq")
    xb = A.tile([128, D], BF16, "xb")
    ss = A.tile([128, 1], F32, "ss")
    hst = [A.tile([128, KC, 128], F32, "hst%d" % i) for i in range(2)]
    for i in (range(17) if 'X' in PH else []):
        rows = 128 if i < 16 else NS * TS
        src = xp[i * 128:(i + 1) * 128, :] if i < 16 else xs[:, :]
        c0 = i * 128
        cti = min(i // 4, 4)
        xt = xt2[i % 2]
        dma_in(xt[0:rows, :], xt.t, src)
        act(sq[0:rows, :], xt[0:rows, :], AF.Square, [xt.t], [sq.t])
        P.op("dve", lambda e, rows=rows: e.reduce_sum(out=ss[0:rows, :], in_=sq[0:rows, :], axis=AX.X), [sq.t], [ss.t])
        rsqrt(ss[0:rows, :], ss[0:rows, :], 1.0 / D, [ss.t], [ss.t])
        ts(xb[0:rows, :], xt[0:rows, :], ss[0:rows, 0:1], None, ALU.mult, None, [xt.t, ss.t], [xb.t])
        for half in range(2):
            pt = psum[half]
            pv = psb(half)
            for k in range(8):
                kc = half * 8 + k
                tr(pv[:, k * 128:k * 128 + rows], xb[0:rows, kc * 128:(kc + 1) * 128], ident_b[0:rows, 0:rows],
                   [xb.t, ident_b.t], [pt.t])
            tt(xnT[:, half * 8:(half + 1) * 8, c0:c0 + rows],
               pv.rearrange("p (k t) -> p k t", k=8)[:, :, 0:rows],
               gT[:, 0, half * 8:(half + 1) * 8].unsqueeze(2).to_broadcast([128, 8, rows]),
               ALU.mult, [pt.t, gT.t], [xn_tok[cti]])
        hs = hst[i % 2]
        for q4 in range(4):
            pt = psum[2 + q4 % 2]
            for k in range(4):
                kc = q4 * 4 + k
                tr(pt[:, k * 128:k * 128 + rows], xt[0:rows, kc * 128:(kc + 1) * 128], ident_f[0:rows, 0:rows],
                   [xt.t, ident_f.t], [pt.t])
            cp("act", hs[:, q4 * 4:(q4 + 1) * 4, 0:rows],
               pt[:].rearrange("p (k t) -> p k t", k=4)[:, :, 0:rows], [pt.t], [hs.t])
        P.dma("pool", lambda e, hs=hs, c0=c0, rows=rows: [
            e.dma_start(out=hT_d[:, :, c0:c0 + rows].rearrange("k p t -> p k t"), in_=hs[:, :, 0:rows])],
            1, hs.t, reads=[hs.t], writes=[hT_tok[k][cti] for k in range(KC)])
    A.release(mX)

    def phase_M(l):
        m0 = A.mark()
        memT = A.tile([128, KC, MEM], BF16, "memT")
        mtile = A.tile([128, D], F32, "mtile")
        mtile_b = A.tile([128, D], BF16, "mtile_b")
        for t2 in range(MEM // 128):
            dma_in(mtile[:], mtile.t, memp[t2 * 128:(t2 + 1) * 128, :])
            cp("dve", mtile_b[:], mtile[:], [mtile.t], [mtile_b.t])
            for half in range(2):
                pt = psum[half]
                pv = psb(half)
                for k in range(8):
                    kc = half * 8 + k
                    tr(pv[:, k * 128:(k + 1) * 128], mtile_b[:, kc * 128:(kc + 1) * 128], ident_b[:],
                       [mtile_b.t, ident_b.t], [pt.t])
                cp("act", memT[:, half * 8:(half + 1) * 8, t2 * 128:(t2 + 1) * 128],
                   pv.rearrange("p (k t) -> p k t", k=8), [pt.t], [memT.t])
        wmk = A.tile([128, KC, 512], BF16, "wmk")
        osb = [A.tile([128, 512], F32, "osb%d" % i) for i in range(2)]
        oi = 0
        for nb in range(2):
            for kc in range(KC):
                load_cast(wmk[:, kc, :], wmk.t, w_mem_kv[l, kc * 128:(kc + 1) * 128, nb * 512:(nb + 1) * 512],
                          [128, 512])
            for t2 in range(2):
                pt = psum[2 + (oi % 2)]
                for kc in range(KC):
                    mm(pt[:], memT[:, kc, t2 * 128:(t2 + 1) * 128], wmk[:, kc, :], kc == 0, kc == KC - 1,
                       [memT.t, wmk.t], [pt.t])
                ob = osb[oi % 2]
                oi += 1
                cp("dve", ob[:], pt[:], [pt.t], [ob.t])
                dma_out(o_nm[l, t2 * 128:(t2 + 1) * 128, nb * 512:(nb + 1) * 512], ob[:], ob.t)
                if nb == 1:
                    cp("pool", memV[:, 0, t2, :], ob[:], [ob.t], [memV.t])
            if nb == 0:
                for hm in range(4):
                    pt = psum[4 + hm % 2]
                    for kc in range(KC):
                        mm(pt[:, 0:MEM], wmk[:, kc, hm * 128:(hm + 1) * 128], memT[:, kc, :], kc == 0, kc == KC - 1,
                           [memT.t, wmk.t], [pt.t])
                    cp("act", memKT[:, 0, hm, :], pt[:, 0:MEM], [pt.t], [memKT.t])
        cmt = A.tile([128, 1024], F32, "cmt")
        cmb = A.tile([128, 512], BF16, "cmb")
        for b in range(NS):
            for t2 in range(2):
                dma_in(cmt[:], cmt.t, cmem[l, b, t2 * 128:(t2 + 1) * 128].rearrange("m a h d -> m (a h d)"))
                cp("dve", memV[:, 1 + b, t2, :], cmt[:, 512:1024], [cmt.t], [memV.t])
                cp("dve", cmb[:], cmt[:, 0:512], [cmt.t], [cmb.t])
                pt = psum[6]
                pv = psb(6)
                for hm in range(4):
                    tr(pv[:, hm * 128:(hm + 1) * 128], cmb[:, hm * 128:(hm + 1) * 128], ident_b[:],
                       [cmb.t, ident_b.t], [pt.t])
                cp("act", memKT[:, 1 + b, :, t2 * 128:(t2 + 1) * 128],
                   pv[:, 0:512].rearrange("p (h m) -> p h m", h=4), [pt.t], [memKT.t])
        A.release(m0)

    phase_M(0)

    mA = A.mark()
    wb4 = [A.tile([128, KC, 128], BF16, "wb%d" % i) for i in range(4)]
    qT = A.tile([128, 512], BF16, "qT")
    kT = A.tile([128, 512], BF16, "kT")
    kpT = A.tile([128, 512], BF16, "kpT")
    vT = A.tile([128, 512], BF16, "vT")
    ogT = A.tile([128, 512], BF16, "ogT")
    mixo = A.tile([128, NTOK], BF16, "mixo")
    ebl = A.tile([128, 16], F32, "ebl")
    kp_tm = A.tile([32, 16, 128], BF16, "kp_tm")
    v_tm = A.tile([32, 16, 128], BF16, "v_tm")
    sig = A.tile([128, 512], F32, "sig")
    q32 = A.tile([128, 512], F32, "q32")
    k32 = A.tile([128, 512], F32, "k32")
    lf = A.tile([128, 512], F32, "lf")
    bb = A.tile([128, 512], F32, "bb")
    e1 = A.tile([128, 512], F32, "e1")
    e2 = A.tile([128, 512], F32, "e2")
    kf = A.tile([128, 512], F32, "kf")
    Sst = A.tile([128, 128], F32, "Sst")
    Sbf = [A.tile([128, 128], BF16, "Sbf%d" % i) for i in range(2)]
    AT = A.tile([32, 512], BF16, "AT")
    osq = A.tile([128, 512], BF16, "osq")
    rstd = A.tile([128, 512], F32, "rstd")
    tmpn = A.tile([128, 512], F32, "tmpn")
    o32 = A.tile([128, 512], F32, "o32")
    U_tok = [P.tok("U%d" % i) for i in range(4)]

    def proj(widx, ti, pbank):
        c0, n = CT[ti]
        pt = psum[pbank]
        for kc in range(KC):
            mm(pt[:, 0:n], wb4[widx][:, kc, :], xnT[:, kc, c0:c0 + n], kc == 0, kc == KC - 1,
               [wb4[widx].t, xn_tok[ti]], [pt.t])
        return pt

    for h in (range(int(os.environ.get('NH', H))) if 'A' in PH else []):
        for j in range(4):
            col = j * W + h * 128
            load_cast(wb4[j][:], wb4[j].t,
                      w_in_a[0, :, col:col + 128].rearrange("(k p) c -> p k c", p=128), [128, KC, 128])
        sbi = 0
        for ti in range(5):
            c0, n = CT[ti]
            L = 32 if ti < 4 else TS
            nch = n // L
            ch0 = c0 // 32 if ti < 4 else 64
            pq = proj(0, ti, 0)
            act(q32[:, 0:n], pq[:, 0:n], AF.Silu, [pq.t], [q32.t])
            pf = proj(1, ti, 1)
            act(sig[:, 0:n], pf[:, 0:n], AF.Sigmoid, [pf.t], [sig.t], scale=-1.0)
            ts(k32[:, 0:n], sig[:, 0:n], oml[:, h:h + 1], None, ALU.mult, None, [sig.t, oml.t], [k32.t])
            pvv = proj(2, ti, 0)
            cp("act", vT[:, 0:n], pvv[:, 0:n], [pvv.t], [vT.t])
            po = proj(3, ti, 1)
            act(ogT[:, 0:n], po[:, 0:n], AF.Sigmoid, [po.t], [ogT.t])
            act(lf[:, 0:n], k32[:, 0:n], AF.Ln, [k32.t], [lf.t], scale=-1.0, bias=1.0)
            rm = rmask if ti < 4 else rmask_s
            P.op("dve", lambda e, n=n, rm=rm: e.tensor_tensor_scan(out=bb[:, 0:n], data0=rm[:, 0:n], data1=lf[:, 0:n],
                                                                 initial=0.0, op0=ALU.mult, op1=ALU.add),
                 [rm.t, lf.t], [bb.t])
            act(e1[:, 0:n], bb[:, 0:n], AF.Exp, [bb.t], [e1.t])
            act(e2[:, 0:n], bb[:, 0:n], AF.Exp, [bb.t], [e2.t], scale=-1.0)
            blast = bb[:, 0:n].rearrange("p (c j) -> p c j", j=L)[:, :, L - 1]
            act(ebl[:, 0:nch], blast, AF.Exp, [bb.t], [ebl.t])
            tt(qT[:, 0:n], q32[:, 0:n], e1[:, 0:n], ALU.mult, [q32.t, e1.t], [qT.t])
            tt(kf[:, 0:n], k32[:, 0:n], e2[:, 0:n], ALU.mult, [k32.t, e2.t], [kf.t])
            cp("pool", kT[:, 0:n], kf[:, 0:n], [kf.t], [kT.t])
            tt(kpT[:, 0:n].rearrange("p (c j) -> p c j", j=L),
               kf[:, 0:n].rearrange("p (c j) -> p c j", j=L),
               ebl[:, 0:nch].unsqueeze(2).to_broadcast([128, nch, L]), ALU.mult, [kf.t, ebl.t], [kpT.t])
            p3 = psum[3]
            for ci in range(nch):
                cc = ci * L
                mm(p3[0:L, ci * 32:ci * 32 + L], kT[:, cc:cc + L], qT[:, cc:cc + L], True, True,
                   [kT.t, qT.t], [p3.t])
            tt(AT[0:L, 0:nch * 32].rearrange("p (c j) -> p c j", j=32)[:, :, 0:L],
               p3[0:L, 0:nch * 32].rearrange("p (c j) -> p c j", j=32)[:, :, 0:L],
               cmask[0:L, 0:nch, 0:L], ALU.mult, [p3.t, cmask.t], [AT.t])
            for g0 in range(0, nch, 8):
                g1 = min(nch, g0 + 8)
                for (srcT, dst, pbk) in ((kpT, kp_tm, 2), (vT, v_tm, 7)):
                    pt = psum[pbk]
                    pv2 = psb(pbk)
                    for ci in range(g0, g1):
                        cc = ci * L
                        tr(pv2[0:L, (ci - g0) * 128:(ci - g0 + 1) * 128], srcT[:, cc:cc + L], ident_b[:],
                           [srcT.t, ident_b.t], [pt.t])
                    cp("act", dst[0:L, g0:g1, :],
                       pv2[0:L, 0:(g1 - g0) * 128].rearrange("p (c d) -> p c d", d=128), [pt.t], [dst.t])
            p4 = psum[4]
            for ci in range(nch):
                ch = ch0 + ci
                cc = ci * L
                fresh = (ti < 4 and ch == 0)
                if ti == 4:
                    dma_in(Sst[:], Sst.t, st_in[ci, h])
                    sbi += 1
                    cp("act", Sbf[sbi % 2][:], Sst[:], [Sst.t], [Sbf[sbi % 2].t])
                mm(p4[:, ci * L:(ci + 1) * L], v_tm[0:L, ci, :], AT[0:L, ci * 32:ci * 32 + L], True, fresh,
                   [v_tm.t, AT.t], [p4.t])
                if not fresh:
                    sb = Sbf[sbi % 2]
                    mm(p4[:, ci * L:(ci + 1) * L], sb[:], qT[:, cc:cc + L], False, True, [sb.t, qT.t], [p4.t])
                ut = U_tok[ch % 4]
                pu = psum[5][:, (ch % 4) * 128:(ch % 4 + 1) * 128]
                mm(pu, kp_tm[0:L, ci, :], v_tm[0:L, ci, :], True, True, [kp_tm.t, v_tm.t], [ut])
                if fresh:
                    cp("dve", Sst[:], pu, [ut], [Sst.t])
                else:
                    stt(Sst[:], Sst[:], ebl[:, ci:ci + 1], pu, ALU.mult, ALU.add, [Sst.t, ebl.t, ut], [Sst.t])
                last = (ti < 4 and ch == 63) or ti == 4
                if last:
                    dst = o_hgp[h] if ti < 4 else o_hgs[ci, h]
                    dma_out(dst, Sst[:], Sst.t)
                else:
                    sbi += 1
                    cp("act", Sbf[sbi % 2][:], Sst[:], [Sst.t], [Sbf[sbi % 2].t])
            cp("dve", o32[:, 0:n], p4[:, 0:n], [p4.t], [o32.t])
            act(osq[:, 0:n], o32[:, 0:n], AF.Square, [o32.t], [osq.t])
            p6 = psum[6]
            mm(p6[:, 0:n], ones_b[:], osq[:, 0:n], True, True, [ones_b.t, osq.t], [p6.t])
            rsqrt(rstd[:, 0:n], p6[:, 0:n], 1.0 / 128, [p6.t], [rstd.t])
            tt(tmpn[:, 0:n], o32[:, 0:n], rstd[:, 0:n], ALU.mult, [o32.t, rstd.t], [tmpn.t])
            stt(mixo[:, c0:c0 + n], tmpn[:, 0:n], gnT[:, h:h + 1], ogT[:, 0:n], ALU.mult, ALU.mult,
                [tmpn.t, gnT.t, ogT.t], [mixo.t])
        dma_out(mixT_d[h], mixo[:], mixo.t, writes=[mix_tok[h]])

    pT2 = [A.tile([128, 512], BF16, "pT%d" % i) for i in range(2)]
    rz = A.tile([128, 512], F32, "rz")

    def mem_attn(l, wsrc):
        for hm in (range(4) if 'A' in PH else []):
            load_cast(wb4[0][:], wb4[0].t, wsrc(hm), [128, KC, 128])
            for ti in range(5):
                c0, n = CT[ti]
                pq = proj(0, ti, 0)
                act(qT[:, 0:n], pq[:, 0:n], AF.Copy, [pq.t], [qT.t], scale=SCALE)
                segs = [(0, 0, n)] if ti < 4 else [(1 + b_, b_ * TS, TS) for b_ in range(NS)]
                p4 = psum[4]
                p6 = psum[6]
                for (sq_, o0, nn) in segs:
                    for mt in range(2):
                        pscore = psum[1 if mt == 0 else 7]
                        mm(pscore[:, 0:nn], memKT[:, sq_, hm, mt * 128:(mt + 1) * 128], qT[:, o0:o0 + nn],
                           True, True, [memKT.t, qT.t], [pscore.t])
                        act(pT2[mt][:, 0:nn], pscore[:, 0:nn], AF.Exp, [pscore.t], [pT2[mt].t])
                    for mt in range(2):
                        mm(p4[:, o0:o0 + nn], memV[:, sq_, mt, hm * 128:(hm + 1) * 128], pT2[mt][:, 0:nn],
                           mt == 0, mt == 1, [memV.t, pT2[mt].t], [p4.t])
                    for mt in range(2):
                        mm(p6[:, o0:o0 + nn], ones_b[:], pT2[mt][:, 0:nn], mt == 0, mt == 1,
                           [ones_b.t, pT2[mt].t], [p6.t])
                P.op("dve", lambda e, n=n, p6=p6, rz=rz: e.reciprocal(rz[:, 0:n], p6[:, 0:n]), [p6.t], [rz.t])
                tt(mixo[:, c0:c0 + n], p4[:, 0:n], rz[:, 0:n], ALU.mult, [p4.t, rz.t], [mixo.t])
            dma_out(mixT_d[12 + hm], mixo[:], mixo.t, writes=[mix_tok[12 + hm]])

    mem_attn(0, lambda hm: w_in_a[0, :, 4 * W + hm * 128:4 * W + (hm + 1) * 128].rearrange("(k p) c -> p k c", p=128))
    A.release(mXA)

    def phase_B(l):
        mB = A.mark()
        wo_b = A.tile([128, KC, D], BF16, "wo_b")
        for kc in range(KC):
            load_cast(wo_b[:, kc, :], wo_b.t, w_o[l, kc * 128:(kc + 1) * 128, :], [128, D])
        mt_t = A.tile([128, KC, 256], BF16, "mt_t")
        h_t = A.tile([128, KC, 256], F32, "h_t")
        o_t = A.tile([128, KC, 256], F32, "o_t")
        x2_t = A.tile([128, KC, 256], BF16, "x2_t")
        sqb = [A.tile([128, 512], BF16, "sqb%d" % i) for i in range(2)]
        r1 = A.tile([128, 512], F32, "r1")
        r2 = A.tile([128, 512], F32, "r2")
        tmpb = A.tile([128, 512], F32, "tmpb")
        for ti, c0, n in [(ti, CT[ti][0] + hf * 256, min(256, CT[ti][1] - hf * 256)) for ti in range(5)
                          for hf in range(2) if CT[ti][1] - hf * 256 > 0]:
            P.dma("sp", lambda e, c0=c0, n=n: [e.dma_start(out=mt_t[:, :, 0:n],
                                                           in_=mixT_d[:, :, c0:c0 + n].rearrange("k p t -> p k t"))],
                  1, mt_t.t, reads=mix_tok, writes=[mt_t.t])
            P.dma("sp", lambda e, c0=c0, n=n: [e.dma_start(out=h_t[:, :, 0:n],
                                                           in_=hT_d[:, :, c0:c0 + n].rearrange("k p t -> p k t"))],
                  1, h_t.t, reads=[hT_tok[k][ti] for k in range(KC)], writes=[h_t.t])
            for oc in range(KC):
                po = psum[oc % 2]
                for kc in range(KC):
                    mm(po[:, 0:n], wo_b[:, kc, oc * 128:(oc + 1) * 128], mt_t[:, kc, 0:n], kc == 0, kc == KC - 1,
                       [wo_b.t, mt_t.t], [po.t])
                cp("dve", o_t[:, oc, 0:n], po[:, 0:n], [po.t], [o_t.t])
                sb = sqb[oc % 2]
                act(sb[:, 0:n], o_t[:, oc, 0:n], AF.Square, [o_t.t], [sb.t])
                mm(psum[2][:, 0:n], ones_b[:], sb[:, 0:n], oc == 0, oc == KC - 1, [ones_b.t, sb.t], [psum[2].t])
            rsqrt(r1[:, 0:n], psum[2][:, 0:n], 1.0 / D, [psum[2].t], [r1.t])
            for oc in range(KC):
                tt(tmpb[:, 0:n], o_t[:, oc, 0:n], r1[:, 0:n], ALU.mult, [o_t.t, r1.t], [tmpb.t])
                stt(h_t[:, oc, 0:n], tmpb[:, 0:n], gT[:, l * 4 + 1, oc:oc + 1], h_t[:, oc, 0:n], ALU.mult, ALU.add,
                    [tmpb.t, gT.t, h_t.t], [h_t.t])
                sb = sqb[oc % 2]
                act(sb[:, 0:n], h_t[:, oc, 0:n], AF.Square, [h_t.t], [sb.t])
                mm(psum[3][:, 0:n], ones_b[:], sb[:, 0:n], oc == 0, oc == KC - 1, [ones_b.t, sb.t], [psum[3].t])
            rsqrt(r2[:, 0:n], psum[3][:, 0:n], 1.0 / D, [psum[3].t], [r2.t])
            for oc in range(KC):
                stt(x2_t[:, oc, 0:n], h_t[:, oc, 0:n], gT[:, l * 4 + 2, oc:oc + 1], r2[:, 0:n], ALU.mult, ALU.mult,
                    [h_t.t, gT.t, r2.t], [x2_t.t])
            P.dma("pool", lambda e, c0=c0, n=n: [e.dma_start(out=hT_d[:, :, c0:c0 + n].rearrange("k p t -> p k t"),
                                                             in_=h_t[:, :, 0:n])],
                  1, h_t.t, reads=[h_t.t], writes=[hT_tok[k][ti] for k in range(KC)])
            P.dma("pool", lambda e, c0=c0, n=n: [e.dma_start(out=xn2T_d[:, :, c0:c0 + n].rearrange("k p t -> p k t"),
                                                             in_=x2_t[:, :, 0:n])],
                  1, x2_t.t, reads=[x2_t.t], writes=[xn2_tok[ti]])
        A.release(mB)

    if 'B' in PH:
        phase_B(0)

    BLK = [(0, 688), (688, 688), (1376, 688)]
    SO = T - 1376
    LASTB = len(BLK) - 1
    GC = 2.0 * (2.0 / np.pi) ** 0.5

    def phase_C(l):
        mC = A.mark()
        convo = A.tile([128, NJ, 2 + NS * 2], F32, "convo")
        aprev = A.tile([128, NJ, 2], F32, "aprev")
        rf = A.tile([128, 688], F32, "rf")
        r3 = A.tile([128, 688], F32, "r3")
        sqb = [A.tile([128, 512], BF16, "sqc%d" % i) for i in range(2)]
        for bi, (b0, NB) in enumerate(BLK):
            NT = [(o, min(512, NB - o)) for o in range(0, NB, 512)]
            mC1 = A.mark()
            x2b = A.tile([128, KC, 688], BF16, "x2b")
            yT = A.tile([128, NJ, 688], BF16, "yT")
            wab = [[A.tile([128, KC, 128], BF16, "wab%d%d" % (i, k)) for k in range(2)] for i in range(2)]
            a_ext = A.tile([128, 690], F32, "a_ext")
            t1 = A.tile([128, 688], F32, "t1")
            u1 = A.tile([128, 688], F32, "u1")
            exs = A.tile([128, NS, 6], F32, "exs")
            tis = [ti for ti in range(5) if CT[ti][0] < b0 + NB and CT[ti][0] + CT[ti][1] > b0]
            P.dma("sp", lambda e, b0=b0, NB=NB: [e.dma_start(out=x2b[:, :, 0:NB],
                                                             in_=xn2T_d[:, :, b0:b0 + NB].rearrange("k p t -> p k t"))],
                  1, x2b.t, reads=[xn2_tok[ti] for ti in tis], writes=[x2b.t])
            for j in range(NJ):
                wa, wb = wab[j % 2]
                load_cast(wa[:], wa.t, w_ffn_in[l, :, j * 128:(j + 1) * 128].rearrange("(k p) c -> p k c", p=128),
                          [128, KC, 128])
                load_cast(wb[:], wb.t,
                          w_ffn_in[l, :, DFF + j * 128:DFF + (j + 1) * 128].rearrange("(k p) c -> p k c", p=128),
                          [128, KC, 128])
                if bi == 0:
                    memset("pool", a_ext[:, 0:2], 0.0, [a_ext.t])
                else:
                    cp("pool", a_ext[:, 0:2], aprev[:, j, :], [aprev.t], [a_ext.t])
                for ni, (o, n) in enumerate(NT):
                    pa = psum[(j % 2) * 2 + ni]
                    for kc in range(KC):
                        mm(pa[:, 0:n], wa[:, kc, :], x2b[:, kc, o:o + n], kc == 0, kc == KC - 1, [wa.t, x2b.t], [pa.t])
                    cp("act", a_ext[:, 2 + o:2 + o + n], pa[:, 0:n], [pa.t], [a_ext.t])
                for ni, (o, n) in enumerate(NT):
                    pb = psum[4 + (j % 2) * 2 + ni]
                    for kc in range(KC):
                        mm(pb[:, 0:n], wb[:, kc, :], x2b[:, kc, o:o + n], kc == 0, kc == KC - 1, [wb.t, x2b.t], [pb.t])
                w0 = wcT[:, l * 3 + 0, j:j + 1]
                w1 = wcT[:, l * 3 + 1, j:j + 1]
                w2 = wcT[:, l * 3 + 2, j:j + 1]
                bc = bcT[:, l, j:j + 1]
                ts(t1[:, 0:NB], a_ext[:, 2:2 + NB], w2, bc, ALU.mult, ALU.add, [a_ext.t, wcT.t, bcT.t], [t1.t])
                stt(t1[:, 0:NB], a_ext[:, 1:1 + NB], w1, t1[:, 0:NB], ALU.mult, ALU.add, [a_ext.t, wcT.t, t1.t], [t1.t])
                stt(t1[:, 0:NB], a_ext[:, 0:NB], w0, t1[:, 0:NB], ALU.mult, ALU.add, [a_ext.t, wcT.t, t1.t], [t1.t])
                if bi < LASTB:
                    cp("pool", aprev[:, j, :], a_ext[:, NB:NB + 2], [a_ext.t], [aprev.t])
                else:
                    so = SO
                    cp("pool", exs[:, :, 0:2], ccT[:, l, j, :].rearrange("p (b r) -> p b r", r=2), [ccT.t], [exs.t])
                    cp("pool", exs[:, :, 2:6], a_ext[:, 2 + so:2 + so + 16].rearrange("p (b t) -> p b t", t=TS),
                       [a_ext.t], [exs.t])
                    t1s = t1[:, so:so + 16].rearrange("p (b t) -> p b t", t=TS)
                    ts(t1s, exs[:, :, 2:6], w2, bc, ALU.mult, ALU.add, [exs.t, wcT.t, bcT.t, t1.t], [t1.t])
                    stt(t1s, exs[:, :, 1:5], w1, t1s, ALU.mult, ALU.add, [exs.t, wcT.t, t1.t], [t1.t])
                    stt(t1s, exs[:, :, 0:4], w0, t1s, ALU.mult, ALU.add, [exs.t, wcT.t, t1.t], [t1.t])
                    cp("pool", convo[:, j, 0:2], a_ext[:, SO:SO + 2], [a_ext.t], [convo.t])
                    cp("pool", convo[:, j, 2:2 + NS * 2].rearrange("p (b r) -> p b r", r=2), exs[:, :, 4:6],
                       [exs.t], [convo.t])
                act(u1[:, 0:NB], t1[:, 0:NB], AF.Square, [t1.t], [u1.t])
                act(u1[:, 0:NB], u1[:, 0:NB], AF.Identity, [u1.t, one_t.t], [u1.t], scale=0.044715, bias=one_t[:, 0:1])
                tt(u1[:, 0:NB], u1[:, 0:NB], t1[:, 0:NB], ALU.mult, [u1.t, t1.t], [u1.t])
                act(u1[:, 0:NB], u1[:, 0:NB], AF.Sigmoid, [u1.t], [u1.t], scale=GC)
                tt(u1[:, 0:NB], u1[:, 0:NB], t1[:, 0:NB], ALU.mult, [u1.t, t1.t], [u1.t], eng="pool")
                for ni, (o, n) in enumerate(NT):
                    pb = psum[4 + (j % 2) * 2 + ni]
                    tt(yT[:, j, o:o + n], pb[:, 0:n], u1[:, o:o + n], ALU.mult, [pb.t, u1.t], [yT.t])
            wo2 = [A.tile([128, NJ, 128], BF16, "wo2_%d" % i) for i in range(2)]
            fsb = [A.tile([128, 688], F32, "fsb%d" % i) for i in range(2)]
            for oc in range(KC):
                wt = wo2[oc % 2]
                for (j0, j1) in ((0, 16), (16, 32), (32, NJ)):
                    load_cast(wt[:, j0:j1, :], wt.t,
                              w_ffn_out[l, j0 * 128:j1 * 128, oc * 128:(oc + 1) * 128].rearrange("(j p) c -> p j c", p=128),
                              [128, j1 - j0, 128])
                fs = fsb[oc % 2]
                for ni, (o, n) in enumerate(NT):
                    pf_ = psum[ni]
                    for j in range(NJ):
                        mm(pf_[:, 0:n], wt[:, j, :], yT[:, j, o:o + n], j == 0, j == NJ - 1, [wt.t, yT.t], [pf_.t])
                    cp("dve", fs[:, o:o + n], pf_[:, 0:n], [pf_.t], [fs.t])
                    sb = sqb[ni % 2]
                    act(sb[:, 0:n], fs[:, o:o + n], AF.Square, [fs.t], [sb.t])
                    mm(psum[3 + ni][:, 0:n], ones_b[:], sb[:, 0:n], oc == 0, oc == KC - 1, [ones_b.t, sb.t],
                       [psum[3 + ni].t])
                P.dma("pool", lambda e, fs=fs, oc=oc, b0=b0, NB=NB: [e.dma_start(out=fT_d[oc, :, b0:b0 + NB],
                                                                               in_=fs[:, 0:NB])],
                      1, fs.t, reads=[fs.t], writes=[fT_tok[oc]])
            for ni, (o, n) in enumerate(NT):
                rsqrt(rf[:, o:o + n], psum[3 + ni][:, 0:n], 1.0 / D, [psum[3 + ni].t], [rf.t])
            A.release(mC1)
            mC2 = A.mark()
            h2 = A.tile([128, KC, 688], F32, "h2")
            fl = [A.tile([128, 688], F32, "fl%d" % i) for i in range(2)]
            xo = [A.tile([128, 688], BF16, "xo%d" % i) for i in range(2)]
            P.dma("sp", lambda e, b0=b0, NB=NB: [e.dma_start(out=h2[:, :, 0:NB],
                                                             in_=hT_d[:, :, b0:b0 + NB].rearrange("k p t -> p k t"))],
                  1, h2.t, reads=[hT_tok[k][ti] for k in range(KC) for ti in tis], writes=[h2.t])
            for oc in range(KC):
                f_ = fl[oc % 2]
                P.dma("sp", lambda e, f_=f_, oc=oc, b0=b0, NB=NB: [e.dma_start(out=f_[:, 0:NB],
                                                                              in_=fT_d[oc, :, b0:b0 + NB])],
                      1, f_.t, reads=[fT_tok[oc]], writes=[f_.t])
                tt(f_[:, 0:NB], f_[:, 0:NB], rf[:, 0:NB], ALU.mult, [f_.t, rf.t], [f_.t])
                stt(h2[:, oc, 0:NB], f_[:, 0:NB], gT[:, l * 4 + 3, oc:oc + 1], h2[:, oc, 0:NB], ALU.mult, ALU.add,
                    [f_.t, gT.t, h2.t], [h2.t])
                for ni, (o, n) in enumerate(NT):
                    sb = sqb[ni % 2]
                    act(sb[:, 0:n], h2[:, oc, o:o + n], AF.Square, [h2.t], [sb.t])
                    mm(psum[ni][:, 0:n], ones_b[:], sb[:, 0:n], oc == 0, oc == KC - 1, [ones_b.t, sb.t], [psum[ni].t])
            if l == 0:
                for ni, (o, n) in enumerate(NT):
                    rsqrt(r3[:, o:o + n], psum[ni][:, 0:n], 1.0 / D, [psum[ni].t], [r3.t])
                P.dma("pool", lambda e, b0=b0, NB=NB: [e.dma_start(out=hT_d[:, :, b0:b0 + NB].rearrange("k p t -> p k t"),
                                                                 in_=h2[:, :, 0:NB])],
                      1, h2.t, reads=[h2.t], writes=[hT_tok[k][ti] for k in range(KC) for ti in tis])
                for (gsel, dst, dtok) in ((gT[:, 4, :], xnA_d, xnA_tok[bi]), (kvgT[:, :], xkv_d, xkv_tok[bi])):
                    for oc in range(KC):
                        x_ = xo[oc % 2]
                        stt(x_[:, 0:NB], h2[:, oc, 0:NB], gsel[:, oc:oc + 1], r3[:, 0:NB], ALU.mult, ALU.mult,
                            [h2.t, gT.t, kvgT.t, r3.t], [x_.t])
                        P.dma("pool", lambda e, x_=x_, oc=oc, b0=b0, NB=NB, dst=dst: [
                            e.dma_start(out=dst[oc, :, b0:b0 + NB], in_=x_[:, 0:NB])],
                            1, x_.t, reads=[x_.t], writes=[dtok])
            else:
                yt = [A.tile([128, D], F32, "yt%d" % i) for i in range(2)]
                segs_ = []
                pend = NB if bi < LASTB else SO
                for s0 in range(0, pend, 128):
                    segs_.append((s0, min(128, pend - s0)))
                if bi == LASTB:
                    segs_.append((SO, NS * TS))
                for si, (s0, rows) in enumerate(segs_):
                    y_ = yt[si % 2]
                    for q4 in range(4):
                        pt = psum[4 + q4 % 2]
                        for k in range(4):
                            oc = q4 * 4 + k
                            tr(pt[0:rows, k * 128:(k + 1) * 128], h2[:, oc, s0:s0 + rows], ident_f[:],
                               [h2.t, ident_f.t], [pt.t])
                        cp("act", y_[0:rows, q4 * 512:(q4 + 1) * 512], pt[0:rows, :], [pt.t], [y_.t])
                    g0_ = b0 + s0
                    dst = o_yp[g0_:g0_ + rows, :] if g0_ < T else o_ys[:, :]
                    dma_out(dst, y_[0:rows, :], y_.t)
            A.release(mC2)
        cvt = A.tile([2 + NS * 2, DFF], F32, "cvt")
        nr = 2 + NS * 2
        for j0 in range(0, NJ, 4):
            pt = psum[6 + (j0 // 4) % 2]
            for k in range(4):
                tr(pt[0:nr, k * 128:(k + 1) * 128], convo[:, j0 + k, :], ident_f[:], [convo.t, ident_f.t], [pt.t])
            cp("act", cvt[:, j0 * 128:(j0 + 4) * 128], pt[0:nr, :], [pt.t], [cvt.t])
        dma_out(o_cvp[l], cvt[0:2, :], cvt.t)
        dma_out(o_cvs[l].rearrange("b r f -> (b r) f"), cvt[2:nr, :], cvt.t)
        A.release(mC)

    if 'C' in PH:
        phase_C(0)

    kvT_d = nc.dram_tensor("kvT_d", [8, 128, NTOK], BF16, kind="Internal").ap()
    kvT_tok = P.tok("kvT_d")
    vtm_d = nc.dram_tensor("vtm_d", [17, 128, 512], BF16, kind="Internal").ap()
    vtm_tok = P.tok("vtm_d")
    FB = [0, 1, 2, 3, 4, 5, 8, 9]

    def phase_K():
        mK = A.mark()
        wkv = A.tile([128, KC, 1536], BF16, "wkv")
        for kc in range(KC):
            load_cast(wkv[:, kc, :], wkv.t, w_kv_b[kc * 128:(kc + 1) * 128, :], [128, 1536])
        xk2 = [A.tile([128, KC, 512], BF16, "xk%d" % i) for i in range(2)]
        kvo = [A.tile([128, 1536], F32, "kvo%d" % i) for i in range(2)]
        vst = [A.tile([128, 512], BF16, "vst%d" % i) for i in range(2)]
        kst = [A.tile([128, 8, 512], BF16, "kst%d" % i) for i in range(2)]
        i = 0
        for ti in range(5):
            c0t, nt = CT[ti]
            xk = xk2[ti % 2]
            P.dma("sp", lambda e, xk=xk, c0t=c0t, nt=nt: [e.dma_start(out=xk[:, :, 0:nt],
                                                                       in_=xkv_d[:, :, c0t:c0t + nt].rearrange("k p t -> p k t"))],
                  1, xk.t, reads=xkv_tok, writes=[xk.t])
            for sub in range((nt + 127) // 128):
                rows = min(128, nt - sub * 128)
                c0 = c0t + sub * 128
                lo = sub * 128
                ko = kvo[i % 2]
                vs = vst[i % 2]
                for nb in range(3):
                    pt = psum[nb + 3 * (i % 2)]
                    for kc in range(KC):
                        mm(pt[0:rows, :], xk[:, kc, lo:lo + rows], wkv[:, kc, nb * 512:(nb + 1) * 512], kc == 0,
                           kc == KC - 1, [xk.t, wkv.t], [pt.t])
                    cp("act" if nb % 2 else "dve", ko[0:rows, nb * 512:(nb + 1) * 512], pt[0:rows, :], [pt.t], [ko.t])
                cp("pool", vs[0:rows, 0:256], ko[0:rows, 768:1024], [ko.t], [vs.t])
                cp("pool", vs[0:rows, 256:512], ko[0:rows, 1280:1536], [ko.t], [vs.t])
                P.dma("pool", lambda e, vs=vs, i=i, rows=rows: [e.dma_start(out=vtm_d[i, 0:rows, :], in_=vs[0:rows, :])],
                      1, vs.t, reads=[vs.t], writes=[vtm_tok])
                if i < 16:
                    dma_out(o_kvp[c0:c0 + 128, :], ko[:, 0:1024], ko.t)
                    if i >= 12:
                        dma_out(o_winp[(i - 12) * 128:(i - 11) * 128, :], ko[:, 1024:1536], ko.t)
                else:
                    dma_out(o_kvs[:, :], ko[0:rows, 0:1024], ko.t)
                    dma_out(o_wins[:, :], ko[0:rows, 1024:1536], ko.t)
                i += 1
            ks = kst[ti % 2]
            for fi, cb in enumerate(FB):
                pt = psum[6 + fi % 2]
                for kc in range(KC):
                    mm(pt[:, 0:nt], wkv[:, kc, cb * 128:(cb + 1) * 128], xk[:, kc, 0:nt], kc == 0, kc == KC - 1,
                       [xk.t, wkv.t], [pt.t])
                cp("act", ks[:, fi, 0:nt], pt[:, 0:nt], [pt.t], [ks.t])
            P.dma("pool", lambda e, ks=ks, c0t=c0t, nt=nt: [e.dma_start(out=kvT_d[:, :, c0t:c0t + nt].rearrange("f p t -> p f t"),
                                                                        in_=ks[:, :, 0:nt])],
                  1, ks.t, reads=[ks.t], writes=[kvT_tok])
        A.release(mK)

    if 'K' in PH:
        phase_K()

    SLOPE = [2.0 ** (-8.0 * (i + 1) / H) for i in range(H)]
    NEGM = -30000.0

    def phase_N():
        dist0 = A.tile([128, 512], F32, "dist0")
        distc = A.tile([128, 512], F32, "distc")
        itmp = A.tile([128, 512], I32, "itmp")
        P.op("pool", lambda e: e.iota(itmp[:], pattern=[[1, 512]], base=0, channel_multiplier=-1), (), [itmp.t])
        cp("pool", dist0[:], itmp[:], [itmp.t], [dist0.t])
        P.op("pool", lambda e: e.iota(itmp[:], pattern=[[1, 512]], base=-31, channel_multiplier=-16), [dist0.t], [itmp.t])
        cp("pool", distc[:], itmp[:], [itmp.t], [distc.t])
        maug = A.tile([128, 33], BF16, "maug")
        memset("pool", maug[:], 1.0, [maug.t])
        P.op("pool", lambda e: e.affine_select(out=maug[:], in_=maug[:], pattern=[[-4, 33]], compare_op=ALU.is_ge,
                                               fill=0.0, base=1, channel_multiplier=1), [maug.t], [maug.t])
        P.op("pool", lambda e: e.affine_select(out=maug[:], in_=maug[:], pattern=[[4, 33]], compare_op=ALU.is_ge,
                                               fill=0.0, base=3, channel_multiplier=-1), [maug.t], [maug.t])
        memset("pool", maug[:, 32:33], 1.0, [maug.t])
        eall = A.tile([32, 16, 128], BF16, "eall")
        memset("pool", eall[:], 1.0, [eall.t])
        P.op("pool", lambda e: e.affine_select(out=eall[:], in_=eall[:], pattern=[[128, 16], [1, 128]],
                                               compare_op=ALU.is_ge, fill=0.0, base=0, channel_multiplier=-64),
             [eall.t], [eall.t])
        P.op("pool", lambda e: e.affine_select(out=eall[:], in_=eall[:], pattern=[[-128, 16], [-1, 128]],
                                               compare_op=ALU.is_ge, fill=0.0, base=63, channel_multiplier=64),
             [eall.t], [eall.t])
        memset("pool", sel36[:], 1.0, [sel36.t])
        P.op("pool", lambda e: e.affine_select(out=sel36[:], in_=sel36[:], pattern=[[-1, 36], [0, 128]],
                                               compare_op=ALU.is_equal, fill=0.0, base=0, channel_multiplier=1),
             [sel36.t], [sel36.t])
        validm = A.tile([128, 16, 32], F32, "validm")
        forced = A.tile([128, 16, 32], F32, "forced")
        VB = A.tile([128, 16, 32], F32, "VB")
        memset("pool", validm[:], 1.0, [validm.t])
        P.op("pool", lambda e: e.affine_select(out=validm[:], in_=validm[:], pattern=[[128, 16], [-64, 32]],
                                               compare_op=ALU.is_ge, fill=0.0, base=0, channel_multiplier=1),
             [validm.t], [validm.t])
        P.op("pool", lambda e: e.affine_select(out=forced[:], in_=validm[:], pattern=[[-128, 16], [64, 32]],
                                               compare_op=ALU.is_gt, fill=0.0, base=128, channel_multiplier=-1),
             [validm.t], [forced.t])
        memset("pool", forced[:, :, 0:1], 1.0, [forced.t])
        ts(VB[:], validm[:], -1.0, 1e30, ALU.add, ALU.mult, [validm.t], [VB.t], eng="pool")
        stt(VB[:], forced[:], 1e4, VB[:], ALU.mult, ALU.add, [forced.t, VB.t], [VB.t])
        tiny = 1e-30

        g36 = A.tile([36, NTOK], BF16, "g36")
        kcT = A.tile([128, 2, 128], BF16, "kcT")
        vc_tm = A.tile([128, 2, 128], BF16, "vc_tm")
        qh = A.tile([128, NTOK], BF16, "qh")
        impacc = A.tile([128, 2, 16, 32], F32, "impacc")
        bt = [None, None]
        mk = [A.tile([128, 512], F32, "mk0")] * 2
        BIG = 1.0e6
        dmc = [A.tile([128, 512], F32, "dmc%d" % i) for i in range(4)]
        for qt_ in range(4):
            ts(mk[0][:], distc[:], float(-512 * qt_), BIG, ALU.is_lt, ALU.mult, [distc.t], [mk[0].t], eng="pool")
            stt(dmc[qt_][:], distc[:], float(512 * qt_), mk[0][:], ALU.add, ALU.add, [distc.t, mk[0].t], [dmc[qt_].t])
        kk16 = A.tile([128, 16], F32, "kk16")
        P.op("pool", lambda e: e.iota(itmp[:, 0:16], pattern=[[128, 16]], base=0, channel_multiplier=0), [distc.t],
             [itmp.t])
        cp("pool", kk16[:], itmp[:, 0:16], [itmp.t], [kk16.t])
        hb = A.tile([128, 16], F32, "hb")
        sc = [A.tile([128, 512], F32, "sc%d" % i) for i in range(2)]
        pc = [A.tile([128, 512], BF16, "pc%d" % i) for i in range(2)]
        rzt = A.tile([128, 512], F32, "rzt")
        o32 = A.tile([128, 512], F32, "o32n")
        ps33 = A.tile([128, 4, 33], F32, "ps33")
        zr = A.tile([128, 4], F32, "zr")
        mN1 = A.mark()
        xnT = A.tile([128, KC, NTOK], BF16, "xnT1")
        P.dma("sp", lambda e: [e.dma_start(out=xnT[:], in_=xnA_d.rearrange("k p t -> p k t"))], 1, xnT.t,
              reads=xnA_tok, writes=[xnT.t])
        wq = A.tile([128, KC, 128], BF16, "wq")

        def projn(wt, ncols, ti, pbank):
            c0, n = CT[ti]
            pt = psum[pbank]
            for kc in range(KC):
                mm(pt[0:ncols, 0:n], wt[:, kc, 0:ncols], xnT[:, kc, c0:c0 + n], kc == 0, kc == KC - 1,
                   [wt.t, xnT.t], [pt.t])
            return pt

        load_cast(wq[:, :, 0:36], wq.t, w_in_b[0, :, W:W + 36].rearrange("(k p) c -> p k c", p=128), [128, KC, 36])
        for ti in range(5):
            c0, n = CT[ti]
            pt = projn(wq, 36, ti, ti % 2)
            act(g36[:, c0:c0 + n], pt[0:36, 0:n], AF.Sigmoid, [pt.t], [g36.t])
        cp("pool", g36s[:], g36[:, T:NTOK], [g36.t], [g36s.t])

        mcp = A.mark()
        kvc = A.tile([128, 2, T], BF16, "kvc")
        w1b = A.tile([128, 32, 128], BF16, "w1b")
        w2b = A.tile([128, 2, 128], BF16, "w2b")
        pe32 = A.tile([128, 2, 32], F32, "pe32")
        peb = A.tile([128, 2, 32], BF16, "peb")
        preb = A.tile([128, 2], F32, "preb")
        tg = A.tile([128, 128], F32, "tg")
        ug = A.tile([128, 128], F32, "ug")
        gl = A.tile([128, 128], BF16, "gl")
        dma_in(pe32[:], pe32.t, cmp_pos.rearrange("k j d -> d k j"), nonc=True)
        cp("pool", peb[:], pe32[:], [pe32.t], [peb.t])
        for kv in range(2):
            load_cast(w2b[:, kv, :], w2b.t, w_cmp2[kv], [128, 128])
        for kv in range(2):
            for hf in range(2):
                load_cast(w1b[:, hf * 16:(hf + 1) * 16, :], w1b.t,
                          w_cmp1[kv, hf * 2048:(hf + 1) * 2048, :].rearrange("(j d) h -> d j h", d=128), [128, 16, 128])
            pp = psum[2]
            for js in range(32):
                mm(pp[:, 0:1], w1b[:, js, :], peb[:, kv, js:js + 1], js == 0, js == 31, [w1b.t, peb.t], [pp.t])
            cp("dve", preb[:, kv:kv + 1], pp[:, 0:1], [pp.t], [preb.t])
            P.dma("sp", lambda e, kv=kv: [e.dma_start(out=kvc[:], in_=kvT_d[kv * 2:kv * 2 + 2, :, 0:T].rearrange("f p t -> p f t"))],
                  1, kvc.t, reads=[kvT_tok], writes=[kvc.t])
            for g in range(2):
                kview = kvc[:, g, :].rearrange("p (c s) -> p c s", s=16)
                pp = psum[3]
                for js in range(32):
                    j_, s_ = js // 16, js % 16
                    mm(pp[:, 0:127], w1b[:, js, :], kview[:, j_:j_ + 127, s_], js == 0, js == 31, [w1b.t, kvc.t], [pp.t])
                act(tg[:, 0:127], pp[:, 0:127], AF.Identity, [pp.t, preb.t], [tg.t], bias=preb[:, kv:kv + 1])
                act(ug[:, 0:127], tg[:, 0:127], AF.Square, [tg.t], [ug.t])
                ts(ug[:, 0:127], ug[:, 0:127], 0.044715, 1.0, ALU.mult, ALU.add, [ug.t], [ug.t])
                tt(ug[:, 0:127], ug[:, 0:127], tg[:, 0:127], ALU.mult, [ug.t, tg.t], [ug.t])
                act(ug[:, 0:127], ug[:, 0:127], AF.Sigmoid, [ug.t], [ug.t], scale=GC)
                tt(gl[:, 0:127], ug[:, 0:127], tg[:, 0:127], ALU.mult, [ug.t, tg.t], [gl.t])
                pq_ = psum[4]
                if kv == 0:
                    mm(pq_[:, 0:127], w2b[:, 0, :], gl[:, 0:127], True, True, [w2b.t, gl.t], [pq_.t])
                    cp("act", kcT[:, g, 0:127], pq_[:, 0:127], [pq_.t], [kcT.t])
                else:
                    mm(pq_[0:127, 0:128], gl[:, 0:127], w2b[:, 1, :], True, True, [w2b.t, gl.t], [pq_.t])
                    cp("act", vc_tm[0:127, g, :], pq_[0:127, 0:128], [pq_.t], [vc_tm.t])
        A.release(mcp)

        memset("pool", impacc[:], 0.0, [impacc.t])
        it = 0
        for hh in range(H):
            g, r = hh // 6, hh % 6
            sl = SLOPE[hh]
            load_cast(wq[:], wq.t, w_in_b[0, :, hh * 128:(hh + 1) * 128].rearrange("(k p) c -> p k c", p=128),
                      [128, KC, 128])
            for ti in range(5):
                c0, n = CT[ti]
                pt = projn(wq, 128, ti, ti % 2)
                act(qh[:, c0:c0 + n], pt[:, 0:n], AF.Copy, [pt.t], [qh.t], scale=SCALE)
            dma_out(qT_d[hh], qh[:], qh.t, writes=[qT_tok[hh]])
            for qt in range(4):
                t0 = qt * 512
                b_, m_, s_, p_ = bt[it % 2], mk[it % 2], sc[it % 2], pc[it % 2]
                it += 1
                psc = psum[2 + it % 2]
                mm(psc[0:127, :], kcT[:, g, 0:127], qh[:, t0:t0 + 512], True, True, [kcT.t, qh.t], [psc.t])
                stt(s_[0:127, :], dmc[qt][0:127, :], -sl, psc[0:127, :], ALU.mult, ALU.add, [dmc[qt].t, psc.t], [s_.t])
                act(p_[0:127, :], s_[0:127, :], AF.Exp, [s_.t], [p_.t])
                po, pz, pp = psum[4], psum[5], psum[6]
                mm(po[:, :], vc_tm[0:127, g, :], p_[0:127, :], True, True, [vc_tm.t, p_.t], [po.t])
                mm(pz[:, :], ones_b[0:127, :], p_[0:127, :], True, True, [ones_b.t, p_.t], [pz.t])
                for sub in range(4):
                    mm(pp[:, sub * 33:(sub + 1) * 33], p_[0:127, sub * 128:(sub + 1) * 128], maug[0:127, :], True, True,
                       [p_.t, maug.t], [pp.t])
                ts(rzt[:], pz[:], tiny, None, ALU.max, None, [pz.t], [rzt.t])
                P.op("dve", lambda e: e.reciprocal(rzt[:], rzt[:]), [rzt.t], [rzt.t])
                tt(o32[:], po[:], rzt[:], ALU.mult, [po.t, rzt.t], [o32.t])
                P.dma("pool", lambda e, hh=hh, t0=t0: [e.dma_start(out=ocmp_d[hh, :, t0:t0 + 512], in_=o32[:])], 1, o32.t,
                      reads=[o32.t], writes=[ocmp_tok[hh]])
                cp("dve", ps33[:], pp[:, 0:132].rearrange("p (a b) -> p a b", b=33), [pp.t], [ps33.t])
                ts(zr[:], ps33[:, :, 32], tiny, None, ALU.max, None, [ps33.t], [zr.t])
                P.op("dve", lambda e: e.reciprocal(zr[:], zr[:]), [zr.t], [zr.t])
                for sub in range(4):
                    stt(impacc[:, g, qt * 4 + sub, :], ps33[:, sub, 0:32], zr[:, sub:sub + 1],
                        impacc[:, g, qt * 4 + sub, :], ALU.mult, ALU.add, [ps33.t, zr.t, impacc.t], [impacc.t])

        nonlocal wb4, qT, mixo, pT2, rz, xn_tok
        wb4 = [A.tile([128, KC, 128], BF16, "wbn")]
        qT = A.tile([128, 512], BF16, "qTn")
        mixo = A.tile([128, NTOK], BF16, "mixo2")
        pT2 = [A.tile([128, 512], BF16, "pTn%d" % i) for i in range(2)]
        rz = A.tile([128, 512], F32, "rzn")
        xn_tok = [xnT.t] * 5
        mem_attn_l1(xnT)
        A.release(mN1)

        kvs = A.tile([128, 4, T], BF16, "kvs")
        P.dma("sp", lambda e: [e.dma_start(out=kvs[:], in_=kvT_d[4:8, :, 0:T].rearrange("f p t -> p f t"))], 1, kvs.t,
              reads=[kvT_tok], writes=[kvs.t])
        vtm = A.tile([128, 16, 512], BF16, "vtm")
        P.dma("sp", lambda e: [e.dma_start(out=vtm[:], in_=vtm_d[0:16].rearrange("i p c -> p i c"))], 1, vtm.t,
              reads=[vtm_tok], writes=[vtm.t])
        negselT = A.tile([32, 2, T], BF16, "negselT")
        dmb = {}
        for d_ in (-384, -256, -128, 0):
            tl_ = A.tile([128, 512], F32, "dmb%d" % (-d_))
            ts(mk[0][:], dist0[:], float(-d_), BIG, ALU.is_lt, ALU.mult, [dist0.t], [mk[0].t], eng="pool")
            stt(tl_[:], dist0[:], float(d_), mk[0][:], ALU.add, ALU.add, [dist0.t, mk[0].t], [tl_.t])
            dmb[d_] = tl_
        dmw = {}
        for d_ in (128, 256, 384, 512):
            tl_ = A.tile([128, 512], F32, "dmw%d" % d_)
            ts(mk[0][:], dist0[:], float(512 - d_), BIG, ALU.is_ge, ALU.mult, [dist0.t], [mk[0].t], eng="pool")
            stt(tl_[:], dist0[:], float(d_), mk[0][:], ALU.add, ALU.add, [dist0.t, mk[0].t], [tl_.t])
            dmw[d_] = tl_
        scr = A.tile([128, 32], F32, "scr")
        scr2 = A.tile([128, 32], F32, "scr2")
        m8a = A.tile([128, 8], F32, "m8a")
        m8b = A.tile([128, 8], F32, "m8b")
        selm = A.tile([128, 32], F32, "selm")
        for g in range(2):
            for t16 in range(16):
                tt(scr[:], impacc[:, g, t16, :], validm[:, t16, :], ALU.mult, [impacc.t, validm.t], [scr.t])
                tt(scr[:], scr[:], VB[:, t16, :], ALU.add, [scr.t, VB.t], [scr.t])
                P.op("dve", lambda e: e.max(out=m8a[:], in_=scr[:]), [scr.t], [m8a.t])
                P.op("dve", lambda e: e.match_replace(out=scr2[:], in_to_replace=m8a[:], in_values=scr[:],
                                                      imm_value=-1e30), [m8a.t, scr.t], [scr2.t])
                P.op("dve", lambda e: e.max(out=m8b[:], in_=scr2[:]), [scr2.t], [m8b.t])
                ts(selm[:], scr[:], m8b[:, 7:8], None, ALU.is_ge, None, [scr.t, m8b.t], [selm.t])
                tt(selm[:], selm[:], validm[:, t16, :], ALU.mult, [selm.t, validm.t], [selm.t])
                ts(selm[:], selm[:], -1.0, -NEGM, ALU.add, ALU.mult, [selm.t], [selm.t])
                ptn = psum[t16 % 2]
                tr(ptn[0:32, 0:128], selm[:], ident_f[:], [selm.t, ident_f.t], [ptn.t])
                cp("act", negselT[:, g, t16 * 128:(t16 + 1) * 128], ptn[0:32, 0:128], [ptn.t], [negselT.t])

        mix32 = A.tile([128, 512], F32, "mix32")
        oc32 = A.tile([128, 512], F32, "oc32")
        gsb = A.tile([128, 512], F32, "gsb")
        mixon = A.tile([128, NTOK], BF16, "mixon")
        for hh in range(H):
            g, r = hh // 6, hh % 6
            sl = SLOPE[hh]
            P.dma("sp", lambda e, hh=hh: [e.dma_start(out=qh[:], in_=qT_d[hh])], 1, qh.t, reads=[qT_tok[hh]],
                  writes=[qh.t])
            ts(hb[:], kk16[:], -sl, None, ALU.mult, None, [kk16.t], [hb.t], eng="pool")
            for qt in range(4):
                t0 = qt * 512
                P.dma("sp", lambda e, hh=hh, t0=t0: [e.dma_start(out=oc32[:], in_=ocmp_d[hh, :, t0:t0 + 512])], 1, oc32.t,
                      reads=[ocmp_tok[hh]], writes=[oc32.t])
                pg = psum[7]
                mm(pg[:, :], sel36[0:36, hh * 3 + 0, :], g36[0:36, t0:t0 + 512], True, True, [sel36.t, g36.t], [pg.t])
                tt(mix32[:], oc32[:], pg[:], ALU.mult, [oc32.t, pg.t], [mix32.t])
                for br in (1, 2):
                    kb_lo = 0 if br == 1 else max(0, 4 * qt - 4)
                    kb_hi = 4 * qt + 3
                    po, pz = (psum[4], psum[5]) if br == 1 else (psum[0], psum[6])
                    for kb in range(kb_lo, kb_hi + 1):
                        delta = t0 - kb * 128
                        b_, m_, s_, p_ = bt[it % 2], mk[it % 2], sc[it % 2], pc[it % 2]
                        it += 1
                        if delta <= 0:
                            dm_, eb_ = dmb[delta], None
                        elif br == 2:
                            dm_, eb_ = dmw[delta], None
                        else:
                            dm_, eb_ = dist0, hb[:, delta // 128:delta // 128 + 1]
                        psc = psum[2 + it % 2]
                        kblk = (0 if br == 1 else 2) + g
                        mm(psc[:, :], kvs[:, kblk, kb * 128:(kb + 1) * 128], qh[:, t0:t0 + 512], True, br == 2,
                           [kvs.t, qh.t], [psc.t])
                        if br == 1:
                            mm(psc[:, :], eall[:, kb, :], negselT[:, g, t0:t0 + 512], False, True,
                               [eall.t, negselT.t], [psc.t])
                        stt(s_[:], dm_[:], -sl, psc[:], ALU.mult, ALU.add, [dm_.t, psc.t], [s_.t])
                        if eb_ is None:
                            act(p_[:], s_[:], AF.Exp, [s_.t], [p_.t])
                        else:
                            act(p_[:], s_[:], AF.Exp, [s_.t, hb.t], [p_.t], bias=eb_)
                        vblk = (0 if br == 1 else 2) + g
                        mm(po[:, :], vtm[:, kb, vblk * 128:(vblk + 1) * 128], p_[:], kb == kb_lo, kb == kb_hi,
                           [vtm.t, p_.t], [po.t])
                        mm(pz[:, :], ones_b[:], p_[:], kb == kb_lo, kb == kb_hi, [ones_b.t, p_.t], [pz.t])
                    ts(rzt[:], pz[:], tiny, None, ALU.max, None, [pz.t], [rzt.t])
                    P.op("dve", lambda e: e.reciprocal(rzt[:], rzt[:]), [rzt.t], [rzt.t])
                    tt(o32[:], po[:], rzt[:], ALU.mult, [po.t, rzt.t], [o32.t])
                    pg = psum[7]
                    mm(pg[:, :], sel36[0:36, hh * 3 + br, :], g36[0:36, t0:t0 + 512], True, True, [sel36.t, g36.t], [pg.t])
                    tt(gsb[:], o32[:], pg[:], ALU.mult, [o32.t, pg.t], [gsb.t])
                    if br == 1:
                        tt(mix32[:], mix32[:], gsb[:], ALU.add, [mix32.t, gsb.t], [mix32.t])
                    else:
                        tt(mixon[:, t0:t0 + 512], mix32[:], gsb[:], ALU.add, [mix32.t, gsb.t], [mixon.t])
            P.dma("pool", lambda e, hh=hh: [e.dma_start(out=mixT_d[hh, :, 0:T], in_=mixon[:, 0:T])], 1, mixon.t,
                  reads=[mixon.t], writes=[mix_tok[hh]])

    def phase_S():
        tiny = 1e-30
        ckv_rows = ckv.rearrange("n p (h r) g d -> (n p h) (r g d)", h=2)
        w1b2 = A.tile([128, 2, 32, 128], BF16, "w1b2")
        w2b = A.tile([128, 2, 128], BF16, "w2bs")
        pe32 = A.tile([128, 2, 32], F32, "pe32s")
        peb = A.tile([128, 2, 32], BF16, "pebs")
        preb = A.tile([128, 2], F32, "prebs")
        dma_in(pe32[:], pe32.t, cmp_pos.rearrange("k j d -> d k j"), nonc=True)
        cp("pool", peb[:], pe32[:], [pe32.t], [peb.t])
        for kv in range(2):
            load_cast(w2b[:, kv, :], w2b.t, w_cmp2[kv], [128, 128])
            for hf in range(2):
                load_cast(w1b2[:, kv, hf * 16:(hf + 1) * 16, :], w1b2.t,
                          w_cmp1[kv, hf * 2048:(hf + 1) * 2048, :].rearrange("(j d) h -> d j h", d=128), [128, 16, 128])
            pp = psum[2]
            for js in range(32):
                mm(pp[:, 0:1], w1b2[:, kv, js, :], peb[:, kv, js:js + 1], js == 0, js == 31, [w1b2.t, peb.t], [pp.t])
            cp("dve", preb[:, kv:kv + 1], pp[:, 0:1], [pp.t], [preb.t])
        qs = A.tile([128, H, NS * TS], BF16, "qs")
        P.dma("sp", lambda e: [e.dma_start(out=qs[:], in_=qT_d[:, :, T:NTOK].rearrange("h p t -> p h t"))], 1, qs.t,
              reads=qT_tok, writes=[qs.t])
        gbs = A.tile([128, 3, H, NS * TS], F32, "gbs")
        for br in range(3):
            pg = psum[br]
            for hh in range(H):
                mm(pg[:, hh * 16:(hh + 1) * 16], sel36[0:36, hh * 3 + br, :], g36s[0:36, :], True, True,
                   [sel36.t, g36s.t], [pg.t])
            cp("act", gbs[:, br, :, :], pg[:, 0:H * 16].rearrange("p (h t) -> p h t", t=16), [pg.t], [gbs.t])

        def qsg(b_, g):
            return qs[:, g * 6:(g + 1) * 6, b_ * TS:(b_ + 1) * TS].rearrange("p r t -> p t r")

        ptb = A.tile([128, NS * 64], I32, "ptb")
        dma_in(ptb[:], ptb.t, ptab.rearrange("b n -> (b n)").partition_broadcast(128))
        piota = A.tile([128, NS * 64], I32, "piota")
        P.op("pool", lambda e: e.iota(piota[:], pattern=[[0, NS * 64]], base=0, channel_multiplier=1), (), [piota.t])
        idx_all = A.tile([128, NS * 64], I32, "idx_all")
        stt(idx_all[:], ptb[:], 128.0, piota[:], ALU.mult, ALU.add, [ptb.t, piota.t], [idx_all.t])
        idx_h = [A.tile([128, NS * 64], I32, "idx_h%d" % i) for i in range(2)]
        ts(idx_h[0][:], idx_all[:], 2.0, None, ALU.mult, None, [idx_all.t], [idx_h[0].t])
        ts(idx_h[1][:], idx_all[:], 2.0, 1.0, ALU.mult, ALU.add, [idx_all.t], [idx_h[1].t])
        it32 = A.tile([128, 16], I32, "it32")
        dcf = A.tile([128, 16], F32, "dcf")
        bcs = A.tile([128, 2, 4, 4, 6], F32, "bcs")
        P.op("pool", lambda e: e.iota(it32[:], pattern=[[2048, 4], [-1, 4]], base=31 - 8192, channel_multiplier=16),
             (), [it32.t])
        cp("pool", dcf[:], it32[:], [it32.t], [dcf.t])
        for hh in range(H):
            g, r = hh // 6, hh % 6
            ts(bcs[:, g, :, :, r], dcf[:].rearrange("p (i t) -> p i t", t=4), SLOPE[hh], None, ALU.mult, None,
               [dcf.t], [bcs.t], eng="pool")
        for g in range(2):
            P.op("pool", lambda e, g=g: e.affine_select(out=bcs[:, g, 3, :, :], in_=bcs[:, g, 3, :, :],
                                                        pattern=[[0, 4], [0, 6]], compare_op=ALU.is_ge, fill=NEGM,
                                                        base=126, channel_multiplier=-1), [bcs.t], [bcs.t])
        dsf = A.tile([128, 4], F32, "dsf")
        bS0 = A.tile([128, 2, 4, 6], F32, "bS0")
        slS = A.tile([128, 2, 4, 6], F32, "slS")
        P.op("pool", lambda e: e.iota(it32[:, 0:4], pattern=[[-1, 4]], base=-8192, channel_multiplier=1), [dcf.t], [it32.t])
        cp("pool", dsf[:], it32[:, 0:4], [it32.t], [dsf.t])
        for hh in range(H):
            g, r = hh // 6, hh % 6
            ts(bS0[:, g, :, r], dsf[:], SLOPE[hh], None, ALU.mult, None, [dsf.t], [bS0.t], eng="pool")
            memset("pool", slS[:, g, :, r], SLOPE[hh] * 128.0, [slS.t])
        bSall = A.tile([128, 64, 48], F32, "bSall")
        for pg_ in range(64):
            stt(bSall[:, pg_, :], slS[:].rearrange("p g t r -> p (g t r)"), float(pg_),
                bS0[:].rearrange("p g t r -> p (g t r)"), ALU.mult, ALU.add, [slS.t, bS0.t], [bSall.t])
        bN = A.tile([4, 2, 4, 6], F32, "bN")
        dnf = A.tile([4, 4], F32, "dnf")
        P.op("pool", lambda e: e.iota(it32[0:4, 0:4], pattern=[[-1, 4]], base=0, channel_multiplier=1), [dsf.t], [it32.t])
        cp("pool", dnf[:], it32[0:4, 0:4], [it32.t], [dnf.t])
        for hh in range(H):
            g, r = hh // 6, hh % 6
            ts(bN[:, g, :, r], dnf[:], SLOPE[hh], None, ALU.mult, None, [dnf.t], [bN.t], eng="pool")
        P.op("pool", lambda e: e.affine_select(out=bN[:], in_=bN[:], pattern=[[0, 2], [1, 4], [0, 6]],
                                               compare_op=ALU.is_ge, fill=NEGM, base=0, channel_multiplier=-1),
             [bN.t], [bN.t])
        bW = A.tile([128, 4, 2, 4, 6], F32, "bW")
        dwf = A.tile([128, 16], F32, "dwf")
        P.op("pool", lambda e: e.iota(it32[:], pattern=[[128, 4], [-1, 4]], base=-512, channel_multiplier=1), [dnf.t],
             [it32.t])
        cp("pool", dwf[:], it32[:], [it32.t], [dwf.t])
        for hh in range(H):
            g, r = hh // 6, hh % 6
            ts(bW[:, :, g, :, r], dwf[:].rearrange("p (w t) -> p w t", t=4), SLOPE[hh], None, ALU.mult, None,
               [dwf.t], [bW.t], eng="pool")
        for g in range(2):
            P.op("pool", lambda e, g=g: e.affine_select(out=bW[:, :, g, :, :], in_=bW[:, :, g, :, :],
                                                        pattern=[[128, 4], [-1, 4], [0, 6]], compare_op=ALU.is_ge,
                                                        fill=NEGM, base=-1, channel_multiplier=1), [bW.t], [bW.t])
        rselT = A.tile([4, 2, 4, 6], BF16, "rselT")
        memset("pool", rselT[:], 1.0, [rselT.t])
        P.op("pool", lambda e: e.affine_select(out=rselT[:], in_=rselT[:], pattern=[[0, 2], [1, 4], [0, 6]],
                                               compare_op=ALU.is_equal, fill=0.0, base=0, channel_multiplier=-1),
             [rselT.t], [rselT.t])
        rself = A.tile([24, 4], F32, "rself")
        memset("pool", rself[:], 1.0, [rself.t])
        P.op("pool", lambda e: e.affine_select(out=rself[:], in_=rself[:], pattern=[[-6, 4]], compare_op=ALU.is_ge,
                                               fill=0.0, base=0, channel_multiplier=1), [rself.t], [rself.t])
        P.op("pool", lambda e: e.affine_select(out=rself[:], in_=rself[:], pattern=[[6, 4]], compare_op=ALU.is_ge,
                                               fill=0.0, base=5, channel_multiplier=-1), [rself.t], [rself.t])
        maug_s = A.tile([128, 4, 130], BF16, "maug_s")
        memset("pool", maug_s[:], 1.0, [maug_s.t])
        P.op("pool", lambda e: e.affine_select(out=maug_s[:], in_=maug_s[:], pattern=[[128, 4], [-4, 130]],
                                               compare_op=ALU.is_ge, fill=0.0, base=1, channel_multiplier=1),
             [maug_s.t], [maug_s.t])
        P.op("pool", lambda e: e.affine_select(out=maug_s[:], in_=maug_s[:], pattern=[[-128, 4], [4, 130]],
                                               compare_op=ALU.is_ge, fill=0.0, base=3, channel_multiplier=-1),
             [maug_s.t], [maug_s.t])
        memset("pool", maug_s[:, :, 129:130], 1.0, [maug_s.t])
        VBs = A.tile([4, 129], F32, "VBs")
        memset("pool", VBs[:], 0.0, [VBs.t])
        memset("pool", VBs[:, 0:1], 1e4, [VBs.t])
        memset("pool", VBs[:, 127:129], 1e4, [VBs.t])

        kcmpT = A.tile([128, 4, 8192], BF16, "kcmpT")
        nse_h = nc.alloc_sbuf_tensor_at("nse_alias", [4, 2, 129 * 64], BF16, offset=int(kcmpT.h.manual_sbuf_range[0]))
        nse = Tile(nse_h, kcmpT.t)
        gt2 = [A.tile([128, 512], F32, "gt%d" % i) for i in range(2)]
        gb2 = [A.tile([128, 512], BF16, "gb%d" % i) for i in range(2)]
        ksT2 = [A.tile([128, 2, 128], BF16, "ksT%d" % i) for i in range(2)]
        kcT_s = A.tile([128, 2, 512], BF16, "kcT_s")
        vc_s = A.tile([128, 4, 2, 128], BF16, "vc_s")
        memset("pool", kcT_s[:], 0.0, [kcT_s.t])
        memset("pool", vc_s[:], 0.0, [vc_s.t])
        tg = A.tile([128, 512], F32, "tgs")
        ug = A.tile([128, 512], F32, "ugs")
        gl = A.tile([128, 512], BF16, "gls")
        s192 = A.tile([128, 192], F32, "s192")
        p192 = A.tile([128, 192], BF16, "p192")
        s48 = [A.tile([128, 48], F32, "s48_%d" % i) for i in range(2)]
        p48 = [A.tile([128, 48], BF16, "p48_%d" % i) for i in range(2)]
        rz48 = A.tile([128, 48], F32, "rz48")
        o48 = A.tile([128, 2, 4, 6], F32, "o48")
        acc48 = A.tile([128, 2, 4, 6], F32, "acc48")
        ps_s = A.tile([24, 2, 130], F32, "ps_s")
        zr_s = A.tile([24, 2], F32, "zr_s")
        pn_s = A.tile([24, 2, 129], F32, "pn_s")
        scs = A.tile([4, 2, 129], F32, "scs")
        scs2 = A.tile([4, 129], F32, "scs2")
        m8a = A.tile([4, 8], F32, "m8as")
        m8b = A.tile([4, 8], F32, "m8bs")
        sels = A.tile([4, 129], F32, "sels")
        knew = A.tile([128, 4, TS], BF16, "knew")
        vnew = A.tile([4, 512], BF16, "vnew")
        mixs = A.tile([128, H, NS * TS], BF16, "mixs")
        gi = 0

        def gather(b_, pg_, c0):
            nonlocal gi
            gt, gb = gt2[gi % 2], gb2[gi % 2]
            gi += 1
            col = b_ * 64 + pg_
            ih = idx_h[c0 // 512]
            P.dma("pool", lambda e: [e.indirect_dma_start(
                out=gt[:, :], out_offset=None, in_=ckv_rows[:, :],
                in_offset=bass.IndirectOffsetOnAxis(ap=ih[:, col:col + 1], axis=0))], 1, gt.t,
                reads=[ih.t], writes=[gt.t])
            cp("dve" if gi % 2 else "act", gb[:], gt[:], [gt.t], [gb.t])
            return gb

        def finish_branch(po, pz, br, b_, first):
            ts(rz48[:], pz[:, 0:48], tiny, None, ALU.max, None, [pz.t], [rz48.t])
            P.op("dve", lambda e: e.reciprocal(rz48[:], rz48[:]), [rz48.t], [rz48.t])
            o48f = o48[:].rearrange("p g t r -> p (g t r)")
            tt(o48f, po[:, 0:48], rz48[:], ALU.mult, [po.t, rz48.t], [o48.t])
            gview = gbs[:, br, :, b_ * TS:(b_ + 1) * TS].rearrange("p (g r) t -> p g t r", g=2)
            if first:
                tt(acc48[:], o48[:], gview, ALU.mult, [o48.t, gbs.t], [acc48.t])
            else:
                tt(o48[:], o48[:], gview, ALU.mult, [o48.t, gbs.t], [o48.t])
                tt(acc48[:], acc48[:], o48[:], ALU.add, [acc48.t, o48.t], [acc48.t])

        def new_tile(b_, kf0, vcol0, po, pz):
            P.dma("sp", lambda e: [e.dma_start(out=knew[:, 0:2, :],
                                               in_=kvT_d[kf0:kf0 + 2, :, T + b_ * TS:T + (b_ + 1) * TS].rearrange("f p t -> p f t"))],
                  1, knew.t, reads=[kvT_tok], writes=[knew.t])
            P.dma("sp", lambda e: [e.dma_start(out=vnew[:, :], in_=vtm_d[16, b_ * TS:(b_ + 1) * TS, :])], 1, vnew.t,
                  reads=[vtm_tok], writes=[vnew.t])
            psn = psum[3]
            for g in range(2):
                mm(psn[0:4, g * 24:(g + 1) * 24], knew[:, g, :], qsg(b_, g), True, True, [knew.t, qs.t], [psn.t])
            s_, p_ = s48[0], p48[0]
            tt(s_[0:4, :], psn[0:4, 0:48], bN[:].rearrange("p g t r -> p (g t r)"), ALU.add, [psn.t, bN.t], [s_.t])
            act(p_[0:4, :], s_[0:4, :], AF.Exp, [s_.t], [p_.t])
            for g in range(2):
                mm(po[:, g * 24:(g + 1) * 24], vnew[0:4, vcol0 + g * 128:vcol0 + (g + 1) * 128], p_[0:4, g * 24:(g + 1) * 24],
                   False, True, [vnew.t, p_.t], [po.t])
            mm(pz[:, 0:48], ones_b[0:4, :], p_[0:4, 0:48], False, True, [ones_b.t, p_.t], [pz.t])

        for b_ in range(NS):
            for pg_ in range(64):
                gb = gather(b_, pg_, 0)
                pt = psum[pg_ % 2]
                pv = psb(pg_ % 2)
                for k in range(4):
                    tr(pv[:, k * 128:(k + 1) * 128], gb[:, k * 128:(k + 1) * 128], ident_b[:], [gb.t, ident_b.t], [pt.t])
                cp("act" if pg_ % 2 else "dve", kcmpT[:, :, pg_ * 128:(pg_ + 1) * 128],
                   pv[:, 0:512].rearrange("p (k t) -> p k t", k=4), [pt.t], [kcmpT.t])
            for kv in range(2):
                for g in range(2):
                    kview = kcmpT[:, kv * 2 + g, :].rearrange("p (c s) -> p c s", s=16)
                    pp = psum[2]
                    for js in range(32):
                        j_, s_i = js // 16, js % 16
                        mm(pp[:, 0:511], w1b2[:, kv, js, :], kview[:, j_:j_ + 511, s_i], js == 0, js == 31,
                           [w1b2.t, kcmpT.t], [pp.t])
                    act(tg[:, 0:511], pp[:, 0:511], AF.Identity, [pp.t, preb.t], [tg.t], bias=preb[:, kv:kv + 1])
                    act(ug[:, 0:511], tg[:, 0:511], AF.Square, [tg.t], [ug.t])
                    ts(ug[:, 0:511], ug[:, 0:511], 0.044715, 1.0, ALU.mult, ALU.add, [ug.t], [ug.t])
                    tt(ug[:, 0:511], ug[:, 0:511], tg[:, 0:511], ALU.mult, [ug.t, tg.t], [ug.t])
                    act(ug[:, 0:511], ug[:, 0:511], AF.Sigmoid, [ug.t], [ug.t], scale=GC)
                    tt(gl[:, 0:511], ug[:, 0:511], tg[:, 0:511], ALU.mult, [ug.t, tg.t], [gl.t])
                    pq_ = psum[3]
                    if kv == 0:
                        mm(pq_[:, 0:511], w2b[:, 0, :], gl[:, 0:511], True, True, [w2b.t, gl.t], [pq_.t])
                        cp("act", kcT_s[:, g, 0:511], pq_[:, 0:511], [pq_.t], [kcT_s.t])
                    else:
                        for it_ in range(4):
                            n_i = 128 if it_ < 3 else 127
                            mm(pq_[0:n_i, it_ * 128:(it_ + 1) * 128], gl[:, it_ * 128:it_ * 128 + n_i], w2b[:, 1, :],
                               True, True, [w2b.t, gl.t], [pq_.t])
                        for it_ in range(4):
                            n_i = 128 if it_ < 3 else 127
                            cp("act", vc_s[0:n_i, it_, g, :], pq_[0:n_i, it_ * 128:(it_ + 1) * 128], [pq_.t], [vc_s.t])
            psc = psum[4]
            for g in range(2):
                for it_ in range(4):
                    c_ = (g * 4 + it_) * 24
                    mm(psc[:, c_:c_ + 24], kcT_s[:, g, it_ * 128:(it_ + 1) * 128], qsg(b_, g), True, True,
                       [kcT_s.t, qs.t], [psc.t])
            tt(s192[:], psc[:, 0:192], bcs[:].rearrange("p g i t r -> p (g i t r)"), ALU.add, [psc.t, bcs.t], [s192.t])
            act(p192[:], s192[:], AF.Exp, [s192.t], [p192.t])
            po, pz, pp = psum[5], psum[6], psum[7]
            for g in range(2):
                for it_ in range(4):
                    c_ = (g * 4 + it_) * 24
                    mm(po[:, g * 24:(g + 1) * 24], vc_s[:, it_, g, :], p192[:, c_:c_ + 24], it_ == 0, it_ == 3,
                       [vc_s.t, p192.t], [po.t])
                for it_ in range(4):
                    c_ = (g * 4 + it_) * 24
                    mm(pz[:, g * 24:(g + 1) * 24], ones_b[:], p192[:, c_:c_ + 24], it_ == 0, it_ == 3,
                       [ones_b.t, p192.t], [pz.t])
                for it_ in range(4):
                    c_ = (g * 4 + it_) * 24
                    mm(pp[0:24, g * 130:(g + 1) * 130], p192[:, c_:c_ + 24], maug_s[:, it_, :], it_ == 0, it_ == 3,
                       [p192.t, maug_s.t], [pp.t])
            finish_branch(po, pz, 0, b_, True)
            cp("dve", ps_s[:], pp[0:24, 0:260].rearrange("p (g j) -> p g j", g=2), [pp.t], [ps_s.t])
            ts(zr_s[:], ps_s[:, :, 129], tiny, None, ALU.max, None, [ps_s.t], [zr_s.t])
            P.op("dve", lambda e: e.reciprocal(zr_s[:], zr_s[:]), [zr_s.t], [zr_s.t])
            for g in range(2):
                ts(pn_s[:, g, :], ps_s[:, g, 0:129], zr_s[:, g:g + 1], None, ALU.mult, None, [ps_s.t, zr_s.t], [pn_s.t])
            pi_ = psum[3]
            for g in range(2):
                mm(pi_[0:4, g * 129:(g + 1) * 129], rself[:], pn_s[:, g, :], True, True, [rself.t, pn_s.t], [pi_.t])
            for g in range(2):
                tt(scs[:, g, :], pi_[0:4, g * 129:(g + 1) * 129], VBs[:], ALU.add, [pi_.t, VBs.t], [scs.t])
            for g in range(2):
                P.op("dve", lambda e, g=g: e.max(out=m8a[:], in_=scs[:, g, :]), [scs.t], [m8a.t])
                P.op("dve", lambda e, g=g: e.match_replace(out=scs2[:], in_to_replace=m8a[:], in_values=scs[:, g, :],
                                                           imm_value=-1e30), [m8a.t, scs.t], [scs2.t])
                P.op("dve", lambda e: e.max(out=m8b[:], in_=scs2[:]), [scs2.t], [m8b.t])
                ts(sels[:], scs[:, g, :], m8b[:, 7:8], None, ALU.is_ge, None, [scs.t, m8b.t], [sels.t])
                ts(sels[:], sels[:], -1.0, -NEGM, ALU.add, ALU.mult, [sels.t], [sels.t])
                cp("dve", nse[:, g, :].rearrange("p (j k) -> p j k", k=64),
                   sels[:].unsqueeze(2).to_broadcast([4, 129, 64]), [sels.t], [nse.t])
            po, pz = psum[5], psum[6]
            for pg_ in range(64):
                gb = gather(b_, pg_, 512)
                pt = psum[pg_ % 2]
                pv = psb(pg_ % 2)
                ks = ksT2[pg_ % 2]
                for k in range(2):
                    tr(pv[:, k * 128:(k + 1) * 128], gb[:, k * 128:(k + 1) * 128], ident_b[:], [gb.t, ident_b.t], [pt.t])
                cp("act", ks[:], pv[:, 0:256].rearrange("p (k t) -> p k t", k=2), [pt.t], [ks.t])
                psc = psum[2 + pg_ % 2]
                for g in range(2):
                    mm(psc[:, g * 24:(g + 1) * 24], ks[:, g, :], qsg(b_, g), True, False, [ks.t, qs.t], [psc.t])
                    mm(psc[:, g * 24:(g + 1) * 24], nse[0:4, g, pg_ * 128:(pg_ + 1) * 128],
                       rselT[0:4, g, :, :].rearrange("p t r -> p (t r)"), False, True, [nse.t, rselT.t], [psc.t])
                s_, p_ = s48[pg_ % 2], p48[pg_ % 2]
                tt(s_[:], psc[:, 0:48], bSall[:, pg_, :], ALU.add, [psc.t, bSall.t], [s_.t])
                act(p_[:], s_[:], AF.Exp, [s_.t], [p_.t])
                for g in range(2):
                    mm(po[:, g * 24:(g + 1) * 24], gb[:, 256 + g * 128:256 + (g + 1) * 128], p_[:, g * 24:(g + 1) * 24],
                       pg_ == 0, False, [gb.t, p_.t], [po.t])
                mm(pz[:, 0:48], ones_b[:], p_[:, 0:48], pg_ == 0, False, [ones_b.t, p_.t], [pz.t])
            new_tile(b_, 4, 0, po, pz)
            finish_branch(po, pz, 1, b_, False)
            for wt in range(4):
                gt, gb = gt2[gi % 2], gb2[gi % 2]
                gi += 1
                dma_in(gt[:], gt.t, cwin[b_, wt * 128:(wt + 1) * 128].rearrange("s a g d -> s (a g d)"))
                cp("dve", gb[:], gt[:], [gt.t], [gb.t])
                pt = psum[wt % 2]
                pv = psb(wt % 2)
                ks = ksT2[wt % 2]
                for k in range(2):
                    tr(pv[:, k * 128:(k + 1) * 128], gb[:, k * 128:(k + 1) * 128], ident_b[:], [gb.t, ident_b.t], [pt.t])
                cp("act", ks[:], pv[:, 0:256].rearrange("p (k t) -> p k t", k=2), [pt.t], [ks.t])
                psc = psum[2 + wt % 2]
                for g in range(2):
                    mm(psc[:, g * 24:(g + 1) * 24], ks[:, g, :], qsg(b_, g), True, True, [ks.t, qs.t], [psc.t])
                s_, p_ = s48[wt % 2], p48[wt % 2]
                tt(s_[:], psc[:, 0:48], bW[:, wt].rearrange("p g t r -> p (g t r)"), ALU.add, [psc.t, bW.t], [s_.t])
                act(p_[:], s_[:], AF.Exp, [s_.t], [p_.t])
                for g in range(2):
                    mm(po[:, g * 24:(g + 1) * 24], gb[:, 256 + g * 128:256 + (g + 1) * 128], p_[:, g * 24:(g + 1) * 24],
                       wt == 0, False, [gb.t, p_.t], [po.t])
                mm(pz[:, 0:48], ones_b[:], p_[:, 0:48], wt == 0, False, [ones_b.t, p_.t], [pz.t])
            new_tile(b_, 6, 256, po, pz)
            finish_branch(po, pz, 2, b_, False)
            cp("dve", mixs[:, :, b_ * TS:(b_ + 1) * TS].rearrange("p (g r) t -> p g t r", g=2), acc48[:],
               [acc48.t], [mixs.t])
        P.dma("pool", lambda e: [e.dma_start(out=mixT_d[0:H, :, T:NTOK].rearrange("h p t -> p h t"), in_=mixs[:])], 1,
              mixs.t, reads=[mixs.t], writes=mix_tok[0:H])

    def mem_attn_l1(xnT1):
        nonlocal xnT
        xnT = xnT1
        mem_attn(1, lambda hm: w_in_b[0, :, W + 36 + hm * 128:W + 36 + (hm + 1) * 128].rearrange("(k p) c -> p k c", p=128))

    if 'N' in PH:
        phase_M(1)
        sel36 = A.tile([36, 36, 128], BF16, "sel36")
        g36s = A.tile([36, NS * TS], BF16, "g36s")
        mN = A.mark()
        phase_N()
        A.release(mN)
        if 'S' in PH:
            phase_S()
            A.release(mN)
        phase_B(1)
        phase_C(1)

    P.emit()
    print("SBUF peak", A.peak, "ops", {e: len(P.ops[e]) for e in P.ENGS}, "signals",
          {e: sum(1 for o in P.ops[e] if o.signal and not o.ndma) for e in P.ENGS},
          "max dma val", max(16 * t.cnt for t in P.slots), "slots", len(P.slots))
    return nc


_NC_CACHE = {}


def kernel(x_prompt, x_sample, mem_prompt, state_hgrn, cache_conv, cache_mem, cache_kv, cache_win,
           page_table, norm_gains, w_in_a, lb_logits, hgrn_norm, w_in_b, w_o, w_mem_kv, kv_norm,
           w_kv_b, cmp_pos, w_cmp1, w_cmp2, w_ffn_in, w_ffn_conv, b_ffn_conv, w_ffn_out):
    f = lambda a: np.ascontiguousarray(np.asarray(a, dtype=np.float32))
    if "nc" not in _NC_CACHE:
        _NC_CACHE["nc"] = build()
    nc = _NC_CACHE["nc"]
    x_prompt = f(x_prompt); x_sample = f(x_sample); mem_prompt = f(mem_prompt)
    state_hgrn = f(state_hgrn); cache_mem = f(cache_mem)
    cache_conv = f(cache_conv)
    cache_kv_f = f(cache_kv)
    cache_win_f = f(cache_win)
    page_table_i = np.ascontiguousarray(np.asarray(page_table, dtype=np.int32))
    shared = {
        "norm_gains": f(norm_gains), "w_in_a": f(w_in_a), "lb_logits": f(lb_logits), "hgrn_norm": f(hgrn_norm),
        "w_o": f(w_o), "w_mem_kv": f(w_mem_kv), "kv_norm": f(kv_norm), "w_kv_b": f(w_kv_b),
        "w_in_b": f(w_in_b), "cmp_pos": f(cmp_pos), "w_cmp1": f(w_cmp1), "w_cmp2": f(w_cmp2),
        "w_ffn_in": f(w_ffn_in), "w_ffn_conv": f(w_ffn_conv), "b_ffn_conv": f(b_ffn_conv), "w_ffn_out": f(w_ffn_out),
    }
    in_maps = []
    for c in range(NCORES):
        s0, s1 = NS * c, NS * (c + 1)
        m = {
            "xp": x_prompt[c],
            "xs": x_sample[s0:s1].reshape(NS * TS, D),
            "memp": mem_prompt[c],
            "st_in": state_hgrn[0, s0:s1],
            "cmem": np.ascontiguousarray(cache_mem[:, s0:s1]),
            "cconv": np.ascontiguousarray(cache_conv[:, s0:s1]),
            "ckv": cache_kv_f,
            "cwin": np.ascontiguousarray(cache_win_f[s0:s1]),
            "ptab": np.ascontiguousarray(page_table_i[s0:s1]),
        }
        m.update(shared)
        in_maps.append(m)
    res = run_bass_kernel_spmd(nc, in_maps, core_ids=list(range(NCORES)))
    R = res.results
    B = 8
    SB = 32
    cat = lambda k, ax=0: np.concatenate([R[c][k] for c in range(NCORES)], axis=ax)
    stk = lambda k, ax=0: np.stack([R[c][k] for c in range(NCORES)], axis=ax)
    y_prompt = stk("o_yp")
    y_sample = cat("o_ys").reshape(SB, TS, D)
    hg_p = np.stack([R[c]["o_hgp"] for c in range(NCORES)], axis=0)[None]
    hg_s = np.concatenate([R[c]["o_hgs"] for c in range(NCORES)], axis=0)[None]
    cv_p = stk("o_cvp", 1)
    cv_s = cat("o_cvs", 1)
    nm = np.stack([R[c]["o_nm"] for c in range(NCORES)], axis=1).reshape(2, B, MEM, 2, 4, 128)
    kv_p = stk("o_kvp").reshape(B, T // 128, 128, 4, 2, 128)
    kv_s = cat("o_kvs").reshape(SB, TS, 4, 2, 128)
    win_p = stk("o_winp").reshape(B, 512, 2, 2, 128)
    win_s = cat("o_wins").reshape(SB, TS, 2, 2, 128)
    return (y_prompt, y_sample, hg_p, hg_s, cv_p, cv_s, nm, kv_p, kv_s, win_p, win_s)
```
